# Optimizing a Trainium2 kernel written in Bass

```python
import numpy as np
import jax
import jax.numpy as jnp
from jax import lax

D_MODEL = 1024
BATCH = 4
SEQ = 4096
DEPTH = 4
DEC_BATCH = 16
DEC_SEQ = 4096
PAST_LEN = 128

N_MIXERS = 4
BRANCH = 1024
PLE_DIM = 256
GRID_W = 64
EPS = 1e-6

FN_GROUPS = 8
FN_GDIM = BRANCH // FN_GROUPS

NA_HEADS = 32
NA_HDIM = BRANCH // NA_HEADS
NA_KH_MAX = 8
NA_KW = 16
NA_QCB = 16
NA_KCB = 32

MLA_HEADS = 8
MLA_NOPE = 128
MLA_ROPE = 64
MLA_V = BRANCH // MLA_HEADS
MLA_Q_LORA = 384
MLA_KV_LORA = 256
MLA_QBLOCK = 128
ROPE_THETA = 10000.0

HG_EXPAND = 128
HG_HEADS = BRANCH // HG_EXPAND
HG_DV = BRANCH // HG_HEADS
HG_CHUNK = 64

kernel_name = "hybrid_bidir_fnet_natten_mla_hgrn2"


def rmsnorm(x, g):
    xf = x.astype(jnp.float32)
    y = xf * lax.rsqrt(jnp.mean(xf * xf, axis=-1, keepdims=True) + EPS)
    return (y * g.astype(jnp.float32)).astype(x.dtype)


def fourier_mixer(h, w_in, w_mix, w_out):
    b, s, _ = h.shape
    u, z = jnp.split(h @ w_in, 2, axis=-1)
    u = u.reshape(b, s, FN_GROUPS, FN_GDIM).astype(jnp.float32)
    f = jnp.fft.fft2(u, axes=(1, 3), norm="ortho").real.astype(h.dtype)
    y = jnp.einsum('bsgc,gcd->bsgd', f, w_mix).reshape(b, s, BRANCH)
    return (y * jax.nn.silu(z)) @ w_out


def na_tables(rows):
    kh = min(NA_KH_MAX, rows)
    r = np.arange(rows)
    row_start = np.clip(r - kh // 2, 0, rows - kh)
    row_off = row_start[:, None] + np.arange(kh)[None, :] - r[:, None] + (NA_KH_MAX - 1)
    ncb = GRID_W // NA_QCB
    j = np.arange(ncb)
    kc_start = np.clip(j * NA_QCB - NA_KW // 2, 0, GRID_W - NA_KCB)
    qcol = j[:, None] * NA_QCB + np.arange(NA_QCB)[None, :]
    kcol = kc_start[:, None] + np.arange(NA_KCB)[None, :]
    win = np.clip(qcol - NA_KW // 2, 0, GRID_W - NA_KW)
    col_mask = (kcol[:, None, :] >= win[:, :, None]) & (kcol[:, None, :] < win[:, :, None] + NA_KW)
    col_off = np.clip(kcol[:, None, :] - qcol[:, :, None] + (NA_KW - 1), 0, 2 * NA_KW - 2)
    return kh, row_start, row_off, kcol, col_mask, col_off


def na_mixer(h, w_in, rpb, w_out):
    b, s, _ = h.shape
    rows = s // GRID_W
    kh, row_start, row_off, kcol, col_mask, col_off = na_tables(rows)
    ncb = GRID_W // NA_QCB
    q, k, v, z = jnp.split(h @ w_in, 4, axis=-1)
    grid = lambda t: t.reshape(b, rows, GRID_W, NA_HEADS, NA_HDIM)
    q = grid(q) * (NA_HDIM ** -0.5)
    k, v = grid(k), grid(v)
    qb = jnp.moveaxis(q.reshape(b, rows, ncb, NA_QCB, NA_HEADS, NA_HDIM), 1, 0)
    mask = jnp.asarray(col_mask)[:, :, None, :]
    col_off_j = jnp.asarray(col_off)

    def one_row(args):
        q_r, rs, ro = args
        k_rows = lax.dynamic_slice_in_dim(k, rs, kh, axis=1)
        v_rows = lax.dynamic_slice_in_dim(v, rs, kh, axis=1)
        k_blk = k_rows[:, :, kcol]
        v_blk = v_rows[:, :, kcol]
        sc = jnp.einsum('bjqhd,brjkhd->bhjqrk', q_r, k_blk).astype(jnp.float32)
        bias = rpb[:, ro, :][:, :, col_off_j]
        sc = sc + jnp.transpose(bias, (0, 2, 3, 1, 4)).astype(jnp.float32)
        sc = jnp.where(mask, sc, -jnp.inf)
        shp = sc.shape
        p = jax.nn.softmax(sc.reshape(shp[:-2] + (kh * NA_KCB,)), axis=-1).reshape(shp)
        return jnp.einsum('bhjqrk,brjkhd->bjqhd', p.astype(v.dtype), v_blk)

    o = lax.map(one_row, (qb, jnp.asarray(row_start, jnp.int32), jnp.asarray(row_off, jnp.int32)))
    o = jnp.moveaxis(o, 0, 1).reshape(b, s, BRANCH)
    return (o * jax.nn.silu(z)) @ w_out


def apply_rope(x, s):
    half = MLA_ROPE // 2
    inv = ROPE_THETA ** (-jnp.arange(half, dtype=jnp.float32) / half)
    ang = jnp.arange(s, dtype=jnp.float32)[:, None] * inv[None, :]
    ang = ang.reshape((1, s) + (1,) * (x.ndim - 3) + (half,))
    cos, sin = jnp.cos(ang), jnp.sin(ang)
    xf = x.astype(jnp.float32)
    x1, x2 = xf[..., :half], xf[..., half:]
    return jnp.concatenate([x1 * cos - x2 * sin, x1 * sin + x2 * cos], axis=-1).astype(x.dtype)


def mla_mixer(h, w_in, g_q, w_uq, g_kv, w_ukv, w_out):
    b, s, _ = h.shape
    c_q, c_kv, k_pe, z = jnp.split(h @ w_in, [MLA_Q_LORA, MLA_Q_LORA + MLA_KV_LORA,
                                              MLA_Q_LORA + MLA_KV_LORA + MLA_ROPE], axis=-1)
    scale = (MLA_NOPE + MLA_ROPE) ** -0.5
    q = (rmsnorm(c_q, g_q) @ w_uq).reshape(b, s, MLA_HEADS, MLA_NOPE + MLA_ROPE) * scale
    q_nope, q_pe = q[..., :MLA_NOPE], apply_rope(q[..., MLA_NOPE:], s)
    k_pe = apply_rope(k_pe, s)
    kv = (rmsnorm(c_kv, g_kv) @ w_ukv).reshape(b, s, MLA_HEADS, MLA_NOPE + MLA_V)
    k_nope, v = kv[..., :MLA_NOPE], kv[..., MLA_NOPE:]
    nblk = s // MLA_QBLOCK
    qn = jnp.moveaxis(q_nope.reshape(b, nblk, MLA_QBLOCK, MLA_HEADS, MLA_NOPE), 1, 0)
    qp = jnp.moveaxis(q_pe.reshape(b, nblk, MLA_QBLOCK, MLA_HEADS, MLA_ROPE), 1, 0)

    def block(args):
        qn_i, qp_i = args
        sc = (jnp.einsum('bqhd,bkhd->bhqk', qn_i, k_nope).astype(jnp.float32)
              + jnp.einsum('bqhr,bkr->bhqk', qp_i, k_pe).astype(jnp.float32))
        p = jax.nn.softmax(sc, axis=-1).astype(v.dtype)
        return jnp.einsum('bhqk,bkhd->bqhd', p, v)

    o = lax.map(block, (qn, qp))
    o = jnp.moveaxis(o, 0, 1).reshape(b, s, BRANCH)
    return (o * jax.nn.silu(z)) @ w_out


def gla_chunk_scan(q, k, v, logf):
    b, s, nh, dk = q.shape
    dv = v.shape[-1]
    n = s // HG_CHUNK
    to_chunks = lambda t: jnp.moveaxis(t.reshape(b, n, HG_CHUNK, nh, t.shape[-1]), 1, 0)
    causal = jnp.tril(jnp.ones((HG_CHUNK, HG_CHUNK), bool))[None, :, :, None, None]

    def step(S, inp):
        qc, kc, vc, gc = inp
        B = jnp.cumsum(gc, axis=1)
        decay = jnp.exp(jnp.where(causal, B[:, :, None] - B[:, None, :], -jnp.inf))
        A = jnp.einsum('bthc,btshc,bshc->bhts', qc, decay, kc)
        o = jnp.einsum('bhts,bshv->bthv', A, vc) + jnp.einsum('bthc,bhcv->bthv', qc * jnp.exp(B), S)
        BL = B[:, -1]
        S_new = jnp.exp(BL)[..., None] * S + jnp.einsum('bshc,bshv->bhcv', kc * jnp.exp(BL[:, None] - B), vc)
        return S_new, o

    S0 = jnp.zeros((b, nh, dk, dv), jnp.float32)
    _, o = lax.scan(step, S0, (to_chunks(q), to_chunks(k), to_chunks(v), to_chunks(logf)))
    return jnp.moveaxis(o, 0, 1).reshape(b, s, nh, dv)


def hgrn2_mixer(h, w_in, lower, g_out, w_out):
    b, s, _ = h.shape
    q, f_fw, f_bw, i, z = jnp.split(h @ w_in, 5, axis=-1)
    heads = lambda t: t.astype(jnp.float32).reshape(b, s, HG_HEADS, -1)
    q = jax.nn.silu(heads(q))
    i = heads(i)

    def direction(f_raw, lb):
        f = heads(lb + (1.0 - lb) * jax.nn.sigmoid(f_raw.astype(jnp.float32)))
        return jnp.log(f), 1.0 - f

    lf_f, k_f = direction(f_fw, lower[0])
    lf_b, k_b = direction(f_bw, lower[1])
    o_f = gla_chunk_scan(q, k_f, i, lf_f)
    flip = lambda t: jnp.flip(t, axis=1)
    o_b = flip(gla_chunk_scan(flip(q), flip(k_b), flip(i), flip(lf_b)))
    o = o_f + o_b
    o = o * lax.rsqrt(jnp.mean(o * o, axis=-1, keepdims=True) + EPS)
    o = (o.reshape(b, s, BRANCH) * g_out.astype(jnp.float32)).astype(h.dtype)
    return (o * jax.nn.silu(z)) @ w_out


def trunk(x, p, norm_g, fn_w_in, fn_w_mix, fn_w_out, na_w_in, na_rpb, na_w_out,
          mla_w_in, mla_g_q, mla_w_uq, mla_g_kv, mla_w_ukv, mla_w_out,
          hg_w_in, hg_lb_raw, hg_g_out, hg_w_out, ple_w, ple_gate_w, final_g):
    sm = jax.nn.softmax(hg_lb_raw.astype(jnp.float32), axis=0)
    lower = jnp.cumsum(sm, axis=0) - sm[0]
    for li in range(DEPTH):
        m, j = li % N_MIXERS, li // N_MIXERS
        h = rmsnorm(x, norm_g[li])
        if m == 0:
            y = fourier_mixer(h, fn_w_in[j], fn_w_mix[j], fn_w_out[j])
        elif m == 1:
            y = na_mixer(h, na_w_in[j], na_rpb[j], na_w_out[j])
        elif m == 2:
            y = mla_mixer(h, mla_w_in[j], mla_g_q[j], mla_w_uq[j], mla_g_kv[j], mla_w_ukv[j], mla_w_out[j])
        else:
            y = hgrn2_mixer(h, hg_w_in[j], lower[li], hg_g_out[j], hg_w_out[j])
        x = x + y
        x = x + jax.nn.sigmoid(x @ ple_gate_w[li]) * (p[li] @ ple_w[li])
    return rmsnorm(x, final_g)


def setup_inputs(seed: int = 0) -> dict:
    key = jax.random.key(seed)
    ks = iter(jax.random.split(key, 32))
    nrm = lambda shape, scale: jax.random.normal(next(ks), shape, jnp.float32) * scale
    gain = lambda shape: 1.0 + nrm(shape, 0.02)
    nA, nB, nC, nD = (len(range(m, DEPTH, N_MIXERS)) for m in range(N_MIXERS))
    D = D_MODEL
    return {
        "x_prompt": nrm((BATCH, SEQ, D), 1.0),
        "x_sample": nrm((DEC_BATCH, DEC_SEQ, D), 1.0),
        "p_prompt": nrm((DEPTH, BATCH, SEQ, PLE_DIM), 1.0),
        "p_sample": nrm((DEPTH, DEC_BATCH, DEC_SEQ, PLE_DIM), 1.0),
        "norm_g": gain((DEPTH, D)),
        "fn_w_in": nrm((nA, D, 2 * BRANCH), D ** -0.5),
        "fn_w_mix": nrm((nA, FN_GROUPS, FN_GDIM, FN_GDIM), FN_GDIM ** -0.5),
        "fn_w_out": nrm((nA, BRANCH, D), BRANCH ** -0.5),
        "na_w_in": nrm((nB, D, 4 * BRANCH), D ** -0.5),
        "na_rpb": nrm((nB, NA_HEADS, 2 * NA_KH_MAX - 1, 2 * NA_KW - 1), 0.1),
        "na_w_out": nrm((nB, BRANCH, D), BRANCH ** -0.5),
        "mla_w_in": nrm((nC, D, MLA_Q_LORA + MLA_KV_LORA + MLA_ROPE + BRANCH), D ** -0.5),
        "mla_g_q": gain((nC, MLA_Q_LORA)),
        "mla_w_uq": nrm((nC, MLA_Q_LORA, MLA_HEADS * (MLA_NOPE + MLA_ROPE)), MLA_Q_LORA ** -0.5),
        "mla_g_kv": gain((nC, MLA_KV_LORA)),
        "mla_w_ukv": nrm((nC, MLA_KV_LORA, MLA_HEADS * (MLA_NOPE + MLA_V)), MLA_KV_LORA ** -0.5),
        "mla_w_out": nrm((nC, BRANCH, D), BRANCH ** -0.5),
        "hg_w_in": nrm((nD, D, 5 * BRANCH), D ** -0.5),
        "hg_lb_raw": nrm((DEPTH, 2, BRANCH), 0.1),
        "hg_g_out": gain((nD, BRANCH)),
        "hg_w_out": nrm((nD, BRANCH, D), BRANCH ** -0.5),
        "ple_w": nrm((DEPTH, PLE_DIM, D), PLE_DIM ** -0.5),
        "ple_gate_w": nrm((DEPTH, D, D), D ** -0.5),
        "final_g": gain((D,)),
    }


def reference(x_prompt, x_sample, p_prompt, p_sample, norm_g, fn_w_in, fn_w_mix, fn_w_out,
              na_w_in, na_rpb, na_w_out, mla_w_in, mla_g_q, mla_w_uq, mla_g_kv, mla_w_ukv, mla_w_out,
              hg_w_in, hg_lb_raw, hg_g_out, hg_w_out, ple_w, ple_gate_w, final_g):
    y_prompt = trunk(x_prompt, p_prompt, norm_g, fn_w_in, fn_w_mix, fn_w_out, na_w_in, na_rpb, na_w_out,
                     mla_w_in, mla_g_q, mla_w_uq, mla_g_kv, mla_w_ukv, mla_w_out,
                     hg_w_in, hg_lb_raw, hg_g_out, hg_w_out, ple_w, ple_gate_w, final_g)
    y_sample = trunk(x_sample, p_sample, norm_g, fn_w_in, fn_w_mix, fn_w_out, na_w_in, na_rpb, na_w_out,
                     mla_w_in, mla_g_q, mla_w_uq, mla_g_kv, mla_w_ukv, mla_w_out,
                     hg_w_in, hg_lb_raw, hg_g_out, hg_w_out, ple_w, ple_gate_w, final_g)
    return (y_prompt, y_sample)
```

```python
import math
import numpy as np
from contextlib import ExitStack
import concourse.bass as bass
import concourse.mybir as mybir
from concourse.bass_utils import run_bass_kernel_spmd

F32 = mybir.dt.float32
BF16 = mybir.dt.bfloat16
AF = mybir.ActivationFunctionType
ALU = mybir.AluOpType
AX = mybir.AxisListType

ENGS = ("pe", "act", "dve", "pool", "sp")
S = 4096
D = 1024
NT = 8
TS = 512
EPS = 1e-6
NEG = -30000.0


class Res:
    __slots__ = ("name", "w", "rs")

    def __init__(self, name):
        self.name = name
        self.w = None
        self.rs = []


class Op:
    __slots__ = ("eng", "fn", "deps", "dma", "ticket", "inc")

    def __init__(self, eng, fn, dma):
        self.eng = eng
        self.fn = fn
        self.deps = []
        self.dma = dma
        self.ticket = None
        self.inc = dma


class Prog:
    DMA_K = 8
    DMA_CAP = 1500
    CMP_CAP = 30000

    def __init__(self, nc, stack):
        self.nc = nc
        self.stack = stack
        self.ops = []
        self.eobj = {"pe": nc.tensor, "act": nc.scalar, "dve": nc.vector, "pool": nc.gpsimd, "sp": nc.sync}
        self.dma_hist = {e: [] for e in ENGS}
        self.last = {e: None for e in ENGS}
        self.nres = 0

    def res(self, name=None):
        self.nres += 1
        return Res(name or f"r{self.nres}")

    def op(self, eng, fn, r=(), w=(), dma=False):
        o = Op(eng, fn, dma)
        deps = {}
        for x in r:
            if x.w is not None:
                deps[id(x.w)] = (x.w, "raw")
        for x in w:
            if x.w is not None and id(x.w) not in deps:
                deps[id(x.w)] = (x.w, "waw")
            for rd in x.rs:
                if id(rd) not in deps:
                    deps[id(rd)] = (rd, "war")
        if dma:
            h = self.dma_hist[eng]
            if len(h) >= self.DMA_K:
                p = h[-self.DMA_K]
                deps.setdefault(id(p), (p, "rot"))
            h.append(o)
        for d, kind in deps.values():
            if d is o:
                continue
            if (not d.dma) and (not dma) and d.eng == eng:
                if eng == "pe" or kind != "raw":
                    continue
            o.deps.append(d)
            d.inc = True
        for x in w:
            x.w = o
            x.rs = []
        for x in r:
            x.rs.append(o)
        self.ops.append(o)
        if not dma:
            self.last[eng] = o
        return o

    def barrier(self):
        deps = []
        for e in ENGS:
            if self.last[e] is not None:
                deps.append(self.last[e])
            deps.extend(self.dma_hist[e][-self.DMA_K:])
        for e in ENGS:
            o = Op(e, None, False)
            for d in deps:
                if (not d.dma) and d.eng == e:
                    continue
                o.deps.append(d)
                d.inc = True
            self.ops.append(o)

    def emit(self):
        nc = self.nc
        cmp_sem = {}
        cmp_cnt = {}
        dma_sems = {}
        dma_idx = {e: 0 for e in ENGS}
        known = {e: {} for e in ENGS}
        self.nsem = 0

        def newsem(nm):
            self.nsem += 1
            return self.stack.enter_context(nc.semaphore(f"{nm}_{self.nsem}"))

        nwait = 0
        for o in self.ops:
            E = self.eobj[o.eng]
            kn = known[o.eng]
            for d in o.deps:
                sem, val = d.ticket
                key = id(sem)
                if kn.get(key, 0) >= val:
                    continue
                E.wait_ge(sem, val)
                nwait += 1
                kn[key] = val
            if o.fn is None:
                continue
            ins = o.fn(E)
            if o.dma:
                i = dma_idx[o.eng]
                dma_idx[o.eng] = i + 1
                st = i // (self.DMA_K * self.DMA_CAP)
                j = i % (self.DMA_K * self.DMA_CAP)
                keyq = (o.eng, st)
                if keyq not in dma_sems:
                    dma_sems[keyq] = [newsem(f"d{o.eng}") for _ in range(self.DMA_K)]
                sem = dma_sems[keyq][j % self.DMA_K]
                val = 16 * (j // self.DMA_K + 1)
                ins.then_inc(sem, 16)
                o.ticket = (sem, val)
            elif o.inc:
                c = cmp_cnt.get(o.eng, 0)
                if o.eng not in cmp_sem or c >= self.CMP_CAP:
                    cmp_sem[o.eng] = newsem(f"c{o.eng}")
                    c = 0
                c += 1
                cmp_cnt[o.eng] = c
                ins.then_inc(cmp_sem[o.eng], 1)
                o.ticket = (cmp_sem[o.eng], c)
            o.fn = None
        return dict(nops=len(self.ops), nwait=nwait, nsem=self.nsem)


C_ONES, C_ID, C_CC, C_SC, C_M3, C_MU, C_ML, C_MC = 0, 128, 256, 384, 512, 640, 768, 896
C_TOT = 900


def make_consts():
    c = np.zeros((128, C_TOT), np.float64)
    c[:, C_ONES:C_ONES + 128] = 1.0
    c[:, C_ID:C_ID + 128] = np.eye(128)
    a = np.arange(128)
    ang = 2 * np.pi * np.outer(a, a) / 128.0
    c[:, C_CC:C_CC + 128] = np.cos(ang) / math.sqrt(128.0)
    c[:, C_SC:C_SC + 128] = np.sin(ang) / math.sqrt(128.0)
    s0 = np.arange(64)
    k1 = np.arange(64)
    phi = 2 * np.pi * np.outer(s0, k1) / 64.0
    m3 = np.zeros((128, 128))
    m3[0:64, 0:64] = np.cos(phi)
    m3[64:128, 0:64] = np.sin(phi)
    m3[0:64, 64:128] = -np.sin(phi)
    m3[64:128, 64:128] = np.cos(phi)
    c[:, C_M3:C_M3 + 128] = m3 / 8.0
    c[:, C_MU:C_MU + 128] = (a[:, None] <= a[None, :])
    c[:, C_ML:C_ML + 128] = (a[:, None] >= a[None, :])
    for hm in range(4):
        c[:, C_MC + hm] = np.where(a // 32 == hm, 32.0 ** -0.5, 0.0)
    m1 = np.zeros((128, 32, 128))
    p = np.arange(128)
    s1 = (p % 64)[:, None, None]
    half = (p // 64)[:, None, None]
    j = np.arange(32)[None, :, None]
    k0 = np.arange(64)[None, None, :]
    s0v = 32 * half + j
    th = 2 * np.pi * (s1 * k0 / 64.0 + s0v * k0 / 4096.0)
    m1[:, :, 0:64] = np.cos(th) / 8.0
    m1[:, :, 64:128] = -np.sin(th) / 8.0
    half_d = 32
    inv = 10000.0 ** (-np.arange(half_d, dtype=np.float64) / half_d)
    inv = inv.astype(np.float32).astype(np.float64)
    pos = np.arange(S, dtype=np.float64)
    angf = (pos[None, :].astype(np.float32) * inv[:, None].astype(np.float32)).astype(np.float64)
    cs = np.cos(angf)
    sn = np.sin(angf)
    cosT = np.concatenate([cs, cs], 0)
    sinT = np.concatenate([-sn, sn], 0)
    sc = 192.0 ** -0.5
    rope = np.stack([cosT * sc, sinT * sc, cosT, sinT], 1)
    return c.astype(np.float32), m1.reshape(128, 32 * 128).astype(np.float32), rope.astype(np.float32)


def na_rows():
    kh = 8
    r = np.arange(64)
    rs = np.clip(r - kh // 2, 0, 64 - kh)
    return rs


W_SPECS = [
    ("fn_w_in", 1024, 2048), ("fn_w_out", 1024, 1024),
    ("na_w_in", 1024, 4096), ("na_w_out", 1024, 1024),
    ("mla_w_in", 1024, 1728), ("mla_w_uq", 384, 1536), ("mla_w_ukv", 256, 2048), ("mla_w_out", 1024, 1024),
    ("hg_w_in", 1024, 5120), ("hg_w_out", 1024, 1024),
]


class Builder:
    def __init__(self, nseq, layers, final_norm=True):
        self.nseq = nseq
        self.layers = layers
        self.final_norm = final_norm

    def dma(self, q, out, in_, r=(), w=()):
        return self.P.op(q, lambda e: e.dma_start(out=out, in_=in_), r=r, w=w, dma=True)

    def dma_slow(self, q, out, in_, r=(), w=()):
        return self.P.op(q, lambda e: e.dma_start(out=out, in_=in_, allow_slow_non_contiguous=True), r=r, w=w, dma=True)

    def mm(self, out, lhsT, rhs, start, stop, r=(), w=()):
        return self.P.op("pe", lambda e: e.matmul(out, lhsT=lhsT, rhs=rhs, start=start, stop=stop), r=r, w=w)

    def tr(self, out, in_, ident, r=(), w=()):
        return self.P.op("pe", lambda e: e.transpose(out=out, in_=in_, identity=ident), r=r, w=w)

    def act(self, out, in_, func, r=(), w=(), scale=1.0, bias=None, accum=None):
        def f(e):
            kw = {}
            if bias is not None:
                kw["bias"] = bias
            if accum is not None:
                kw["accum_out"] = accum
            return e.activation(out=out, in_=in_, func=func, scale=scale, **kw)
        return self.P.op("act", f, r=r, w=w)

    def tt(self, eng, out, a, b, op, r=(), w=()):
        return self.P.op(eng, lambda e: e.tensor_tensor(out=out, in0=a, in1=b, op=op), r=r, w=w)

    def ts(self, eng, out, a, s1, s2, op0, op1=None, r=(), w=()):
        def f(e):
            if op1 is None:
                return e.tensor_scalar(out=out, in0=a, scalar1=s1, scalar2=None, op0=op0)
            return e.tensor_scalar(out=out, in0=a, scalar1=s1, scalar2=s2, op0=op0, op1=op1)
        return self.P.op(eng, f, r=r, w=w)

    def stt(self, out, in0, scalar, in1, op0, op1, r=(), w=()):
        return self.P.op("dve", lambda e: e.scalar_tensor_tensor(out=out, in0=in0, scalar=scalar, in1=in1, op0=op0, op1=op1), r=r, w=w)

    def cp(self, eng, out, in_, r=(), w=()):
        if eng == "act":
            return self.act(out, in_, AF.Copy, r=r, w=w)
        return self.P.op(eng, lambda e: e.tensor_copy(out=out, in_=in_), r=r, w=w)

    def memset(self, eng, ap, val, r=(), w=()):
        return self.P.op(eng, lambda e: e.memset(ap, val), r=r, w=w)

    def recip(self, out, in_, r=(), w=()):
        return self.P.op("dve", lambda e: e.reciprocal(out=out, in_=in_), r=r, w=w)

    def areset(self):
        self.aoff = 0

    def alloc(self, cols, dt=BF16):
        nb = cols * (4 if dt == F32 else 2)
        nb = (nb + 63) // 64 * 64
        off = self.aoff
        self.aoff += nb
        assert self.aoff <= self.asize, f"arena overflow {self.aoff} > {self.asize}"
        v = self.arena[:, off // 2:(off + nb) // 2]
        if dt == F32:
            v = v.bitcast(F32)
            return v[:, 0:cols]
        return v[:, 0:cols]

    def bank(self):
        bl = self.bank_list
        i = bl[self.bank_i % len(bl)]
        self.bank_i += 1
        return self.ps[i], self.psr[i]

    def build(self):
        nc = bass.Bass("TRN2", target_bir_lowering=False)
        self.nc = nc
        NS = self.nseq
        dt_in = lambda name, shape: nc.dram_tensor(name, shape, F32, kind="ExternalInput").ap()
        self.xT = dt_in("xT", [NS, D, S])
        self.pT = dt_in("pT", [4, NS, 256, S])
        self.cst_d = dt_in("cst", [128, C_TOT])
        self.m1_d = dt_in("m1", [128, 32 * 128])
        self.rope_d = dt_in("rope", [64, 4, S])
        self.wd = {}
        for name, k, n in W_SPECS:
            self.wd[name] = dt_in(name, [k, n])
        self.wd["ple_w"] = dt_in("ple_w", [4, 256, D])
        self.wd["ple_gate_w"] = dt_in("ple_gate_w", [4, D, D])
        self.norm_g_d = dt_in("norm_g", [4, D])
        self.final_g_d = dt_in("final_g", [D])
        self.fn_w_mix_d = dt_in("fn_w_mix", [8, 128, 128])
        self.rpbf_d = dt_in("rpb_flip", [32, 15, 31])
        self.g_q_d = dt_in("mla_g_q", [384])
        self.g_kv_d = dt_in("mla_g_kv", [256])
        self.lb_d = dt_in("hg_lb_raw", [4, 2, D])
        self.g_out_d = dt_in("hg_g_out", [D])
        self.yT = nc.dram_tensor("yT", [NS, D, S], F32, kind="ExternalOutput").ap()
        scr = lambda name, shape, dt=BF16: nc.dram_tensor(name, shape, dt, kind="Internal").ap()
        self.X = scr("X", [NS, D, S], F32)
        self.SA = {}
        for nm, shp, dt_ in [("SZ", [D, S], BF16), ("O", [D, S], BF16), ("Ud", [S, D], BF16), ("Td", [2, 64, 64, D], BF16),
                             ("QT", [D, S], BF16), ("KT", [D, S], BF16), ("VA", [S, 32 * 33], BF16),
                             ("QN", [D, S], BF16), ("QP", [8, 64, S], BF16), ("KN", [D, S], BF16), ("KP", [64, S], BF16),
                             ("V2", [S, D], BF16), ("QS", [D, S], BF16), ("LF", [2, D, S], F32), ("VI", [S, D], BF16)]:
            self.SA[nm] = scr(nm, [NS] + shp, dt_)
        self.w16 = {}
        for name, k, n in W_SPECS:
            self.w16[name] = scr("b_" + name, [k, n])
        self.w16["ple_w"] = scr("b_ple_w", [4, 256, D])
        self.w16["ple_gate_w"] = scr("b_ple_gate_w", [4, D, D])
        self.BT = scr("BT", [128, 32 * 14 * 64])

        with ExitStack() as st:
            self.P = Prog(nc, st)
            P = self.P
            sbt = lambda name, shape, dt: st.enter_context(nc.sbuf_tensor(name, shape, dt))
            self.cst = sbt("cstf", [128, C_TOT], F32)
            self.cstb = sbt("cstb", [128, C_TOT], BF16)
            self.ng = sbt("ng", [128, 4, 8], F32)
            self.fg = sbt("fg", [128, 8], F32)
            self.gq = sbt("gq", [128, 3], F32)
            self.gkv = sbt("gkv", [128, 2], F32)
            self.gout = sbt("gout", [128, 8], F32)
            self.lbraw = sbt("lbraw", [128, 4, 2, 8], F32)
            self.lb = sbt("lb", [128, 2, 8], F32)
            self.oml = sbt("oml", [128, 2, 8], F32)
            self.epsb = sbt("epsb", [128, 1], F32)
            self.asize = 200 * 1024
            self.arena = sbt("arena", [128, self.asize // 2], BF16)
            self.ps = [st.enter_context(nc.psum_tensor(f"ps{i}", [128, 512], F32)) for i in range(8)]
            self.psr = [P.res(f"ps{i}") for i in range(8)]
            self.bank_i = 0
            self.bank_list = list(range(8))
            self.rc = P.res("consts")

            self.phase_prep()
            for li in self.layers:
                last = (li == self.layers[-1])
                [self.pre0, self.pre1, self.pre2, self.pre3][li]()
                for sq in range(NS):
                    self.bind(sq)
                    [self.mid0, self.mid1, self.mid2, self.mid3][li](sq)
                self.post_layer(li, last, last and self.final_norm)
            stats = P.emit()
            self.stats = stats
        return nc

    def bind(self, sq):
        for nm, ap in self.SA.items():
            setattr(self, nm, ap[sq])

    def xsrc_of(self, li, sq):
        return self.xT[sq] if li == self.layers[0] else self.X[sq]

    def pre_iter(self, li):
        tiles = [(sq, t) for sq in range(self.nseq) for t in range(NT)]
        nxt = self.pre_tile(li, self.xsrc_of(li, tiles[0][0]), tiles[0][1], 0)
        for i, (sq, t) in enumerate(tiles):
            cur = nxt
            if i + 1 < len(tiles):
                nxt = self.pre_tile(li, self.xsrc_of(li, tiles[i + 1][0]), tiles[i + 1][1], i + 1)
            self.bind(sq)
            yield i, sq, t, cur[0], cur[1]

    def cb(self, off, n=128):
        return self.cstb[:, off:off + n]

    def cf(self, off, n=128):
        return self.cst[:, off:off + n]

    def phase_prep(self):
        P = self.P
        rc = self.rc
        self.areset()
        self.dma("sp", self.cst[:], self.cst_d[:, :], w=[rc])
        self.cp("dve", self.cstb[:], self.cst[:], r=[rc], w=[rc])
        self.memset("pool", self.epsb[:], EPS, w=[rc])
        self.dma_slow("sp", self.ng[:], self.norm_g_d.rearrange("l (c p) -> p l c", p=128), w=[rc])
        self.dma_slow("sp", self.fg[:], self.final_g_d.rearrange("(c p) -> p c", p=128), w=[rc])
        self.dma_slow("sp", self.gq[:], self.g_q_d.rearrange("(c p) -> p c", p=128), w=[rc])
        self.dma_slow("sp", self.gkv[:], self.g_kv_d.rearrange("(c p) -> p c", p=128), w=[rc])
        self.dma_slow("sp", self.gout[:], self.g_out_d.rearrange("(c p) -> p c", p=128), w=[rc])
        for l in range(4):
            for d in range(2):
                self.dma_slow("sp", self.lbraw[:, l, d, :], self.lb_d[l, d].rearrange("(c p) -> p c", p=128), w=[rc])
        ex = self.alloc(64, F32)
        sm = self.alloc(16, F32)
        exv = ex.rearrange("p (l x) -> p l x", l=4)
        self.act(exv, self.lbraw[:].rearrange("p l d c -> p l (d c)"), AF.Exp, r=[rc], w=[rc])
        self.tt("dve", sm, exv[:, 0, :], exv[:, 1, :], ALU.add, r=[rc], w=[rc])
        self.tt("dve", sm, sm, exv[:, 2, :], ALU.add, r=[rc], w=[rc])
        self.tt("dve", sm, sm, exv[:, 3, :], ALU.add, r=[rc], w=[rc])
        self.recip(sm, sm, r=[rc], w=[rc])
        omlv = self.oml[:].rearrange("p d c -> p (d c)")
        lbv = self.lb[:].rearrange("p d c -> p (d c)")
        self.tt("dve", omlv, exv[:, 0, :], sm, ALU.mult, r=[rc], w=[rc])
        self.ts("dve", lbv, omlv, -1.0, 1.0, ALU.mult, ALU.add, r=[rc], w=[rc])
        stg = [self.alloc(5120, F32) for _ in range(2)]
        stb = [self.alloc(5120, BF16) for _ in range(2)]
        rs_ = [P.res() for _ in range(2)]
        rb_ = [P.res() for _ in range(2)]
        cnt = 0
        engs = ["dve", "pool", "act"]
        jobs = []
        for name, k, n in W_SPECS:
            src = self.wd[name]
            dst = self.w16[name]
            for c in range(k // 128):
                jobs.append((src[c * 128:(c + 1) * 128, :], dst[c * 128:(c + 1) * 128, :], n))
        for l in range(4):
            for c in range(2):
                jobs.append((self.wd["ple_w"][l, c * 128:(c + 1) * 128, :], self.w16["ple_w"][l, c * 128:(c + 1) * 128, :], D))
            for c in range(8):
                jobs.append((self.wd["ple_gate_w"][l, c * 128:(c + 1) * 128, :], self.w16["ple_gate_w"][l, c * 128:(c + 1) * 128, :], D))
        for src, dst, n in jobs:
            b = cnt % 2
            self.dma("sp", stg[b][:, 0:n], src, w=[rs_[b]])
            self.cp(engs[cnt % 3], stb[b][:, 0:n], stg[b][:, 0:n], r=[rs_[b]], w=[rb_[b]])
            self.dma("act" if False else "sp", dst, stb[b][:, 0:n], r=[rb_[b]])
            cnt += 1
        P.barrier()
        if 1 in self.layers:
            self.prep_na_bias()

    def prep_na_bias(self):
        P = self.P
        self.areset()
        bt = self.alloc(32 * 14 * 64, BF16)
        rb = P.res()
        btv = bt.rearrange("p (h s q) -> p h s q", h=32, s=14)
        self.memset("pool", bt[:, 0:14336], NEG, w=[rb])
        self.memset("dve", bt[:, 14336:28672], NEG, w=[rb])
        qs = np.arange(64)
        win = np.clip(qs - 8, 0, 48)
        for a in range(2):
            for kc in range(64):
                valid = np.nonzero((win <= kc) & (kc < win + 16))[0]
                qlo, qhi = int(valid[0]), int(valid[-1])
                assert len(valid) == qhi - qlo + 1
                nq = qhi - qlo + 1
                j0 = 15 - kc + qlo
                assert 0 <= j0 and j0 + nq <= 31
                p = a * 64 + kc
                src = self.rpbf_d[:, a:a + 14, j0:j0 + nq]
                self.dma_slow("pool", btv[p:p + 1, :, :, qlo:qhi + 1], src.unsqueeze(0), w=[rb])
        self.dma("sp", self.BT[:, :], bt, r=[rb])
        P.barrier()

    def load_w(self, name, k, n, sub=None):
        t = self.alloc((k // 128) * n, BF16)
        v = t.rearrange("p (c n) -> p c n", n=n)
        src = self.w16[name] if sub is None else self.w16[name][sub]
        r = self.P.res()
        kc = k // 128
        hc = max(1, kc // 2)
        srcv = src.rearrange("(c p) n -> p c n", p=128)
        self.dma("sp", v[:, 0:hc, :], srcv[:, 0:hc, :], w=[r])
        if kc > hc:
            self.dma("pool", v[:, hc:kc, :], srcv[:, hc:kc, :], w=[r])
        return v, r

    def pre_setup(self):
        P = self.P
        self.areset()
        self.bank_i = 0
        self.xt = [self.alloc(8 * TS, F32).rearrange("p (c s) -> p c s", c=8) for _ in range(2)]
        self.rxt = [P.res() for _ in range(2)]
        self.sq = [self.alloc(8 * TS, BF16).rearrange("p (c s) -> p c s", c=8) for _ in range(2)]
        self.rsq = [P.res() for _ in range(2)]
        self.hT = [self.alloc(8 * TS, BF16).rearrange("p (c s) -> p c s", c=8) for _ in range(2)]
        self.rhT = [P.res() for _ in range(2)]
        self.rstd = [self.alloc(TS, F32) for _ in range(2)]
        self.rrstd = [P.res() for _ in range(2)]

    def rms_to(self, xt, rx, nch, inv_n, gcol, out, rout, b):
        sq, rsq = self.sq[b], self.rsq[b]
        rstd, rr = self.rstd[b], self.rrstd[b]
        self.tt("dve", sq[:, 0:nch, :], xt, xt, ALU.mult, r=[rx], w=[rsq])
        pb, rpb = self.bank()
        for c in range(nch):
            self.mm(pb[:], self.cb(C_ONES), sq[:, c, :], c == 0, c == nch - 1, r=[rsq, self.rc], w=[rpb])
        self.act(rstd, pb[:], AF.Sqrt, r=[rpb, self.rc], w=[rr], scale=inv_n, bias=self.epsb[:, 0:1])
        self.recip(rstd, rstd, r=[rr], w=[rr])
        for c in range(nch):
            self.stt(out[:, c, :], xt[:, c, :], gcol(c), rstd, ALU.mult, ALU.mult, r=[rx, rr, self.rc], w=[rout])

    def pre_tile(self, li, xsrc, t, i):
        b = i % 2
        xt, rx = self.xt[b], self.rxt[b]
        xv = xsrc.rearrange("(c p) s -> p c s", p=128)
        self.dma("sp", xt[:, 0:4, :], xv[:, 0:4, t * TS:(t + 1) * TS], w=[rx])
        self.dma("pool", xt[:, 4:8, :], xv[:, 4:8, t * TS:(t + 1) * TS], w=[rx])
        self.rms_to(xt, rx, 8, 1.0 / D, lambda c: self.ng[:, li, c:c + 1], self.hT[b], self.rhT[b], b)
        return self.hT[b], self.rhT[b]

    def proj_fm(self, w, rw, col0, hT, rh, m=128):
        pb, rpb = self.bank()
        for k in range(8):
            self.mm(pb[0:m, :], w[:, k, col0:col0 + m], hT[:, k, :], k == 0, k == 7, r=[rw, rh], w=[rpb])
        return pb, rpb

    def proj_tm(self, w, rw, col0, hT, rh, sub):
        pb, rpb = self.bank()
        for k in range(8):
            self.mm(pb[:], hT[:, k, sub * 128:(sub + 1) * 128], w[:, k, col0:col0 + 512], k == 0, k == 7, r=[rw, rh], w=[rpb])
        return pb, rpb

    def z_tile(self, w, rw, zcol0, hT, rh, t, b):
        szt, rsz = self.szt[b], self.rszt[b]
        for j in range(8):
            pb, rpb = self.proj_fm(w, rw, zcol0 + j * 128, hT, rh)
            self.act(szt[:, j, :], pb[:], AF.Silu, r=[rpb], w=[rsz])
        self.dma("sp", self.SZ.rearrange("(c p) s -> p c s", p=128)[:, :, t * TS:(t + 1) * TS], szt, r=[rsz])

    def alloc_szt(self):
        self.szt = [self.alloc(8 * TS, BF16).rearrange("p (c s) -> p c s", c=8) for _ in range(2)]
        self.rszt = [self.P.res() for _ in range(2)]

    def pre0(self):
        P = self.P
        self.pre_setup()
        w, rw = self.load_w("fn_w_in", 1024, 2048)
        self.alloc_szt()
        ut = [self.alloc(4 * 1024, BF16).rearrange("p (a f) -> p a f", a=4) for _ in range(2)]
        rut = [P.res() for _ in range(2)]
        ev = 0
        for i, sq, t, hT, rh in self.pre_iter(0):
            b = i % 2
            udv = self.Ud.rearrange("(n p) f -> p n f", p=128)
            for sub in range(4):
                for ct in range(2):
                    pb, rpb = self.proj_tm(w, rw, ct * 512, hT, rh, sub)
                    self.cp("act" if ev % 2 == 0 else "dve", ut[b][:, sub, ct * 512:(ct + 1) * 512], pb[:], r=[rpb], w=[rut[b]])
                    ev += 1
            self.dma("sp", udv[:, 4 * t:4 * t + 4, :], ut[b], r=[rut[b]])
            self.z_tile(w, rw, 1024, hT, rh, t, b)
        P.barrier()

    def mid0(self, sq):
        P = self.P
        rc = self.rc
        self.areset()
        self.bank_i = 0
        wm = self.alloc(8 * 128, F32).rearrange("p (g d) -> p g d", g=8)
        rwm = P.res()
        self.dma("sp", wm, self.fn_w_mix_d.rearrange("g m d -> m g d"), w=[rwm])
        AB = self.alloc(2 * 8 * 128, BF16).rearrange("p (x g d) -> p x g d", x=2, g=8)
        rAB = P.res()
        for x, off in ((0, C_CC), (1, C_SC)):
            for gh in range(2):
                pb, rpb = self.bank()
                for g4 in range(4):
                    g = gh * 4 + g4
                    self.mm(pb[:, g4 * 128:(g4 + 1) * 128], self.cf(off), wm[:, g, :], True, True, r=[rc, rwm], w=[rpb])
                self.cp("dve", AB[:, x, gh * 4:(gh + 1) * 4, :], pb[:].rearrange("p (g d) -> p g d", g=4), r=[rpb], w=[rAB])
        mark = self.aoff
        m1f = self.alloc(32 * 128, F32)
        m1 = self.alloc(32 * 128, BF16).rearrange("p (j m) -> p j m", j=32)
        rm1 = P.res()
        self.dma("sp", m1f, self.m1_d[:, :], w=[rm1])
        self.cp("pool", m1.rearrange("p j m -> p (j m)"), m1f, r=[rm1], w=[rm1])
        ud1 = self.alloc(32 * 1024, BF16).rearrange("p (j f) -> p j f", j=32)
        rud = P.res()
        udv = self.Ud.rearrange("(a b) f -> a b f", b=64)
        self.dma("sp", ud1[0:64, :, :], udv[:, 0:32, :], w=[rud])
        self.dma("pool", ud1[64:128, :, :], udv[:, 32:64, :], w=[rud])
        t1 = [self.alloc(8 * 1024, BF16).rearrange("p (s f) -> p s f", s=8) for _ in range(2)]
        rt1 = [P.res() for _ in range(2)]
        tdv = self.Td.rearrange("r k s f -> (r k) s f")
        ev = 0
        for sb_ in range(8):
            b = sb_ % 2
            for sl in range(8):
                s0 = sb_ * 8 + sl
                half, j = s0 // 32, s0 % 32
                for ft in range(2):
                    pb, rpb = self.bank()
                    self.mm(pb[:], m1[64 * half:64 * half + 64, j, :], ud1[64 * half:64 * half + 64, j, ft * 512:(ft + 1) * 512],
                            True, True, r=[rm1, rud], w=[rpb])
                    self.cp("act" if ev % 2 == 0 else "dve", t1[b][:, sl, ft * 512:(ft + 1) * 512], pb[:], r=[rpb], w=[rt1[b]])
                    ev += 1
            self.dma("sp", tdv[:, sb_ * 8:(sb_ + 1) * 8, :], t1[b], r=[rt1[b]])
        P.barrier()
        self.aoff = mark
        xT = self.alloc(4 * 2 * S, BF16).rearrange("p (g r k) -> p g r k", g=4, r=2)
        rxT = [P.res() for _ in range(4)]
        t3 = [self.alloc(16 * 512, BF16).rearrange("p (k f) -> p k f", k=16) for _ in range(2)]
        rt3 = [P.res() for _ in range(2)]
        ot = [self.alloc(S, BF16) for _ in range(2)]
        rot = [P.res() for _ in range(2)]
        tdr = self.Td.rearrange("r k s f -> r s k f")
        for hf in range(2):
            for kb in range(4):
                b = kb % 2
                for r_ in range(2):
                    self.dma("sp" if r_ == 0 else "pool", t3[b][64 * r_:64 * r_ + 64, :, :],
                             tdr[r_, :, kb * 16:(kb + 1) * 16, hf * 512:(hf + 1) * 512], w=[rt3[b]])
                for k4 in range(4):
                    for g in range(4):
                        pb, rpb = self.bank()
                        for kk in range(4):
                            kl = k4 * 4 + kk
                            self.mm(pb[:, kk * 128:(kk + 1) * 128], t3[b][:, kl, g * 128:(g + 1) * 128], self.cb(C_M3),
                                    True, True, r=[rt3[b], rc], w=[rpb])
                        k0 = kb * 16 + k4 * 4
                        outv = xT[:, g, :, :].rearrange("p r (k1 k0) -> p r k1 k0", k0=64)[:, :, :, k0:k0 + 4]
                        inv = pb[:].rearrange("p (a r k) -> p r k a", a=4, r=2)
                        self.cp("act" if ev % 2 == 0 else "dve", outv, inv, r=[rpb], w=[rxT[g]])
                        ev += 1
            for g in range(4):
                gg = hf * 4 + g
                b = g % 2
                for t in range(NT):
                    pb, rpb = self.bank()
                    self.mm(pb[:], AB[:, 0, gg, :], xT[:, g, 0, t * TS:(t + 1) * TS], True, False, r=[rAB, rxT[g]], w=[rpb])
                    self.mm(pb[:], AB[:, 1, gg, :], xT[:, g, 1, t * TS:(t + 1) * TS], False, True, r=[rAB, rxT[g]], w=[rpb])
                    self.cp("act" if ev % 2 == 0 else "dve", ot[b][:, t * TS:(t + 1) * TS], pb[:], r=[rpb], w=[rot[b]])
                    ev += 1
                self.dma("sp", self.O[gg * 128:(gg + 1) * 128, :], ot[b], r=[rot[b]])
        P.barrier()

    def post_layer(self, li, last, final):
        P = self.P
        self.areset()
        self.bank_i = 0
        self.bank_list = list(range(8))
        wname = ["fn_w_out", "na_w_out", "mla_w_out", "hg_w_out"][li]
        wo, rwo = self.load_w(wname, 1024, 1024)
        wg, rwg = self.load_w("ple_gate_w", 1024, 1024, sub=li)
        wp, rwp = self.load_w("ple_w", 256, 1024, sub=li)
        A = lambda cols, dt: [self.alloc(cols, dt) for _ in range(2)]
        R = lambda: [P.res() for _ in range(2)]
        v8 = lambda t_: t_.rearrange("p (c s) -> p c s", c=8)
        ot = [v8(x) for x in A(8 * TS, BF16)]; rot = R()
        sz = [v8(x) for x in A(8 * TS, BF16)]; rsz = R()
        xt = [v8(x) for x in A(8 * TS, F32)]; rx = R()
        pt = [x.rearrange("p (c s) -> p c s", c=2) for x in A(2 * TS, F32)]; rpt = R()
        ptb = [x.rearrange("p (c s) -> p c s", c=2) for x in A(2 * TS, BF16)]; rptb = R()
        g = [v8(x) for x in A(8 * TS, BF16)]; rg = R()
        x1b = [v8(x) for x in A(8 * TS, BF16)]; rx1b = R()
        sg = A(TS, F32); rsg = R()
        if final:
            self.sq = [v8(x) for x in A(8 * TS, BF16)]; self.rsq = R()
            self.rstd = A(TS, F32); self.rrstd = R()
        tiles = [(sq, t) for sq in range(self.nseq) for t in range(NT)]
        fm = lambda ap: ap.rearrange("(c p) s -> p c s", p=128)

        xt3 = xt + [v8(self.alloc(8 * TS, F32))]
        rx3 = rx + [P.res()]

        def loads(i):
            sq, t = tiles[i]
            b = i % 2
            b3 = i % 3
            sl = slice(t * TS, (t + 1) * TS)
            xv = fm(self.xsrc_of(li, sq))
            self.dma("sp", ot[b], fm(self.SA["O"][sq])[:, :, sl], w=[rot[b]])
            self.dma("sp", sz[b], fm(self.SA["SZ"][sq])[:, :, sl], w=[rsz[b]])
            self.dma("sp", xt3[b3], xv[:, :, sl], w=[rx3[b3]])
            self.dma("sp", pt[b], fm(self.pT[li, sq])[:, :, sl], w=[rpt[b]])

        def stage_a(i):
            sq, t = tiles[i]
            b = i % 2
            b3 = i % 3
            self.tt("dve", g[b], ot[b], sz[b], ALU.mult, r=[rot[b], rsz[b]], w=[rg[b]])
            self.cp("pool", ptb[b], pt[b], r=[rpt[b]], w=[rptb[b]])
            for j in range(8):
                pb, rpb = self.bank()
                for k in range(8):
                    self.mm(pb[:], wo[:, k, j * 128:(j + 1) * 128], g[b][:, k, :], k == 0, k == 7, r=[rwo, rg[b]], w=[rpb])
                self.tt("dve", xt3[b3][:, j, :], xt3[b3][:, j, :], pb[:], ALU.add, r=[rpb, rx3[b3]], w=[rx3[b3]])
                self.cp("act", x1b[b][:, j, :], xt3[b3][:, j, :], r=[rx3[b3]], w=[rx1b[b]])

        def stage_b(i):
            sq, t = tiles[i]
            b = i % 2
            b3 = i % 3
            sl = slice(t * TS, (t + 1) * TS)
            xdv = fm(self.yT[sq] if last else self.X[sq])
            for j in range(8):
                pg, rpg = self.bank()
                for k in range(8):
                    self.mm(pg[:], wg[:, k, j * 128:(j + 1) * 128], x1b[b][:, k, :], k == 0, k == 7, r=[rwg, rx1b[b]], w=[rpg])
                pp, rpp = self.bank()
                for k in range(2):
                    self.mm(pp[:], wp[:, k, j * 128:(j + 1) * 128], ptb[b][:, k, :], k == 0, k == 1, r=[rwp, rptb[b]], w=[rpp])
                self.act(sg[b], pg[:], AF.Sigmoid, r=[rpg], w=[rsg[b]])
                self.tt("dve", sg[b], sg[b], pp[:], ALU.mult, r=[rsg[b], rpp], w=[rsg[b]])
                self.tt("pool", xt3[b3][:, j, :], xt3[b3][:, j, :], sg[b], ALU.add, r=[rsg[b], rx3[b3]], w=[rx3[b3]])
            if final:
                self.rms_to(xt3[b3], rx3[b3], 8, 1.0 / D, lambda c: self.fg[:, c:c + 1], xt3[b3], rx3[b3], b)
            self.dma("sp", xdv[:, 0:4, sl], xt3[b3][:, 0:4, :], r=[rx3[b3]])
            self.dma("sp", xdv[:, 4:8, sl], xt3[b3][:, 4:8, :], r=[rx3[b3]])

        loads(0)
        loads(1)
        stage_a(0)
        for i in range(len(tiles)):
            if i + 1 < len(tiles):
                stage_a(i + 1)
            if i + 2 < len(tiles):
                loads(i + 2)
            stage_b(i)
        P.barrier()

    def pre1(self):
        P = self.P
        self.pre_setup()
        self.bank_list = list(range(8))
        w, rw = self.load_w("na_w_in", 1024, 4096)
        self.alloc_szt()
        qk = [self.alloc(8 * TS, BF16).rearrange("p (c s) -> p c s", c=8) for _ in range(2)]
        rqk = [P.res() for _ in range(2)]
        vt = [self.alloc(4 * 1056, BF16).rearrange("p (a h d) -> p a h d", a=4, h=32) for _ in range(2)]
        rvt = [P.res() for _ in range(2)]
        for b in range(2):
            self.memset("pool", vt[b][:, :, :, 32:33], 1.0, w=[rvt[b]])
        ev = 0
        nq = 0
        for i, sq, t, hT, rh in self.pre_iter(1):
            b = i % 2
            sl = slice(t * TS, (t + 1) * TS)
            vav = self.VA.rearrange("(n p) f -> p n f", p=128)
            for which, dst in ((0, self.QT), (1, self.KT)):
                bq = nq % 2
                nq += 1
                for j in range(8):
                    pb, rpb = self.proj_fm(w, rw, which * 1024 + j * 128, hT, rh)
                    self.cp("act" if ev % 2 == 0 else "dve", qk[bq][:, j, :], pb[:], r=[rpb], w=[rqk[bq]])
                    ev += 1
                self.dma("sp", dst.rearrange("(c p) s -> p c s", p=128)[:, :, sl], qk[bq], r=[rqk[bq]])
            for sub in range(4):
                for ct in range(2):
                    pb, rpb = self.proj_tm(w, rw, 2048 + ct * 512, hT, rh, sub)
                    self.cp("act" if ev % 2 == 0 else "dve", vt[b][:, sub, 16 * ct:16 * ct + 16, 0:32],
                            pb[:].rearrange("p (h d) -> p h d", d=32), r=[rpb], w=[rvt[b]])
                    ev += 1
            self.dma("pool", vav[:, 4 * t:4 * t + 4, :], vt[b].rearrange("p a h d -> p a (h d)"), r=[rvt[b]])
            self.z_tile(w, rw, 3072, hT, rh, t, b)
        P.barrier()

    def mid1(self, sq):
        P = self.P
        rc = self.rc
        self.areset()
        self.bank_i = 0
        self.bank_list = [0, 1, 2, 3, 4]
        obanks = [5, 6]
        tbank = 7
        bts = self.alloc(32 * 14 * 64, BF16).rearrange("p (h s q) -> p h s q", h=32, s=14)
        rbt = P.res()
        self.dma("sp", bts[:, 0:16], self.BT.rearrange("p (h x) -> p h x", h=32)[:, 0:16, :].rearrange("p h (s q) -> p h s q", s=14), w=[rbt])
        self.dma("pool", bts[:, 16:32], self.BT.rearrange("p (h x) -> p h x", h=32)[:, 16:32, :].rearrange("p h (s q) -> p h s q", s=14), w=[rbt])
        ktb = [self.alloc(8 * 1024, BF16).rearrange("p (c s) -> p c s", c=8) for _ in range(2)]
        rkt = [P.res() for _ in range(2)]
        vbe = self.alloc(8 * 1056, BF16).rearrange("p (m h d) -> p m h d", m=8, h=32)
        vbo = self.alloc(7 * 1056, BF16).rearrange("p (m h d) -> p m h d", m=7, h=32)
        rvb = P.res()
        qraw = [self.alloc(8 * TS, BF16).rearrange("p (c s) -> p c s", c=8) for _ in range(2)]
        rqr = [P.res() for _ in range(2)]
        qm = self.alloc(8 * 4 * TS, BF16).rearrange("p (c m s) -> p c m s", c=8, m=4)
        rqm = P.res()
        NPT = 4
        ptf = [self.alloc(1152, BF16) for _ in range(NPT)]
        rpt = [P.res() for _ in range(NPT)]
        for i in range(NPT):
            self.memset("pool", ptf[i], 0.0, w=[rpt[i]])
        ptw = [x[:, 0:1152].rearrange("p (r x) -> p r x", x=576)[:, :, 0:512].rearrange("p r (k y) -> p r k y", y=128)[:, :, :, 0:64] for x in ptf]
        ptr_ = [x[:, 0:1024].rearrange("p (r k y) -> p r k y", r=2, k=4) for x in ptf]
        osb = [self.alloc(1024, BF16) for _ in range(2)]
        rosb = [P.res() for _ in range(2)]
        otb = [self.alloc(8 * TS, BF16).rearrange("p (c s) -> p c s", c=8) for _ in range(2)]
        rotb = [P.res() for _ in range(2)]
        rc8 = [self.alloc(8, F32) for _ in range(2)]
        rrc8 = [P.res() for _ in range(2)]
        ktv = self.KT.rearrange("(c p) s -> p c s", p=128)
        qtv = self.QT.rearrange("(c p) s -> p c s", p=128)
        ov = self.O.rearrange("(c p) s -> p c s", p=128)
        ev = [0]
        units = [(t, pr, hg, hl) for t in range(NT) for pr in range(4) for hg in range(4) for hl in range(8)]
        state = {}

        def band_setup(t):
            b = t % 2
            kw0 = int(np.clip(8 * t - 4, 0, 48))
            self.dma("sp", ktb[b], ktv[:, :, 64 * kw0:64 * kw0 + 1024], w=[rkt[b]])
            self.dma("pool", qraw[b], qtv[:, :, t * TS:(t + 1) * TS], w=[rqr[b]])
            for j in range(8):
                for hm in range(4):
                    e3 = ev[0] % 3
                    if e3 == 2:
                        self.act(qm[:, j, hm, :], qraw[b][:, j, :], AF.Copy, r=[rqr[b], rc], w=[rqm], scale=self.cf(C_MC + hm, 1))
                    else:
                        self.ts("dve" if e3 == 0 else "pool", qm[:, j, hm, :], qraw[b][:, j, :], self.cf(C_MC + hm, 1), 0.0,
                                ALU.mult, ALU.add, r=[rqr[b], rc], w=[rqm])
                    ev[0] += 1

        def v_setup(t):
            kw0 = int(np.clip(8 * t - 4, 0, 48))
            self.dma("pool", vbe.rearrange("p m h d -> p m (h d)"),
                     self.VA[64 * kw0:64 * kw0 + 1024, :].rearrange("(m p) f -> p m f", p=128), w=[rvb])
            self.dma("sp", vbo.rearrange("p m h d -> p m (h d)"),
                     self.VA[64 * (kw0 + 1):64 * (kw0 + 1) + 896, :].rearrange("(m p) f -> p m f", p=128), w=[rvb])

        def geom(u):
            t, pr, hg, hl = u
            kw0 = int(np.clip(8 * t - 4, 0, 48))
            rows = (8 * t + 2 * pr, 8 * t + 2 * pr + 1)
            return kw0, rows

        def S_(ui):
            u = units[ui]
            t, pr, hg, hl = u
            b = t % 2
            kw0, rows = geom(u)
            h = hg * 8 + hl
            j, hm = h // 4, h % 4
            sb_, rsb = self.bank()
            sv = sb_[:].rearrange("p (r k q) -> p r k q", r=2, k=4)
            for ri, r_ in enumerate(rows):
                rs = int(np.clip(r_ - 4, 0, 56))
                ro0 = rs - r_ + 7
                for kc in range(4):
                    koff = 64 * (rs + 2 * kc - kw0)
                    self.mm(sv[:, ri, kc, :], ktb[b][:, j, koff:koff + 128], qm[:, j, hm, (r_ - 8 * t) * 64:(r_ - 8 * t) * 64 + 64],
                            True, False, r=[rkt[b], rqm], w=[rsb])
                    self.mm(sv[:, ri, kc, :], self.cb(C_ID), bts[:, h, ro0 + 2 * kc, :], False, True, r=[rbt, rc], w=[rsb])
            state[ui] = (sv, rsb)

        def PV_(ui):
            u = units[ui]
            t, pr, hg, hl = u
            b = t % 2
            kw0, rows = geom(u)
            h = hg * 8 + hl
            sv, rsb = state.pop(ui)
            pi = ui % NPT
            self.act(ptw[pi], sv, AF.Exp, r=[rsb], w=[rpt[pi]])
            gi = ui // 8
            obi = obanks[gi % 2]
            ob, rob = self.ps[obi], self.psr[obi]
            obv = ob[:, 0:264].rearrange("p (h d) -> p h d", h=8)
            for ri, r_ in enumerate(rows):
                rs = int(np.clip(r_ - 4, 0, 56))
                dlt = rs - kw0
                for kc in range(4):
                    if dlt % 2 == 0:
                        vsrc = vbe[:, dlt // 2 + kc, h, :]
                    else:
                        vsrc = vbo[:, (dlt - 1) // 2 + kc, h, :]
                    self.mm(obv[:, hl, :], ptr_[pi][:, ri, kc, :], vsrc, ri == 0 and kc == 0, ri == 1 and kc == 3,
                            r=[rpt[pi], rvb], w=[rob])
            ob_ = (ui // 32) % 2
            if hl == 7:
                self.recip(rc8[hg % 2], obv[:, :, 32], r=[rob], w=[rrc8[hg % 2]])
                self.tt("dve", osb[ob_][:, hg * 256:(hg + 1) * 256].rearrange("p (h d) -> p h d", h=8), obv[:, :, 0:32],
                        rc8[hg % 2].unsqueeze(2).to_broadcast([128, 8, 32]), ALU.mult, r=[rob, rrc8[hg % 2]], w=[rosb[ob_]])
                if hg == 3:
                    tb, rtb = self.ps[tbank], self.psr[tbank]
                    tbv = tb[:].bitcast(BF16)
                    for jj in range(8):
                        self.tr(tbv[:, jj * 128:(jj + 1) * 128], osb[ob_][:, jj * 128:(jj + 1) * 128], self.cb(C_ID), r=[rosb[ob_], rc], w=[rtb])
                    self.cp("dve", otb[b][:, :, pr * 128:(pr + 1) * 128], tbv.rearrange("p (c s) -> p c s", c=8), r=[rtb], w=[rotb[b]])
                    if pr == 3:
                        self.dma("sp", ov[:, :, t * TS:(t + 1) * TS], otb[b], r=[rotb[b]])

        SK = 3
        N_ = len(units)
        band_setup(0)
        v_setup(0)
        for k in range(SK):
            S_(k)
        for ui in range(N_):
            nxt = ui + SK
            if nxt < N_:
                if units[nxt][0] != units[nxt - 1][0]:
                    band_setup(units[nxt][0])
                S_(nxt)
            PV_(ui)
            if ui + 1 < N_ and units[ui + 1][0] != units[ui][0]:
                v_setup(units[ui + 1][0])
        self.bank_list = list(range(8))
        P.barrier()


    def pre2(self):
        P = self.P
        rc = self.rc
        self.pre_setup()
        self.bank_list = list(range(8))
        w, rw = self.load_w("mla_w_in", 1024, 1728)
        wq, rwq = self.load_w("mla_w_uq", 384, 1536)
        wkv, rwkv = self.load_w("mla_w_ukv", 256, 2048)
        self.alloc_szt()
        c3 = lambda x, n: x.rearrange("p (c s) -> p c s", c=n)
        cq = c3(self.alloc(3 * TS, F32), 3); rcq = P.res()
        ckv = c3(self.alloc(2 * TS, F32), 2); rckv = P.res()
        cqn = c3(self.alloc(3 * TS, BF16), 3); rcqn = P.res()
        ckvn = c3(self.alloc(2 * TS, BF16), 2); rckvn = P.res()
        rp = [c3(self.alloc(4 * TS, F32), 4) for _ in range(2)]; rrp = [P.res() for _ in range(2)]
        qnt = c3(self.alloc(8 * TS, BF16), 8); rqnt = P.res()
        qpt = c3(self.alloc(8 * TS, BF16), 8); rqpt = P.res()
        knt = c3(self.alloc(8 * TS, BF16), 8); rknt = P.res()
        kpt = self.alloc(TS, BF16); rkpt = P.res()
        vt = self.alloc(4 * 1024, BF16).rearrange("p (a f) -> p a f", a=4); rvt = P.res()
        t1 = self.alloc(TS, F32); rt1 = P.res()
        t2 = self.alloc(TS, F32); rt2 = P.res()
        wkv4 = wkv.rearrange("p c (h t d) -> p c h t d", h=8, t=2)
        ev = [0]

        def evac(out, in_, r, w_):
            self.cp("act" if ev[0] % 2 == 0 else "dve", out, in_, r=r, w=w_)
            ev[0] += 1

        def swapped(wt, rwt_, nk, c0, rhs, rrhs):
            pb, rpb = self.bank()
            for k in range(nk):
                self.mm(pb[0:32, :], wt[:, k, c0 + 32:c0 + 64], rhs[:, k, :], k == 0, k == nk - 1, r=[rwt_, rrhs], w=[rpb])
            for k in range(nk):
                self.mm(pb[32:64, :], wt[:, k, c0:c0 + 32], rhs[:, k, :], k == 0, k == nk - 1, r=[rwt_, rrhs], w=[rpb])
            return pb, rpb

        def rope(pa, rpa, pbs, rpbs, ct, st_, rtab, out):
            self.tt("dve", t1[0:64, :], pa[0:64, :], ct, ALU.mult, r=[rpa, rtab], w=[rt1])
            self.tt("dve", t2[0:64, :], pbs[0:64, :], st_, ALU.mult, r=[rpbs, rtab], w=[rt2])
            return t1, t2

        for i, sq, t, hT, rh in self.pre_iter(2):
            b = i % 2
            sl = slice(t * TS, (t + 1) * TS)
            qnv = self.QN.rearrange("(h p) s -> p h s", p=128)
            knv = self.KN.rearrange("(h p) s -> p h s", p=128)
            qpv = self.QP.rearrange("h p s -> p h s")
            v2v = self.V2.rearrange("(n p) f -> p n f", p=128)
            self.dma("pool", rp[b][0:64], self.rope_d[:, :, sl], w=[rrp[b]])
            for c in range(3):
                pb, rpb = self.proj_fm(w, rw, c * 128, hT, rh)
                evac(cq[:, c, :], pb[:], [rpb], [rcq])
            for c in range(2):
                pb, rpb = self.proj_fm(w, rw, 384 + c * 128, hT, rh)
                evac(ckv[:, c, :], pb[:], [rpb], [rckv])
            pa, rpa = self.proj_fm(w, rw, 640, hT, rh, m=64)
            pbs, rpbs = swapped(w, rw, 8, 640, hT, rh)
            self.tt("dve", t1[0:64, :], pa[0:64, :], rp[b][0:64, 2, :], ALU.mult, r=[rpa, rrp[b]], w=[rt1])
            self.tt("dve", t2[0:64, :], pbs[0:64, :], rp[b][0:64, 3, :], ALU.mult, r=[rpbs, rrp[b]], w=[rt2])
            self.tt("pool", kpt[0:64, :], t1[0:64, :], t2[0:64, :], ALU.add, r=[rt1, rt2], w=[rkpt])
            self.dma("sp", self.KP[:, sl], kpt[0:64, :], r=[rkpt])
            self.z_tile(w, rw, 704, hT, rh, t, b)
            self.rms_to(cq, rcq, 3, 1.0 / 384, lambda c: self.gq[:, c:c + 1], cqn, rcqn, b)
            self.rms_to(ckv, rckv, 2, 1.0 / 256, lambda c: self.gkv[:, c:c + 1], ckvn, rckvn, b)
            for h in range(8):
                pb, rpb = self.bank()
                for c in range(3):
                    self.mm(pb[:], wq[:, c, h * 192:h * 192 + 128], cqn[:, c, :], c == 0, c == 2, r=[rwq, rcqn], w=[rpb])
                self.act(qnt[:, h, :], pb[:], AF.Copy, r=[rpb], w=[rqnt], scale=192.0 ** -0.5)
                pa, rpa = self.bank()
                for c in range(3):
                    self.mm(pa[0:64, :], wq[:, c, h * 192 + 128:h * 192 + 192], cqn[:, c, :], c == 0, c == 2, r=[rwq, rcqn], w=[rpa])
                pbs, rpbs = swapped(wq, rwq, 3, h * 192 + 128, cqn, rcqn)
                self.tt("dve", t1[0:64, :], pa[0:64, :], rp[b][0:64, 0, :], ALU.mult, r=[rpa, rrp[b]], w=[rt1])
                self.tt("dve", t2[0:64, :], pbs[0:64, :], rp[b][0:64, 1, :], ALU.mult, r=[rpbs, rrp[b]], w=[rt2])
                self.tt("pool", qpt[0:64, h, :], t1[0:64, :], t2[0:64, :], ALU.add, r=[rt1, rt2], w=[rqpt])
            for h in range(8):
                pb, rpb = self.bank()
                for c in range(2):
                    self.mm(pb[:], wkv[:, c, h * 256:h * 256 + 128], ckvn[:, c, :], c == 0, c == 1, r=[rwkv, rckvn], w=[rpb])
                evac(knt[:, h, :], pb[:], [rpb], [rknt])
            for sub in range(4):
                for hh in range(2):
                    pb, rpb = self.bank()
                    for c in range(2):
                        self.mm(pb[:].rearrange("p (h d) -> p h d", h=4), ckvn[:, c, sub * 128:(sub + 1) * 128],
                                wkv4[:, c, 4 * hh:4 * hh + 4, 1, :], c == 0, c == 1, r=[rwkv, rckvn], w=[rpb])
                    evac(vt[:, sub, hh * 512:(hh + 1) * 512], pb[:], [rpb], [rvt])
            self.dma("sp", qnv[:, :, sl], qnt, r=[rqnt])
            self.dma("pool", qpv[:, :, sl], qpt[0:64], r=[rqpt])
            self.dma("sp", knv[:, :, sl], knt, r=[rknt])
            self.dma("pool", v2v[:, 4 * t:4 * t + 4, :], vt, r=[rvt])
        P.barrier()

    def mid2(self, sq):
        P = self.P
        rc = self.rc
        self.areset()
        self.bank_i = 0
        self.bank_list = [0, 1, 2, 3]
        kpb = self.alloc(S, BF16); rkp = P.res()
        self.dma("sp", kpb[0:64, :], self.KP[:, :], w=[rkp])
        kng = self.alloc(4 * S, BF16).rearrange("p (h s) -> p h s", h=4); rkn = P.res()
        vg = self.alloc(32 * 512, BF16).rearrange("p (n f) -> p n f", n=32); rvg = P.res()
        qn = [self.alloc(4 * TS, BF16).rearrange("p (h s) -> p h s", h=4) for _ in range(2)]; rqn = [P.res() for _ in range(2)]
        qp = [self.alloc(4 * TS, BF16).rearrange("p (h s) -> p h s", h=4) for _ in range(2)]; rqp = [P.res() for _ in range(2)]
        NPT = 4
        pt = [self.alloc(TS, BF16) for _ in range(NPT)]; rpt = [P.res() for _ in range(NPT)]
        rec = [self.alloc(TS, F32) for _ in range(2)]; rrec = [P.res() for _ in range(2)]
        ot = [self.alloc(TS, BF16) for _ in range(2)]; rot = [P.res() for _ in range(2)]
        qnv = self.QN.rearrange("(h p) s -> p h s", p=128)
        knv = self.KN.rearrange("(h p) s -> p h s", p=128)
        qpv = self.QP.rearrange("h p s -> p h s")
        v2v = self.V2.rearrange("(n p) f -> p n f", p=128)
        it = 0
        ipt = 0
        ih = 0
        for g in range(2):
            self.dma("sp", kng[:, 0:2], knv[:, 4 * g:4 * g + 2, :], w=[rkn])
            self.dma("pool", kng[:, 2:4], knv[:, 4 * g + 2:4 * g + 4, :], w=[rkn])
            self.dma("sp", vg[:, 0:16], v2v[:, 0:16, 512 * g:512 * g + 512], w=[rvg])
            self.dma("pool", vg[:, 16:32], v2v[:, 16:32, 512 * g:512 * g + 512], w=[rvg])
            for t in range(NT):
                b = it % 2
                it += 1
                sl = slice(t * TS, (t + 1) * TS)
                self.dma("sp", qn[b], qnv[:, 4 * g:4 * g + 4, sl], w=[rqn[b]])
                self.dma("pool", qp[b][0:64], qpv[:, 4 * g:4 * g + 4, sl], w=[rqp[b]])
                for hl in range(4):
                    h = 4 * g + hl
                    hb = ih % 2
                    ih += 1
                    OT, rOT = self.ps[4 + hb], self.psr[4 + hb]
                    DS, rDS = self.ps[6 + hb], self.psr[6 + hb]
                    sbanks = {}

                    def score(m):
                        sb_, rsb = self.bank()
                        self.mm(sb_[:], kng[:, hl, m * 128:(m + 1) * 128], qn[b][:, hl, :], True, False, r=[rkn, rqn[b]], w=[rsb])
                        self.mm(sb_[:], kpb[0:64, m * 128:(m + 1) * 128], qp[b][0:64, hl, :], False, True, r=[rkp, rqp[b]], w=[rsb])
                        sbanks[m] = (sb_, rsb)

                    score(0)
                    score(1)
                    for m in range(32):
                        if m + 2 < 32:
                            score(m + 2)
                        sb_, rsb = sbanks.pop(m)
                        pi = ipt % NPT
                        ipt += 1
                        self.act(pt[pi], sb_[:], AF.Exp, r=[rsb], w=[rpt[pi]])
                        self.mm(OT[:], vg[:, m, hl * 128:(hl + 1) * 128], pt[pi], m == 0, m == 31, r=[rvg, rpt[pi]], w=[rOT])
                        self.mm(DS[:], self.cb(C_ONES), pt[pi], m == 0, m == 31, r=[rc, rpt[pi]], w=[rDS])
                    self.recip(rec[hb], DS[:], r=[rDS], w=[rrec[hb]])
                    self.tt("dve", ot[hb], OT[:], rec[hb], ALU.mult, r=[rOT, rrec[hb]], w=[rot[hb]])
                    self.dma("sp", self.O[h * 128:(h + 1) * 128, sl], ot[hb], r=[rot[hb]])
        self.bank_list = list(range(8))
        P.barrier()

    def pre3(self):
        P = self.P
        rc = self.rc
        self.pre_setup()
        self.bank_list = list(range(8))
        w, rw = self.load_w("hg_w_in", 1024, 5120)
        self.alloc_szt()
        qst = self.alloc(8 * TS, BF16).rearrange("p (c s) -> p c s", c=8); rqst = P.res()
        lf = self.alloc(8 * TS, F32).rearrange("p (c s) -> p c s", c=8); rlf = P.res()
        vt = self.alloc(4 * 1024, BF16).rearrange("p (a f) -> p a f", a=4); rvt = P.res()
        ev = 0
        for i, sq, t, hT, rh in self.pre_iter(3):
            b = i % 2
            sl = slice(t * TS, (t + 1) * TS)
            qsv = self.QS.rearrange("(c p) s -> p c s", p=128)
            viv = self.VI.rearrange("(n p) f -> p n f", p=128)
            for j in range(8):
                pb, rpb = self.proj_fm(w, rw, j * 128, hT, rh)
                self.act(qst[:, j, :], pb[:], AF.Silu, r=[rpb], w=[rqst])
            self.dma("sp", qsv[:, :, sl], qst, r=[rqst])
            for d in range(2):
                for j in range(8):
                    pb, rpb = self.proj_fm(w, rw, 1024 * (1 + d) + j * 128, hT, rh)
                    self.act(lf[:, j, :], pb[:], AF.Sigmoid, r=[rpb], w=[rlf])
                    self.ts("dve", lf[:, j, :], lf[:, j, :], self.oml[:, d, j:j + 1], self.lb[:, d, j:j + 1], ALU.mult, ALU.add,
                            r=[rlf, rc], w=[rlf])
                self.act(lf, lf, AF.Ln, r=[rlf], w=[rlf])
                lfv = self.LF[d].rearrange("(c p) s -> p c s", p=128)
                self.dma("sp", lfv[:, 0:4, sl], lf[:, 0:4, :], r=[rlf])
                self.dma("pool", lfv[:, 4:8, sl], lf[:, 4:8, :], r=[rlf])
            for sub in range(4):
                for ct in range(2):
                    pb, rpb = self.proj_tm(w, rw, 3072 + ct * 512, hT, rh, sub)
                    self.cp("act" if ev % 2 == 0 else "dve", vt[:, sub, ct * 512:(ct + 1) * 512], pb[:], r=[rpb], w=[rvt])
                    ev += 1
            self.dma("pool", viv[:, 4 * t:4 * t + 4, :], vt, r=[rvt])
            self.z_tile(w, rw, 4096, hT, rh, t, b)
        P.barrier()

    def mid3(self, sq):
        P = self.P
        rc = self.rc
        self.areset()
        self.bank_i = 0
        self.bank_list = [0, 1, 2, 3, 4, 5]
        L = 128
        NCH = 32
        rmask = self.alloc(S, BF16); rrm = P.res()
        self.memset("pool", rmask, 1.0, w=[rrm])
        self.memset("pool", rmask.rearrange("p (n l) -> p n l", l=L)[:, :, 0:1], 0.0, w=[rrm])
        qs = self.alloc(S, BF16); rqs = P.res()
        lf = [self.alloc(S, F32) for _ in range(2)]; rlf = [P.res() for _ in range(2)]
        C = self.alloc(S, F32); rC = P.res()
        E = self.alloc(S, F32); rE = P.res()
        K1 = self.alloc(S, F32); rK1 = P.res()
        qt = [self.alloc(S, BF16) for _ in range(2)]; rqt = [P.res() for _ in range(2)]
        kt = [self.alloc(S, BF16) for _ in range(2)]; rkt = [P.res() for _ in range(2)]
        ktm = [self.alloc(S, BF16).rearrange("p (n c) -> p n c", n=NCH) for _ in range(2)]; rktm = [P.res() for _ in range(2)]
        vh = self.alloc(S, BF16).rearrange("p (n d) -> p n d", n=NCH); rvh = P.res()
        Sb = [self.alloc(S, BF16).rearrange("p (n d) -> p n d", n=NCH) for _ in range(2)]; rSb = [P.res() for _ in range(2)]
        Wst = [[self.alloc(128, F32) for _ in range(2)] for _ in range(2)]; rW = [[P.res() for _ in range(2)] for _ in range(2)]
        ecl = [self.alloc(NCH, F32) for _ in range(2)]; recl = [P.res() for _ in range(2)]
        oT = self.alloc(S, BF16); roT = P.res()
        Am4 = [self.alloc(4 * 256, BF16).rearrange("p (k x) -> p k x", k=4) for _ in range(2)]; rAm4 = [P.res() for _ in range(2)]
        on4 = [self.alloc(512, BF16) for _ in range(2)]; ron4 = [P.res() for _ in range(2)]
        sq4 = [self.alloc(512, F32) for _ in range(2)]; rsq4 = [P.res() for _ in range(2)]
        ss = self.alloc(NCH, F32); rss = P.res()
        rs_ = self.alloc(NCH, F32); rrs = P.res()
        viv = self.VI.rearrange("(n p) f -> p n f", p=128)
        ch = lambda x, n: x[:, n * L:(n + 1) * L]
        for h in range(8):
            hs = slice(h * 128, (h + 1) * 128)
            self.dma("sp", qs, self.QS[hs, :], w=[rqs])
            self.dma("pool", vh, viv[:, :, hs], w=[rvh])
            self.dma("sp", lf[0], self.LF[0, hs, :], w=[rlf[0]])
            self.dma("pool", lf[1], self.LF[1, hs, :], w=[rlf[1]])
            self.memset("pool", ss, 0.0, w=[rss])
            for d in range(2):
                self.act(K1, lf[d], AF.Exp, r=[rlf[d]], w=[rK1])
                self.P.op("dve", lambda e, d=d: e.tensor_tensor_scan(out=C, data0=rmask, data1=lf[d], initial=0.0, op0=ALU.mult, op1=ALU.add),
                          r=[rrm, rlf[d]], w=[rC])
                Cl = C.rearrange("p (n l) -> p n l", l=L)[:, :, L - 1]
                if d == 0:
                    self.act(E, C, AF.Exp, r=[rC], w=[rE])
                    self.cp("dve", ecl[0], E.rearrange("p (n l) -> p n l", l=L)[:, :, L - 1], r=[rE], w=[recl[0]])
                    self.act(C, C, AF.Exp, r=[rC], w=[rC], scale=-1.0)
                else:
                    self.act(ecl[1], Cl, AF.Exp, r=[rC], w=[recl[1]])
                    self.tt("dve", C, C, lf[1], ALU.subtract, r=[rC, rlf[1]], w=[rC])
                    self.act(E, C, AF.Exp, r=[rC], w=[rE], scale=-1.0)
                    self.act(C, C, AF.Exp, r=[rC], w=[rC])
                self.ts("dve", K1, K1, -1.0, 1.0, ALU.mult, ALU.add, r=[rK1], w=[rK1])
                self.tt("dve", qt[d], qs, E, ALU.mult, r=[rqs, rE], w=[rqt[d]])
                self.tt("dve", kt[d], K1, C, ALU.mult, r=[rK1, rC], w=[rkt[d]])
            ev = 0
            for d in range(2):
                for n8 in range(4):
                    tb, rtb = self.bank()
                    tbv = tb[:].bitcast(BF16)
                    for k in range(8):
                        n = n8 * 8 + k
                        self.tr(tbv[:, k * 128:(k + 1) * 128], ch(kt[d], n), self.cb(C_ID), r=[rkt[d], rc], w=[rtb])
                    self.cp("act" if ev % 2 == 0 else "dve", ktm[d][:, n8 * 8:(n8 + 1) * 8, :].rearrange("p n c -> p (n c)"), tbv, r=[rtb], w=[rktm[d]])
                    ev += 1
            self.memset("pool", Sb[0][:, 0, :], 0.0, w=[rSb[0]])
            self.memset("pool", Sb[1][:, NCH - 1, :], 0.0, w=[rSb[1]])
            ub = [None, None]
            wi = [0, 0]
            for i in range(NCH):
                for d in range(2):
                    n = i if d == 0 else NCH - 1 - i
                    if i % 4 == 0:
                        ub[d] = self.bank()
                        for k in range(4):
                            nn = i + k if d == 0 else NCH - 1 - i - k
                            self.mm(ub[d][0][:, k * 128:(k + 1) * 128], ktm[d][:, nn, :], vh[:, nn, :], True, True, r=[rktm[d], rvh], w=[ub[d][1]])
                    U = ub[d][0][:, (i % 4) * 128:(i % 4 + 1) * 128]
                    rU = ub[d][1]
                    cur = wi[d]
                    if i == 0:
                        self.cp("dve", Wst[d][cur], U, r=[rU], w=[rW[d][cur]])
                    else:
                        ei = (i - 1) if d == 0 else n
                        esc = ecl[d][:, ei:ei + 1]
                        self.act(Sb[d][:, n, :], Wst[d][cur], AF.Copy, r=[rW[d][cur], recl[d]], w=[rSb[d]], scale=esc)
                        nxt = 1 - cur
                        self.stt(Wst[d][nxt], Wst[d][cur], esc, U, ALU.mult, ALU.add, r=[rW[d][cur], recl[d], rU], w=[rW[d][nxt]])
                        wi[d] = nxt
            mask2 = self.cf(C_MU, 256).unsqueeze(1).to_broadcast([128, 2, 256])
            for gq in range(NCH // 4):
                gi = gq % 2
                for half in range(2):
                    ab, rab = self.bank()
                    for k2 in range(2):
                        n = gq * 4 + half * 2 + k2
                        self.mm(ab[:, k2 * 256:k2 * 256 + 128], ch(kt[0], n), ch(qt[0], n), True, True, r=[rkt[0], rqt[0]], w=[rab])
                        self.mm(ab[:, k2 * 256 + 128:k2 * 256 + 256], ch(kt[1], n), ch(qt[1], n), True, True, r=[rkt[1], rqt[1]], w=[rab])
                    self.tt("dve", Am4[gi][:, half * 2:half * 2 + 2, :], ab[:].rearrange("p (k x) -> p k x", k=2), mask2, ALU.mult,
                            r=[rab, rc], w=[rAm4[gi]])
                ob, rob = self.bank()
                ov4 = ob[:].rearrange("p (k d) -> p k d", k=4)
                for k in range(4):
                    n = gq * 4 + k
                    o = ov4[:, k, :]
                    self.mm(o, Am4[gi][:, k, 0:128], vh[:, n, :], True, False, r=[rAm4[gi], rvh], w=[rob])
                    self.mm(o, ch(qt[0], n), Sb[0][:, n, :], False, False, r=[rqt[0], rSb[0]], w=[rob])
                    self.mm(o, Am4[gi][:, k, 128:256], vh[:, n, :], False, False, r=[rAm4[gi], rvh], w=[rob])
                    self.mm(o, ch(qt[1], n), Sb[1][:, n, :], False, True, r=[rqt[1], rSb[1]], w=[rob])
                self.act(sq4[gi], ob[:], AF.Square, r=[rob], w=[rsq4[gi]])
                ssg = ss[:, gq * 4:gq * 4 + 4]
                rsg = rs_[:, gq * 4:gq * 4 + 4]
                self.P.op("dve", lambda e, o_=ssg, i_=sq4[gi].rearrange("p (k d) -> p k d", k=4): e.tensor_reduce(out=o_, in_=i_, axis=AX.X, op=ALU.add),
                          r=[rsq4[gi]], w=[rss])
                self.act(rsg, ssg, AF.Sqrt, r=[rss, rc], w=[rrs], scale=1.0 / 128, bias=self.epsb[:, 0:1])
                self.recip(rsg, rsg, r=[rrs], w=[rrs])
                self.tt("dve", on4[gi].rearrange("p (k d) -> p k d", k=4), ov4, rsg.unsqueeze(2).to_broadcast([128, 4, 128]), ALU.mult,
                        r=[rob, rrs], w=[ron4[gi]])
                if gq % 2 == 0:
                    ti = 6 + (gq // 2) % 2
                    tb, rtb = self.ps[ti], self.psr[ti]
                    tbv = tb[:].bitcast(BF16)
                for k in range(4):
                    col = ((gq % 2) * 4 + k) * 128
                    self.tr(tbv[:, col:col + 128], on4[gi][:, k * 128:(k + 1) * 128], self.cb(C_ID), r=[ron4[gi], rc], w=[rtb])
                if gq % 2 == 1:
                    n0 = (gq - 1) * 4
                    self.act(oT[:, n0 * L:(n0 + 8) * L], tbv, AF.Copy, r=[rtb, rc], w=[roT], scale=self.gout[:, h:h + 1])
            self.dma("sp", self.O[hs, :], oT, r=[roT])
        self.bank_list = list(range(8))
        P.barrier()


_CACHE = {}


def _common_inputs(inp):
    cst, m1, rope = make_consts()
    d = {"cst": cst, "m1": m1, "rope": rope}
    for name, k, n in W_SPECS:
        d[name] = np.ascontiguousarray(np.asarray(inp[name], np.float32)[0])
    d["ple_w"] = np.ascontiguousarray(np.asarray(inp["ple_w"], np.float32))
    d["ple_gate_w"] = np.ascontiguousarray(np.asarray(inp["ple_gate_w"], np.float32))
    d["norm_g"] = np.ascontiguousarray(np.asarray(inp["norm_g"], np.float32))
    d["final_g"] = np.ascontiguousarray(np.asarray(inp["final_g"], np.float32))
    d["fn_w_mix"] = np.ascontiguousarray(np.asarray(inp["fn_w_mix"], np.float32)[0])
    d["rpb_flip"] = np.ascontiguousarray(np.asarray(inp["na_rpb"], np.float32)[0][:, :, ::-1])
    d["mla_g_q"] = np.ascontiguousarray(np.asarray(inp["mla_g_q"], np.float32)[0])
    d["mla_g_kv"] = np.ascontiguousarray(np.asarray(inp["mla_g_kv"], np.float32)[0])
    d["hg_lb_raw"] = np.ascontiguousarray(np.asarray(inp["hg_lb_raw"], np.float32))
    d["hg_g_out"] = np.ascontiguousarray(np.asarray(inp["hg_g_out"], np.float32)[0])
    return d


def run_seqs(inp, xs, ps, nseq, layers, final_norm, ncores, trace=False):
    key = (nseq, tuple(layers), final_norm)
    if key not in _CACHE:
        bld = Builder(nseq, layers, final_norm)
        nc = bld.build()
        _CACHE[key] = (nc, bld.stats)
    nc, stats = _CACHE[key]
    com = _common_inputs(inp)
    in_maps = []
    for c in range(ncores):
        m = dict(com)
        m["xT"] = np.ascontiguousarray(np.transpose(xs[c], (0, 2, 1)))
        m["pT"] = np.ascontiguousarray(np.transpose(ps[c], (0, 1, 3, 2)))
        in_maps.append(m)
    if trace:
        res = run_bass_kernel_spmd(nc, in_maps, core_ids=list(range(ncores)), trace=True)
        print('EXEC_TIME_NS', res.exec_time_ns)
    else:
        res = run_bass_kernel_spmd(nc, in_maps, core_ids=list(range(ncores)))
    return [np.transpose(r["yT"], (0, 2, 1)) for r in res.results]


def kernel(**inp):
    xp = np.asarray(inp["x_prompt"], np.float32)
    xsm = np.asarray(inp["x_sample"], np.float32)
    pp = np.asarray(inp["p_prompt"], np.float32)
    psm = np.asarray(inp["p_sample"], np.float32)
    xall = np.concatenate([xp, xsm], 0)
    pall = np.concatenate([pp, psm], 1)
    nseq = 3
    idx = [[c, c + 8, (c + 16) if c + 16 < 20 else c] for c in range(8)]
    xs = [xall[idx[c]] for c in range(8)]
    ps = [pall[:, idx[c]] for c in range(8)]
    outs = run_seqs(inp, xs, ps, nseq, [0, 1, 2, 3], True, 8)
    yall = np.zeros_like(xall)
    for c in range(8):
        for s_, gi in enumerate(idx[c]):
            if s_ < 2 or c + 16 < 20:
                yall[gi] = outs[c][s_]
    return (np.ascontiguousarray(yall[:4]), np.ascontiguousarray(yall[4:]))
```

```python
import math
import numpy as np
from contextlib import ExitStack
import concourse.bass as bass
import concourse.mybir as mybir
from concourse.bass_utils import run_bass_kernel_spmd

F32 = mybir.dt.float32
BF16 = mybir.dt.bfloat16
AF = mybir.ActivationFunctionType
ALU = mybir.AluOpType
AX = mybir.AxisListType

ENGS = ("pe", "act", "dve", "pool", "sp")
S = 4096
D = 1024
NT = 8
TS = 512
EPS = 1e-6
NEG = -30000.0


class Res:
    __slots__ = ("name", "w", "rs")

    def __init__(self, name):
        self.name = name
        self.w = None
        self.rs = []


class Op:
    __slots__ = ("eng", "fn", "deps", "dma", "ticket", "inc")

    def __init__(self, eng, fn, dma):
        self.eng = eng
        self.fn = fn
        self.deps = []
        self.dma = dma
        self.ticket = None
        self.inc = dma


class Prog:
    DMA_K = 8
    DMA_CAP = 1500
    CMP_CAP = 30000

    def __init__(self, nc, stack):
        self.nc = nc
        self.stack = stack
        self.ops = []
        self.eobj = {"pe": nc.tensor, "act": nc.scalar, "dve": nc.vector, "pool": nc.gpsimd, "sp": nc.sync}
        self.dma_hist = {e: [] for e in ENGS}
        self.last = {e: None for e in ENGS}
        self.nres = 0

    def res(self, name=None):
        self.nres += 1
        return Res(name or f"r{self.nres}")

    def op(self, eng, fn, r=(), w=(), dma=False):
        o = Op(eng, fn, dma)
        deps = {}
        for x in r:
            if x.w is not None:
                deps[id(x.w)] = (x.w, "raw")
        for x in w:
            if x.w is not None and id(x.w) not in deps:
                deps[id(x.w)] = (x.w, "waw")
            for rd in x.rs:
                if id(rd) not in deps:
                    deps[id(rd)] = (rd, "war")
        if dma:
            h = self.dma_hist[eng]
            if len(h) >= self.DMA_K:
                p = h[-self.DMA_K]
                deps.setdefault(id(p), (p, "rot"))
            h.append(o)
        for d, kind in deps.values():
            if d is o:
                continue
            if (not d.dma) and (not dma) and d.eng == eng:
                if eng == "pe" or kind != "raw":
                    continue
            o.deps.append(d)
            d.inc = True
        for x in w:
            x.w = o
            x.rs = []
        for x in r:
            x.rs.append(o)
        self.ops.append(o)
        if not dma:
            self.last[eng] = o
        return o

    def barrier(self):
        deps = []
        for e in ENGS:
            if self.last[e] is not None:
                deps.append(self.last[e])
            deps.extend(self.dma_hist[e][-self.DMA_K:])
        for e in ENGS:
            o = Op(e, None, False)
            for d in deps:
                if (not d.dma) and d.eng == e:
                    continue
                o.deps.append(d)
                d.inc = True
            self.ops.append(o)

    def emit(self):
        nc = self.nc
        cmp_sem = {}
        cmp_cnt = {}
        dma_sems = {}
        dma_idx = {e: 0 for e in ENGS}
        known = {e: {} for e in ENGS}
        self.nsem = 0

        def newsem(nm):
            self.nsem += 1
            return self.stack.enter_context(nc.semaphore(f"{nm}_{self.nsem}"))

        nwait = 0
        for o in self.ops:
            E = self.eobj[o.eng]
            kn = known[o.eng]
            for d in o.deps:
                sem, val = d.ticket
                key = id(sem)
                if kn.get(key, 0) >= val:
                    continue
                E.wait_ge(sem, val)
                nwait += 1
                kn[key] = val
            if o.fn is None:
                continue
            ins = o.fn(E)
            if o.dma:
                i = dma_idx[o.eng]
                dma_idx[o.eng] = i + 1
                st = i // (self.DMA_K * self.DMA_CAP)
                j = i % (self.DMA_K * self.DMA_CAP)
                keyq = (o.eng, st)
                if keyq not in dma_sems:
                    dma_sems[keyq] = [newsem(f"d{o.eng}") for _ in range(self.DMA_K)]
                sem = dma_sems[keyq][j % self.DMA_K]
                val = 16 * (j // self.DMA_K + 1)
                ins.then_inc(sem, 16)
                o.ticket = (sem, val)
            elif o.inc:
                c = cmp_cnt.get(o.eng, 0)
                if o.eng not in cmp_sem or c >= self.CMP_CAP:
                    cmp_sem[o.eng] = newsem(f"c{o.eng}")
                    c = 0
                c += 1
                cmp_cnt[o.eng] = c
                ins.then_inc(cmp_sem[o.eng], 1)
                o.ticket = (cmp_sem[o.eng], c)
            o.fn = None
        return dict(nops=len(self.ops), nwait=nwait, nsem=self.nsem)


C_ONES, C_ID, C_CC, C_SC, C_M3, C_MU, C_ML, C_MC = 0, 128, 256, 384, 512, 640, 768, 896
C_TOT = 900


def make_consts():
    c = np.zeros((128, C_TOT), np.float64)
    c[:, C_ONES:C_ONES + 128] = 1.0
    c[:, C_ID:C_ID + 128] = np.eye(128)
    a = np.arange(128)
    ang = 2 * np.pi * np.outer(a, a) / 128.0
    c[:, C_CC:C_CC + 128] = np.cos(ang) / math.sqrt(128.0)
    c[:, C_SC:C_SC + 128] = np.sin(ang) / math.sqrt(128.0)
    s0 = np.arange(64)
    k1 = np.arange(64)
    phi = 2 * np.pi * np.outer(s0, k1) / 64.0
    m3 = np.zeros((128, 128))
    m3[0:64, 0:64] = np.cos(phi)
    m3[64:128, 0:64] = np.sin(phi)
    m3[0:64, 64:128] = -np.sin(phi)
    m3[64:128, 64:128] = np.cos(phi)
    c[:, C_M3:C_M3 + 128] = m3 / 8.0
    c[:, C_MU:C_MU + 128] = (a[:, None] <= a[None, :])
    c[:, C_ML:C_ML + 128] = (a[:, None] >= a[None, :])
    for hm in range(4):
        c[:, C_MC + hm] = np.where(a // 32 == hm, 32.0 ** -0.5, 0.0)
    m1 = np.zeros((128, 32, 128))
    p = np.arange(128)
    s1 = (p % 64)[:, None, None]
    half = (p // 64)[:, None, None]
    j = np.arange(32)[None, :, None]
    k0 = np.arange(64)[None, None, :]
    s0v = 32 * half + j
    th = 2 * np.pi * (s1 * k0 / 64.0 + s0v * k0 / 4096.0)
    m1[:, :, 0:64] = np.cos(th) / 8.0
    m1[:, :, 64:128] = -np.sin(th) / 8.0
    half_d = 32
    inv = 10000.0 ** (-np.arange(half_d, dtype=np.float64) / half_d)
    inv = inv.astype(np.float32).astype(np.float64)
    pos = np.arange(S, dtype=np.float64)
    angf = (pos[None, :].astype(np.float32) * inv[:, None].astype(np.float32)).astype(np.float64)
    cs = np.cos(angf)
    sn = np.sin(angf)
    cosT = np.concatenate([cs, cs], 0)
    sinT = np.concatenate([-sn, sn], 0)
    sc = 192.0 ** -0.5
    rope = np.stack([cosT * sc, sinT * sc, cosT, sinT], 1)
    return c.astype(np.float32), m1.reshape(128, 32 * 128).astype(np.float32), rope.astype(np.float32)


def na_rows():
    kh = 8
    r = np.arange(64)
    rs = np.clip(r - kh // 2, 0, 64 - kh)
    return rs


W_SPECS = [
    ("fn_w_in", 1024, 2048), ("fn_w_out", 1024, 1024),
    ("na_w_in", 1024, 4096), ("na_w_out", 1024, 1024),
    ("mla_w_in", 1024, 1728), ("mla_w_uq", 384, 1536), ("mla_w_ukv", 256, 2048), ("mla_w_out", 1024, 1024),
    ("hg_w_in", 1024, 5120), ("hg_w_out", 1024, 1024),
]


class Builder:
    def __init__(self, nseq, layers, final_norm=True):
        self.nseq = nseq
        self.layers = layers
        self.final_norm = final_norm

    def dma(self, q, out, in_, r=(), w=()):
        return self.P.op(q, lambda e: e.dma_start(out=out, in_=in_), r=r, w=w, dma=True)

    def dma_slow(self, q, out, in_, r=(), w=()):
        return self.P.op(q, lambda e: e.dma_start(out=out, in_=in_, allow_slow_non_contiguous=True), r=r, w=w, dma=True)

    def mm(self, out, lhsT, rhs, start, stop, r=(), w=()):
        return self.P.op("pe", lambda e: e.matmul(out, lhsT=lhsT, rhs=rhs, start=start, stop=stop), r=r, w=w)

    def tr(self, out, in_, ident, r=(), w=()):
        return self.P.op("pe", lambda e: e.transpose(out=out, in_=in_, identity=ident), r=r, w=w)

    def act(self, out, in_, func, r=(), w=(), scale=1.0, bias=None, accum=None):
        def f(e):
            kw = {}
            if bias is not None:
                kw["bias"] = bias
            if accum is not None:
                kw["accum_out"] = accum
            return e.activation(out=out, in_=in_, func=func, scale=scale, **kw)
        return self.P.op("act", f, r=r, w=w)

    def tt(self, eng, out, a, b, op, r=(), w=()):
        return self.P.op(eng, lambda e: e.tensor_tensor(out=out, in0=a, in1=b, op=op), r=r, w=w)

    def ts(self, eng, out, a, s1, s2, op0, op1=None, r=(), w=()):
        def f(e):
            if op1 is None:
                return e.tensor_scalar(out=out, in0=a, scalar1=s1, scalar2=None, op0=op0)
            return e.tensor_scalar(out=out, in0=a, scalar1=s1, scalar2=s2, op0=op0, op1=op1)
        return self.P.op(eng, f, r=r, w=w)

    def stt(self, out, in0, scalar, in1, op0, op1, r=(), w=()):
        return self.P.op("dve", lambda e: e.scalar_tensor_tensor(out=out, in0=in0, scalar=scalar, in1=in1, op0=op0, op1=op1), r=r, w=w)

    def cp(self, eng, out, in_, r=(), w=()):
        if eng == "act":
            return self.act(out, in_, AF.Copy, r=r, w=w)
        return self.P.op(eng, lambda e: e.tensor_copy(out=out, in_=in_), r=r, w=w)

    def memset(self, eng, ap, val, r=(), w=()):
        return self.P.op(eng, lambda e: e.memset(ap, val), r=r, w=w)

    def recip(self, out, in_, r=(), w=()):
        return self.P.op("dve", lambda e: e.reciprocal(out=out, in_=in_), r=r, w=w)

    def areset(self):
        self.aoff = 0

    def alloc(self, cols, dt=BF16):
        nb = cols * (4 if dt == F32 else 2)
        nb = (nb + 63) // 64 * 64
        off = self.aoff
        self.aoff += nb
        assert self.aoff <= self.asize, f"arena overflow {self.aoff} > {self.asize}"
        v = self.arena[:, off // 2:(off + nb) // 2]
        if dt == F32:
            v = v.bitcast(F32)
            return v[:, 0:cols]
        return v[:, 0:cols]

    def bank(self):
        bl = self.bank_list
        i = bl[self.bank_i % len(bl)]
        self.bank_i += 1
        return self.ps[i], self.psr[i]

    def build(self):
        nc = bass.Bass("TRN2", target_bir_lowering=False)
        self.nc = nc
        NS = self.nseq
        dt_in = lambda name, shape: nc.dram_tensor(name, shape, F32, kind="ExternalInput").ap()
        self.xT = dt_in("xT", [NS, D, S])
        self.pT = dt_in("pT", [4, NS, 256, S])
        self.cst_d = dt_in("cst", [128, C_TOT])
        self.m1_d = dt_in("m1", [128, 32 * 128])
        self.rope_d = dt_in("rope", [64, 4, S])
        self.wd = {}
        for name, k, n in W_SPECS:
            self.wd[name] = dt_in(name, [k, n])
        self.wd["ple_w"] = dt_in("ple_w", [4, 256, D])
        self.wd["ple_gate_w"] = dt_in("ple_gate_w", [4, D, D])
        self.norm_g_d = dt_in("norm_g", [4, D])
        self.final_g_d = dt_in("final_g", [D])
        self.fn_w_mix_d = dt_in("fn_w_mix", [8, 128, 128])
        self.rpbf_d = dt_in("rpb_flip", [32, 15, 31])
        self.g_q_d = dt_in("mla_g_q", [384])
        self.g_kv_d = dt_in("mla_g_kv", [256])
        self.lb_d = dt_in("hg_lb_raw", [4, 2, D])
        self.g_out_d = dt_in("hg_g_out", [D])
        self.yT = nc.dram_tensor("yT", [NS, D, S], F32, kind="ExternalOutput").ap()
        scr = lambda name, shape, dt=BF16: nc.dram_tensor(name, shape, dt, kind="Internal").ap()
        self.X = scr("X", [NS, D, S], F32)
        self.SA = {}
        for nm, shp, dt_ in [("SZ", [D, S], BF16), ("O", [D, S], BF16), ("Ud", [S, D], BF16), ("Td", [2, 64, 64, D], BF16),
                             ("QT", [D, S], BF16), ("KT", [D, S], BF16), ("VA", [S, 32 * 33], BF16),
                             ("QN", [D, S], BF16), ("QP", [8, 64, S], BF16), ("KN", [D, S], BF16), ("KP", [64, S], BF16),
                             ("V2", [S, D], BF16), ("QS", [D, S], BF16), ("LF", [2, D, S], F32), ("VI", [S, D], BF16)]:
            self.SA[nm] = scr(nm, [NS] + shp, dt_)
        self.w16 = {}
        for name, k, n in W_SPECS:
            self.w16[name] = scr("b_" + name, [k, n])
        self.w16["ple_w"] = scr("b_ple_w", [4, 256, D])
        self.w16["ple_gate_w"] = scr("b_ple_gate_w", [4, D, D])
        self.BT = scr("BT", [128, 32 * 14 * 64])

        with ExitStack() as st:
            self.P = Prog(nc, st)
            P = self.P
            sbt = lambda name, shape, dt: st.enter_context(nc.sbuf_tensor(name, shape, dt))
            self.cst = sbt("cstf", [128, C_TOT], F32)
            self.cstb = sbt("cstb", [128, C_TOT], BF16)
            self.ng = sbt("ng", [128, 4, 8], F32)
            self.fg = sbt("fg", [128, 8], F32)
            self.gq = sbt("gq", [128, 3], F32)
            self.gkv = sbt("gkv", [128, 2], F32)
            self.gout = sbt("gout", [128, 8], F32)
            self.lbraw = sbt("lbraw", [128, 4, 2, 8], F32)
            self.lb = sbt("lb", [128, 2, 8], F32)
            self.oml = sbt("oml", [128, 2, 8], F32)
            self.epsb = sbt("epsb", [128, 1], F32)
            self.asize = 200 * 1024
            self.arena = sbt("arena", [128, self.asize // 2], BF16)
            self.ps = [st.enter_context(nc.psum_tensor(f"ps{i}", [128, 512], F32)) for i in range(8)]
            self.psr = [P.res(f"ps{i}") for i in range(8)]
            self.bank_i = 0
            self.bank_list = list(range(8))
            self.rc = P.res("consts")

            self.phase_prep()
            for li in self.layers:
                last = (li == self.layers[-1])
                [self.pre0, self.pre1, self.pre2, self.pre3][li]()
                for sq in range(NS):
                    self.bind(sq)
                    [self.mid0, self.mid1, self.mid2, self.mid3][li](sq)
                self.post_layer(li, last, last and self.final_norm)
            stats = P.emit()
            self.stats = stats
        return nc

    def bind(self, sq):
        for nm, ap in self.SA.items():
            setattr(self, nm, ap[sq])

    def xsrc_of(self, li, sq):
        return self.xT[sq] if li == self.layers[0] else self.X[sq]

    def pre_iter(self, li):
        tiles = [(sq, t) for sq in range(self.nseq) for t in range(NT)]
        nxt = self.pre_tile(li, self.xsrc_of(li, tiles[0][0]), tiles[0][1], 0)
        for i, (sq, t) in enumerate(tiles):
            cur = nxt
            if i + 1 < len(tiles):
                nxt = self.pre_tile(li, self.xsrc_of(li, tiles[i + 1][0]), tiles[i + 1][1], i + 1)
            self.bind(sq)
            yield i, sq, t, cur[0], cur[1]

    def cb(self, off, n=128):
        return self.cstb[:, off:off + n]

    def cf(self, off, n=128):
        return self.cst[:, off:off + n]

    def phase_prep(self):
        P = self.P
        rc = self.rc
        self.areset()
        self.dma("sp", self.cst[:], self.cst_d[:, :], w=[rc])
        self.cp("dve", self.cstb[:], self.cst[:], r=[rc], w=[rc])
        self.memset("pool", self.epsb[:], EPS, w=[rc])
        self.dma_slow("sp", self.ng[:], self.norm_g_d.rearrange("l (c p) -> p l c", p=128), w=[rc])
        self.dma_slow("sp", self.fg[:], self.final_g_d.rearrange("(c p) -> p c", p=128), w=[rc])
        self.dma_slow("sp", self.gq[:], self.g_q_d.rearrange("(c p) -> p c", p=128), w=[rc])
        self.dma_slow("sp", self.gkv[:], self.g_kv_d.rearrange("(c p) -> p c", p=128), w=[rc])
        self.dma_slow("sp", self.gout[:], self.g_out_d.rearrange("(c p) -> p c", p=128), w=[rc])
        for l in range(4):
            for d in range(2):
                self.dma_slow("sp", self.lbraw[:, l, d, :], self.lb_d[l, d].rearrange("(c p) -> p c", p=128), w=[rc])
        ex = self.alloc(64, F32)
        sm = self.alloc(16, F32)
        exv = ex.rearrange("p (l x) -> p l x", l=4)
        self.act(exv, self.lbraw[:].rearrange("p l d c -> p l (d c)"), AF.Exp, r=[rc], w=[rc])
        self.tt("dve", sm, exv[:, 0, :], exv[:, 1, :], ALU.add, r=[rc], w=[rc])
        self.tt("dve", sm, sm, exv[:, 2, :], ALU.add, r=[rc], w=[rc])
        self.tt("dve", sm, sm, exv[:, 3, :], ALU.add, r=[rc], w=[rc])
        self.recip(sm, sm, r=[rc], w=[rc])
        omlv = self.oml[:].rearrange("p d c -> p (d c)")
        lbv = self.lb[:].rearrange("p d c -> p (d c)")
        self.tt("dve", omlv, exv[:, 0, :], sm, ALU.mult, r=[rc], w=[rc])
        self.ts("dve", lbv, omlv, -1.0, 1.0, ALU.mult, ALU.add, r=[rc], w=[rc])
        stg = [self.alloc(5120, F32) for _ in range(2)]
        stb = [self.alloc(5120, BF16) for _ in range(2)]
        rs_ = [P.res() for _ in range(2)]
        rb_ = [P.res() for _ in range(2)]
        cnt = 0
        engs = ["dve", "pool", "act"]
        jobs = []
        for name, k, n in W_SPECS:
            src = self.wd[name]
            dst = self.w16[name]
            for c in range(k // 128):
                jobs.append((src[c * 128:(c + 1) * 128, :], dst[c * 128:(c + 1) * 128, :], n))
        for l in range(4):
            for c in range(2):
                jobs.append((self.wd["ple_w"][l, c * 128:(c + 1) * 128, :], self.w16["ple_w"][l, c * 128:(c + 1) * 128, :], D))
            for c in range(8):
                jobs.append((self.wd["ple_gate_w"][l, c * 128:(c + 1) * 128, :], self.w16["ple_gate_w"][l, c * 128:(c + 1) * 128, :], D))
        for src, dst, n in jobs:
            b = cnt % 2
            self.dma("sp", stg[b][:, 0:n], src, w=[rs_[b]])
            self.cp(engs[cnt % 3], stb[b][:, 0:n], stg[b][:, 0:n], r=[rs_[b]], w=[rb_[b]])
            self.dma("act" if False else "sp", dst, stb[b][:, 0:n], r=[rb_[b]])
            cnt += 1
        P.barrier()
        if 1 in self.layers:
            self.prep_na_bias()

    def prep_na_bias(self):
        P = self.P
        self.areset()
        bt = self.alloc(32 * 14 * 64, BF16)
        rb = P.res()
        btv = bt.rearrange("p (h s q) -> p h s q", h=32, s=14)
        self.memset("pool", bt[:, 0:14336], NEG, w=[rb])
        self.memset("dve", bt[:, 14336:28672], NEG, w=[rb])
        qs = np.arange(64)
        win = np.clip(qs - 8, 0, 48)
        for a in range(2):
            for kc in range(64):
                valid = np.nonzero((win <= kc) & (kc < win + 16))[0]
                qlo, qhi = int(valid[0]), int(valid[-1])
                assert len(valid) == qhi - qlo + 1
                nq = qhi - qlo + 1
                j0 = 15 - kc + qlo
                assert 0 <= j0 and j0 + nq <= 31
                p = a * 64 + kc
                src = self.rpbf_d[:, a:a + 14, j0:j0 + nq]
                self.dma_slow("pool", btv[p:p + 1, :, :, qlo:qhi + 1], src.unsqueeze(0), w=[rb])
        self.dma("sp", self.BT[:, :], bt, r=[rb])
        P.barrier()

    def load_w(self, name, k, n, sub=None):
        t = self.alloc((k // 128) * n, BF16)
        v = t.rearrange("p (c n) -> p c n", n=n)
        src = self.w16[name] if sub is None else self.w16[name][sub]
        r = self.P.res()
        kc = k // 128
        hc = max(1, kc // 2)
        srcv = src.rearrange("(c p) n -> p c n", p=128)
        self.dma("sp", v[:, 0:hc, :], srcv[:, 0:hc, :], w=[r])
        if kc > hc:
            self.dma("pool", v[:, hc:kc, :], srcv[:, hc:kc, :], w=[r])
        return v, r

    def pre_setup(self):
        P = self.P
        self.areset()
        self.bank_i = 0
        self.xt = [self.alloc(8 * TS, F32).rearrange("p (c s) -> p c s", c=8) for _ in range(2)]
        self.rxt = [P.res() for _ in range(2)]
        self.sq = [self.alloc(8 * TS, BF16).rearrange("p (c s) -> p c s", c=8) for _ in range(2)]
        self.rsq = [P.res() for _ in range(2)]
        self.hT = [self.alloc(8 * TS, BF16).rearrange("p (c s) -> p c s", c=8) for _ in range(2)]
        self.rhT = [P.res() for _ in range(2)]
        self.rstd = [self.alloc(TS, F32) for _ in range(2)]
        self.rrstd = [P.res() for _ in range(2)]

    def rms_to(self, xt, rx, nch, inv_n, gcol, out, rout, b):
        sq, rsq = self.sq[b], self.rsq[b]
        rstd, rr = self.rstd[b], self.rrstd[b]
        self.tt("dve", sq[:, 0:nch, :], xt, xt, ALU.mult, r=[rx], w=[rsq])
        pb, rpb = self.bank()
        for c in range(nch):
            self.mm(pb[:], self.cb(C_ONES), sq[:, c, :], c == 0, c == nch - 1, r=[rsq, self.rc], w=[rpb])
        self.act(rstd, pb[:], AF.Sqrt, r=[rpb, self.rc], w=[rr], scale=inv_n, bias=self.epsb[:, 0:1])
        self.recip(rstd, rstd, r=[rr], w=[rr])
        for c in range(nch):
            self.stt(out[:, c, :], xt[:, c, :], gcol(c), rstd, ALU.mult, ALU.mult, r=[rx, rr, self.rc], w=[rout])

    def pre_tile(self, li, xsrc, t, i):
        b = i % 2
        xt, rx = self.xt[b], self.rxt[b]
        xv = xsrc.rearrange("(c p) s -> p c s", p=128)
        self.dma("sp", xt[:, 0:4, :], xv[:, 0:4, t * TS:(t + 1) * TS], w=[rx])
        self.dma("pool", xt[:, 4:8, :], xv[:, 4:8, t * TS:(t + 1) * TS], w=[rx])
        self.rms_to(xt, rx, 8, 1.0 / D, lambda c: self.ng[:, li, c:c + 1], self.hT[b], self.rhT[b], b)
        return self.hT[b], self.rhT[b]

    def proj_fm(self, w, rw, col0, hT, rh, m=128):
        pb, rpb = self.bank()
        for k in range(8):
            self.mm(pb[0:m, :], w[:, k, col0:col0 + m], hT[:, k, :], k == 0, k == 7, r=[rw, rh], w=[rpb])
        return pb, rpb

    def proj_tm(self, w, rw, col0, hT, rh, sub):
        pb, rpb = self.bank()
        for k in range(8):
            self.mm(pb[:], hT[:, k, sub * 128:(sub + 1) * 128], w[:, k, col0:col0 + 512], k == 0, k == 7, r=[rw, rh], w=[rpb])
        return pb, rpb

    def z_tile(self, w, rw, zcol0, hT, rh, t, b):
        szt, rsz = self.szt[b], self.rszt[b]
        for j in range(8):
            pb, rpb = self.proj_fm(w, rw, zcol0 + j * 128, hT, rh)
            self.act(szt[:, j, :], pb[:], AF.Silu, r=[rpb], w=[rsz])
        self.dma("sp", self.SZ.rearrange("(c p) s -> p c s", p=128)[:, :, t * TS:(t + 1) * TS], szt, r=[rsz])

    def alloc_szt(self):
        self.szt = [self.alloc(8 * TS, BF16).rearrange("p (c s) -> p c s", c=8) for _ in range(2)]
        self.rszt = [self.P.res() for _ in range(2)]

    def pre0(self):
        P = self.P
        self.pre_setup()
        w, rw = self.load_w("fn_w_in", 1024, 2048)
        self.alloc_szt()
        ut = [self.alloc(4 * 1024, BF16).rearrange("p (a f) -> p a f", a=4) for _ in range(2)]
        rut = [P.res() for _ in range(2)]
        ev = 0
        for i, sq, t, hT, rh in self.pre_iter(0):
            b = i % 2
            udv = self.Ud.rearrange("(n p) f -> p n f", p=128)
            for sub in range(4):
                for ct in range(2):
                    pb, rpb = self.proj_tm(w, rw, ct * 512, hT, rh, sub)
                    self.cp("act" if ev % 2 == 0 else "dve", ut[b][:, sub, ct * 512:(ct + 1) * 512], pb[:], r=[rpb], w=[rut[b]])
                    ev += 1
            self.dma("sp", udv[:, 4 * t:4 * t + 4, :], ut[b], r=[rut[b]])
            self.z_tile(w, rw, 1024, hT, rh, t, b)
        P.barrier()

    def mid0(self, sq):
        P = self.P
        rc = self.rc
        self.areset()
        self.bank_i = 0
        wm = self.alloc(8 * 128, F32).rearrange("p (g d) -> p g d", g=8)
        rwm = P.res()
        self.dma("sp", wm, self.fn_w_mix_d.rearrange("g m d -> m g d"), w=[rwm])
        AB = self.alloc(2 * 8 * 128, BF16).rearrange("p (x g d) -> p x g d", x=2, g=8)
        rAB = P.res()
        for x, off in ((0, C_CC), (1, C_SC)):
            for gh in range(2):
                pb, rpb = self.bank()
                for g4 in range(4):
                    g = gh * 4 + g4
                    self.mm(pb[:, g4 * 128:(g4 + 1) * 128], self.cf(off), wm[:, g, :], True, True, r=[rc, rwm], w=[rpb])
                self.cp("dve", AB[:, x, gh * 4:(gh + 1) * 4, :], pb[:].rearrange("p (g d) -> p g d", g=4), r=[rpb], w=[rAB])
        mark = self.aoff
        m1f = self.alloc(32 * 128, F32)
        m1 = self.alloc(32 * 128, BF16).rearrange("p (j m) -> p j m", j=32)
        rm1 = P.res()
        self.dma("sp", m1f, self.m1_d[:, :], w=[rm1])
        self.cp("pool", m1.rearrange("p j m -> p (j m)"), m1f, r=[rm1], w=[rm1])
        ud1 = self.alloc(32 * 1024, BF16).rearrange("p (j f) -> p j f", j=32)
        rud = P.res()
        udv = self.Ud.rearrange("(a b) f -> a b f", b=64)
        self.dma("sp", ud1[0:64, :, :], udv[:, 0:32, :], w=[rud])
        self.dma("pool", ud1[64:128, :, :], udv[:, 32:64, :], w=[rud])
        t1 = [self.alloc(8 * 1024, BF16).rearrange("p (s f) -> p s f", s=8) for _ in range(2)]
        rt1 = [P.res() for _ in range(2)]
        tdv = self.Td.rearrange("r k s f -> (r k) s f")
        ev = 0
        for sb_ in range(8):
            b = sb_ % 2
            for sl in range(8):
                s0 = sb_ * 8 + sl
                half, j = s0 // 32, s0 % 32
                for ft in range(2):
                    pb, rpb = self.bank()
                    self.mm(pb[:], m1[64 * half:64 * half + 64, j, :], ud1[64 * half:64 * half + 64, j, ft * 512:(ft + 1) * 512],
                            True, True, r=[rm1, rud], w=[rpb])
                    self.cp("act" if ev % 2 == 0 else "dve", t1[b][:, sl, ft * 512:(ft + 1) * 512], pb[:], r=[rpb], w=[rt1[b]])
                    ev += 1
            self.dma("sp", tdv[:, sb_ * 8:(sb_ + 1) * 8, :], t1[b], r=[rt1[b]])
        P.barrier()
        self.aoff = mark
        xT = self.alloc(4 * 2 * S, BF16).rearrange("p (g r k) -> p g r k", g=4, r=2)
        rxT = [P.res() for _ in range(4)]
        t3 = [self.alloc(16 * 512, BF16).rearrange("p (k f) -> p k f", k=16) for _ in range(2)]
        rt3 = [P.res() for _ in range(2)]
        ot = [self.alloc(S, BF16) for _ in range(2)]
        rot = [P.res() for _ in range(2)]
        tdr = self.Td.rearrange("r k s f -> r s k f")
        for hf in range(2):
            for kb in range(4):
                b = kb % 2
                for r_ in range(2):
                    self.dma("sp" if r_ == 0 else "pool", t3[b][64 * r_:64 * r_ + 64, :, :],
                             tdr[r_, :, kb * 16:(kb + 1) * 16, hf * 512:(hf + 1) * 512], w=[rt3[b]])
                for k4 in range(4):
                    for g in range(4):
                        pb, rpb = self.bank()
                        for kk in range(4):
                            kl = k4 * 4 + kk
                            self.mm(pb[:, kk * 128:(kk + 1) * 128], t3[b][:, kl, g * 128:(g + 1) * 128], self.cb(C_M3),
                                    True, True, r=[rt3[b], rc], w=[rpb])
                        k0 = kb * 16 + k4 * 4
                        outv = xT[:, g, :, :].rearrange("p r (k1 k0) -> p r k1 k0", k0=64)[:, :, :, k0:k0 + 4]
                        inv = pb[:].rearrange("p (a r k) -> p r k a", a=4, r=2)
                        self.cp("act" if ev % 2 == 0 else "dve", outv, inv, r=[rpb], w=[rxT[g]])
                        ev += 1
            for g in range(4):
                gg = hf * 4 + g
                b = g % 2
                for t in range(NT):
                    pb, rpb = self.bank()
                    self.mm(pb[:], AB[:, 0, gg, :], xT[:, g, 0, t * TS:(t + 1) * TS], True, False, r=[rAB, rxT[g]], w=[rpb])
                    self.mm(pb[:], AB[:, 1, gg, :], xT[:, g, 1, t * TS:(t + 1) * TS], False, True, r=[rAB, rxT[g]], w=[rpb])
                    self.cp("act" if ev % 2 == 0 else "dve", ot[b][:, t * TS:(t + 1) * TS], pb[:], r=[rpb], w=[rot[b]])
                    ev += 1
                self.dma("sp", self.O[gg * 128:(gg + 1) * 128, :], ot[b], r=[rot[b]])
        P.barrier()

    def post_layer(self, li, last, final):
        P = self.P
        self.areset()
        self.bank_i = 0
        self.bank_list = list(range(8))
        wname = ["fn_w_out", "na_w_out", "mla_w_out", "hg_w_out"][li]
        wo, rwo = self.load_w(wname, 1024, 1024)
        wg, rwg = self.load_w("ple_gate_w", 1024, 1024, sub=li)
        wp, rwp = self.load_w("ple_w", 256, 1024, sub=li)
        A = lambda cols, dt: [self.alloc(cols, dt) for _ in range(2)]
        R = lambda: [P.res() for _ in range(2)]
        v8 = lambda t_: t_.rearrange("p (c s) -> p c s", c=8)
        ot = [v8(x) for x in A(8 * TS, BF16)]; rot = R()
        sz = [v8(x) for x in A(8 * TS, BF16)]; rsz = R()
        xt = [v8(x) for x in A(8 * TS, F32)]; rx = R()
        pt = [x.rearrange("p (c s) -> p c s", c=2) for x in A(2 * TS, F32)]; rpt = R()
        ptb = [x.rearrange("p (c s) -> p c s", c=2) for x in A(2 * TS, BF16)]; rptb = R()
        g = [v8(x) for x in A(8 * TS, BF16)]; rg = R()
        x1b = [v8(x) for x in A(8 * TS, BF16)]; rx1b = R()
        sg = A(TS, F32); rsg = R()
        if final:
            self.sq = [v8(x) for x in A(8 * TS, BF16)]; self.rsq = R()
            self.rstd = A(TS, F32); self.rrstd = R()
        tiles = [(sq, t) for sq in range(self.nseq) for t in range(NT)]
        fm = lambda ap: ap.rearrange("(c p) s -> p c s", p=128)

        xt3 = xt + [v8(self.alloc(8 * TS, F32))]
        rx3 = rx + [P.res()]
        ry3 = [P.res() for _ in range(3)]

        def loads(i):
            sq, t = tiles[i]
            b = i % 2
            b3 = i % 3
            sl = slice(t * TS, (t + 1) * TS)
            xv = fm(self.xsrc_of(li, sq))
            self.dma("sp", ot[b], fm(self.SA["O"][sq])[:, :, sl], w=[rot[b]])
            self.dma("sp", sz[b], fm(self.SA["SZ"][sq])[:, :, sl], w=[rsz[b]])
            self.dma("sp", xt3[b3], xv[:, :, sl], w=[rx3[b3]])
            self.dma("sp", pt[b], fm(self.pT[li, sq])[:, :, sl], w=[rpt[b]])

        def stage_a(i):
            sq, t = tiles[i]
            b = i % 2
            b3 = i % 3
            self.tt("dve", g[b], ot[b], sz[b], ALU.mult, r=[rot[b], rsz[b]], w=[rg[b]])
            self.cp("pool", ptb[b], pt[b], r=[rpt[b]], w=[rptb[b]])
            for j in range(8):
                pb, rpb = self.bank()
                for k in range(8):
                    self.mm(pb[:], wo[:, k, j * 128:(j + 1) * 128], g[b][:, k, :], k == 0, k == 7, r=[rwo, rg[b]], w=[rpb])
                self.tt("dve", xt3[b3][:, j, :], xt3[b3][:, j, :], pb[:], ALU.add, r=[rpb, rx3[b3]], w=[rx3[b3]])
                self.cp("act", x1b[b][:, j, :], xt3[b3][:, j, :], r=[rx3[b3]], w=[rx1b[b]])

        def stage_b(i):
            sq, t = tiles[i]
            b = i % 2
            b3 = i % 3
            sl = slice(t * TS, (t + 1) * TS)
            xdv = fm(self.yT[sq] if last else self.X[sq])
            for j in range(8):
                pg, rpg = self.bank()
                for k in range(8):
                    self.mm(pg[:], wg[:, k, j * 128:(j + 1) * 128], x1b[b][:, k, :], k == 0, k == 7, r=[rwg, rx1b[b]], w=[rpg])
                pp, rpp = self.bank()
                for k in range(2):
                    self.mm(pp[:], wp[:, k, j * 128:(j + 1) * 128], ptb[b][:, k, :], k == 0, k == 1, r=[rwp, rptb[b]], w=[rpp])
                self.act(sg[b], pg[:], AF.Sigmoid, r=[rpg], w=[rsg[b]])
                self.tt("dve", sg[b], sg[b], pp[:], ALU.mult, r=[rsg[b], rpp], w=[rsg[b]])
                self.tt("pool", xt3[b3][:, j, :], xt3[b3][:, j, :], sg[b], ALU.add, r=[rsg[b], rx3[b3]], w=[rx3[b3]])
            if final:
                self.rms_to(xt3[b3], rx3[b3], 8, 1.0 / D, lambda c: self.fg[:, c:c + 1], xt3[b3], ry3[b3], b)
            self.dma("sp", xdv[:, 0:4, sl], xt3[b3][:, 0:4, :], r=[rx3[b3], ry3[b3]])
            self.dma("sp", xdv[:, 4:8, sl], xt3[b3][:, 4:8, :], r=[rx3[b3], ry3[b3]])

        loads(0)
        loads(1)
        stage_a(0)
        for i in range(len(tiles)):
            if i + 1 < len(tiles):
                stage_a(i + 1)
            if i + 2 < len(tiles):
                loads(i + 2)
            stage_b(i)
        P.barrier()

    def pre1(self):
        P = self.P
        self.pre_setup()
        self.bank_list = list(range(8))
        w, rw = self.load_w("na_w_in", 1024, 4096)
        self.alloc_szt()
        qk = [self.alloc(8 * TS, BF16).rearrange("p (c s) -> p c s", c=8) for _ in range(2)]
        rqk = [P.res() for _ in range(2)]
        vt = [self.alloc(4 * 1056, BF16).rearrange("p (a h d) -> p a h d", a=4, h=32) for _ in range(2)]
        rvt = [P.res() for _ in range(2)]
        for b in range(2):
            self.memset("pool", vt[b][:, :, :, 32:33], 1.0, w=[rvt[b]])
        ev = 0
        nq = 0
        for i, sq, t, hT, rh in self.pre_iter(1):
            b = i % 2
            sl = slice(t * TS, (t + 1) * TS)
            vav = self.VA.rearrange("(n p) f -> p n f", p=128)
            for which, dst in ((0, self.QT), (1, self.KT)):
                bq = nq % 2
                nq += 1
                for j in range(8):
                    pb, rpb = self.proj_fm(w, rw, which * 1024 + j * 128, hT, rh)
                    self.cp("act" if ev % 2 == 0 else "dve", qk[bq][:, j, :], pb[:], r=[rpb], w=[rqk[bq]])
                    ev += 1
                self.dma("sp", dst.rearrange("(c p) s -> p c s", p=128)[:, :, sl], qk[bq], r=[rqk[bq]])
            for sub in range(4):
                for ct in range(2):
                    pb, rpb = self.proj_tm(w, rw, 2048 + ct * 512, hT, rh, sub)
                    self.cp("act" if ev % 2 == 0 else "dve", vt[b][:, sub, 16 * ct:16 * ct + 16, 0:32],
                            pb[:].rearrange("p (h d) -> p h d", d=32), r=[rpb], w=[rvt[b]])
                    ev += 1
            self.dma("pool", vav[:, 4 * t:4 * t + 4, :], vt[b].rearrange("p a h d -> p a (h d)"), r=[rvt[b]])
            self.z_tile(w, rw, 3072, hT, rh, t, b)
        P.barrier()

    def mid1(self, sq):
        P = self.P
        rc = self.rc
        self.areset()
        self.bank_i = 0
        self.bank_list = [0, 1, 2, 3, 4]
        obanks = [5, 6]
        tbank = 7
        bts = self.alloc(32 * 14 * 64, BF16).rearrange("p (h s q) -> p h s q", h=32, s=14)
        rbt = P.res()
        self.dma("sp", bts[:, 0:16], self.BT.rearrange("p (h x) -> p h x", h=32)[:, 0:16, :].rearrange("p h (s q) -> p h s q", s=14), w=[rbt])
        self.dma("pool", bts[:, 16:32], self.BT.rearrange("p (h x) -> p h x", h=32)[:, 16:32, :].rearrange("p h (s q) -> p h s q", s=14), w=[rbt])
        ktb = [self.alloc(8 * 1024, BF16).rearrange("p (c s) -> p c s", c=8) for _ in range(2)]
        rkt = [P.res() for _ in range(2)]
        vbe = self.alloc(8 * 1056, BF16).rearrange("p (m h d) -> p m h d", m=8, h=32)
        vbo = self.alloc(7 * 1056, BF16).rearrange("p (m h d) -> p m h d", m=7, h=32)
        rvb = [P.res(), P.res()]
        qraw = [self.alloc(8 * TS, BF16).rearrange("p (c s) -> p c s", c=8) for _ in range(2)]
        rqr = [P.res() for _ in range(2)]
        qm = self.alloc(8 * 4 * TS, BF16).rearrange("p (c m s) -> p c m s", c=8, m=4)
        rqm = P.res()
        NPT = 4
        ptf = [self.alloc(1152, BF16) for _ in range(NPT)]
        rpt = [P.res() for _ in range(NPT)]
        for i in range(NPT):
            self.memset("pool", ptf[i], 0.0, w=[rpt[i]])
        ptw = [x[:, 0:1152].rearrange("p (r x) -> p r x", x=576)[:, :, 0:512].rearrange("p r (k y) -> p r k y", y=128)[:, :, :, 0:64] for x in ptf]
        ptr_ = [x[:, 0:1024].rearrange("p (r k y) -> p r k y", r=2, k=4) for x in ptf]
        osb = [self.alloc(1024, BF16) for _ in range(2)]
        rosb = [P.res() for _ in range(2)]
        otb = [self.alloc(8 * TS, BF16).rearrange("p (c s) -> p c s", c=8) for _ in range(2)]
        rotb = [P.res() for _ in range(2)]
        rc8 = [self.alloc(8, F32) for _ in range(2)]
        rrc8 = [P.res() for _ in range(2)]
        ktv = self.KT.rearrange("(c p) s -> p c s", p=128)
        qtv = self.QT.rearrange("(c p) s -> p c s", p=128)
        ov = self.O.rearrange("(c p) s -> p c s", p=128)
        ev = [0]
        units = [(t, pr, hg, hl) for t in range(NT) for pr in range(4) for hg in range(4) for hl in range(8)]
        state = {}

        def band_setup(t):
            b = t % 2
            kw0 = int(np.clip(8 * t - 4, 0, 48))
            self.dma("sp", ktb[b], ktv[:, :, 64 * kw0:64 * kw0 + 1024], w=[rkt[b]])
            self.dma("pool", qraw[b], qtv[:, :, t * TS:(t + 1) * TS], w=[rqr[b]])
            for j in range(8):
                for hm in range(4):
                    e3 = ev[0] % 3
                    if e3 == 2:
                        self.act(qm[:, j, hm, :], qraw[b][:, j, :], AF.Copy, r=[rqr[b], rc], w=[rqm], scale=self.cf(C_MC + hm, 1))
                    else:
                        self.ts("dve" if e3 == 0 else "pool", qm[:, j, hm, :], qraw[b][:, j, :], self.cf(C_MC + hm, 1), 0.0,
                                ALU.mult, ALU.add, r=[rqr[b], rc], w=[rqm])
                    ev[0] += 1

        def v_setup(t, hf):
            kw0 = int(np.clip(8 * t - 4, 0, 48))
            hs_ = slice(16 * hf, 16 * hf + 16)
            cs_ = slice(528 * hf, 528 * hf + 528)
            self.dma("pool", vbe[:, :, hs_, :].rearrange("p m h d -> p m (h d)"),
                     self.VA[64 * kw0:64 * kw0 + 1024, cs_].rearrange("(m p) f -> p m f", p=128), w=[rvb[hf]])
            self.dma("sp", vbo[:, :, hs_, :].rearrange("p m h d -> p m (h d)"),
                     self.VA[64 * (kw0 + 1):64 * (kw0 + 1) + 896, cs_].rearrange("(m p) f -> p m f", p=128), w=[rvb[hf]])

        def geom(u):
            t, pr, hg, hl = u
            kw0 = int(np.clip(8 * t - 4, 0, 48))
            rows = (8 * t + 2 * pr, 8 * t + 2 * pr + 1)
            return kw0, rows

        def S_(ui):
            u = units[ui]
            t, pr, hg, hl = u
            b = t % 2
            kw0, rows = geom(u)
            h = hg * 8 + hl
            j, hm = h // 4, h % 4
            sb_, rsb = self.bank()
            sv = sb_[:].rearrange("p (r k q) -> p r k q", r=2, k=4)
            for ri, r_ in enumerate(rows):
                rs = int(np.clip(r_ - 4, 0, 56))
                ro0 = rs - r_ + 7
                for kc in range(4):
                    koff = 64 * (rs + 2 * kc - kw0)
                    self.mm(sv[:, ri, kc, :], ktb[b][:, j, koff:koff + 128], qm[:, j, hm, (r_ - 8 * t) * 64:(r_ - 8 * t) * 64 + 64],
                            True, False, r=[rkt[b], rqm], w=[rsb])
                    self.mm(sv[:, ri, kc, :], self.cb(C_ID), bts[:, h, ro0 + 2 * kc, :], False, True, r=[rbt, rc], w=[rsb])
            state[ui] = (sv, rsb)

        def PV_(ui):
            u = units[ui]
            t, pr, hg, hl = u
            b = t % 2
            kw0, rows = geom(u)
            h = hg * 8 + hl
            sv, rsb = state.pop(ui)
            pi = ui % NPT
            self.act(ptw[pi], sv, AF.Exp, r=[rsb], w=[rpt[pi]])
            gi = ui // 8
            obi = obanks[gi % 2]
            ob, rob = self.ps[obi], self.psr[obi]
            obv = ob[:, 0:264].rearrange("p (h d) -> p h d", h=8)
            for ri, r_ in enumerate(rows):
                rs = int(np.clip(r_ - 4, 0, 56))
                dlt = rs - kw0
                for kc in range(4):
                    if dlt % 2 == 0:
                        vsrc = vbe[:, dlt // 2 + kc, h, :]
                    else:
                        vsrc = vbo[:, (dlt - 1) // 2 + kc, h, :]
                    self.mm(obv[:, hl, :], ptr_[pi][:, ri, kc, :], vsrc, ri == 0 and kc == 0, ri == 1 and kc == 3,
                            r=[rpt[pi], rvb[h // 16]], w=[rob])
            ob_ = (ui // 32) % 2
            if hl == 7:
                self.recip(rc8[hg % 2], obv[:, :, 32], r=[rob], w=[rrc8[hg % 2]])
                self.tt("dve", osb[ob_][:, hg * 256:(hg + 1) * 256].rearrange("p (h d) -> p h d", h=8), obv[:, :, 0:32],
                        rc8[hg % 2].unsqueeze(2).to_broadcast([128, 8, 32]), ALU.mult, r=[rob, rrc8[hg % 2]], w=[rosb[ob_]])
                if hg == 3:
                    tb, rtb = self.ps[tbank], self.psr[tbank]
                    tbv = tb[:].bitcast(BF16)
                    for jj in range(8):
                        self.tr(tbv[:, jj * 128:(jj + 1) * 128], osb[ob_][:, jj * 128:(jj + 1) * 128], self.cb(C_ID), r=[rosb[ob_], rc], w=[rtb])
                    self.cp("dve", otb[b][:, :, pr * 128:(pr + 1) * 128], tbv.rearrange("p (c s) -> p c s", c=8), r=[rtb], w=[rotb[b]])
                    if pr == 3:
                        self.dma("sp", ov[:, :, t * TS:(t + 1) * TS], otb[b], r=[rotb[b]])

        SK = 3
        N_ = len(units)
        band_setup(0)
        v_setup(0, 0)
        v_setup(0, 1)
        for k in range(SK):
            S_(k)
        for ui in range(N_):
            nxt = ui + SK
            if nxt < N_:
                if units[nxt][0] != units[nxt - 1][0]:
                    band_setup(units[nxt][0])
                S_(nxt)
            PV_(ui)
            t_, pr_, hg_, hl_ = units[ui]
            if t_ + 1 < NT and pr_ == 3 and hl_ == 7:
                if hg_ == 1:
                    v_setup(t_ + 1, 0)
                elif hg_ == 3:
                    v_setup(t_ + 1, 1)
        self.bank_list = list(range(8))
        P.barrier()


    def pre2(self):
        P = self.P
        rc = self.rc
        self.pre_setup()
        self.bank_list = list(range(8))
        w, rw = self.load_w("mla_w_in", 1024, 1728)
        wq, rwq = self.load_w("mla_w_uq", 384, 1536)
        wkv, rwkv = self.load_w("mla_w_ukv", 256, 2048)
        self.alloc_szt()
        c3 = lambda x, n: x.rearrange("p (c s) -> p c s", c=n)
        cq = c3(self.alloc(3 * TS, F32), 3); rcq = P.res()
        ckv = c3(self.alloc(2 * TS, F32), 2); rckv = P.res()
        cqn = c3(self.alloc(3 * TS, BF16), 3); rcqn = P.res()
        ckvn = c3(self.alloc(2 * TS, BF16), 2); rckvn = P.res()
        rp = [c3(self.alloc(4 * TS, F32), 4) for _ in range(2)]; rrp = [P.res() for _ in range(2)]
        qnt = c3(self.alloc(8 * TS, BF16), 8); rqnt = P.res()
        qpt = c3(self.alloc(8 * TS, BF16), 8); rqpt = P.res()
        knt = c3(self.alloc(8 * TS, BF16), 8); rknt = P.res()
        kpt = self.alloc(TS, BF16); rkpt = P.res()
        vt = self.alloc(4 * 1024, BF16).rearrange("p (a f) -> p a f", a=4); rvt = P.res()
        t1 = self.alloc(TS, F32); rt1 = P.res()
        t2 = self.alloc(TS, F32); rt2 = P.res()
        wkv4 = wkv.rearrange("p c (h t d) -> p c h t d", h=8, t=2)
        ev = [0]

        def evac(out, in_, r, w_):
            self.cp("act" if ev[0] % 2 == 0 else "dve", out, in_, r=r, w=w_)
            ev[0] += 1

        def swapped(wt, rwt_, nk, c0, rhs, rrhs):
            pb, rpb = self.bank()
            for k in range(nk):
                self.mm(pb[0:32, :], wt[:, k, c0 + 32:c0 + 64], rhs[:, k, :], k == 0, k == nk - 1, r=[rwt_, rrhs], w=[rpb])
            for k in range(nk):
                self.mm(pb[32:64, :], wt[:, k, c0:c0 + 32], rhs[:, k, :], k == 0, k == nk - 1, r=[rwt_, rrhs], w=[rpb])
            return pb, rpb

        def rope(pa, rpa, pbs, rpbs, ct, st_, rtab, out):
            self.tt("dve", t1[0:64, :], pa[0:64, :], ct, ALU.mult, r=[rpa, rtab], w=[rt1])
            self.tt("dve", t2[0:64, :], pbs[0:64, :], st_, ALU.mult, r=[rpbs, rtab], w=[rt2])
            return t1, t2

        for i, sq, t, hT, rh in self.pre_iter(2):
            b = i % 2
            sl = slice(t * TS, (t + 1) * TS)
            qnv = self.QN.rearrange("(h p) s -> p h s", p=128)
            knv = self.KN.rearrange("(h p) s -> p h s", p=128)
            qpv = self.QP.rearrange("h p s -> p h s")
            v2v = self.V2.rearrange("(n p) f -> p n f", p=128)
            self.dma("pool", rp[b][0:64], self.rope_d[:, :, sl], w=[rrp[b]])
            for c in range(3):
                pb, rpb = self.proj_fm(w, rw, c * 128, hT, rh)
                evac(cq[:, c, :], pb[:], [rpb], [rcq])
            for c in range(2):
                pb, rpb = self.proj_fm(w, rw, 384 + c * 128, hT, rh)
                evac(ckv[:, c, :], pb[:], [rpb], [rckv])
            pa, rpa = self.proj_fm(w, rw, 640, hT, rh, m=64)
            pbs, rpbs = swapped(w, rw, 8, 640, hT, rh)
            self.tt("dve", t1[0:64, :], pa[0:64, :], rp[b][0:64, 2, :], ALU.mult, r=[rpa, rrp[b]], w=[rt1])
            self.tt("dve", t2[0:64, :], pbs[0:64, :], rp[b][0:64, 3, :], ALU.mult, r=[rpbs, rrp[b]], w=[rt2])
            self.tt("pool", kpt[0:64, :], t1[0:64, :], t2[0:64, :], ALU.add, r=[rt1, rt2], w=[rkpt])
            self.dma("sp", self.KP[:, sl], kpt[0:64, :], r=[rkpt])
            self.z_tile(w, rw, 704, hT, rh, t, b)
            self.rms_to(cq, rcq, 3, 1.0 / 384, lambda c: self.gq[:, c:c + 1], cqn, rcqn, b)
            self.rms_to(ckv, rckv, 2, 1.0 / 256, lambda c: self.gkv[:, c:c + 1], ckvn, rckvn, b)
            for h in range(8):
                pb, rpb = self.bank()
                for c in range(3):
                    self.mm(pb[:], wq[:, c, h * 192:h * 192 + 128], cqn[:, c, :], c == 0, c == 2, r=[rwq, rcqn], w=[rpb])
                self.act(qnt[:, h, :], pb[:], AF.Copy, r=[rpb], w=[rqnt], scale=192.0 ** -0.5)
                pa, rpa = self.bank()
                for c in range(3):
                    self.mm(pa[0:64, :], wq[:, c, h * 192 + 128:h * 192 + 192], cqn[:, c, :], c == 0, c == 2, r=[rwq, rcqn], w=[rpa])
                pbs, rpbs = swapped(wq, rwq, 3, h * 192 + 128, cqn, rcqn)
                self.tt("dve", t1[0:64, :], pa[0:64, :], rp[b][0:64, 0, :], ALU.mult, r=[rpa, rrp[b]], w=[rt1])
                self.tt("dve", t2[0:64, :], pbs[0:64, :], rp[b][0:64, 1, :], ALU.mult, r=[rpbs, rrp[b]], w=[rt2])
                self.tt("pool", qpt[0:64, h, :], t1[0:64, :], t2[0:64, :], ALU.add, r=[rt1, rt2], w=[rqpt])
            for h in range(8):
                pb, rpb = self.bank()
                for c in range(2):
                    self.mm(pb[:], wkv[:, c, h * 256:h * 256 + 128], ckvn[:, c, :], c == 0, c == 1, r=[rwkv, rckvn], w=[rpb])
                evac(knt[:, h, :], pb[:], [rpb], [rknt])
            for sub in range(4):
                for hh in range(2):
                    pb, rpb = self.bank()
                    for c in range(2):
                        self.mm(pb[:].rearrange("p (h d) -> p h d", h=4), ckvn[:, c, sub * 128:(sub + 1) * 128],
                                wkv4[:, c, 4 * hh:4 * hh + 4, 1, :], c == 0, c == 1, r=[rwkv, rckvn], w=[rpb])
                    evac(vt[:, sub, hh * 512:(hh + 1) * 512], pb[:], [rpb], [rvt])
            self.dma("sp", qnv[:, :, sl], qnt, r=[rqnt])
            self.dma("pool", qpv[:, :, sl], qpt[0:64], r=[rqpt])
            self.dma("sp", knv[:, :, sl], knt, r=[rknt])
            self.dma("pool", v2v[:, 4 * t:4 * t + 4, :], vt, r=[rvt])
        P.barrier()

    def mid2(self, sq):
        P = self.P
        rc = self.rc
        self.areset()
        self.bank_i = 0
        self.bank_list = [0, 1, 2, 3]
        kpb = self.alloc(S, BF16); rkp = P.res()
        self.dma("sp", kpb[0:64, :], self.KP[:, :], w=[rkp])
        self.dma("pool", kpb[64:128, :], self.KP[:, :], w=[rkp])
        acc = [[self.alloc(TS, F32) for _ in range(2)] for _ in range(2)]
        racc = [[P.res() for _ in range(2)] for _ in range(2)]
        kng = self.alloc(4 * S, BF16).rearrange("p (h s) -> p h s", h=4); rkn = P.res()
        vg = self.alloc(32 * 512, BF16).rearrange("p (n f) -> p n f", n=32); rvg = P.res()
        qn = [self.alloc(4 * TS, BF16).rearrange("p (h s) -> p h s", h=4) for _ in range(2)]; rqn = [P.res() for _ in range(2)]
        qp = [self.alloc(4 * TS, BF16).rearrange("p (h s) -> p h s", h=4) for _ in range(2)]; rqp = [P.res() for _ in range(2)]
        NPT = 6
        pt = [self.alloc(TS, BF16) for _ in range(NPT)]; rpt = [P.res() for _ in range(NPT)]
        rec = [self.alloc(TS, F32) for _ in range(2)]; rrec = [P.res() for _ in range(2)]
        ot = [self.alloc(TS, BF16) for _ in range(2)]; rot = [P.res() for _ in range(2)]
        qnv = self.QN.rearrange("(h p) s -> p h s", p=128)
        knv = self.KN.rearrange("(h p) s -> p h s", p=128)
        qpv = self.QP.rearrange("h p s -> p h s")
        v2v = self.V2.rearrange("(n p) f -> p n f", p=128)
        it = 0
        ipt = 0
        ih = 0
        for g in range(2):
            self.dma("sp", kng[:, 0:2], knv[:, 4 * g:4 * g + 2, :], w=[rkn])
            self.dma("pool", kng[:, 2:4], knv[:, 4 * g + 2:4 * g + 4, :], w=[rkn])
            self.dma("sp", vg[:, 0:16], v2v[:, 0:16, 512 * g:512 * g + 512], w=[rvg])
            self.dma("pool", vg[:, 16:32], v2v[:, 16:32, 512 * g:512 * g + 512], w=[rvg])
            for t in range(NT):
                b = it % 2
                it += 1
                sl = slice(t * TS, (t + 1) * TS)
                self.dma("sp", qn[b], qnv[:, 4 * g:4 * g + 4, sl], w=[rqn[b]])
                self.dma("pool", qp[b][0:64], qpv[:, 4 * g:4 * g + 4, sl], w=[rqp[b]])
                self.dma("sp", qp[b][64:128], qpv[:, 4 * g:4 * g + 4, sl], w=[rqp[b]])
                for hl in range(4):
                    h = 4 * g + hl
                    hb = ih % 2
                    ih += 1
                    OT, rOT = self.ps[4 + hb], self.psr[4 + hb]
                    DS, rDS = self.ps[6 + hb], self.psr[6 + hb]
                    sbanks = {}

                    def score2(m0):
                        bs = []
                        for m in (m0, m0 + 1):
                            sb_, rsb = self.bank()
                            self.mm(sb_[:], kng[:, hl, m * 128:(m + 1) * 128], qn[b][:, hl, :], True, False, r=[rkn, rqn[b]], w=[rsb])
                            bs.append((sb_, rsb))
                        for k_, m in enumerate((m0, m0 + 1)):
                            sb_, rsb = bs[k_]
                            p0 = 64 * k_
                            self.mm(sb_[:], kpb[p0:p0 + 64, m * 128:(m + 1) * 128], qp[b][p0:p0 + 64, hl, :], False, True, r=[rkp, rqp[b]], w=[rsb])
                            sbanks[m] = (sb_, rsb)

                    score2(0)
                    for m in range(32):
                        if m % 2 == 0 and m + 2 < 32:
                            score2(m + 2)
                        sb_, rsb = sbanks.pop(m)
                        pi = ipt % NPT
                        ipt += 1
                        self.act(pt[pi], sb_[:], AF.Exp, r=[rsb], w=[rpt[pi]])
                        self.mm(OT[:], vg[:, m, hl * 128:(hl + 1) * 128], pt[pi], m == 0, m == 31, r=[rvg, rpt[pi]], w=[rOT])
                        e_ = m % 2
                        eng = "pool" if e_ == 0 else "dve"
                        if m < 2:
                            self.cp(eng, acc[hb][e_], pt[pi], r=[rpt[pi]], w=[racc[hb][e_]])
                        else:
                            self.tt(eng, acc[hb][e_], acc[hb][e_], pt[pi], ALU.add, r=[rpt[pi], racc[hb][e_]], w=[racc[hb][e_]])
                    self.mm(DS[:], self.cf(C_ONES), acc[hb][0], True, False, r=[rc, racc[hb][0]], w=[rDS])
                    self.mm(DS[:], self.cf(C_ONES), acc[hb][1], False, True, r=[rc, racc[hb][1]], w=[rDS])
                    self.recip(rec[hb], DS[:], r=[rDS], w=[rrec[hb]])
                    self.tt("dve", ot[hb], OT[:], rec[hb], ALU.mult, r=[rOT, rrec[hb]], w=[rot[hb]])
                    self.dma("sp", self.O[h * 128:(h + 1) * 128, sl], ot[hb], r=[rot[hb]])
        self.bank_list = list(range(8))
        P.barrier()

    def pre3(self):
        P = self.P
        rc = self.rc
        self.pre_setup()
        self.bank_list = list(range(8))
        w, rw = self.load_w("hg_w_in", 1024, 5120)
        self.alloc_szt()
        qst = self.alloc(8 * TS, BF16).rearrange("p (c s) -> p c s", c=8); rqst = P.res()
        lf = self.alloc(8 * TS, F32).rearrange("p (c s) -> p c s", c=8); rlf = P.res()
        vt = self.alloc(4 * 1024, BF16).rearrange("p (a f) -> p a f", a=4); rvt = P.res()
        ev = 0
        for i, sq, t, hT, rh in self.pre_iter(3):
            b = i % 2
            sl = slice(t * TS, (t + 1) * TS)
            qsv = self.QS.rearrange("(c p) s -> p c s", p=128)
            viv = self.VI.rearrange("(n p) f -> p n f", p=128)
            for j in range(8):
                pb, rpb = self.proj_fm(w, rw, j * 128, hT, rh)
                self.act(qst[:, j, :], pb[:], AF.Silu, r=[rpb], w=[rqst])
            self.dma("sp", qsv[:, :, sl], qst, r=[rqst])
            for d in range(2):
                for j in range(8):
                    pb, rpb = self.proj_fm(w, rw, 1024 * (1 + d) + j * 128, hT, rh)
                    self.act(lf[:, j, :], pb[:], AF.Sigmoid, r=[rpb], w=[rlf])
                    self.ts("dve", lf[:, j, :], lf[:, j, :], self.oml[:, d, j:j + 1], self.lb[:, d, j:j + 1], ALU.mult, ALU.add,
                            r=[rlf, rc], w=[rlf])
                self.act(lf, lf, AF.Ln, r=[rlf], w=[rlf])
                lfv = self.LF[d].rearrange("(c p) s -> p c s", p=128)
                self.dma("sp", lfv[:, 0:4, sl], lf[:, 0:4, :], r=[rlf])
                self.dma("pool", lfv[:, 4:8, sl], lf[:, 4:8, :], r=[rlf])
            for sub in range(4):
                for ct in range(2):
                    pb, rpb = self.proj_tm(w, rw, 3072 + ct * 512, hT, rh, sub)
                    self.cp("act" if ev % 2 == 0 else "dve", vt[:, sub, ct * 512:(ct + 1) * 512], pb[:], r=[rpb], w=[rvt])
                    ev += 1
            self.dma("pool", viv[:, 4 * t:4 * t + 4, :], vt, r=[rvt])
            self.z_tile(w, rw, 4096, hT, rh, t, b)
        P.barrier()

    def mid3(self, sq):
        P = self.P
        rc = self.rc
        self.areset()
        self.bank_i = 0
        self.bank_list = [0, 1, 2, 3, 4, 5]
        L = 128
        NCH = 32
        rmask = self.alloc(S, BF16); rrm = P.res()
        self.memset("pool", rmask, 1.0, w=[rrm])
        self.memset("pool", rmask.rearrange("p (n l) -> p n l", l=L)[:, :, 0:1], 0.0, w=[rrm])
        qs = self.alloc(S, BF16); rqs = P.res()
        lf = [self.alloc(S, F32) for _ in range(2)]; rlf = [P.res() for _ in range(2)]
        C = self.alloc(S, F32); rC = P.res()
        E = self.alloc(S, F32); rE = P.res()
        K1 = self.alloc(S, F32); rK1 = P.res()
        qt = [self.alloc(S, BF16) for _ in range(2)]; rqt = [P.res() for _ in range(2)]
        kt = [self.alloc(S, BF16) for _ in range(2)]; rkt = [P.res() for _ in range(2)]
        ktm = [self.alloc(S, BF16).rearrange("p (n c) -> p n c", n=NCH) for _ in range(2)]; rktm = [P.res() for _ in range(2)]
        vh = self.alloc(S, BF16).rearrange("p (n d) -> p n d", n=NCH); rvh = P.res()
        Sb = [self.alloc(S, BF16).rearrange("p (n d) -> p n d", n=NCH) for _ in range(2)]; rSb = [P.res() for _ in range(2)]
        Wst = [[self.alloc(128, F32) for _ in range(2)] for _ in range(2)]; rW = [[P.res() for _ in range(2)] for _ in range(2)]
        ecl = [self.alloc(NCH, F32) for _ in range(2)]; recl = [P.res() for _ in range(2)]
        oT = self.alloc(S, BF16); roT = P.res()
        Am4 = [self.alloc(4 * 256, BF16).rearrange("p (k x) -> p k x", k=4) for _ in range(2)]; rAm4 = [P.res() for _ in range(2)]
        on4 = [self.alloc(512, BF16) for _ in range(2)]; ron4 = [P.res() for _ in range(2)]
        sq4 = [self.alloc(512, F32) for _ in range(2)]; rsq4 = [P.res() for _ in range(2)]
        ss = self.alloc(NCH, F32); rss = P.res()
        rs_ = self.alloc(NCH, F32); rrs = P.res()
        viv = self.VI.rearrange("(n p) f -> p n f", p=128)
        ch = lambda x, n: x[:, n * L:(n + 1) * L]
        for h in range(8):
            hs = slice(h * 128, (h + 1) * 128)
            self.dma("sp", qs, self.QS[hs, :], w=[rqs])
            self.dma("pool", vh, viv[:, :, hs], w=[rvh])
            self.dma("sp", lf[0], self.LF[0, hs, :], w=[rlf[0]])
            self.dma("pool", lf[1], self.LF[1, hs, :], w=[rlf[1]])
            self.memset("pool", ss, 0.0, w=[rss])
            for d in range(2):
                self.act(K1, lf[d], AF.Exp, r=[rlf[d]], w=[rK1])
                self.P.op("dve", lambda e, d=d: e.tensor_tensor_scan(out=C, data0=rmask, data1=lf[d], initial=0.0, op0=ALU.mult, op1=ALU.add),
                          r=[rrm, rlf[d]], w=[rC])
                Cl = C.rearrange("p (n l) -> p n l", l=L)[:, :, L - 1]
                if d == 0:
                    self.act(E, C, AF.Exp, r=[rC], w=[rE])
                    self.cp("dve", ecl[0], E.rearrange("p (n l) -> p n l", l=L)[:, :, L - 1], r=[rE], w=[recl[0]])
                    self.act(C, C, AF.Exp, r=[rC], w=[rC], scale=-1.0)
                else:
                    self.act(ecl[1], Cl, AF.Exp, r=[rC], w=[recl[1]])
                    self.tt("dve", C, C, lf[1], ALU.subtract, r=[rC, rlf[1]], w=[rC])
                    self.act(E, C, AF.Exp, r=[rC], w=[rE], scale=-1.0)
                    self.act(C, C, AF.Exp, r=[rC], w=[rC])
                self.ts("dve", K1, K1, -1.0, 1.0, ALU.mult, ALU.add, r=[rK1], w=[rK1])
                self.tt("dve", qt[d], qs, E, ALU.mult, r=[rqs, rE], w=[rqt[d]])
                self.tt("dve", kt[d], K1, C, ALU.mult, r=[rK1, rC], w=[rkt[d]])
            ev = 0
            for d in range(2):
                for n8 in range(4):
                    tb, rtb = self.bank()
                    tbv = tb[:].bitcast(BF16)
                    for k in range(8):
                        n = n8 * 8 + k
                        self.tr(tbv[:, k * 128:(k + 1) * 128], ch(kt[d], n), self.cb(C_ID), r=[rkt[d], rc], w=[rtb])
                    self.cp("act" if ev % 2 == 0 else "dve", ktm[d][:, n8 * 8:(n8 + 1) * 8, :].rearrange("p n c -> p (n c)"), tbv, r=[rtb], w=[rktm[d]])
                    ev += 1
            self.memset("pool", Sb[0][:, 0, :], 0.0, w=[rSb[0]])
            self.memset("pool", Sb[1][:, NCH - 1, :], 0.0, w=[rSb[1]])
            ub = [None, None]
            wi = [0, 0]
            for i in range(NCH):
                for d in range(2):
                    n = i if d == 0 else NCH - 1 - i
                    if i % 4 == 0:
                        ub[d] = self.bank()
                        for k in range(4):
                            nn = i + k if d == 0 else NCH - 1 - i - k
                            self.mm(ub[d][0][:, k * 128:(k + 1) * 128], ktm[d][:, nn, :], vh[:, nn, :], True, True, r=[rktm[d], rvh], w=[ub[d][1]])
                    U = ub[d][0][:, (i % 4) * 128:(i % 4 + 1) * 128]
                    rU = ub[d][1]
                    cur = wi[d]
                    if i == 0:
                        self.cp("dve", Wst[d][cur], U, r=[rU], w=[rW[d][cur]])
                    else:
                        ei = (i - 1) if d == 0 else n
                        esc = ecl[d][:, ei:ei + 1]
                        self.act(Sb[d][:, n, :], Wst[d][cur], AF.Copy, r=[rW[d][cur], recl[d]], w=[rSb[d]], scale=esc)
                        nxt = 1 - cur
                        self.stt(Wst[d][nxt], Wst[d][cur], esc, U, ALU.mult, ALU.add, r=[rW[d][cur], recl[d], rU], w=[rW[d][nxt]])
                        wi[d] = nxt
            mask2 = self.cf(C_MU, 256).unsqueeze(1).to_broadcast([128, 2, 256])
            for gq in range(NCH // 4):
                gi = gq % 2
                for half in range(2):
                    ab, rab = self.bank()
                    for k2 in range(2):
                        n = gq * 4 + half * 2 + k2
                        self.mm(ab[:, k2 * 256:k2 * 256 + 128], ch(kt[0], n), ch(qt[0], n), True, True, r=[rkt[0], rqt[0]], w=[rab])
                        self.mm(ab[:, k2 * 256 + 128:k2 * 256 + 256], ch(kt[1], n), ch(qt[1], n), True, True, r=[rkt[1], rqt[1]], w=[rab])
                    self.tt("dve", Am4[gi][:, half * 2:half * 2 + 2, :], ab[:].rearrange("p (k x) -> p k x", k=2), mask2, ALU.mult,
                            r=[rab, rc], w=[rAm4[gi]])
                ob, rob = self.bank()
                ov4 = ob[:].rearrange("p (k d) -> p k d", k=4)
                for k in range(4):
                    n = gq * 4 + k
                    o = ov4[:, k, :]
                    self.mm(o, Am4[gi][:, k, 0:128], vh[:, n, :], True, False, r=[rAm4[gi], rvh], w=[rob])
                    self.mm(o, ch(qt[0], n), Sb[0][:, n, :], False, False, r=[rqt[0], rSb[0]], w=[rob])
                    self.mm(o, Am4[gi][:, k, 128:256], vh[:, n, :], False, False, r=[rAm4[gi], rvh], w=[rob])
                    self.mm(o, ch(qt[1], n), Sb[1][:, n, :], False, True, r=[rqt[1], rSb[1]], w=[rob])
                self.act(sq4[gi], ob[:], AF.Square, r=[rob], w=[rsq4[gi]])
                ssg = ss[:, gq * 4:gq * 4 + 4]
                rsg = rs_[:, gq * 4:gq * 4 + 4]
                self.P.op("dve", lambda e, o_=ssg, i_=sq4[gi].rearrange("p (k d) -> p k d", k=4): e.tensor_reduce(out=o_, in_=i_, axis=AX.X, op=ALU.add),
                          r=[rsq4[gi]], w=[rss])
                self.act(rsg, ssg, AF.Sqrt, r=[rss, rc], w=[rrs], scale=1.0 / 128, bias=self.epsb[:, 0:1])
                self.recip(rsg, rsg, r=[rrs], w=[rrs])
                self.tt("dve", on4[gi].rearrange("p (k d) -> p k d", k=4), ov4, rsg.unsqueeze(2).to_broadcast([128, 4, 128]), ALU.mult,
                        r=[rob, rrs], w=[ron4[gi]])
                if gq % 2 == 0:
                    ti = 6 + (gq // 2) % 2
                    tb, rtb = self.ps[ti], self.psr[ti]
                    tbv = tb[:].bitcast(BF16)
                for k in range(4):
                    col = ((gq % 2) * 4 + k) * 128
                    self.tr(tbv[:, col:col + 128], on4[gi][:, k * 128:(k + 1) * 128], self.cb(C_ID), r=[ron4[gi], rc], w=[rtb])
                if gq % 2 == 1:
                    n0 = (gq - 1) * 4
                    self.act(oT[:, n0 * L:(n0 + 8) * L], tbv, AF.Copy, r=[rtb, rc], w=[roT], scale=self.gout[:, h:h + 1])
            self.dma("sp", self.O[hs, :], oT, r=[roT])
        self.bank_list = list(range(8))
        P.barrier()


_CACHE = {}


def _common_inputs(inp):
    cst, m1, rope = make_consts()
    d = {"cst": cst, "m1": m1, "rope": rope}
    for name, k, n in W_SPECS:
        d[name] = np.ascontiguousarray(np.asarray(inp[name], np.float32)[0])
    d["ple_w"] = np.ascontiguousarray(np.asarray(inp["ple_w"], np.float32))
    d["ple_gate_w"] = np.ascontiguousarray(np.asarray(inp["ple_gate_w"], np.float32))
    d["norm_g"] = np.ascontiguousarray(np.asarray(inp["norm_g"], np.float32))
    d["final_g"] = np.ascontiguousarray(np.asarray(inp["final_g"], np.float32))
    d["fn_w_mix"] = np.ascontiguousarray(np.asarray(inp["fn_w_mix"], np.float32)[0])
    d["rpb_flip"] = np.ascontiguousarray(np.asarray(inp["na_rpb"], np.float32)[0][:, :, ::-1])
    d["mla_g_q"] = np.ascontiguousarray(np.asarray(inp["mla_g_q"], np.float32)[0])
    d["mla_g_kv"] = np.ascontiguousarray(np.asarray(inp["mla_g_kv"], np.float32)[0])
    d["hg_lb_raw"] = np.ascontiguousarray(np.asarray(inp["hg_lb_raw"], np.float32))
    d["hg_g_out"] = np.ascontiguousarray(np.asarray(inp["hg_g_out"], np.float32)[0])
    return d


def run_seqs(inp, xs, ps, nseq, layers, final_norm, ncores, trace=False):
    key = (nseq, tuple(layers), final_norm)
    if key not in _CACHE:
        bld = Builder(nseq, layers, final_norm)
        nc = bld.build()
        _CACHE[key] = (nc, bld.stats)
    nc, stats = _CACHE[key]
    com = _common_inputs(inp)
    in_maps = []
    for c in range(ncores):
        m = dict(com)
        m["xT"] = np.ascontiguousarray(np.transpose(xs[c], (0, 2, 1)))
        m["pT"] = np.ascontiguousarray(np.transpose(ps[c], (0, 1, 3, 2)))
        in_maps.append(m)
    if trace:
        res = run_bass_kernel_spmd(nc, in_maps, core_ids=list(range(ncores)), trace=True)
        print('EXEC_TIME_NS', res.exec_time_ns)
    else:
        res = run_bass_kernel_spmd(nc, in_maps, core_ids=list(range(ncores)))
    return [np.transpose(r["yT"], (0, 2, 1)) for r in res.results]


def kernel(**inp):
    xp = np.asarray(inp["x_prompt"], np.float32)
    xsm = np.asarray(inp["x_sample"], np.float32)
    pp = np.asarray(inp["p_prompt"], np.float32)
    psm = np.asarray(inp["p_sample"], np.float32)
    xall = np.concatenate([xp, xsm], 0)
    pall = np.concatenate([pp, psm], 1)
    nseq = 3
    idx = [[c, c + 8, (c + 16) if c + 16 < 20 else c] for c in range(8)]
    xs = [xall[idx[c]] for c in range(8)]
    ps = [pall[:, idx[c]] for c in range(8)]
    outs = run_seqs(inp, xs, ps, nseq, [0, 1, 2, 3], True, 8)
    yall = np.zeros_like(xall)
    for c in range(8):
        for s_, gi in enumerate(idx[c]):
            if s_ < 2 or c + 16 < 20:
                yall[gi] = outs[c][s_]
    return (np.ascontiguousarray(yall[:4]), np.ascontiguousarray(yall[4:]))
```

```python
import math
import numpy as np
from contextlib import ExitStack
import concourse.bass as bass
import concourse.mybir as mybir
from concourse.bass_utils import run_bass_kernel_spmd

F32 = mybir.dt.float32
BF16 = mybir.dt.bfloat16
AF = mybir.ActivationFunctionType
ALU = mybir.AluOpType
AX = mybir.AxisListType

ENGS = ("pe", "act", "dve", "pool", "sp")
S = 4096
D = 1024
NT = 8
TS = 512
EPS = 1e-6
NEG = -30000.0


class Res:
    __slots__ = ("name", "w", "rs")

    def __init__(self, name):
        self.name = name
        self.w = None
        self.rs = []


class Op:
    __slots__ = ("eng", "fn", "deps", "dma", "ticket", "inc")

    def __init__(self, eng, fn, dma):
        self.eng = eng
        self.fn = fn
        self.deps = []
        self.dma = dma
        self.ticket = None
        self.inc = dma


class Prog:
    DMA_K = 8
    DMA_CAP = 1500
    CMP_CAP = 30000

    def __init__(self, nc, stack):
        self.nc = nc
        self.stack = stack
        self.ops = []
        self.eobj = {"pe": nc.tensor, "act": nc.scalar, "dve": nc.vector, "pool": nc.gpsimd, "sp": nc.sync}
        self.dma_hist = {e: [] for e in ENGS}
        self.last = {e: None for e in ENGS}
        self.nres = 0

    def res(self, name=None):
        self.nres += 1
        return Res(name or f"r{self.nres}")

    def op(self, eng, fn, r=(), w=(), dma=False):
        o = Op(eng, fn, dma)
        deps = {}
        for x in r:
            if x.w is not None:
                deps[id(x.w)] = (x.w, "raw")
        for x in w:
            if x.w is not None and id(x.w) not in deps:
                deps[id(x.w)] = (x.w, "waw")
            for rd in x.rs:
                if id(rd) not in deps:
                    deps[id(rd)] = (rd, "war")
        if dma:
            h = self.dma_hist[eng]
            if len(h) >= self.DMA_K:
                p = h[-self.DMA_K]
                deps.setdefault(id(p), (p, "rot"))
            h.append(o)
        for d, kind in deps.values():
            if d is o:
                continue
            if (not d.dma) and (not dma) and d.eng == eng:
                if eng == "pe" or kind != "raw":
                    continue
            o.deps.append(d)
            d.inc = True
        for x in w:
            x.w = o
            x.rs = []
        for x in r:
            x.rs.append(o)
        self.ops.append(o)
        if not dma:
            self.last[eng] = o
        return o

    def barrier(self):
        deps = []
        for e in ENGS:
            if self.last[e] is not None:
                deps.append(self.last[e])
            deps.extend(self.dma_hist[e][-self.DMA_K:])
        for e in ENGS:
            o = Op(e, None, False)
            for d in deps:
                if (not d.dma) and d.eng == e:
                    continue
                o.deps.append(d)
                d.inc = True
            self.ops.append(o)

    def emit(self):
        nc = self.nc
        cmp_sem = {}
        cmp_cnt = {}
        dma_sems = {}
        dma_idx = {e: 0 for e in ENGS}
        known = {e: {} for e in ENGS}
        self.nsem = 0

        def newsem(nm):
            self.nsem += 1
            return self.stack.enter_context(nc.semaphore(f"{nm}_{self.nsem}"))

        nwait = 0
        for o in self.ops:
            E = self.eobj[o.eng]
            kn = known[o.eng]
            for d in o.deps:
                sem, val = d.ticket
                key = id(sem)
                if kn.get(key, 0) >= val:
                    continue
                E.wait_ge(sem, val)
                nwait += 1
                kn[key] = val
            if o.fn is None:
                continue
            ins = o.fn(E)
            if o.dma:
                i = dma_idx[o.eng]
                dma_idx[o.eng] = i + 1
                st = i // (self.DMA_K * self.DMA_CAP)
                j = i % (self.DMA_K * self.DMA_CAP)
                keyq = (o.eng, st)
                if keyq not in dma_sems:
                    dma_sems[keyq] = [newsem(f"d{o.eng}") for _ in range(self.DMA_K)]
                sem = dma_sems[keyq][j % self.DMA_K]
                val = 16 * (j // self.DMA_K + 1)
                ins.then_inc(sem, 16)
                o.ticket = (sem, val)
            elif o.inc:
                c = cmp_cnt.get(o.eng, 0)
                if o.eng not in cmp_sem or c >= self.CMP_CAP:
                    cmp_sem[o.eng] = newsem(f"c{o.eng}")
                    c = 0
                c += 1
                cmp_cnt[o.eng] = c
                ins.then_inc(cmp_sem[o.eng], 1)
                o.ticket = (cmp_sem[o.eng], c)
            o.fn = None
        return dict(nops=len(self.ops), nwait=nwait, nsem=self.nsem)


C_ONES, C_ID, C_CC, C_SC, C_M3, C_MU, C_ML, C_MC = 0, 128, 256, 384, 512, 640, 768, 896
C_TOT = 900


def make_consts():
    c = np.zeros((128, C_TOT), np.float64)
    c[:, C_ONES:C_ONES + 128] = 1.0
    c[:, C_ID:C_ID + 128] = np.eye(128)
    a = np.arange(128)
    ang = 2 * np.pi * np.outer(a, a) / 128.0
    c[:, C_CC:C_CC + 128] = np.cos(ang) / math.sqrt(128.0)
    c[:, C_SC:C_SC + 128] = np.sin(ang) / math.sqrt(128.0)
    s0 = np.arange(64)
    k1 = np.arange(64)
    phi = 2 * np.pi * np.outer(s0, k1) / 64.0
    m3 = np.zeros((128, 128))
    m3[0:64, 0:64] = np.cos(phi)
    m3[64:128, 0:64] = np.sin(phi)
    m3[0:64, 64:128] = -np.sin(phi)
    m3[64:128, 64:128] = np.cos(phi)
    c[:, C_M3:C_M3 + 128] = m3 / 8.0
    c[:, C_MU:C_MU + 128] = (a[:, None] <= a[None, :])
    c[:, C_ML:C_ML + 128] = (a[:, None] >= a[None, :])
    for hm in range(4):
        c[:, C_MC + hm] = np.where(a // 32 == hm, 32.0 ** -0.5, 0.0)
    m1 = np.zeros((128, 32, 128))
    p = np.arange(128)
    s1 = (p % 64)[:, None, None]
    half = (p // 64)[:, None, None]
    j = np.arange(32)[None, :, None]
    k0 = np.arange(64)[None, None, :]
    s0v = 32 * half + j
    th = 2 * np.pi * (s1 * k0 / 64.0 + s0v * k0 / 4096.0)
    m1[:, :, 0:64] = np.cos(th) / 8.0
    m1[:, :, 64:128] = -np.sin(th) / 8.0
    half_d = 32
    inv = 10000.0 ** (-np.arange(half_d, dtype=np.float64) / half_d)
    inv = inv.astype(np.float32).astype(np.float64)
    pos = np.arange(S, dtype=np.float64)
    angf = (pos[None, :].astype(np.float32) * inv[:, None].astype(np.float32)).astype(np.float64)
    cs = np.cos(angf)
    sn = np.sin(angf)
    cosT = np.concatenate([cs, cs], 0)
    sinT = np.concatenate([-sn, sn], 0)
    sc = 192.0 ** -0.5
    rope = np.stack([cosT * sc, sinT * sc, cosT, sinT], 1)
    return c.astype(np.float32), m1.reshape(128, 32 * 128).astype(np.float32), rope.astype(np.float32)


def na_rows():
    kh = 8
    r = np.arange(64)
    rs = np.clip(r - kh // 2, 0, 64 - kh)
    return rs


W_SPECS = [
    ("fn_w_in", 1024, 2048), ("fn_w_out", 1024, 1024),
    ("na_w_in", 1024, 4096), ("na_w_out", 1024, 1024),
    ("mla_w_in", 1024, 1728), ("mla_w_uq", 384, 1536), ("mla_w_ukv", 256, 2048), ("mla_w_out", 1024, 1024),
    ("hg_w_in", 1024, 5120), ("hg_w_out", 1024, 1024),
]


class Builder:
    def __init__(self, nseq, layers, final_norm=True):
        self.nseq = nseq
        self.layers = layers
        self.final_norm = final_norm

    def dma(self, q, out, in_, r=(), w=()):
        if q == "pool" and getattr(self, "pool_busy", False):
            q = "sp"
        return self.P.op(q, lambda e: e.dma_start(out=out, in_=in_), r=r, w=w, dma=True)

    def dma_slow(self, q, out, in_, r=(), w=()):
        return self.P.op(q, lambda e: e.dma_start(out=out, in_=in_, allow_slow_non_contiguous=True), r=r, w=w, dma=True)

    def mm(self, out, lhsT, rhs, start, stop, r=(), w=()):
        return self.P.op("pe", lambda e: e.matmul(out, lhsT=lhsT, rhs=rhs, start=start, stop=stop), r=r, w=w)

    def tr(self, out, in_, ident, r=(), w=()):
        return self.P.op("pe", lambda e: e.transpose(out=out, in_=in_, identity=ident), r=r, w=w)

    def act(self, out, in_, func, r=(), w=(), scale=1.0, bias=None, accum=None):
        def f(e):
            kw = {}
            if bias is not None:
                kw["bias"] = bias
            if accum is not None:
                kw["accum_out"] = accum
            return e.activation(out=out, in_=in_, func=func, scale=scale, **kw)
        return self.P.op("act", f, r=r, w=w)

    def tt(self, eng, out, a, b, op, r=(), w=()):
        return self.P.op(eng, lambda e: e.tensor_tensor(out=out, in0=a, in1=b, op=op), r=r, w=w)

    def ts(self, eng, out, a, s1, s2, op0, op1=None, r=(), w=()):
        def f(e):
            if op1 is None:
                return e.tensor_scalar(out=out, in0=a, scalar1=s1, scalar2=None, op0=op0)
            return e.tensor_scalar(out=out, in0=a, scalar1=s1, scalar2=s2, op0=op0, op1=op1)
        return self.P.op(eng, f, r=r, w=w)

    def stt(self, out, in0, scalar, in1, op0, op1, r=(), w=()):
        return self.P.op("dve", lambda e: e.scalar_tensor_tensor(out=out, in0=in0, scalar=scalar, in1=in1, op0=op0, op1=op1), r=r, w=w)

    def cp(self, eng, out, in_, r=(), w=()):
        if eng == "act":
            return self.act(out, in_, AF.Copy, r=r, w=w)
        return self.P.op(eng, lambda e: e.tensor_copy(out=out, in_=in_), r=r, w=w)

    def memset(self, eng, ap, val, r=(), w=()):
        return self.P.op(eng, lambda e: e.memset(ap, val), r=r, w=w)

    def recip(self, out, in_, r=(), w=()):
        return self.P.op("dve", lambda e: e.reciprocal(out=out, in_=in_), r=r, w=w)

    def areset(self):
        self.aoff = 0

    def alloc(self, cols, dt=BF16):
        nb = cols * (4 if dt == F32 else 2)
        nb = (nb + 63) // 64 * 64
        off = self.aoff
        self.aoff += nb
        assert self.aoff <= self.asize, f"arena overflow {self.aoff} > {self.asize}"
        v = self.arena[:, off // 2:(off + nb) // 2]
        if dt == F32:
            v = v.bitcast(F32)
            return v[:, 0:cols]
        return v[:, 0:cols]

    def bank(self):
        bl = self.bank_list
        i = bl[self.bank_i % len(bl)]
        self.bank_i += 1
        return self.ps[i], self.psr[i]

    def build(self):
        nc = bass.Bass("TRN2", target_bir_lowering=False)
        self.nc = nc
        NS = self.nseq
        dt_in = lambda name, shape: nc.dram_tensor(name, shape, F32, kind="ExternalInput").ap()
        self.xT = dt_in("xT", [NS, D, S])
        self.pT = dt_in("pT", [4, NS, 256, S])
        self.cst_d = dt_in("cst", [128, C_TOT])
        self.m1_d = dt_in("m1", [128, 32 * 128])
        self.rope_d = dt_in("rope", [64, 4, S])
        self.wd = {}
        for name, k, n in W_SPECS:
            self.wd[name] = dt_in(name, [k, n])
        self.wd["ple_w"] = dt_in("ple_w", [4, 256, D])
        self.wd["ple_gate_w"] = dt_in("ple_gate_w", [4, D, D])
        self.norm_g_d = dt_in("norm_g", [4, D])
        self.final_g_d = dt_in("final_g", [D])
        self.fn_w_mix_d = dt_in("fn_w_mix", [8, 128, 128])
        self.rpbf_d = dt_in("rpb_flip", [32, 15, 31])
        self.g_q_d = dt_in("mla_g_q", [384])
        self.g_kv_d = dt_in("mla_g_kv", [256])
        self.lb_d = dt_in("hg_lb_raw", [4, 2, D])
        self.g_out_d = dt_in("hg_g_out", [D])
        self.yT = nc.dram_tensor("yT", [NS, D, S], F32, kind="ExternalOutput").ap()
        scr = lambda name, shape, dt=BF16: nc.dram_tensor(name, shape, dt, kind="Internal").ap()
        self.X = scr("X", [NS, D, S], F32)
        self.SA = {}
        for nm, shp, dt_ in [("SZ", [D, S], BF16), ("O", [D, S], BF16), ("Ud", [S, D], BF16), ("Td", [2, 64, 64, D], BF16),
                             ("QT", [D, S], BF16), ("KT", [D, S], BF16), ("VA", [S, 32 * 33], BF16),
                             ("QN", [D, S], BF16), ("QP", [8, 64, S], BF16), ("KN", [D, S], BF16), ("KP", [64, S], BF16),
                             ("V2", [S, D], BF16), ("QS", [D, S], BF16), ("LF", [2, D, S], F32), ("VI", [S, D], BF16)]:
            self.SA[nm] = scr(nm, [NS] + shp, dt_)
        self.w16 = {}
        for name, k, n in W_SPECS:
            self.w16[name] = scr("b_" + name, [k, n])
        self.w16["ple_w"] = scr("b_ple_w", [4, 256, D])
        self.w16["ple_gate_w"] = scr("b_ple_gate_w", [4, D, D])
        self.BT = scr("BT", [128, 32 * 14 * 64])

        with ExitStack() as st:
            self.P = Prog(nc, st)
            P = self.P
            sbt = lambda name, shape, dt: st.enter_context(nc.sbuf_tensor(name, shape, dt))
            self.cst = sbt("cstf", [128, C_TOT], F32)
            self.cstb = sbt("cstb", [128, C_TOT], BF16)
            self.ng = sbt("ng", [128, 4, 8], F32)
            self.fg = sbt("fg", [128, 8], F32)
            self.gq = sbt("gq", [128, 3], F32)
            self.gkv = sbt("gkv", [128, 2], F32)
            self.gout = sbt("gout", [128, 8], F32)
            self.lbraw = sbt("lbraw", [128, 4, 2, 8], F32)
            self.lb = sbt("lb", [128, 2, 8], F32)
            self.oml = sbt("oml", [128, 2, 8], F32)
            self.epsb = sbt("epsb", [128, 1], F32)
            self.asize = 200 * 1024
            self.arena = sbt("arena", [128, self.asize // 2], BF16)
            self.ps = [st.enter_context(nc.psum_tensor(f"ps{i}", [128, 512], F32)) for i in range(8)]
            self.psr = [P.res(f"ps{i}") for i in range(8)]
            self.bank_i = 0
            self.bank_list = list(range(8))
            self.rc = P.res("consts")

            self.phase_prep()
            for li in self.layers:
                last = (li == self.layers[-1])
                [self.pre0, self.pre1, self.pre2, self.pre3][li]()
                for sq in range(NS):
                    self.bind(sq)
                    [self.mid0, self.mid1, self.mid2, self.mid3][li](sq)
                self.post_layer(li, last, last and self.final_norm)
            stats = P.emit()
            self.stats = stats
        return nc

    def bind(self, sq):
        for nm, ap in self.SA.items():
            setattr(self, nm, ap[sq])

    def xsrc_of(self, li, sq):
        return self.xT[sq] if li == self.layers[0] else self.X[sq]

    def pre_iter(self, li):
        tiles = [(sq, t) for sq in range(self.nseq) for t in range(NT)]
        nxt = self.pre_tile(li, self.xsrc_of(li, tiles[0][0]), tiles[0][1], 0)
        for i, (sq, t) in enumerate(tiles):
            cur = nxt
            if i + 1 < len(tiles):
                nxt = self.pre_tile(li, self.xsrc_of(li, tiles[i + 1][0]), tiles[i + 1][1], i + 1)
            self.bind(sq)
            yield i, sq, t, cur[0], cur[1]

    def cb(self, off, n=128):
        return self.cstb[:, off:off + n]

    def cf(self, off, n=128):
        return self.cst[:, off:off + n]

    def phase_prep(self):
        P = self.P
        rc = self.rc
        self.areset()
        self.dma("sp", self.cst[:], self.cst_d[:, :], w=[rc])
        self.cp("dve", self.cstb[:], self.cst[:], r=[rc], w=[rc])
        self.memset("pool", self.epsb[:], EPS, w=[rc])
        self.dma_slow("sp", self.ng[:], self.norm_g_d.rearrange("l (c p) -> p l c", p=128), w=[rc])
        self.dma_slow("sp", self.fg[:], self.final_g_d.rearrange("(c p) -> p c", p=128), w=[rc])
        self.dma_slow("sp", self.gq[:], self.g_q_d.rearrange("(c p) -> p c", p=128), w=[rc])
        self.dma_slow("sp", self.gkv[:], self.g_kv_d.rearrange("(c p) -> p c", p=128), w=[rc])
        self.dma_slow("sp", self.gout[:], self.g_out_d.rearrange("(c p) -> p c", p=128), w=[rc])
        for l in range(4):
            for d in range(2):
                self.dma_slow("sp", self.lbraw[:, l, d, :], self.lb_d[l, d].rearrange("(c p) -> p c", p=128), w=[rc])
        ex = self.alloc(64, F32)
        sm = self.alloc(16, F32)
        exv = ex.rearrange("p (l x) -> p l x", l=4)
        self.act(exv, self.lbraw[:].rearrange("p l d c -> p l (d c)"), AF.Exp, r=[rc], w=[rc])
        self.tt("dve", sm, exv[:, 0, :], exv[:, 1, :], ALU.add, r=[rc], w=[rc])
        self.tt("dve", sm, sm, exv[:, 2, :], ALU.add, r=[rc], w=[rc])
        self.tt("dve", sm, sm, exv[:, 3, :], ALU.add, r=[rc], w=[rc])
        self.recip(sm, sm, r=[rc], w=[rc])
        omlv = self.oml[:].rearrange("p d c -> p (d c)")
        lbv = self.lb[:].rearrange("p d c -> p (d c)")
        self.tt("dve", omlv, exv[:, 0, :], sm, ALU.mult, r=[rc], w=[rc])
        self.ts("dve", lbv, omlv, -1.0, 1.0, ALU.mult, ALU.add, r=[rc], w=[rc])
        stg = [self.alloc(5120, F32) for _ in range(2)]
        stb = [self.alloc(5120, BF16) for _ in range(2)]
        rs_ = [P.res() for _ in range(2)]
        rb_ = [P.res() for _ in range(2)]
        cnt = 0
        engs = ["dve", "pool", "act"]
        jobs = []
        for name, k, n in W_SPECS:
            src = self.wd[name]
            dst = self.w16[name]
            for c in range(k // 128):
                jobs.append((src[c * 128:(c + 1) * 128, :], dst[c * 128:(c + 1) * 128, :], n))
        for l in range(4):
            for c in range(2):
                jobs.append((self.wd["ple_w"][l, c * 128:(c + 1) * 128, :], self.w16["ple_w"][l, c * 128:(c + 1) * 128, :], D))
            for c in range(8):
                jobs.append((self.wd["ple_gate_w"][l, c * 128:(c + 1) * 128, :], self.w16["ple_gate_w"][l, c * 128:(c + 1) * 128, :], D))
        for src, dst, n in jobs:
            b = cnt % 2
            self.dma("sp", stg[b][:, 0:n], src, w=[rs_[b]])
            self.cp(engs[cnt % 3], stb[b][:, 0:n], stg[b][:, 0:n], r=[rs_[b]], w=[rb_[b]])
            self.dma("act" if False else "sp", dst, stb[b][:, 0:n], r=[rb_[b]])
            cnt += 1
        P.barrier()
        if 1 in self.layers:
            self.prep_na_bias()

    def prep_na_bias(self):
        P = self.P
        overlap = (self.layers[0] == 0)
        self.aoff = 143 * 1024 if overlap else 0
        bt = self.alloc(32 * 14 * 64, BF16)
        rb = P.res()
        btv = bt.rearrange("p (h s q) -> p h s q", h=32, s=14)
        self.memset("pool", bt[:, 0:14336], NEG, w=[rb])
        self.memset("dve", bt[:, 14336:28672], NEG, w=[rb])
        qs = np.arange(64)
        win = np.clip(qs - 8, 0, 48)
        for a in range(2):
            for kc in range(64):
                valid = np.nonzero((win <= kc) & (kc < win + 16))[0]
                qlo, qhi = int(valid[0]), int(valid[-1])
                assert len(valid) == qhi - qlo + 1
                nq = qhi - qlo + 1
                j0 = 15 - kc + qlo
                assert 0 <= j0 and j0 + nq <= 31
                p = a * 64 + kc
                src = self.rpbf_d[:, a:a + 14, j0:j0 + nq]
                self.dma_slow("pool", btv[p:p + 1, :, :, qlo:qhi + 1], src.unsqueeze(0), w=[rb])
        self.P.op("pool", lambda e: e.dma_start(out=self.BT[:, :], in_=bt), r=[rb], dma=True)
        if overlap:
            self.pool_busy = True
        else:
            P.barrier()

    def load_w(self, name, k, n, sub=None):
        t = self.alloc((k // 128) * n, BF16)
        v = t.rearrange("p (c n) -> p c n", n=n)
        src = self.w16[name] if sub is None else self.w16[name][sub]
        r = self.P.res()
        kc = k // 128
        hc = max(1, kc // 2)
        srcv = src.rearrange("(c p) n -> p c n", p=128)
        self.dma("sp", v[:, 0:hc, :], srcv[:, 0:hc, :], w=[r])
        if kc > hc:
            self.dma("pool", v[:, hc:kc, :], srcv[:, hc:kc, :], w=[r])
        return v, r

    def pre_setup(self):
        P = self.P
        self.areset()
        self.bank_i = 0
        self.xt = [self.alloc(8 * TS, F32).rearrange("p (c s) -> p c s", c=8) for _ in range(2)]
        self.rxt = [P.res() for _ in range(2)]
        self.sq = [self.alloc(8 * TS, BF16).rearrange("p (c s) -> p c s", c=8) for _ in range(2)]
        self.rsq = [P.res() for _ in range(2)]
        self.hT = [self.alloc(8 * TS, BF16).rearrange("p (c s) -> p c s", c=8) for _ in range(2)]
        self.rhT = [P.res() for _ in range(2)]
        self.rstd = [self.alloc(TS, F32) for _ in range(2)]
        self.rrstd = [P.res() for _ in range(2)]

    def rms_to(self, xt, rx, nch, inv_n, gcol, out, rout, b):
        sq, rsq = self.sq[b], self.rsq[b]
        rstd, rr = self.rstd[b], self.rrstd[b]
        self.tt("dve", sq[:, 0:nch, :], xt, xt, ALU.mult, r=[rx], w=[rsq])
        pb, rpb = self.bank()
        for c in range(nch):
            self.mm(pb[:], self.cb(C_ONES), sq[:, c, :], c == 0, c == nch - 1, r=[rsq, self.rc], w=[rpb])
        self.act(rstd, pb[:], AF.Sqrt, r=[rpb, self.rc], w=[rr], scale=inv_n, bias=self.epsb[:, 0:1])
        self.recip(rstd, rstd, r=[rr], w=[rr])
        for c in range(nch):
            self.stt(out[:, c, :], xt[:, c, :], gcol(c), rstd, ALU.mult, ALU.mult, r=[rx, rr, self.rc], w=[rout])

    def pre_tile(self, li, xsrc, t, i):
        b = i % 2
        xt, rx = self.xt[b], self.rxt[b]
        xv = xsrc.rearrange("(c p) s -> p c s", p=128)
        self.dma("sp", xt[:, 0:4, :], xv[:, 0:4, t * TS:(t + 1) * TS], w=[rx])
        self.dma("pool", xt[:, 4:8, :], xv[:, 4:8, t * TS:(t + 1) * TS], w=[rx])
        self.rms_to(xt, rx, 8, 1.0 / D, lambda c: self.ng[:, li, c:c + 1], self.hT[b], self.rhT[b], b)
        return self.hT[b], self.rhT[b]

    def proj_fm(self, w, rw, col0, hT, rh, m=128):
        pb, rpb = self.bank()
        for k in range(8):
            self.mm(pb[0:m, :], w[:, k, col0:col0 + m], hT[:, k, :], k == 0, k == 7, r=[rw, rh], w=[rpb])
        return pb, rpb

    def proj_tm(self, w, rw, col0, hT, rh, sub):
        pb, rpb = self.bank()
        for k in range(8):
            self.mm(pb[:], hT[:, k, sub * 128:(sub + 1) * 128], w[:, k, col0:col0 + 512], k == 0, k == 7, r=[rw, rh], w=[rpb])
        return pb, rpb

    def z_tile(self, w, rw, zcol0, hT, rh, t, b):
        szt, rsz = self.szt[b], self.rszt[b]
        for j in range(8):
            pb, rpb = self.proj_fm(w, rw, zcol0 + j * 128, hT, rh)
            self.act(szt[:, j, :], pb[:], AF.Silu, r=[rpb], w=[rsz])
        self.dma("sp", self.SZ.rearrange("(c p) s -> p c s", p=128)[:, :, t * TS:(t + 1) * TS], szt, r=[rsz])

    def alloc_szt(self):
        self.szt = [self.alloc(8 * TS, BF16).rearrange("p (c s) -> p c s", c=8) for _ in range(2)]
        self.rszt = [self.P.res() for _ in range(2)]

    def pre0(self):
        P = self.P
        self.pre_setup()
        w, rw = self.load_w("fn_w_in", 1024, 2048)
        self.alloc_szt()
        ut = [self.alloc(4 * 1024, BF16).rearrange("p (a f) -> p a f", a=4) for _ in range(2)]
        rut = [P.res() for _ in range(2)]
        ev = 0
        for i, sq, t, hT, rh in self.pre_iter(0):
            b = i % 2
            udv = self.Ud.rearrange("(n p) f -> p n f", p=128)
            for sub in range(4):
                for ct in range(2):
                    pb, rpb = self.proj_tm(w, rw, ct * 512, hT, rh, sub)
                    self.cp("act" if ev % 2 == 0 else "dve", ut[b][:, sub, ct * 512:(ct + 1) * 512], pb[:], r=[rpb], w=[rut[b]])
                    ev += 1
            self.dma("sp", udv[:, 4 * t:4 * t + 4, :], ut[b], r=[rut[b]])
            self.z_tile(w, rw, 1024, hT, rh, t, b)
        P.barrier()
        self.pool_busy = False

    def mid0(self, sq):
        P = self.P
        rc = self.rc
        self.areset()
        self.bank_i = 0
        wm = self.alloc(8 * 128, F32).rearrange("p (g d) -> p g d", g=8)
        rwm = P.res()
        self.dma("sp", wm, self.fn_w_mix_d.rearrange("g m d -> m g d"), w=[rwm])
        AB = self.alloc(2 * 8 * 128, BF16).rearrange("p (x g d) -> p x g d", x=2, g=8)
        rAB = P.res()
        for x, off in ((0, C_CC), (1, C_SC)):
            for gh in range(2):
                pb, rpb = self.bank()
                for g4 in range(4):
                    g = gh * 4 + g4
                    self.mm(pb[:, g4 * 128:(g4 + 1) * 128], self.cf(off), wm[:, g, :], True, True, r=[rc, rwm], w=[rpb])
                self.cp("dve", AB[:, x, gh * 4:(gh + 1) * 4, :], pb[:].rearrange("p (g d) -> p g d", g=4), r=[rpb], w=[rAB])
        mark = self.aoff
        m1f = self.alloc(32 * 128, F32)
        m1 = self.alloc(32 * 128, BF16).rearrange("p (j m) -> p j m", j=32)
        rm1 = P.res()
        self.dma("sp", m1f, self.m1_d[:, :], w=[rm1])
        self.cp("pool", m1.rearrange("p j m -> p (j m)"), m1f, r=[rm1], w=[rm1])
        ud1 = self.alloc(32 * 1024, BF16).rearrange("p (j f) -> p j f", j=32)
        rud = P.res()
        udv = self.Ud.rearrange("(a b) f -> a b f", b=64)
        self.dma("sp", ud1[0:64, :, :], udv[:, 0:32, :], w=[rud])
        self.dma("pool", ud1[64:128, :, :], udv[:, 32:64, :], w=[rud])
        t1 = [self.alloc(8 * 1024, BF16).rearrange("p (s f) -> p s f", s=8) for _ in range(2)]
        rt1 = [P.res() for _ in range(2)]
        tdv = self.Td.rearrange("r k s f -> (r k) s f")
        ev = 0
        for sb_ in range(8):
            b = sb_ % 2
            for sl in range(8):
                s0 = sb_ * 8 + sl
                half, j = s0 // 32, s0 % 32
                for ft in range(2):
                    pb, rpb = self.bank()
                    self.mm(pb[:], m1[64 * half:64 * half + 64, j, :], ud1[64 * half:64 * half + 64, j, ft * 512:(ft + 1) * 512],
                            True, True, r=[rm1, rud], w=[rpb])
                    self.cp("act" if ev % 2 == 0 else "dve", t1[b][:, sl, ft * 512:(ft + 1) * 512], pb[:], r=[rpb], w=[rt1[b]])
                    ev += 1
            self.dma("sp", tdv[:, sb_ * 8:(sb_ + 1) * 8, :], t1[b], r=[rt1[b]])
        P.barrier()
        self.aoff = mark
        xT = self.alloc(4 * 2 * S, BF16).rearrange("p (g r k) -> p g r k", g=4, r=2)
        rxT = [P.res() for _ in range(4)]
        t3 = [self.alloc(16 * 512, BF16).rearrange("p (k f) -> p k f", k=16) for _ in range(2)]
        rt3 = [P.res() for _ in range(2)]
        ot = [self.alloc(S, BF16) for _ in range(2)]
        rot = [P.res() for _ in range(2)]
        tdr = self.Td.rearrange("r k s f -> r s k f")
        for hf in range(2):
            for kb in range(4):
                b = kb % 2
                for r_ in range(2):
                    self.dma("sp" if r_ == 0 else "pool", t3[b][64 * r_:64 * r_ + 64, :, :],
                             tdr[r_, :, kb * 16:(kb + 1) * 16, hf * 512:(hf + 1) * 512], w=[rt3[b]])
                for k4 in range(4):
                    for g in range(4):
                        pb, rpb = self.bank()
                        for kk in range(4):
                            kl = k4 * 4 + kk
                            self.mm(pb[:, kk * 128:(kk + 1) * 128], t3[b][:, kl, g * 128:(g + 1) * 128], self.cb(C_M3),
                                    True, True, r=[rt3[b], rc], w=[rpb])
                        k0 = kb * 16 + k4 * 4
                        outv = xT[:, g, :, :].rearrange("p r (k1 k0) -> p r k1 k0", k0=64)[:, :, :, k0:k0 + 4]
                        inv = pb[:].rearrange("p (a r k) -> p r k a", a=4, r=2)
                        self.cp("act" if ev % 2 == 0 else "dve", outv, inv, r=[rpb], w=[rxT[g]])
                        ev += 1
            for g in range(4):
                gg = hf * 4 + g
                b = g % 2
                for t in range(NT):
                    pb, rpb = self.bank()
                    self.mm(pb[:], AB[:, 0, gg, :], xT[:, g, 0, t * TS:(t + 1) * TS], True, False, r=[rAB, rxT[g]], w=[rpb])
                    self.mm(pb[:], AB[:, 1, gg, :], xT[:, g, 1, t * TS:(t + 1) * TS], False, True, r=[rAB, rxT[g]], w=[rpb])
                    self.cp("act" if ev % 2 == 0 else "dve", ot[b][:, t * TS:(t + 1) * TS], pb[:], r=[rpb], w=[rot[b]])
                    ev += 1
                self.dma("sp", self.O[gg * 128:(gg + 1) * 128, :], ot[b], r=[rot[b]])
        P.barrier()

    def post_layer(self, li, last, final):
        P = self.P
        self.areset()
        self.bank_i = 0
        self.bank_list = list(range(8))
        wname = ["fn_w_out", "na_w_out", "mla_w_out", "hg_w_out"][li]
        wo, rwo = self.load_w(wname, 1024, 1024)
        wg, rwg = self.load_w("ple_gate_w", 1024, 1024, sub=li)
        wp, rwp = self.load_w("ple_w", 256, 1024, sub=li)
        A = lambda cols, dt: [self.alloc(cols, dt) for _ in range(2)]
        R = lambda: [P.res() for _ in range(2)]
        v8 = lambda t_: t_.rearrange("p (c s) -> p c s", c=8)
        ot = [v8(x) for x in A(8 * TS, BF16)]; rot = R()
        sz = [v8(x) for x in A(8 * TS, BF16)]; rsz = R()
        xt = [v8(x) for x in A(8 * TS, F32)]; rx = R()
        pt = [x.rearrange("p (c s) -> p c s", c=2) for x in A(2 * TS, F32)]; rpt = R()
        ptb = [x.rearrange("p (c s) -> p c s", c=2) for x in A(2 * TS, BF16)]; rptb = R()
        g = [v8(x) for x in A(8 * TS, BF16)]; rg = R()
        x1b = [v8(x) for x in A(8 * TS, BF16)]; rx1b = R()
        sg = A(TS, F32); rsg = R()
        if final:
            sq1 = v8(self.alloc(8 * TS, BF16)); rsq1 = P.res()
            self.sq = [sq1, sq1]; self.rsq = [rsq1, rsq1]
            self.rstd = A(TS, F32); self.rrstd = R()
        NX = 4 if final else 3
        tiles = [(sq, t) for sq in range(self.nseq) for t in range(NT)]
        fm = lambda ap: ap.rearrange("(c p) s -> p c s", p=128)

        xt3 = xt + [v8(self.alloc(8 * TS, F32)) for _ in range(NX - 2)]
        rx3 = rx + [P.res() for _ in range(NX - 2)]
        ry3 = [P.res() for _ in range(NX)]

        def loads(i):
            sq, t = tiles[i]
            b = i % 2
            b3 = i % NX
            sl = slice(t * TS, (t + 1) * TS)
            xv = fm(self.xsrc_of(li, sq))
            self.dma("sp", ot[b], fm(self.SA["O"][sq])[:, :, sl], w=[rot[b]])
            self.dma("sp", sz[b], fm(self.SA["SZ"][sq])[:, :, sl], w=[rsz[b]])
            self.dma("sp", xt3[b3], xv[:, :, sl], w=[rx3[b3]])
            self.dma("sp", pt[b], fm(self.pT[li, sq])[:, :, sl], w=[rpt[b]])

        def stage_a(i):
            sq, t = tiles[i]
            b = i % 2
            b3 = i % NX
            self.tt("dve", g[b], ot[b], sz[b], ALU.mult, r=[rot[b], rsz[b]], w=[rg[b]])
            self.cp("pool", ptb[b], pt[b], r=[rpt[b]], w=[rptb[b]])
            for j in range(8):
                pb, rpb = self.bank()
                for k in range(8):
                    self.mm(pb[:], wo[:, k, j * 128:(j + 1) * 128], g[b][:, k, :], k == 0, k == 7, r=[rwo, rg[b]], w=[rpb])
                self.tt("dve", xt3[b3][:, j, :], xt3[b3][:, j, :], pb[:], ALU.add, r=[rpb, rx3[b3]], w=[rx3[b3]])
                self.cp("act", x1b[b][:, j, :], xt3[b3][:, j, :], r=[rx3[b3]], w=[rx1b[b]])

        def stage_b(i):
            sq, t = tiles[i]
            b = i % 2
            b3 = i % NX
            sl = slice(t * TS, (t + 1) * TS)
            xdv = fm(self.yT[sq] if last else self.X[sq])
            for j in range(8):
                pg, rpg = self.bank()
                for k in range(8):
                    self.mm(pg[:], wg[:, k, j * 128:(j + 1) * 128], x1b[b][:, k, :], k == 0, k == 7, r=[rwg, rx1b[b]], w=[rpg])
                pp, rpp = self.bank()
                for k in range(2):
                    self.mm(pp[:], wp[:, k, j * 128:(j + 1) * 128], ptb[b][:, k, :], k == 0, k == 1, r=[rwp, rptb[b]], w=[rpp])
                self.act(sg[b], pg[:], AF.Sigmoid, r=[rpg], w=[rsg[b]])
                self.tt("dve", sg[b], sg[b], pp[:], ALU.mult, r=[rsg[b], rpp], w=[rsg[b]])
                self.tt("pool", xt3[b3][:, j, :], xt3[b3][:, j, :], sg[b], ALU.add, r=[rsg[b], rx3[b3]], w=[rx3[b3]])
            if final:
                self.rms_to(xt3[b3], rx3[b3], 8, 1.0 / D, lambda c: self.fg[:, c:c + 1], xt3[b3], ry3[b3], b)
            self.dma("sp", xdv[:, 0:4, sl], xt3[b3][:, 0:4, :], r=[rx3[b3], ry3[b3]])
            self.dma("sp", xdv[:, 4:8, sl], xt3[b3][:, 4:8, :], r=[rx3[b3], ry3[b3]])

        loads(0)
        loads(1)
        stage_a(0)
        for i in range(len(tiles)):
            if i + 1 < len(tiles):
                stage_a(i + 1)
            if i + 2 < len(tiles):
                loads(i + 2)
            stage_b(i)
        P.barrier()

    def pre1(self):
        P = self.P
        self.pre_setup()
        self.bank_list = list(range(8))
        w, rw = self.load_w("na_w_in", 1024, 4096)
        self.alloc_szt()
        qk = [self.alloc(8 * TS, BF16).rearrange("p (c s) -> p c s", c=8) for _ in range(2)]
        rqk = [P.res() for _ in range(2)]
        vt = [self.alloc(4 * 1056, BF16).rearrange("p (a h d) -> p a h d", a=4, h=32) for _ in range(2)]
        rvt = [P.res() for _ in range(2)]
        for b in range(2):
            self.memset("pool", vt[b][:, :, :, 32:33], 1.0, w=[rvt[b]])
        ev = 0
        nq = 0
        for i, sq, t, hT, rh in self.pre_iter(1):
            b = i % 2
            sl = slice(t * TS, (t + 1) * TS)
            vav = self.VA.rearrange("(n p) f -> p n f", p=128)
            for which, dst in ((0, self.QT), (1, self.KT)):
                bq = nq % 2
                nq += 1
                for j in range(8):
                    pb, rpb = self.proj_fm(w, rw, which * 1024 + j * 128, hT, rh)
                    self.cp("act" if ev % 2 == 0 else "dve", qk[bq][:, j, :], pb[:], r=[rpb], w=[rqk[bq]])
                    ev += 1
                self.dma("sp", dst.rearrange("(c p) s -> p c s", p=128)[:, :, sl], qk[bq], r=[rqk[bq]])
            for sub in range(4):
                for ct in range(2):
                    pb, rpb = self.proj_tm(w, rw, 2048 + ct * 512, hT, rh, sub)
                    self.cp("act" if ev % 2 == 0 else "dve", vt[b][:, sub, 16 * ct:16 * ct + 16, 0:32],
                            pb[:].rearrange("p (h d) -> p h d", d=32), r=[rpb], w=[rvt[b]])
                    ev += 1
            self.dma("pool", vav[:, 4 * t:4 * t + 4, :], vt[b].rearrange("p a h d -> p a (h d)"), r=[rvt[b]])
            self.z_tile(w, rw, 3072, hT, rh, t, b)
        P.barrier()

    def mid1(self, sq):
        P = self.P
        rc = self.rc
        self.areset()
        self.bank_i = 0
        self.bank_list = [0, 1, 2, 3, 4]
        obanks = [5, 6]
        tbank = 7
        bts = self.alloc(32 * 14 * 64, BF16).rearrange("p (h s q) -> p h s q", h=32, s=14)
        rbt = P.res()
        self.dma("sp", bts[:, 0:16], self.BT.rearrange("p (h x) -> p h x", h=32)[:, 0:16, :].rearrange("p h (s q) -> p h s q", s=14), w=[rbt])
        self.dma("pool", bts[:, 16:32], self.BT.rearrange("p (h x) -> p h x", h=32)[:, 16:32, :].rearrange("p h (s q) -> p h s q", s=14), w=[rbt])
        ktb = [self.alloc(8 * 1024, BF16).rearrange("p (c s) -> p c s", c=8) for _ in range(2)]
        rkt = [P.res() for _ in range(2)]
        vbe = self.alloc(8 * 1056, BF16).rearrange("p (m h d) -> p m h d", m=8, h=32)
        vbo = self.alloc(7 * 1056, BF16).rearrange("p (m h d) -> p m h d", m=7, h=32)
        rvb = [P.res(), P.res()]
        qraw = [self.alloc(8 * TS, BF16).rearrange("p (c s) -> p c s", c=8) for _ in range(2)]
        rqr = [P.res() for _ in range(2)]
        qm = self.alloc(8 * 4 * TS, BF16).rearrange("p (c m s) -> p c m s", c=8, m=4)
        rqm = P.res()
        NPT = 4
        ptf = [self.alloc(1152, BF16) for _ in range(NPT)]
        rpt = [P.res() for _ in range(NPT)]
        for i in range(NPT):
            self.memset("pool", ptf[i], 0.0, w=[rpt[i]])
        ptw = [x[:, 0:1152].rearrange("p (r x) -> p r x", x=576)[:, :, 0:512].rearrange("p r (k y) -> p r k y", y=128)[:, :, :, 0:64] for x in ptf]
        ptr_ = [x[:, 0:1024].rearrange("p (r k y) -> p r k y", r=2, k=4) for x in ptf]
        osb = [self.alloc(1024, BF16) for _ in range(2)]
        rosb = [P.res() for _ in range(2)]
        otb = [self.alloc(8 * TS, BF16).rearrange("p (c s) -> p c s", c=8) for _ in range(2)]
        rotb = [P.res() for _ in range(2)]
        rc8 = [self.alloc(8, F32) for _ in range(2)]
        rrc8 = [P.res() for _ in range(2)]
        ktv = self.KT.rearrange("(c p) s -> p c s", p=128)
        qtv = self.QT.rearrange("(c p) s -> p c s", p=128)
        ov = self.O.rearrange("(c p) s -> p c s", p=128)
        ev = [0]
        units = [(t, pr, hg, hl) for t in range(NT) for pr in range(4) for hg in range(4) for hl in range(8)]
        state = {}

        def band_setup(t):
            b = t % 2
            kw0 = int(np.clip(8 * t - 4, 0, 48))
            self.dma("sp", ktb[b], ktv[:, :, 64 * kw0:64 * kw0 + 1024], w=[rkt[b]])
            self.dma("pool", qraw[b], qtv[:, :, t * TS:(t + 1) * TS], w=[rqr[b]])
            for j in range(8):
                for hm in range(4):
                    e3 = ev[0] % 3
                    if e3 == 2:
                        self.act(qm[:, j, hm, :], qraw[b][:, j, :], AF.Copy, r=[rqr[b], rc], w=[rqm], scale=self.cf(C_MC + hm, 1))
                    else:
                        self.ts("dve" if e3 == 0 else "pool", qm[:, j, hm, :], qraw[b][:, j, :], self.cf(C_MC + hm, 1), 0.0,
                                ALU.mult, ALU.add, r=[rqr[b], rc], w=[rqm])
                    ev[0] += 1

        def v_setup(t, hf):
            kw0 = int(np.clip(8 * t - 4, 0, 48))
            hs_ = slice(16 * hf, 16 * hf + 16)
            cs_ = slice(528 * hf, 528 * hf + 528)
            self.dma("pool", vbe[:, :, hs_, :].rearrange("p m h d -> p m (h d)"),
                     self.VA[64 * kw0:64 * kw0 + 1024, cs_].rearrange("(m p) f -> p m f", p=128), w=[rvb[hf]])
            self.dma("sp", vbo[:, :, hs_, :].rearrange("p m h d -> p m (h d)"),
                     self.VA[64 * (kw0 + 1):64 * (kw0 + 1) + 896, cs_].rearrange("(m p) f -> p m f", p=128), w=[rvb[hf]])

        def geom(u):
            t, pr, hg, hl = u
            kw0 = int(np.clip(8 * t - 4, 0, 48))
            rows = (8 * t + 2 * pr, 8 * t + 2 * pr + 1)
            return kw0, rows

        def S_(ui):
            u = units[ui]
            t, pr, hg, hl = u
            b = t % 2
            kw0, rows = geom(u)
            h = hg * 8 + hl
            j, hm = h // 4, h % 4
            sb_, rsb = self.bank()
            sv = sb_[:].rearrange("p (r k q) -> p r k q", r=2, k=4)
            for ri, r_ in enumerate(rows):
                rs = int(np.clip(r_ - 4, 0, 56))
                ro0 = rs - r_ + 7
                self.mm(sv[:, ri, :, :], self.cb(C_ID), bts[:, h, ro0:ro0 + 7:2, :], True, False, r=[rbt, rc], w=[rsb])
                for kc in range(4):
                    koff = 64 * (rs + 2 * kc - kw0)
                    self.mm(sv[:, ri, kc, :], ktb[b][:, j, koff:koff + 128], qm[:, j, hm, (r_ - 8 * t) * 64:(r_ - 8 * t) * 64 + 64],
                            False, True, r=[rkt[b], rqm], w=[rsb])
            state[ui] = (sv, rsb)

        def PV_(ui):
            u = units[ui]
            t, pr, hg, hl = u
            b = t % 2
            kw0, rows = geom(u)
            h = hg * 8 + hl
            sv, rsb = state.pop(ui)
            pi = ui % NPT
            self.act(ptw[pi], sv, AF.Exp, r=[rsb], w=[rpt[pi]])
            gi = ui // 8
            obi = obanks[gi % 2]
            ob, rob = self.ps[obi], self.psr[obi]
            obv = ob[:, 0:264].rearrange("p (h d) -> p h d", h=8)
            for ri, r_ in enumerate(rows):
                rs = int(np.clip(r_ - 4, 0, 56))
                dlt = rs - kw0
                for kc in range(4):
                    if dlt % 2 == 0:
                        vsrc = vbe[:, dlt // 2 + kc, h, :]
                    else:
                        vsrc = vbo[:, (dlt - 1) // 2 + kc, h, :]
                    self.mm(obv[:, hl, :], ptr_[pi][:, ri, kc, :], vsrc, ri == 0 and kc == 0, ri == 1 and kc == 3,
                            r=[rpt[pi], rvb[h // 16]], w=[rob])
            ob_ = (ui // 32) % 2
            if hl == 7:
                self.recip(rc8[hg % 2], obv[:, :, 32], r=[rob], w=[rrc8[hg % 2]])
                self.tt("dve", osb[ob_][:, hg * 256:(hg + 1) * 256].rearrange("p (h d) -> p h d", h=8), obv[:, :, 0:32],
                        rc8[hg % 2].unsqueeze(2).to_broadcast([128, 8, 32]), ALU.mult, r=[rob, rrc8[hg % 2]], w=[rosb[ob_]])
                if hg == 3:
                    tb, rtb = self.ps[tbank], self.psr[tbank]
                    tbv = tb[:].bitcast(BF16)
                    for jj in range(8):
                        self.tr(tbv[:, jj * 128:(jj + 1) * 128], osb[ob_][:, jj * 128:(jj + 1) * 128], self.cb(C_ID), r=[rosb[ob_], rc], w=[rtb])
                    self.cp("dve", otb[b][:, :, pr * 128:(pr + 1) * 128], tbv.rearrange("p (c s) -> p c s", c=8), r=[rtb], w=[rotb[b]])
                    if pr == 3:
                        self.dma("sp", ov[:, :, t * TS:(t + 1) * TS], otb[b], r=[rotb[b]])

        SK = 3
        N_ = len(units)
        band_setup(0)
        v_setup(0, 0)
        v_setup(0, 1)
        for k in range(SK):
            S_(k)
        for ui in range(N_):
            nxt = ui + SK
            if nxt < N_:
                if units[nxt][0] != units[nxt - 1][0]:
                    band_setup(units[nxt][0])
                S_(nxt)
            PV_(ui)
            t_, pr_, hg_, hl_ = units[ui]
            if t_ + 1 < NT and pr_ == 3 and hl_ == 7:
                if hg_ == 1:
                    v_setup(t_ + 1, 0)
                elif hg_ == 3:
                    v_setup(t_ + 1, 1)
        self.bank_list = list(range(8))
        P.barrier()


    def pre2(self):
        P = self.P
        rc = self.rc
        self.pre_setup()
        self.bank_list = list(range(8))
        w, rw = self.load_w("mla_w_in", 1024, 1728)
        wq, rwq = self.load_w("mla_w_uq", 384, 1536)
        wkv, rwkv = self.load_w("mla_w_ukv", 256, 2048)
        self.alloc_szt()
        c3 = lambda x, n: x.rearrange("p (c s) -> p c s", c=n)
        cq = c3(self.alloc(3 * TS, F32), 3); rcq = P.res()
        ckv = c3(self.alloc(2 * TS, F32), 2); rckv = P.res()
        cqn = c3(self.alloc(3 * TS, BF16), 3); rcqn = P.res()
        ckvn = c3(self.alloc(2 * TS, BF16), 2); rckvn = P.res()
        rp = [c3(self.alloc(4 * TS, F32), 4) for _ in range(2)]; rrp = [P.res() for _ in range(2)]
        qnt = c3(self.alloc(8 * TS, BF16), 8); rqnt = P.res()
        qpt = c3(self.alloc(8 * TS, BF16), 8); rqpt = P.res()
        knt = c3(self.alloc(8 * TS, BF16), 8); rknt = P.res()
        kpt = self.alloc(TS, BF16); rkpt = P.res()
        vt = self.alloc(4 * 1024, BF16).rearrange("p (a f) -> p a f", a=4); rvt = P.res()
        t1 = self.alloc(TS, F32); rt1 = P.res()
        t2 = self.alloc(TS, F32); rt2 = P.res()
        wkv4 = wkv.rearrange("p c (h t d) -> p c h t d", h=8, t=2)
        ev = [0]

        def evac(out, in_, r, w_):
            self.cp("act" if ev[0] % 2 == 0 else "dve", out, in_, r=r, w=w_)
            ev[0] += 1

        def swapped(wt, rwt_, nk, c0, rhs, rrhs):
            pb, rpb = self.bank()
            for k in range(nk):
                self.mm(pb[0:32, :], wt[:, k, c0 + 32:c0 + 64], rhs[:, k, :], k == 0, k == nk - 1, r=[rwt_, rrhs], w=[rpb])
            for k in range(nk):
                self.mm(pb[32:64, :], wt[:, k, c0:c0 + 32], rhs[:, k, :], k == 0, k == nk - 1, r=[rwt_, rrhs], w=[rpb])
            return pb, rpb

        def rope(pa, rpa, pbs, rpbs, ct, st_, rtab, out):
            self.tt("dve", t1[0:64, :], pa[0:64, :], ct, ALU.mult, r=[rpa, rtab], w=[rt1])
            self.tt("dve", t2[0:64, :], pbs[0:64, :], st_, ALU.mult, r=[rpbs, rtab], w=[rt2])
            return t1, t2

        for i, sq, t, hT, rh in self.pre_iter(2):
            b = i % 2
            sl = slice(t * TS, (t + 1) * TS)
            qnv = self.QN.rearrange("(h p) s -> p h s", p=128)
            knv = self.KN.rearrange("(h p) s -> p h s", p=128)
            qpv = self.QP.rearrange("h p s -> p h s")
            v2v = self.V2.rearrange("(n p) f -> p n f", p=128)
            self.dma("pool", rp[b][0:64], self.rope_d[:, :, sl], w=[rrp[b]])
            for c in range(3):
                pb, rpb = self.proj_fm(w, rw, c * 128, hT, rh)
                evac(cq[:, c, :], pb[:], [rpb], [rcq])
            for c in range(2):
                pb, rpb = self.proj_fm(w, rw, 384 + c * 128, hT, rh)
                evac(ckv[:, c, :], pb[:], [rpb], [rckv])
            pa, rpa = self.proj_fm(w, rw, 640, hT, rh, m=64)
            pbs, rpbs = swapped(w, rw, 8, 640, hT, rh)
            self.tt("dve", t1[0:64, :], pa[0:64, :], rp[b][0:64, 2, :], ALU.mult, r=[rpa, rrp[b]], w=[rt1])
            self.tt("dve", t2[0:64, :], pbs[0:64, :], rp[b][0:64, 3, :], ALU.mult, r=[rpbs, rrp[b]], w=[rt2])
            self.tt("pool", kpt[0:64, :], t1[0:64, :], t2[0:64, :], ALU.add, r=[rt1, rt2], w=[rkpt])
            self.dma("sp", self.KP[:, sl], kpt[0:64, :], r=[rkpt])
            self.z_tile(w, rw, 704, hT, rh, t, b)
            self.rms_to(cq, rcq, 3, 1.0 / 384, lambda c: self.gq[:, c:c + 1], cqn, rcqn, b)
            self.rms_to(ckv, rckv, 2, 1.0 / 256, lambda c: self.gkv[:, c:c + 1], ckvn, rckvn, b)
            for h in range(8):
                pb, rpb = self.bank()
                for c in range(3):
                    self.mm(pb[:], wq[:, c, h * 192:h * 192 + 128], cqn[:, c, :], c == 0, c == 2, r=[rwq, rcqn], w=[rpb])
                self.act(qnt[:, h, :], pb[:], AF.Copy, r=[rpb], w=[rqnt], scale=192.0 ** -0.5)
                pa, rpa = self.bank()
                for c in range(3):
                    self.mm(pa[0:64, :], wq[:, c, h * 192 + 128:h * 192 + 192], cqn[:, c, :], c == 0, c == 2, r=[rwq, rcqn], w=[rpa])
                pbs, rpbs = swapped(wq, rwq, 3, h * 192 + 128, cqn, rcqn)
                self.tt("dve", t1[0:64, :], pa[0:64, :], rp[b][0:64, 0, :], ALU.mult, r=[rpa, rrp[b]], w=[rt1])
                self.tt("dve", t2[0:64, :], pbs[0:64, :], rp[b][0:64, 1, :], ALU.mult, r=[rpbs, rrp[b]], w=[rt2])
                self.tt("pool", qpt[0:64, h, :], t1[0:64, :], t2[0:64, :], ALU.add, r=[rt1, rt2], w=[rqpt])
            for h in range(8):
                pb, rpb = self.bank()
                for c in range(2):
                    self.mm(pb[:], wkv[:, c, h * 256:h * 256 + 128], ckvn[:, c, :], c == 0, c == 1, r=[rwkv, rckvn], w=[rpb])
                evac(knt[:, h, :], pb[:], [rpb], [rknt])
            for sub in range(4):
                for hh in range(2):
                    pb, rpb = self.bank()
                    for c in range(2):
                        self.mm(pb[:].rearrange("p (h d) -> p h d", h=4), ckvn[:, c, sub * 128:(sub + 1) * 128],
                                wkv4[:, c, 4 * hh:4 * hh + 4, 1, :], c == 0, c == 1, r=[rwkv, rckvn], w=[rpb])
                    evac(vt[:, sub, hh * 512:(hh + 1) * 512], pb[:], [rpb], [rvt])
            self.dma("sp", qnv[:, :, sl], qnt, r=[rqnt])
            self.dma("pool", qpv[:, :, sl], qpt[0:64], r=[rqpt])
            self.dma("sp", knv[:, :, sl], knt, r=[rknt])
            self.dma("pool", v2v[:, 4 * t:4 * t + 4, :], vt, r=[rvt])
        P.barrier()

    def mid2(self, sq):
        P = self.P
        rc = self.rc
        self.areset()
        self.bank_i = 0
        self.bank_list = [0, 1, 2, 3]
        kpb = self.alloc(S, BF16); rkp = P.res()
        self.dma("sp", kpb[0:64, :], self.KP[:, :], w=[rkp])
        self.dma("pool", kpb[64:128, :], self.KP[:, :], w=[rkp])
        acc = [[self.alloc(TS, F32) for _ in range(2)] for _ in range(2)]
        racc = [[P.res() for _ in range(2)] for _ in range(2)]
        kng = self.alloc(4 * S, BF16).rearrange("p (h s) -> p h s", h=4); rkn = P.res()
        vg = self.alloc(32 * 512, BF16).rearrange("p (n f) -> p n f", n=32); rvg = P.res()
        qn = [self.alloc(4 * TS, BF16).rearrange("p (h s) -> p h s", h=4) for _ in range(2)]; rqn = [P.res() for _ in range(2)]
        qp = [self.alloc(4 * TS, BF16).rearrange("p (h s) -> p h s", h=4) for _ in range(2)]; rqp = [P.res() for _ in range(2)]
        NPT = 6
        pt = [self.alloc(TS, BF16) for _ in range(NPT)]; rpt = [P.res() for _ in range(NPT)]
        rec = [self.alloc(TS, F32) for _ in range(2)]; rrec = [P.res() for _ in range(2)]
        ot = [self.alloc(TS, BF16) for _ in range(2)]; rot = [P.res() for _ in range(2)]
        qnv = self.QN.rearrange("(h p) s -> p h s", p=128)
        knv = self.KN.rearrange("(h p) s -> p h s", p=128)
        qpv = self.QP.rearrange("h p s -> p h s")
        v2v = self.V2.rearrange("(n p) f -> p n f", p=128)
        it = 0
        ipt = 0
        ih = 0
        for g in range(2):
            self.dma("sp", kng[:, 0:2], knv[:, 4 * g:4 * g + 2, :], w=[rkn])
            self.dma("pool", kng[:, 2:4], knv[:, 4 * g + 2:4 * g + 4, :], w=[rkn])
            self.dma("sp", vg[:, 0:16], v2v[:, 0:16, 512 * g:512 * g + 512], w=[rvg])
            self.dma("pool", vg[:, 16:32], v2v[:, 16:32, 512 * g:512 * g + 512], w=[rvg])
            for t in range(NT):
                b = it % 2
                it += 1
                sl = slice(t * TS, (t + 1) * TS)
                self.dma("sp", qn[b], qnv[:, 4 * g:4 * g + 4, sl], w=[rqn[b]])
                self.dma("pool", qp[b][0:64], qpv[:, 4 * g:4 * g + 4, sl], w=[rqp[b]])
                self.dma("sp", qp[b][64:128], qpv[:, 4 * g:4 * g + 4, sl], w=[rqp[b]])
                for hl in range(4):
                    h = 4 * g + hl
                    hb = ih % 2
                    ih += 1
                    OT, rOT = self.ps[4 + hb], self.psr[4 + hb]
                    DS, rDS = self.ps[6 + hb], self.psr[6 + hb]
                    sbanks = {}

                    def score2(m0):
                        bs = []
                        for m in (m0, m0 + 1):
                            sb_, rsb = self.bank()
                            self.mm(sb_[:], kng[:, hl, m * 128:(m + 1) * 128], qn[b][:, hl, :], True, False, r=[rkn, rqn[b]], w=[rsb])
                            bs.append((sb_, rsb))
                        for k_, m in enumerate((m0, m0 + 1)):
                            sb_, rsb = bs[k_]
                            p0 = 64 * k_
                            self.mm(sb_[:], kpb[p0:p0 + 64, m * 128:(m + 1) * 128], qp[b][p0:p0 + 64, hl, :], False, True, r=[rkp, rqp[b]], w=[rsb])
                            sbanks[m] = (sb_, rsb)

                    score2(0)
                    for m in range(32):
                        if m % 2 == 0 and m + 2 < 32:
                            score2(m + 2)
                        sb_, rsb = sbanks.pop(m)
                        pi = ipt % NPT
                        ipt += 1
                        self.act(pt[pi], sb_[:], AF.Exp, r=[rsb], w=[rpt[pi]])
                        self.mm(OT[:], vg[:, m, hl * 128:(hl + 1) * 128], pt[pi], m == 0, m == 31, r=[rvg, rpt[pi]], w=[rOT])
                        e_ = m % 2
                        eng = "pool" if e_ == 0 else "dve"
                        if m < 2:
                            self.cp(eng, acc[hb][e_], pt[pi], r=[rpt[pi]], w=[racc[hb][e_]])
                        else:
                            self.tt(eng, acc[hb][e_], acc[hb][e_], pt[pi], ALU.add, r=[rpt[pi], racc[hb][e_]], w=[racc[hb][e_]])
                    self.mm(DS[:], self.cf(C_ONES), acc[hb][0], True, False, r=[rc, racc[hb][0]], w=[rDS])
                    self.mm(DS[:], self.cf(C_ONES), acc[hb][1], False, True, r=[rc, racc[hb][1]], w=[rDS])
                    self.recip(rec[hb], DS[:], r=[rDS], w=[rrec[hb]])
                    self.tt("dve", ot[hb], OT[:], rec[hb], ALU.mult, r=[rOT, rrec[hb]], w=[rot[hb]])
                    self.dma("sp", self.O[h * 128:(h + 1) * 128, sl], ot[hb], r=[rot[hb]])
        self.bank_list = list(range(8))
        P.barrier()

    def pre3(self):
        P = self.P
        rc = self.rc
        self.pre_setup()
        self.bank_list = list(range(8))
        w, rw = self.load_w("hg_w_in", 1024, 5120)
        self.alloc_szt()
        qst = self.alloc(8 * TS, BF16).rearrange("p (c s) -> p c s", c=8); rqst = P.res()
        lf = self.alloc(8 * TS, F32).rearrange("p (c s) -> p c s", c=8); rlf = P.res()
        vt = self.alloc(4 * 1024, BF16).rearrange("p (a f) -> p a f", a=4); rvt = P.res()
        ev = 0
        for i, sq, t, hT, rh in self.pre_iter(3):
            b = i % 2
            sl = slice(t * TS, (t + 1) * TS)
            qsv = self.QS.rearrange("(c p) s -> p c s", p=128)
            viv = self.VI.rearrange("(n p) f -> p n f", p=128)
            for j in range(8):
                pb, rpb = self.proj_fm(w, rw, j * 128, hT, rh)
                self.act(qst[:, j, :], pb[:], AF.Silu, r=[rpb], w=[rqst])
            self.dma("sp", qsv[:, :, sl], qst, r=[rqst])
            for d in range(2):
                for j in range(8):
                    pb, rpb = self.proj_fm(w, rw, 1024 * (1 + d) + j * 128, hT, rh)
                    self.act(lf[:, j, :], pb[:], AF.Sigmoid, r=[rpb], w=[rlf])
                    self.ts("dve", lf[:, j, :], lf[:, j, :], self.oml[:, d, j:j + 1], self.lb[:, d, j:j + 1], ALU.mult, ALU.add,
                            r=[rlf, rc], w=[rlf])
                self.act(lf, lf, AF.Ln, r=[rlf], w=[rlf])
                lfv = self.LF[d].rearrange("(c p) s -> p c s", p=128)
                self.dma("sp", lfv[:, 0:4, sl], lf[:, 0:4, :], r=[rlf])
                self.dma("pool", lfv[:, 4:8, sl], lf[:, 4:8, :], r=[rlf])
            for sub in range(4):
                for ct in range(2):
                    pb, rpb = self.proj_tm(w, rw, 3072 + ct * 512, hT, rh, sub)
                    self.cp("act" if ev % 2 == 0 else "dve", vt[:, sub, ct * 512:(ct + 1) * 512], pb[:], r=[rpb], w=[rvt])
                    ev += 1
            self.dma("pool", viv[:, 4 * t:4 * t + 4, :], vt, r=[rvt])
            self.z_tile(w, rw, 4096, hT, rh, t, b)
        P.barrier()

    def mid3(self, sq):
        P = self.P
        rc = self.rc
        self.areset()
        self.bank_i = 0
        self.bank_list = [0, 1, 2, 3, 4, 5]
        L = 128
        NCH = 32
        rmask = self.alloc(S, BF16); rrm = P.res()
        self.memset("pool", rmask, 1.0, w=[rrm])
        self.memset("pool", rmask.rearrange("p (n l) -> p n l", l=L)[:, :, 0:1], 0.0, w=[rrm])
        qs = self.alloc(S, BF16); rqs = P.res()
        lf = [self.alloc(S, F32) for _ in range(2)]; rlf = [P.res() for _ in range(2)]
        C = self.alloc(S, F32); rC = P.res()
        E = self.alloc(S, F32); rE = P.res()
        K1 = self.alloc(S, F32); rK1 = P.res()
        qt = [self.alloc(S, BF16) for _ in range(2)]; rqt = [P.res() for _ in range(2)]
        kt = [self.alloc(S, BF16) for _ in range(2)]; rkt = [P.res() for _ in range(2)]
        ktm = [self.alloc(S, BF16).rearrange("p (n c) -> p n c", n=NCH) for _ in range(2)]; rktm = [P.res() for _ in range(2)]
        vh = self.alloc(S, BF16).rearrange("p (n d) -> p n d", n=NCH); rvh = P.res()
        Sb = [self.alloc(S, BF16).rearrange("p (n d) -> p n d", n=NCH) for _ in range(2)]; rSb = [P.res() for _ in range(2)]
        Wst = [[self.alloc(128, F32) for _ in range(2)] for _ in range(2)]; rW = [[P.res() for _ in range(2)] for _ in range(2)]
        ecl = [self.alloc(NCH, F32) for _ in range(2)]; recl = [P.res() for _ in range(2)]
        oT = self.alloc(S, BF16); roT = P.res()
        Am4 = [self.alloc(4 * 256, BF16).rearrange("p (k x) -> p k x", k=4) for _ in range(2)]; rAm4 = [P.res() for _ in range(2)]
        on4 = [self.alloc(512, BF16) for _ in range(2)]; ron4 = [P.res() for _ in range(2)]
        sq4 = [self.alloc(512, F32) for _ in range(2)]; rsq4 = [P.res() for _ in range(2)]
        ss = self.alloc(NCH, F32); rss = P.res()
        rs_ = self.alloc(NCH, F32); rrs = P.res()
        viv = self.VI.rearrange("(n p) f -> p n f", p=128)
        ch = lambda x, n: x[:, n * L:(n + 1) * L]
        for h in range(8):
            hs = slice(h * 128, (h + 1) * 128)
            self.dma("sp", qs, self.QS[hs, :], w=[rqs])
            self.dma("pool", vh, viv[:, :, hs], w=[rvh])
            self.dma("sp", lf[0], self.LF[0, hs, :], w=[rlf[0]])
            self.dma("pool", lf[1], self.LF[1, hs, :], w=[rlf[1]])
            self.memset("pool", ss, 0.0, w=[rss])
            for d in range(2):
                self.act(K1, lf[d], AF.Exp, r=[rlf[d]], w=[rK1])
                self.P.op("dve", lambda e, d=d: e.tensor_tensor_scan(out=C, data0=rmask, data1=lf[d], initial=0.0, op0=ALU.mult, op1=ALU.add),
                          r=[rrm, rlf[d]], w=[rC])
                Cl = C.rearrange("p (n l) -> p n l", l=L)[:, :, L - 1]
                if d == 0:
                    self.act(E, C, AF.Exp, r=[rC], w=[rE])
                    self.cp("dve", ecl[0], E.rearrange("p (n l) -> p n l", l=L)[:, :, L - 1], r=[rE], w=[recl[0]])
                    self.act(C, C, AF.Exp, r=[rC], w=[rC], scale=-1.0)
                else:
                    self.act(ecl[1], Cl, AF.Exp, r=[rC], w=[recl[1]])
                    self.tt("dve", C, C, lf[1], ALU.subtract, r=[rC, rlf[1]], w=[rC])
                    self.act(E, C, AF.Exp, r=[rC], w=[rE], scale=-1.0)
                    self.act(C, C, AF.Exp, r=[rC], w=[rC])
                self.ts("dve", K1, K1, -1.0, 1.0, ALU.mult, ALU.add, r=[rK1], w=[rK1])
                self.tt("dve", qt[d], qs, E, ALU.mult, r=[rqs, rE], w=[rqt[d]])
                self.tt("dve", kt[d], K1, C, ALU.mult, r=[rK1, rC], w=[rkt[d]])
            ev = 0
            for d in range(2):
                for n8 in range(4):
                    tb, rtb = self.bank()
                    tbv = tb[:].bitcast(BF16)
                    for k in range(8):
                        n = n8 * 8 + k
                        self.tr(tbv[:, k * 128:(k + 1) * 128], ch(kt[d], n), self.cb(C_ID), r=[rkt[d], rc], w=[rtb])
                    self.cp("act" if ev % 2 == 0 else "dve", ktm[d][:, n8 * 8:(n8 + 1) * 8, :].rearrange("p n c -> p (n c)"), tbv, r=[rtb], w=[rktm[d]])
                    ev += 1
            self.memset("pool", Sb[0][:, 0, :], 0.0, w=[rSb[0]])
            self.memset("pool", Sb[1][:, NCH - 1, :], 0.0, w=[rSb[1]])
            ub = [None, None]
            wi = [0, 0]
            for i in range(NCH):
                for d in range(2):
                    n = i if d == 0 else NCH - 1 - i
                    if i % 4 == 0:
                        ub[d] = self.bank()
                        for k in range(4):
                            nn = i + k if d == 0 else NCH - 1 - i - k
                            self.mm(ub[d][0][:, k * 128:(k + 1) * 128], ktm[d][:, nn, :], vh[:, nn, :], True, True, r=[rktm[d], rvh], w=[ub[d][1]])
                    U = ub[d][0][:, (i % 4) * 128:(i % 4 + 1) * 128]
                    rU = ub[d][1]
                    cur = wi[d]
                    if i == 0:
                        self.cp("dve", Wst[d][cur], U, r=[rU], w=[rW[d][cur]])
                    else:
                        ei = (i - 1) if d == 0 else n
                        esc = ecl[d][:, ei:ei + 1]
                        self.act(Sb[d][:, n, :], Wst[d][cur], AF.Copy, r=[rW[d][cur], recl[d]], w=[rSb[d]], scale=esc)
                        nxt = 1 - cur
                        self.stt(Wst[d][nxt], Wst[d][cur], esc, U, ALU.mult, ALU.add, r=[rW[d][cur], recl[d], rU], w=[rW[d][nxt]])
                        wi[d] = nxt
            mask2 = self.cf(C_MU, 256).unsqueeze(1).to_broadcast([128, 2, 256])
            for gq in range(NCH // 4):
                gi = gq % 2
                for half in range(2):
                    ab, rab = self.bank()
                    for k2 in range(2):
                        n = gq * 4 + half * 2 + k2
                        self.mm(ab[:, k2 * 256:k2 * 256 + 128], ch(kt[0], n), ch(qt[0], n), True, True, r=[rkt[0], rqt[0]], w=[rab])
                        self.mm(ab[:, k2 * 256 + 128:k2 * 256 + 256], ch(kt[1], n), ch(qt[1], n), True, True, r=[rkt[1], rqt[1]], w=[rab])
                    self.tt("dve", Am4[gi][:, half * 2:half * 2 + 2, :], ab[:].rearrange("p (k x) -> p k x", k=2), mask2, ALU.mult,
                            r=[rab, rc], w=[rAm4[gi]])
                ob, rob = self.bank()
                ov4 = ob[:].rearrange("p (k d) -> p k d", k=4)
                for k in range(4):
                    n = gq * 4 + k
                    o = ov4[:, k, :]
                    self.mm(o, Am4[gi][:, k, 0:128], vh[:, n, :], True, False, r=[rAm4[gi], rvh], w=[rob])
                    self.mm(o, ch(qt[0], n), Sb[0][:, n, :], False, False, r=[rqt[0], rSb[0]], w=[rob])
                    self.mm(o, Am4[gi][:, k, 128:256], vh[:, n, :], False, False, r=[rAm4[gi], rvh], w=[rob])
                    self.mm(o, ch(qt[1], n), Sb[1][:, n, :], False, True, r=[rqt[1], rSb[1]], w=[rob])
                self.act(sq4[gi], ob[:], AF.Square, r=[rob], w=[rsq4[gi]])
                ssg = ss[:, gq * 4:gq * 4 + 4]
                rsg = rs_[:, gq * 4:gq * 4 + 4]
                self.P.op("dve", lambda e, o_=ssg, i_=sq4[gi].rearrange("p (k d) -> p k d", k=4): e.tensor_reduce(out=o_, in_=i_, axis=AX.X, op=ALU.add),
                          r=[rsq4[gi]], w=[rss])
                self.act(rsg, ssg, AF.Sqrt, r=[rss, rc], w=[rrs], scale=1.0 / 128, bias=self.epsb[:, 0:1])
                self.recip(rsg, rsg, r=[rrs], w=[rrs])
                self.tt("dve", on4[gi].rearrange("p (k d) -> p k d", k=4), ov4, rsg.unsqueeze(2).to_broadcast([128, 4, 128]), ALU.mult,
                        r=[rob, rrs], w=[ron4[gi]])
                if gq % 2 == 0:
                    ti = 6 + (gq // 2) % 2
                    tb, rtb = self.ps[ti], self.psr[ti]
                    tbv = tb[:].bitcast(BF16)
                for k in range(4):
                    col = ((gq % 2) * 4 + k) * 128
                    self.tr(tbv[:, col:col + 128], on4[gi][:, k * 128:(k + 1) * 128], self.cb(C_ID), r=[ron4[gi], rc], w=[rtb])
                if gq % 2 == 1:
                    n0 = (gq - 1) * 4
                    self.act(oT[:, n0 * L:(n0 + 8) * L], tbv, AF.Copy, r=[rtb, rc], w=[roT], scale=self.gout[:, h:h + 1])
            self.dma("sp", self.O[hs, :], oT, r=[roT])
        self.bank_list = list(range(8))
        P.barrier()


_CACHE = {}


def _common_inputs(inp):
    cst, m1, rope = make_consts()
    d = {"cst": cst, "m1": m1, "rope": rope}
    for name, k, n in W_SPECS:
        d[name] = np.ascontiguousarray(np.asarray(inp[name], np.float32)[0])
    d["ple_w"] = np.ascontiguousarray(np.asarray(inp["ple_w"], np.float32))
    d["ple_gate_w"] = np.ascontiguousarray(np.asarray(inp["ple_gate_w"], np.float32))
    d["norm_g"] = np.ascontiguousarray(np.asarray(inp["norm_g"], np.float32))
    d["final_g"] = np.ascontiguousarray(np.asarray(inp["final_g"], np.float32))
    d["fn_w_mix"] = np.ascontiguousarray(np.asarray(inp["fn_w_mix"], np.float32)[0])
    d["rpb_flip"] = np.ascontiguousarray(np.asarray(inp["na_rpb"], np.float32)[0][:, :, ::-1])
    d["mla_g_q"] = np.ascontiguousarray(np.asarray(inp["mla_g_q"], np.float32)[0])
    d["mla_g_kv"] = np.ascontiguousarray(np.asarray(inp["mla_g_kv"], np.float32)[0])
    d["hg_lb_raw"] = np.ascontiguousarray(np.asarray(inp["hg_lb_raw"], np.float32))
    d["hg_g_out"] = np.ascontiguousarray(np.asarray(inp["hg_g_out"], np.float32)[0])
    return d


def run_seqs(inp, xs, ps, nseq, layers, final_norm, ncores, trace=False):
    key = (nseq, tuple(layers), final_norm)
    if key not in _CACHE:
        bld = Builder(nseq, layers, final_norm)
        nc = bld.build()
        _CACHE[key] = (nc, bld.stats)
    nc, stats = _CACHE[key]
    com = _common_inputs(inp)
    in_maps = []
    for c in range(ncores):
        m = dict(com)
        m["xT"] = np.ascontiguousarray(np.transpose(xs[c], (0, 2, 1)))
        m["pT"] = np.ascontiguousarray(np.transpose(ps[c], (0, 1, 3, 2)))
        in_maps.append(m)
    if trace:
        res = run_bass_kernel_spmd(nc, in_maps, core_ids=list(range(ncores)), trace=True)
        print('EXEC_TIME_NS', res.exec_time_ns)
    else:
        res = run_bass_kernel_spmd(nc, in_maps, core_ids=list(range(ncores)))
    return [np.transpose(r["yT"], (0, 2, 1)) for r in res.results]


def kernel(**inp):
    xp = np.asarray(inp["x_prompt"], np.float32)
    xsm = np.asarray(inp["x_sample"], np.float32)
    pp = np.asarray(inp["p_prompt"], np.float32)
    psm = np.asarray(inp["p_sample"], np.float32)
    xall = np.concatenate([xp, xsm], 0)
    pall = np.concatenate([pp, psm], 1)
    nseq = 3
    idx = [[c, c + 8, (c + 16) if c + 16 < 20 else c] for c in range(8)]
    xs = [xall[idx[c]] for c in range(8)]
    ps = [pall[:, idx[c]] for c in range(8)]
    outs = run_seqs(inp, xs, ps, nseq, [0, 1, 2, 3], True, 8)
    yall = np.zeros_like(xall)
    for c in range(8):
        for s_, gi in enumerate(idx[c]):
            if s_ < 2 or c + 16 < 20:
                yall[gi] = outs[c][s_]
    return (np.ascontiguousarray(yall[:4]), np.ascontiguousarray(yall[4:]))
```

```python
import math
import numpy as np
from contextlib import ExitStack
import concourse.bass as bass
import concourse.mybir as mybir
from concourse.bass_utils import run_bass_kernel_spmd

F32 = mybir.dt.float32
BF16 = mybir.dt.bfloat16
AF = mybir.ActivationFunctionType
ALU = mybir.AluOpType
AX = mybir.AxisListType

ENGS = ("pe", "act", "dve", "pool", "sp")
S = 4096
D = 1024
NT = 8
TS = 512
EPS = 1e-6
NEG = -30000.0


class Res:
    __slots__ = ("name", "w", "rs")

    def __init__(self, name):
        self.name = name
        self.w = None
        self.rs = []


class Op:
    __slots__ = ("eng", "fn", "deps", "dma", "ticket", "inc")

    def __init__(self, eng, fn, dma):
        self.eng = eng
        self.fn = fn
        self.deps = []
        self.dma = dma
        self.ticket = None
        self.inc = dma


class Prog:
    DMA_K = 8
    DMA_CAP = 1500
    CMP_CAP = 30000

    def __init__(self, nc, stack):
        self.nc = nc
        self.stack = stack
        self.ops = []
        self.eobj = {"pe": nc.tensor, "act": nc.scalar, "dve": nc.vector, "pool": nc.gpsimd, "sp": nc.sync}
        self.dma_hist = {e: [] for e in ENGS}
        self.last = {e: None for e in ENGS}
        self.nres = 0

    def res(self, name=None):
        self.nres += 1
        return Res(name or f"r{self.nres}")

    def op(self, eng, fn, r=(), w=(), dma=False):
        o = Op(eng, fn, dma)
        deps = {}
        for x in r:
            if x.w is not None:
                deps[id(x.w)] = (x.w, "raw")
        for x in w:
            if x.w is not None and id(x.w) not in deps:
                deps[id(x.w)] = (x.w, "waw")
            for rd in x.rs:
                if id(rd) not in deps:
                    deps[id(rd)] = (rd, "war")
        if dma:
            h = self.dma_hist[eng]
            if len(h) >= self.DMA_K:
                p = h[-self.DMA_K]
                deps.setdefault(id(p), (p, "rot"))
            h.append(o)
        for d, kind in deps.values():
            if d is o:
                continue
            if (not d.dma) and (not dma) and d.eng == eng:
                if eng == "pe" or kind != "raw":
                    continue
            o.deps.append(d)
            d.inc = True
        for x in w:
            x.w = o
            x.rs = []
        for x in r:
            x.rs.append(o)
        self.ops.append(o)
        if not dma:
            self.last[eng] = o
        return o

    def barrier(self):
        deps = []
        for e in ENGS:
            if self.last[e] is not None:
                deps.append(self.last[e])
            deps.extend(self.dma_hist[e][-self.DMA_K:])
        for e in ENGS:
            o = Op(e, None, False)
            for d in deps:
                if (not d.dma) and d.eng == e:
                    continue
                o.deps.append(d)
                d.inc = True
            self.ops.append(o)

    def emit(self):
        nc = self.nc
        cmp_sem = {}
        cmp_cnt = {}
        dma_sems = {}
        dma_idx = {e: 0 for e in ENGS}
        known = {e: {} for e in ENGS}
        self.nsem = 0

        def newsem(nm):
            self.nsem += 1
            return self.stack.enter_context(nc.semaphore(f"{nm}_{self.nsem}"))

        nwait = 0
        for o in self.ops:
            E = self.eobj[o.eng]
            kn = known[o.eng]
            for d in o.deps:
                sem, val = d.ticket
                key = id(sem)
                if kn.get(key, 0) >= val:
                    continue
                E.wait_ge(sem, val)
                nwait += 1
                kn[key] = val
            if o.fn is None:
                continue
            ins = o.fn(E)
            if o.dma:
                i = dma_idx[o.eng]
                dma_idx[o.eng] = i + 1
                st = i // (self.DMA_K * self.DMA_CAP)
                j = i % (self.DMA_K * self.DMA_CAP)
                keyq = (o.eng, st)
                if keyq not in dma_sems:
                    dma_sems[keyq] = [newsem(f"d{o.eng}") for _ in range(self.DMA_K)]
                sem = dma_sems[keyq][j % self.DMA_K]
                val = 16 * (j // self.DMA_K + 1)
                ins.then_inc(sem, 16)
                o.ticket = (sem, val)
            elif o.inc:
                c = cmp_cnt.get(o.eng, 0)
                if o.eng not in cmp_sem or c >= self.CMP_CAP:
                    cmp_sem[o.eng] = newsem(f"c{o.eng}")
                    c = 0
                c += 1
                cmp_cnt[o.eng] = c
                ins.then_inc(cmp_sem[o.eng], 1)
                o.ticket = (cmp_sem[o.eng], c)
            o.fn = None
        return dict(nops=len(self.ops), nwait=nwait, nsem=self.nsem)


C_ONES, C_ID, C_CC, C_SC, C_M3, C_MU, C_ML, C_MC = 0, 128, 256, 384, 512, 640, 768, 896
C_TOT = 900


def make_consts():
    c = np.zeros((128, C_TOT), np.float64)
    c[:, C_ONES:C_ONES + 128] = 1.0
    c[:, C_ID:C_ID + 128] = np.eye(128)
    a = np.arange(128)
    ang = 2 * np.pi * np.outer(a, a) / 128.0
    c[:, C_CC:C_CC + 128] = np.cos(ang) / math.sqrt(128.0)
    c[:, C_SC:C_SC + 128] = np.sin(ang) / math.sqrt(128.0)
    s0 = np.arange(64)
    k1 = np.arange(64)
    phi = 2 * np.pi * np.outer(s0, k1) / 64.0
    m3 = np.zeros((128, 128))
    m3[0:64, 0:64] = np.cos(phi)
    m3[64:128, 0:64] = np.sin(phi)
    m3[0:64, 64:128] = -np.sin(phi)
    m3[64:128, 64:128] = np.cos(phi)
    c[:, C_M3:C_M3 + 128] = m3 / 8.0
    c[:, C_MU:C_MU + 128] = (a[:, None] <= a[None, :])
    c[:, C_ML:C_ML + 128] = (a[:, None] >= a[None, :])
    for hm in range(4):
        c[:, C_MC + hm] = np.where(a // 32 == hm, 32.0 ** -0.5, 0.0)
    m1 = np.zeros((128, 32, 128))
    p = np.arange(128)
    s1 = (p % 64)[:, None, None]
    half = (p // 64)[:, None, None]
    j = np.arange(32)[None, :, None]
    k0 = np.arange(64)[None, None, :]
    s0v = 32 * half + j
    th = 2 * np.pi * (s1 * k0 / 64.0 + s0v * k0 / 4096.0)
    m1[:, :, 0:64] = np.cos(th) / 8.0
    m1[:, :, 64:128] = -np.sin(th) / 8.0
    half_d = 32
    inv = 10000.0 ** (-np.arange(half_d, dtype=np.float64) / half_d)
    inv = inv.astype(np.float32).astype(np.float64)
    pos = np.arange(S, dtype=np.float64)
    angf = (pos[None, :].astype(np.float32) * inv[:, None].astype(np.float32)).astype(np.float64)
    cs = np.cos(angf)
    sn = np.sin(angf)
    cosT = np.concatenate([cs, cs], 0)
    sinT = np.concatenate([-sn, sn], 0)
    sc = 192.0 ** -0.5
    rope = np.stack([cosT * sc, sinT * sc, cosT, sinT], 1)
    return c.astype(np.float32), m1.reshape(128, 32 * 128).astype(np.float32), rope.astype(np.float32)


def na_rows():
    kh = 8
    r = np.arange(64)
    rs = np.clip(r - kh // 2, 0, 64 - kh)
    return rs


W_SPECS = [
    ("fn_w_in", 1024, 2048), ("fn_w_out", 1024, 1024),
    ("na_w_in", 1024, 4096), ("na_w_out", 1024, 1024),
    ("mla_w_in", 1024, 1728), ("mla_w_uq", 384, 1536), ("mla_w_ukv", 256, 2048), ("mla_w_out", 1024, 1024),
    ("hg_w_in", 1024, 5120), ("hg_w_out", 1024, 1024),
]


class Builder:
    def __init__(self, nseq, layers, final_norm=True):
        self.nseq = nseq
        self.layers = layers
        self.final_norm = final_norm

    def dma(self, q, out, in_, r=(), w=()):
        if q == "pool" and getattr(self, "pool_busy", False):
            q = "sp"
        return self.P.op(q, lambda e: e.dma_start(out=out, in_=in_), r=r, w=w, dma=True)

    def dma_slow(self, q, out, in_, r=(), w=()):
        return self.P.op(q, lambda e: e.dma_start(out=out, in_=in_, allow_slow_non_contiguous=True), r=r, w=w, dma=True)

    def mm(self, out, lhsT, rhs, start, stop, r=(), w=()):
        return self.P.op("pe", lambda e: e.matmul(out, lhsT=lhsT, rhs=rhs, start=start, stop=stop), r=r, w=w)

    def tr(self, out, in_, ident, r=(), w=()):
        return self.P.op("pe", lambda e: e.transpose(out=out, in_=in_, identity=ident), r=r, w=w)

    def act(self, out, in_, func, r=(), w=(), scale=1.0, bias=None, accum=None):
        def f(e):
            kw = {}
            if bias is not None:
                kw["bias"] = bias
            if accum is not None:
                kw["accum_out"] = accum
            return e.activation(out=out, in_=in_, func=func, scale=scale, **kw)
        return self.P.op("act", f, r=r, w=w)

    def tt(self, eng, out, a, b, op, r=(), w=()):
        return self.P.op(eng, lambda e: e.tensor_tensor(out=out, in0=a, in1=b, op=op), r=r, w=w)

    def ts(self, eng, out, a, s1, s2, op0, op1=None, r=(), w=()):
        def f(e):
            if op1 is None:
                return e.tensor_scalar(out=out, in0=a, scalar1=s1, scalar2=None, op0=op0)
            return e.tensor_scalar(out=out, in0=a, scalar1=s1, scalar2=s2, op0=op0, op1=op1)
        return self.P.op(eng, f, r=r, w=w)

    def stt(self, out, in0, scalar, in1, op0, op1, r=(), w=()):
        return self.P.op("dve", lambda e: e.scalar_tensor_tensor(out=out, in0=in0, scalar=scalar, in1=in1, op0=op0, op1=op1), r=r, w=w)

    def cp(self, eng, out, in_, r=(), w=()):
        if eng == "act":
            return self.act(out, in_, AF.Copy, r=r, w=w)
        return self.P.op(eng, lambda e: e.tensor_copy(out=out, in_=in_), r=r, w=w)

    def memset(self, eng, ap, val, r=(), w=()):
        return self.P.op(eng, lambda e: e.memset(ap, val), r=r, w=w)

    def recip(self, out, in_, r=(), w=()):
        return self.P.op("dve", lambda e: e.reciprocal(out=out, in_=in_), r=r, w=w)

    def areset(self):
        self.aoff = 0

    def alloc(self, cols, dt=BF16):
        nb = cols * (4 if dt == F32 else 2)
        nb = (nb + 63) // 64 * 64
        off = self.aoff
        self.aoff += nb
        assert self.aoff <= self.asize, f"arena overflow {self.aoff} > {self.asize}"
        v = self.arena[:, off // 2:(off + nb) // 2]
        if dt == F32:
            v = v.bitcast(F32)
            return v[:, 0:cols]
        return v[:, 0:cols]

    def bank(self):
        bl = self.bank_list
        i = bl[self.bank_i % len(bl)]
        self.bank_i += 1
        return self.ps[i], self.psr[i]

    def build(self):
        nc = bass.Bass("TRN2", target_bir_lowering=False)
        self.nc = nc
        NS = self.nseq
        dt_in = lambda name, shape: nc.dram_tensor(name, shape, F32, kind="ExternalInput").ap()
        self.xT = dt_in("xT", [NS, D, S])
        self.pT = dt_in("pT", [4, NS, 256, S])
        self.cst_d = dt_in("cst", [128, C_TOT])
        self.m1_d = dt_in("m1", [128, 32 * 128])
        self.rope_d = dt_in("rope", [64, 4, S])
        self.wd = {}
        for name, k, n in W_SPECS:
            self.wd[name] = dt_in(name, [k, n])
        self.wd["ple_w"] = dt_in("ple_w", [4, 256, D])
        self.wd["ple_gate_w"] = dt_in("ple_gate_w", [4, D, D])
        self.norm_g_d = dt_in("norm_g", [4, D])
        self.final_g_d = dt_in("final_g", [D])
        self.fn_w_mix_d = dt_in("fn_w_mix", [8, 128, 128])
        self.rpbf_d = dt_in("rpb_flip", [32, 15, 31])
        self.g_q_d = dt_in("mla_g_q", [384])
        self.g_kv_d = dt_in("mla_g_kv", [256])
        self.lb_d = dt_in("hg_lb_raw", [4, 2, D])
        self.g_out_d = dt_in("hg_g_out", [D])
        self.yT = nc.dram_tensor("yT", [NS, D, S], F32, kind="ExternalOutput").ap()
        scr = lambda name, shape, dt=BF16: nc.dram_tensor(name, shape, dt, kind="Internal").ap()
        self.X = scr("X", [NS, D, S], F32)
        self.SA = {}
        for nm, shp, dt_ in [("SZ", [D, S], BF16), ("O", [D, S], BF16), ("Ud", [S, D], BF16), ("Td", [2, 64, 64, D], BF16),
                             ("QT", [D, S], BF16), ("KT", [D, S], BF16), ("VA", [S, 32 * 33], BF16),
                             ("QN", [D, S], BF16), ("QP", [8, 64, S], BF16), ("KN", [D, S], BF16), ("KP", [64, S], BF16),
                             ("V2", [S, D], BF16), ("QS", [D, S], BF16), ("LF", [2, D, S], F32), ("VI", [S, D], BF16)]:
            self.SA[nm] = scr(nm, [NS] + shp, dt_)
        self.w16 = {}
        for name, k, n in W_SPECS:
            self.w16[name] = scr("b_" + name, [k, n])
        self.w16["ple_w"] = scr("b_ple_w", [4, 256, D])
        self.w16["ple_gate_w"] = scr("b_ple_gate_w", [4, D, D])
        self.BT = scr("BT", [128, 32 * 14 * 64])

        with ExitStack() as st:
            self.P = Prog(nc, st)
            P = self.P
            sbt = lambda name, shape, dt: st.enter_context(nc.sbuf_tensor(name, shape, dt))
            self.cst = sbt("cstf", [128, C_TOT], F32)
            self.cstb = sbt("cstb", [128, C_TOT], BF16)
            self.ng = sbt("ng", [128, 4, 8], F32)
            self.fg = sbt("fg", [128, 8], F32)
            self.gq = sbt("gq", [128, 3], F32)
            self.gkv = sbt("gkv", [128, 2], F32)
            self.gout = sbt("gout", [128, 8], F32)
            self.lbraw = sbt("lbraw", [128, 4, 2, 8], F32)
            self.lb = sbt("lb", [128, 2, 8], F32)
            self.oml = sbt("oml", [128, 2, 8], F32)
            self.epsb = sbt("epsb", [128, 1], F32)
            self.asize = 200 * 1024
            self.arena = sbt("arena", [128, self.asize // 2], BF16)
            self.ps = [st.enter_context(nc.psum_tensor(f"ps{i}", [128, 512], F32)) for i in range(8)]
            self.psr = [P.res(f"ps{i}") for i in range(8)]
            self.bank_i = 0
            self.bank_list = list(range(8))
            self.rc = P.res("consts")

            self.phase_prep()
            for li in self.layers:
                last = (li == self.layers[-1])
                [self.pre0, self.pre1, self.pre2, self.pre3][li]()
                for sq in range(NS):
                    self.bind(sq)
                    [self.mid0, self.mid1, self.mid2, self.mid3][li](sq)
                self.post_layer(li, last, last and self.final_norm)
            stats = P.emit()
            self.stats = stats
        return nc

    def bind(self, sq):
        for nm, ap in self.SA.items():
            setattr(self, nm, ap[sq])

    def xsrc_of(self, li, sq):
        return self.xT[sq] if li == self.layers[0] else self.X[sq]

    def pre_iter(self, li):
        tiles = [(sq, t) for sq in range(self.nseq) for t in range(NT)]
        nxt = self.pre_tile(li, self.xsrc_of(li, tiles[0][0]), tiles[0][1], 0)
        for i, (sq, t) in enumerate(tiles):
            cur = nxt
            if i + 1 < len(tiles):
                nxt = self.pre_tile(li, self.xsrc_of(li, tiles[i + 1][0]), tiles[i + 1][1], i + 1)
            self.bind(sq)
            yield i, sq, t, cur[0], cur[1]

    def cb(self, off, n=128):
        return self.cstb[:, off:off + n]

    def cf(self, off, n=128):
        return self.cst[:, off:off + n]

    def phase_prep(self):
        P = self.P
        rc = self.rc
        self.areset()
        self.dma("sp", self.cst[:], self.cst_d[:, :], w=[rc])
        self.cp("dve", self.cstb[:], self.cst[:], r=[rc], w=[rc])
        self.memset("pool", self.epsb[:], EPS, w=[rc])
        self.dma_slow("sp", self.ng[:], self.norm_g_d.rearrange("l (c p) -> p l c", p=128), w=[rc])
        self.dma_slow("sp", self.fg[:], self.final_g_d.rearrange("(c p) -> p c", p=128), w=[rc])
        self.dma_slow("sp", self.gq[:], self.g_q_d.rearrange("(c p) -> p c", p=128), w=[rc])
        self.dma_slow("sp", self.gkv[:], self.g_kv_d.rearrange("(c p) -> p c", p=128), w=[rc])
        self.dma_slow("sp", self.gout[:], self.g_out_d.rearrange("(c p) -> p c", p=128), w=[rc])
        for l in range(4):
            for d in range(2):
                self.dma_slow("sp", self.lbraw[:, l, d, :], self.lb_d[l, d].rearrange("(c p) -> p c", p=128), w=[rc])
        ex = self.alloc(64, F32)
        sm = self.alloc(16, F32)
        exv = ex.rearrange("p (l x) -> p l x", l=4)
        self.act(exv, self.lbraw[:].rearrange("p l d c -> p l (d c)"), AF.Exp, r=[rc], w=[rc])
        self.tt("dve", sm, exv[:, 0, :], exv[:, 1, :], ALU.add, r=[rc], w=[rc])
        self.tt("dve", sm, sm, exv[:, 2, :], ALU.add, r=[rc], w=[rc])
        self.tt("dve", sm, sm, exv[:, 3, :], ALU.add, r=[rc], w=[rc])
        self.recip(sm, sm, r=[rc], w=[rc])
        omlv = self.oml[:].rearrange("p d c -> p (d c)")
        lbv = self.lb[:].rearrange("p d c -> p (d c)")
        self.tt("dve", omlv, exv[:, 0, :], sm, ALU.mult, r=[rc], w=[rc])
        self.ts("dve", lbv, omlv, -1.0, 1.0, ALU.mult, ALU.add, r=[rc], w=[rc])
        stg = [self.alloc(5120, F32) for _ in range(2)]
        stb = [self.alloc(5120, BF16) for _ in range(2)]
        rs_ = [P.res() for _ in range(2)]
        rb_ = [P.res() for _ in range(2)]
        cnt = 0
        engs = ["dve", "pool", "act"]
        jobs = []
        for name, k, n in W_SPECS:
            src = self.wd[name]
            dst = self.w16[name]
            for c in range(k // 128):
                jobs.append((src[c * 128:(c + 1) * 128, :], dst[c * 128:(c + 1) * 128, :], n))
        for l in range(4):
            for c in range(2):
                jobs.append((self.wd["ple_w"][l, c * 128:(c + 1) * 128, :], self.w16["ple_w"][l, c * 128:(c + 1) * 128, :], D))
            for c in range(8):
                jobs.append((self.wd["ple_gate_w"][l, c * 128:(c + 1) * 128, :], self.w16["ple_gate_w"][l, c * 128:(c + 1) * 128, :], D))
        for src, dst, n in jobs:
            b = cnt % 2
            self.dma("sp", stg[b][:, 0:n], src, w=[rs_[b]])
            self.cp(engs[cnt % 3], stb[b][:, 0:n], stg[b][:, 0:n], r=[rs_[b]], w=[rb_[b]])
            self.dma("act" if False else "sp", dst, stb[b][:, 0:n], r=[rb_[b]])
            cnt += 1
        P.barrier()
        if 1 in self.layers:
            self.prep_na_bias()

    def prep_na_bias(self):
        P = self.P
        overlap = (self.layers[0] == 0)
        self.aoff = 143 * 1024 if overlap else 0
        bt = self.alloc(32 * 14 * 64, BF16)
        rb = P.res()
        btv = bt.rearrange("p (h s q) -> p h s q", h=32, s=14)
        self.memset("pool", bt[:, 0:14336], NEG, w=[rb])
        self.memset("dve", bt[:, 14336:28672], NEG, w=[rb])
        qs = np.arange(64)
        win = np.clip(qs - 8, 0, 48)
        for a in range(2):
            for kc in range(64):
                valid = np.nonzero((win <= kc) & (kc < win + 16))[0]
                qlo, qhi = int(valid[0]), int(valid[-1])
                assert len(valid) == qhi - qlo + 1
                nq = qhi - qlo + 1
                j0 = 15 - kc + qlo
                assert 0 <= j0 and j0 + nq <= 31
                p = a * 64 + kc
                src = self.rpbf_d[:, a:a + 14, j0:j0 + nq]
                self.dma_slow("pool", btv[p:p + 1, :, :, qlo:qhi + 1], src.unsqueeze(0), w=[rb])
        self.P.op("pool", lambda e: e.dma_start(out=self.BT[:, :], in_=bt), r=[rb], dma=True)
        if overlap:
            self.pool_busy = True
        else:
            P.barrier()

    def load_w(self, name, k, n, sub=None):
        t = self.alloc((k // 128) * n, BF16)
        v = t.rearrange("p (c n) -> p c n", n=n)
        src = self.w16[name] if sub is None else self.w16[name][sub]
        r = self.P.res()
        kc = k // 128
        hc = max(1, kc // 2)
        srcv = src.rearrange("(c p) n -> p c n", p=128)
        self.dma("sp", v[:, 0:hc, :], srcv[:, 0:hc, :], w=[r])
        if kc > hc:
            self.dma("pool", v[:, hc:kc, :], srcv[:, hc:kc, :], w=[r])
        return v, r

    def pre_setup(self):
        P = self.P
        self.areset()
        self.bank_i = 0
        self.xt = [self.alloc(8 * TS, F32).rearrange("p (c s) -> p c s", c=8) for _ in range(2)]
        self.rxt = [P.res() for _ in range(2)]
        self.sq = [self.alloc(8 * TS, BF16).rearrange("p (c s) -> p c s", c=8) for _ in range(2)]
        self.rsq = [P.res() for _ in range(2)]
        self.hT = [self.alloc(8 * TS, BF16).rearrange("p (c s) -> p c s", c=8) for _ in range(2)]
        self.rhT = [P.res() for _ in range(2)]
        self.rstd = [self.alloc(TS, F32) for _ in range(2)]
        self.rrstd = [P.res() for _ in range(2)]

    def rms_to(self, xt, rx, nch, inv_n, gcol, out, rout, b):
        sq, rsq = self.sq[b], self.rsq[b]
        rstd, rr = self.rstd[b], self.rrstd[b]
        self.tt("dve", sq[:, 0:nch, :], xt, xt, ALU.mult, r=[rx], w=[rsq])
        pb, rpb = self.bank()
        for c in range(nch):
            self.mm(pb[:], self.cb(C_ONES), sq[:, c, :], c == 0, c == nch - 1, r=[rsq, self.rc], w=[rpb])
        self.act(rstd, pb[:], AF.Sqrt, r=[rpb, self.rc], w=[rr], scale=inv_n, bias=self.epsb[:, 0:1])
        self.recip(rstd, rstd, r=[rr], w=[rr])
        for c in range(nch):
            self.stt(out[:, c, :], xt[:, c, :], gcol(c), rstd, ALU.mult, ALU.mult, r=[rx, rr, self.rc], w=[rout])

    def pre_tile(self, li, xsrc, t, i):
        b = i % 2
        xt, rx = self.xt[b], self.rxt[b]
        xv = xsrc.rearrange("(c p) s -> p c s", p=128)
        self.dma("sp", xt[:, 0:4, :], xv[:, 0:4, t * TS:(t + 1) * TS], w=[rx])
        self.dma("pool", xt[:, 4:8, :], xv[:, 4:8, t * TS:(t + 1) * TS], w=[rx])
        self.rms_to(xt, rx, 8, 1.0 / D, lambda c: self.ng[:, li, c:c + 1], self.hT[b], self.rhT[b], b)
        return self.hT[b], self.rhT[b]

    def proj_fm(self, w, rw, col0, hT, rh, m=128):
        pb, rpb = self.bank()
        for k in range(8):
            self.mm(pb[0:m, :], w[:, k, col0:col0 + m], hT[:, k, :], k == 0, k == 7, r=[rw, rh], w=[rpb])
        return pb, rpb

    def proj_tm(self, w, rw, col0, hT, rh, sub):
        pb, rpb = self.bank()
        for k in range(8):
            self.mm(pb[:], hT[:, k, sub * 128:(sub + 1) * 128], w[:, k, col0:col0 + 512], k == 0, k == 7, r=[rw, rh], w=[rpb])
        return pb, rpb

    def z_tile(self, w, rw, zcol0, hT, rh, t, b):
        szt, rsz = self.szt[b], self.rszt[b]
        for j in range(8):
            pb, rpb = self.proj_fm(w, rw, zcol0 + j * 128, hT, rh)
            self.act(szt[:, j, :], pb[:], AF.Silu, r=[rpb], w=[rsz])
        self.dma("sp", self.SZ.rearrange("(c p) s -> p c s", p=128)[:, :, t * TS:(t + 1) * TS], szt, r=[rsz])

    def alloc_szt(self):
        self.szt = [self.alloc(8 * TS, BF16).rearrange("p (c s) -> p c s", c=8) for _ in range(2)]
        self.rszt = [self.P.res() for _ in range(2)]

    def pre0(self):
        P = self.P
        self.pre_setup()
        w, rw = self.load_w("fn_w_in", 1024, 2048)
        self.alloc_szt()
        ut = [self.alloc(4 * 1024, BF16).rearrange("p (a f) -> p a f", a=4) for _ in range(2)]
        rut = [P.res() for _ in range(2)]
        ev = 0
        for i, sq, t, hT, rh in self.pre_iter(0):
            b = i % 2
            udv = self.Ud.rearrange("(n p) f -> p n f", p=128)
            for sub in range(4):
                for ct in range(2):
                    pb, rpb = self.proj_tm(w, rw, ct * 512, hT, rh, sub)
                    self.cp("act" if ev % 2 == 0 else "dve", ut[b][:, sub, ct * 512:(ct + 1) * 512], pb[:], r=[rpb], w=[rut[b]])
                    ev += 1
            self.dma("sp", udv[:, 4 * t:4 * t + 4, :], ut[b], r=[rut[b]])
            self.z_tile(w, rw, 1024, hT, rh, t, b)
        P.barrier()
        self.pool_busy = False

    def mid0(self, sq):
        P = self.P
        rc = self.rc
        self.areset()
        self.bank_i = 0
        wm = self.alloc(8 * 128, F32).rearrange("p (g d) -> p g d", g=8)
        rwm = P.res()
        self.dma("sp", wm, self.fn_w_mix_d.rearrange("g m d -> m g d"), w=[rwm])
        AB = self.alloc(2 * 8 * 128, BF16).rearrange("p (x g d) -> p x g d", x=2, g=8)
        rAB = P.res()
        for x, off in ((0, C_CC), (1, C_SC)):
            for gh in range(2):
                pb, rpb = self.bank()
                for g4 in range(4):
                    g = gh * 4 + g4
                    self.mm(pb[:, g4 * 128:(g4 + 1) * 128], self.cf(off), wm[:, g, :], True, True, r=[rc, rwm], w=[rpb])
                self.cp("dve", AB[:, x, gh * 4:(gh + 1) * 4, :], pb[:].rearrange("p (g d) -> p g d", g=4), r=[rpb], w=[rAB])
        mark = self.aoff
        m1f = self.alloc(32 * 128, F32)
        m1 = self.alloc(32 * 128, BF16).rearrange("p (j m) -> p j m", j=32)
        rm1 = P.res()
        self.dma("sp", m1f, self.m1_d[:, :], w=[rm1])
        self.cp("pool", m1.rearrange("p j m -> p (j m)"), m1f, r=[rm1], w=[rm1])
        ud1 = self.alloc(32 * 1024, BF16).rearrange("p (j f) -> p j f", j=32)
        rud = P.res()
        udv = self.Ud.rearrange("(a b) f -> a b f", b=64)
        self.dma("sp", ud1[0:64, :, :], udv[:, 0:32, :], w=[rud])
        self.dma("pool", ud1[64:128, :, :], udv[:, 32:64, :], w=[rud])
        t1 = [self.alloc(8 * 1024, BF16).rearrange("p (s f) -> p s f", s=8) for _ in range(2)]
        rt1 = [P.res() for _ in range(2)]
        tdv = self.Td.rearrange("r k s f -> (r k) s f")
        ev = 0
        for sb_ in range(8):
            b = sb_ % 2
            for sl in range(8):
                s0 = sb_ * 8 + sl
                half, j = s0 // 32, s0 % 32
                for ft in range(2):
                    pb, rpb = self.bank()
                    self.mm(pb[:], m1[64 * half:64 * half + 64, j, :], ud1[64 * half:64 * half + 64, j, ft * 512:(ft + 1) * 512],
                            True, True, r=[rm1, rud], w=[rpb])
                    self.cp("act" if ev % 2 == 0 else "dve", t1[b][:, sl, ft * 512:(ft + 1) * 512], pb[:], r=[rpb], w=[rt1[b]])
                    ev += 1
            self.dma("sp", tdv[:, sb_ * 8:(sb_ + 1) * 8, :], t1[b], r=[rt1[b]])
        P.barrier()
        self.aoff = mark
        xT = self.alloc(4 * 2 * S, BF16).rearrange("p (g r k) -> p g r k", g=4, r=2)
        rxT = [P.res() for _ in range(4)]
        t3 = [self.alloc(16 * 512, BF16).rearrange("p (k f) -> p k f", k=16) for _ in range(2)]
        rt3 = [P.res() for _ in range(2)]
        ot = [self.alloc(S, BF16) for _ in range(2)]
        rot = [P.res() for _ in range(2)]
        tdr = self.Td.rearrange("r k s f -> r s k f")
        for hf in range(2):
            for kb in range(4):
                b = kb % 2
                for r_ in range(2):
                    self.dma("sp" if r_ == 0 else "pool", t3[b][64 * r_:64 * r_ + 64, :, :],
                             tdr[r_, :, kb * 16:(kb + 1) * 16, hf * 512:(hf + 1) * 512], w=[rt3[b]])
                for k4 in range(4):
                    for g in range(4):
                        pb, rpb = self.bank()
                        for kk in range(4):
                            kl = k4 * 4 + kk
                            self.mm(pb[:, kk * 128:(kk + 1) * 128], t3[b][:, kl, g * 128:(g + 1) * 128], self.cb(C_M3),
                                    True, True, r=[rt3[b], rc], w=[rpb])
                        k0 = kb * 16 + k4 * 4
                        outv = xT[:, g, :, :].rearrange("p r (k1 k0) -> p r k1 k0", k0=64)[:, :, :, k0:k0 + 4]
                        inv = pb[:].rearrange("p (a r k) -> p r k a", a=4, r=2)
                        self.cp("act" if ev % 2 == 0 else "dve", outv, inv, r=[rpb], w=[rxT[g]])
                        ev += 1
            for g in range(4):
                gg = hf * 4 + g
                b = g % 2
                for t in range(NT):
                    pb, rpb = self.bank()
                    self.mm(pb[:], AB[:, 0, gg, :], xT[:, g, 0, t * TS:(t + 1) * TS], True, False, r=[rAB, rxT[g]], w=[rpb])
                    self.mm(pb[:], AB[:, 1, gg, :], xT[:, g, 1, t * TS:(t + 1) * TS], False, True, r=[rAB, rxT[g]], w=[rpb])
                    self.cp("act" if ev % 2 == 0 else "dve", ot[b][:, t * TS:(t + 1) * TS], pb[:], r=[rpb], w=[rot[b]])
                    ev += 1
                self.dma("sp", self.O[gg * 128:(gg + 1) * 128, :], ot[b], r=[rot[b]])
        P.barrier()

    def post_layer(self, li, last, final):
        P = self.P
        self.areset()
        self.bank_i = 0
        self.bank_list = list(range(8))
        wname = ["fn_w_out", "na_w_out", "mla_w_out", "hg_w_out"][li]
        wo, rwo = self.load_w(wname, 1024, 1024)
        wg, rwg = self.load_w("ple_gate_w", 1024, 1024, sub=li)
        wp, rwp = self.load_w("ple_w", 256, 1024, sub=li)
        A = lambda cols, dt: [self.alloc(cols, dt) for _ in range(2)]
        R = lambda: [P.res() for _ in range(2)]
        v8 = lambda t_: t_.rearrange("p (c s) -> p c s", c=8)
        ot = [v8(x) for x in A(8 * TS, BF16)]; rot = R()
        sz = [v8(x) for x in A(8 * TS, BF16)]; rsz = R()
        xt = [v8(x) for x in A(8 * TS, F32)]; rx = R()
        pt = [x.rearrange("p (c s) -> p c s", c=2) for x in A(2 * TS, F32)]; rpt = R()
        ptb = [x.rearrange("p (c s) -> p c s", c=2) for x in A(2 * TS, BF16)]; rptb = R()
        g = [v8(x) for x in A(8 * TS, BF16)]; rg = R()
        x1b = [v8(x) for x in A(8 * TS, BF16)]; rx1b = R()
        sg = A(TS, F32); rsg = R()
        if final:
            sq1 = v8(self.alloc(8 * TS, BF16)); rsq1 = P.res()
            self.sq = [sq1, sq1]; self.rsq = [rsq1, rsq1]
            self.rstd = A(TS, F32); self.rrstd = R()
        NX = 4 if final else 3
        tiles = [(sq, t) for sq in range(self.nseq) for t in range(NT)]
        fm = lambda ap: ap.rearrange("(c p) s -> p c s", p=128)

        xt3 = xt + [v8(self.alloc(8 * TS, F32)) for _ in range(NX - 2)]
        rx3 = rx + [P.res() for _ in range(NX - 2)]
        ry3 = [P.res() for _ in range(NX)]

        def loads(i):
            sq, t = tiles[i]
            b = i % 2
            b3 = i % NX
            sl = slice(t * TS, (t + 1) * TS)
            xv = fm(self.xsrc_of(li, sq))
            self.dma("sp", ot[b], fm(self.SA["O"][sq])[:, :, sl], w=[rot[b]])
            self.dma("sp", sz[b], fm(self.SA["SZ"][sq])[:, :, sl], w=[rsz[b]])
            self.dma("sp", xt3[b3], xv[:, :, sl], w=[rx3[b3]])
            self.dma("sp", pt[b], fm(self.pT[li, sq])[:, :, sl], w=[rpt[b]])

        def stage_a(i):
            sq, t = tiles[i]
            b = i % 2
            b3 = i % NX
            self.tt("dve", g[b], ot[b], sz[b], ALU.mult, r=[rot[b], rsz[b]], w=[rg[b]])
            self.cp("pool", ptb[b], pt[b], r=[rpt[b]], w=[rptb[b]])
            for j in range(8):
                pb, rpb = self.bank()
                for k in range(8):
                    self.mm(pb[:], wo[:, k, j * 128:(j + 1) * 128], g[b][:, k, :], k == 0, k == 7, r=[rwo, rg[b]], w=[rpb])
                self.tt("dve", xt3[b3][:, j, :], xt3[b3][:, j, :], pb[:], ALU.add, r=[rpb, rx3[b3]], w=[rx3[b3]])
                self.cp("act", x1b[b][:, j, :], xt3[b3][:, j, :], r=[rx3[b3]], w=[rx1b[b]])

        def stage_b(i):
            sq, t = tiles[i]
            b = i % 2
            b3 = i % NX
            sl = slice(t * TS, (t + 1) * TS)
            xdv = fm(self.yT[sq] if last else self.X[sq])
            for j in range(8):
                pg, rpg = self.bank()
                for k in range(8):
                    self.mm(pg[:], wg[:, k, j * 128:(j + 1) * 128], x1b[b][:, k, :], k == 0, k == 7, r=[rwg, rx1b[b]], w=[rpg])
                pp, rpp = self.bank()
                for k in range(2):
                    self.mm(pp[:], wp[:, k, j * 128:(j + 1) * 128], ptb[b][:, k, :], k == 0, k == 1, r=[rwp, rptb[b]], w=[rpp])
                self.act(sg[b], pg[:], AF.Sigmoid, r=[rpg], w=[rsg[b]])
                self.tt("dve", sg[b], sg[b], pp[:], ALU.mult, r=[rsg[b], rpp], w=[rsg[b]])
                self.tt("pool", xt3[b3][:, j, :], xt3[b3][:, j, :], sg[b], ALU.add, r=[rsg[b], rx3[b3]], w=[rx3[b3]])
            if final:
                self.rms_to(xt3[b3], rx3[b3], 8, 1.0 / D, lambda c: self.fg[:, c:c + 1], xt3[b3], ry3[b3], b)
            self.dma("sp", xdv[:, 0:4, sl], xt3[b3][:, 0:4, :], r=[rx3[b3], ry3[b3]])
            self.dma("sp", xdv[:, 4:8, sl], xt3[b3][:, 4:8, :], r=[rx3[b3], ry3[b3]])

        loads(0)
        loads(1)
        stage_a(0)
        for i in range(len(tiles)):
            if i + 1 < len(tiles):
                stage_a(i + 1)
            if i + 2 < len(tiles):
                loads(i + 2)
            stage_b(i)
        P.barrier()

    def pre1(self):
        P = self.P
        self.pre_setup()
        self.bank_list = list(range(8))
        w, rw = self.load_w("na_w_in", 1024, 4096)
        self.alloc_szt()
        qk = [self.alloc(8 * TS, BF16).rearrange("p (c s) -> p c s", c=8) for _ in range(2)]
        rqk = [P.res() for _ in range(2)]
        vt = [self.alloc(4 * 1056, BF16).rearrange("p (a h d) -> p a h d", a=4, h=32) for _ in range(2)]
        rvt = [P.res() for _ in range(2)]
        for b in range(2):
            self.memset("pool", vt[b][:, :, :, 32:33], 1.0, w=[rvt[b]])
        ev = 0
        nq = 0
        for i, sq, t, hT, rh in self.pre_iter(1):
            b = i % 2
            sl = slice(t * TS, (t + 1) * TS)
            vav = self.VA.rearrange("(n p) f -> p n f", p=128)
            for which, dst in ((0, self.QT), (1, self.KT)):
                bq = nq % 2
                nq += 1
                for j in range(8):
                    pb, rpb = self.proj_fm(w, rw, which * 1024 + j * 128, hT, rh)
                    self.cp("act" if ev % 2 == 0 else "dve", qk[bq][:, j, :], pb[:], r=[rpb], w=[rqk[bq]])
                    ev += 1
                self.dma("sp", dst.rearrange("(c p) s -> p c s", p=128)[:, :, sl], qk[bq], r=[rqk[bq]])
            for sub in range(4):
                for ct in range(2):
                    pb, rpb = self.proj_tm(w, rw, 2048 + ct * 512, hT, rh, sub)
                    self.cp("act" if ev % 2 == 0 else "dve", vt[b][:, sub, 16 * ct:16 * ct + 16, 0:32],
                            pb[:].rearrange("p (h d) -> p h d", d=32), r=[rpb], w=[rvt[b]])
                    ev += 1
            self.dma("pool", vav[:, 4 * t:4 * t + 4, :], vt[b].rearrange("p a h d -> p a (h d)"), r=[rvt[b]])
            self.z_tile(w, rw, 3072, hT, rh, t, b)
        P.barrier()

    def mid1(self, sq):
        P = self.P
        rc = self.rc
        self.areset()
        self.bank_i = 0
        self.bank_list = [0, 1, 2, 3, 4]
        obanks = [5, 6]
        tbank = 7
        bts = self.alloc(32 * 14 * 64, BF16).rearrange("p (h s q) -> p h s q", h=32, s=14)
        rbt = P.res()
        self.dma("sp", bts[:, 0:16], self.BT.rearrange("p (h x) -> p h x", h=32)[:, 0:16, :].rearrange("p h (s q) -> p h s q", s=14), w=[rbt])
        self.dma("pool", bts[:, 16:32], self.BT.rearrange("p (h x) -> p h x", h=32)[:, 16:32, :].rearrange("p h (s q) -> p h s q", s=14), w=[rbt])
        ktb = [self.alloc(8 * 1024, BF16).rearrange("p (c s) -> p c s", c=8) for _ in range(2)]
        rkt = [P.res() for _ in range(2)]
        vbe = self.alloc(8 * 1056, BF16).rearrange("p (m h d) -> p m h d", m=8, h=32)
        vbo = self.alloc(7 * 1056, BF16).rearrange("p (m h d) -> p m h d", m=7, h=32)
        rvb = [P.res(), P.res()]
        qraw = [self.alloc(8 * TS, BF16).rearrange("p (c s) -> p c s", c=8) for _ in range(2)]
        rqr = [P.res() for _ in range(2)]
        qm = self.alloc(8 * 4 * TS, BF16).rearrange("p (c m s) -> p c m s", c=8, m=4)
        rqm = P.res()
        NPT = 4
        ptf = [self.alloc(1152, BF16) for _ in range(NPT)]
        rpt = [P.res() for _ in range(NPT)]
        for i in range(NPT):
            self.memset("pool", ptf[i], 0.0, w=[rpt[i]])
        ptw = [x[:, 0:1152].rearrange("p (r x) -> p r x", x=576)[:, :, 0:512].rearrange("p r (k y) -> p r k y", y=128)[:, :, :, 0:64] for x in ptf]
        ptr_ = [x[:, 0:1024].rearrange("p (r k y) -> p r k y", r=2, k=4) for x in ptf]
        osb = [self.alloc(1024, BF16) for _ in range(2)]
        rosb = [P.res() for _ in range(2)]
        otb = [self.alloc(8 * TS, BF16).rearrange("p (c s) -> p c s", c=8) for _ in range(2)]
        rotb = [P.res() for _ in range(2)]
        rc8 = [self.alloc(8, F32) for _ in range(2)]
        rrc8 = [P.res() for _ in range(2)]
        ktv = self.KT.rearrange("(c p) s -> p c s", p=128)
        qtv = self.QT.rearrange("(c p) s -> p c s", p=128)
        ov = self.O.rearrange("(c p) s -> p c s", p=128)
        ev = [0]
        units = [(t, pr, hg, hl) for t in range(NT) for pr in range(4) for hg in range(4) for hl in range(8)]
        state = {}

        def band_setup(t):
            b = t % 2
            kw0 = int(np.clip(8 * t - 4, 0, 48))
            self.dma("sp", ktb[b], ktv[:, :, 64 * kw0:64 * kw0 + 1024], w=[rkt[b]])
            self.dma("pool", qraw[b], qtv[:, :, t * TS:(t + 1) * TS], w=[rqr[b]])
            for j in range(8):
                for hm in range(4):
                    e3 = ev[0] % 3
                    if e3 == 2:
                        self.act(qm[:, j, hm, :], qraw[b][:, j, :], AF.Copy, r=[rqr[b], rc], w=[rqm], scale=self.cf(C_MC + hm, 1))
                    else:
                        self.ts("dve" if e3 == 0 else "pool", qm[:, j, hm, :], qraw[b][:, j, :], self.cf(C_MC + hm, 1), 0.0,
                                ALU.mult, ALU.add, r=[rqr[b], rc], w=[rqm])
                    ev[0] += 1

        def v_setup(t, hf):
            kw0 = int(np.clip(8 * t - 4, 0, 48))
            hs_ = slice(16 * hf, 16 * hf + 16)
            cs_ = slice(528 * hf, 528 * hf + 528)
            self.dma("pool", vbe[:, :, hs_, :].rearrange("p m h d -> p m (h d)"),
                     self.VA[64 * kw0:64 * kw0 + 1024, cs_].rearrange("(m p) f -> p m f", p=128), w=[rvb[hf]])
            self.dma("sp", vbo[:, :, hs_, :].rearrange("p m h d -> p m (h d)"),
                     self.VA[64 * (kw0 + 1):64 * (kw0 + 1) + 896, cs_].rearrange("(m p) f -> p m f", p=128), w=[rvb[hf]])

        def geom(u):
            t, pr, hg, hl = u
            kw0 = int(np.clip(8 * t - 4, 0, 48))
            rows = (8 * t + 2 * pr, 8 * t + 2 * pr + 1)
            return kw0, rows

        def S_(ui):
            u = units[ui]
            t, pr, hg, hl = u
            b = t % 2
            kw0, rows = geom(u)
            h = hg * 8 + hl
            j, hm = h // 4, h % 4
            sb_, rsb = self.bank()
            sv = sb_[:].rearrange("p (r k q) -> p r k q", r=2, k=4)
            for ri, r_ in enumerate(rows):
                rs = int(np.clip(r_ - 4, 0, 56))
                ro0 = rs - r_ + 7
                self.mm(sv[:, ri, :, :], self.cb(C_ID), bts[:, h, ro0:ro0 + 7:2, :], True, False, r=[rbt, rc], w=[rsb])
                for kc in range(4):
                    koff = 64 * (rs + 2 * kc - kw0)
                    self.mm(sv[:, ri, kc, :], ktb[b][:, j, koff:koff + 128], qm[:, j, hm, (r_ - 8 * t) * 64:(r_ - 8 * t) * 64 + 64],
                            False, True, r=[rkt[b], rqm], w=[rsb])
            state[ui] = (sv, rsb)

        def PV_(ui):
            u = units[ui]
            t, pr, hg, hl = u
            b = t % 2
            kw0, rows = geom(u)
            h = hg * 8 + hl
            sv, rsb = state.pop(ui)
            pi = ui % NPT
            self.act(ptw[pi], sv, AF.Exp, r=[rsb], w=[rpt[pi]])
            gi = ui // 8
            obi = obanks[gi % 2]
            ob, rob = self.ps[obi], self.psr[obi]
            obv = ob[:, 0:264].rearrange("p (h d) -> p h d", h=8)
            for ri, r_ in enumerate(rows):
                rs = int(np.clip(r_ - 4, 0, 56))
                dlt = rs - kw0
                for kc in range(4):
                    if dlt % 2 == 0:
                        vsrc = vbe[:, dlt // 2 + kc, h, :]
                    else:
                        vsrc = vbo[:, (dlt - 1) // 2 + kc, h, :]
                    self.mm(obv[:, hl, :], ptr_[pi][:, ri, kc, :], vsrc, ri == 0 and kc == 0, ri == 1 and kc == 3,
                            r=[rpt[pi], rvb[h // 16]], w=[rob])
            ob_ = (ui // 32) % 2
            if hl == 7:
                self.recip(rc8[hg % 2], obv[:, :, 32], r=[rob], w=[rrc8[hg % 2]])
                self.tt("dve", osb[ob_][:, hg * 256:(hg + 1) * 256].rearrange("p (h d) -> p h d", h=8), obv[:, :, 0:32],
                        rc8[hg % 2].unsqueeze(2).to_broadcast([128, 8, 32]), ALU.mult, r=[rob, rrc8[hg % 2]], w=[rosb[ob_]])
                if hg == 3:
                    tb, rtb = self.ps[tbank], self.psr[tbank]
                    tbv = tb[:].bitcast(BF16)
                    for jj in range(8):
                        self.tr(tbv[:, jj * 128:(jj + 1) * 128], osb[ob_][:, jj * 128:(jj + 1) * 128], self.cb(C_ID), r=[rosb[ob_], rc], w=[rtb])
                    self.cp("dve", otb[b][:, :, pr * 128:(pr + 1) * 128], tbv.rearrange("p (c s) -> p c s", c=8), r=[rtb], w=[rotb[b]])
                    if pr == 3:
                        self.dma("sp", ov[:, :, t * TS:(t + 1) * TS], otb[b], r=[rotb[b]])

        SK = 3
        N_ = len(units)
        band_setup(0)
        v_setup(0, 0)
        v_setup(0, 1)
        for k in range(SK):
            S_(k)
        for ui in range(N_):
            nxt = ui + SK
            if nxt < N_:
                if units[nxt][0] != units[nxt - 1][0]:
                    band_setup(units[nxt][0])
                S_(nxt)
            PV_(ui)
            t_, pr_, hg_, hl_ = units[ui]
            if t_ + 1 < NT and pr_ == 3 and hl_ == 7:
                if hg_ == 1:
                    v_setup(t_ + 1, 0)
                elif hg_ == 3:
                    v_setup(t_ + 1, 1)
        self.bank_list = list(range(8))
        P.barrier()


    def pre2(self):
        P = self.P
        rc = self.rc
        self.pre_setup()
        self.bank_list = list(range(8))
        w, rw = self.load_w("mla_w_in", 1024, 1728)
        wq, rwq = self.load_w("mla_w_uq", 384, 1536)
        wkv, rwkv = self.load_w("mla_w_ukv", 256, 2048)
        self.alloc_szt()
        c3 = lambda x, n: x.rearrange("p (c s) -> p c s", c=n)
        cq = c3(self.alloc(3 * TS, F32), 3); rcq = P.res()
        ckv = c3(self.alloc(2 * TS, F32), 2); rckv = P.res()
        cqn = c3(self.alloc(3 * TS, BF16), 3); rcqn = P.res()
        ckvn = c3(self.alloc(2 * TS, BF16), 2); rckvn = P.res()
        rp = [c3(self.alloc(4 * TS, F32), 4) for _ in range(2)]; rrp = [P.res() for _ in range(2)]
        qnt = c3(self.alloc(8 * TS, BF16), 8); rqnt = P.res()
        qpt = c3(self.alloc(8 * TS, BF16), 8); rqpt = P.res()
        knt = c3(self.alloc(8 * TS, BF16), 8); rknt = P.res()
        kpt = self.alloc(TS, BF16); rkpt = P.res()
        vt = self.alloc(4 * 1024, BF16).rearrange("p (a f) -> p a f", a=4); rvt = P.res()
        t1 = self.alloc(TS, F32); rt1 = P.res()
        t2 = self.alloc(TS, F32); rt2 = P.res()
        wkv4 = wkv.rearrange("p c (h t d) -> p c h t d", h=8, t=2)
        ev = [0]

        def evac(out, in_, r, w_):
            self.cp("act" if ev[0] % 2 == 0 else "dve", out, in_, r=r, w=w_)
            ev[0] += 1

        def swapped(wt, rwt_, nk, c0, rhs, rrhs):
            pb, rpb = self.bank()
            for k in range(nk):
                self.mm(pb[0:32, :], wt[:, k, c0 + 32:c0 + 64], rhs[:, k, :], k == 0, k == nk - 1, r=[rwt_, rrhs], w=[rpb])
            for k in range(nk):
                self.mm(pb[32:64, :], wt[:, k, c0:c0 + 32], rhs[:, k, :], k == 0, k == nk - 1, r=[rwt_, rrhs], w=[rpb])
            return pb, rpb

        def rope(pa, rpa, pbs, rpbs, ct, st_, rtab, out):
            self.tt("dve", t1[0:64, :], pa[0:64, :], ct, ALU.mult, r=[rpa, rtab], w=[rt1])
            self.tt("dve", t2[0:64, :], pbs[0:64, :], st_, ALU.mult, r=[rpbs, rtab], w=[rt2])
            return t1, t2

        for i, sq, t, hT, rh in self.pre_iter(2):
            b = i % 2
            sl = slice(t * TS, (t + 1) * TS)
            qnv = self.QN.rearrange("(h p) s -> p h s", p=128)
            knv = self.KN.rearrange("(h p) s -> p h s", p=128)
            qpv = self.QP.rearrange("h p s -> p h s")
            v2v = self.V2.rearrange("(n p) f -> p n f", p=128)
            self.dma("pool", rp[b][0:64], self.rope_d[:, :, sl], w=[rrp[b]])
            for c in range(3):
                pb, rpb = self.proj_fm(w, rw, c * 128, hT, rh)
                evac(cq[:, c, :], pb[:], [rpb], [rcq])
            for c in range(2):
                pb, rpb = self.proj_fm(w, rw, 384 + c * 128, hT, rh)
                evac(ckv[:, c, :], pb[:], [rpb], [rckv])
            pa, rpa = self.proj_fm(w, rw, 640, hT, rh, m=64)
            pbs, rpbs = swapped(w, rw, 8, 640, hT, rh)
            self.tt("dve", t1[0:64, :], pa[0:64, :], rp[b][0:64, 2, :], ALU.mult, r=[rpa, rrp[b]], w=[rt1])
            self.tt("dve", t2[0:64, :], pbs[0:64, :], rp[b][0:64, 3, :], ALU.mult, r=[rpbs, rrp[b]], w=[rt2])
            self.tt("pool", kpt[0:64, :], t1[0:64, :], t2[0:64, :], ALU.add, r=[rt1, rt2], w=[rkpt])
            self.dma("sp", self.KP[:, sl], kpt[0:64, :], r=[rkpt])
            self.z_tile(w, rw, 704, hT, rh, t, b)
            self.rms_to(cq, rcq, 3, 1.0 / 384, lambda c: self.gq[:, c:c + 1], cqn, rcqn, b)
            self.rms_to(ckv, rckv, 2, 1.0 / 256, lambda c: self.gkv[:, c:c + 1], ckvn, rckvn, b)
            for h in range(8):
                pb, rpb = self.bank()
                for c in range(3):
                    self.mm(pb[:], wq[:, c, h * 192:h * 192 + 128], cqn[:, c, :], c == 0, c == 2, r=[rwq, rcqn], w=[rpb])
                self.act(qnt[:, h, :], pb[:], AF.Copy, r=[rpb], w=[rqnt], scale=192.0 ** -0.5)
                pa, rpa = self.bank()
                for c in range(3):
                    self.mm(pa[0:64, :], wq[:, c, h * 192 + 128:h * 192 + 192], cqn[:, c, :], c == 0, c == 2, r=[rwq, rcqn], w=[rpa])
                pbs, rpbs = swapped(wq, rwq, 3, h * 192 + 128, cqn, rcqn)
                self.tt("dve", t1[0:64, :], pa[0:64, :], rp[b][0:64, 0, :], ALU.mult, r=[rpa, rrp[b]], w=[rt1])
                self.tt("dve", t2[0:64, :], pbs[0:64, :], rp[b][0:64, 1, :], ALU.mult, r=[rpbs, rrp[b]], w=[rt2])
                self.tt("pool", qpt[0:64, h, :], t1[0:64, :], t2[0:64, :], ALU.add, r=[rt1, rt2], w=[rqpt])
            for h in range(8):
                pb, rpb = self.bank()
                for c in range(2):
                    self.mm(pb[:], wkv[:, c, h * 256:h * 256 + 128], ckvn[:, c, :], c == 0, c == 1, r=[rwkv, rckvn], w=[rpb])
                evac(knt[:, h, :], pb[:], [rpb], [rknt])
            for sub in range(4):
                for hh in range(2):
                    pb, rpb = self.bank()
                    for c in range(2):
                        self.mm(pb[:].rearrange("p (h d) -> p h d", h=4), ckvn[:, c, sub * 128:(sub + 1) * 128],
                                wkv4[:, c, 4 * hh:4 * hh + 4, 1, :], c == 0, c == 1, r=[rwkv, rckvn], w=[rpb])
                    evac(vt[:, sub, hh * 512:(hh + 1) * 512], pb[:], [rpb], [rvt])
            self.dma("sp", qnv[:, :, sl], qnt, r=[rqnt])
            self.dma("pool", qpv[:, :, sl], qpt[0:64], r=[rqpt])
            self.dma("sp", knv[:, :, sl], knt, r=[rknt])
            self.dma("pool", v2v[:, 4 * t:4 * t + 4, :], vt, r=[rvt])
        P.barrier()

    def mid2(self, sq):
        P = self.P
        rc = self.rc
        self.areset()
        self.bank_i = 0
        self.bank_list = [0, 1, 2, 3]
        kpb = self.alloc(S, BF16); rkp = P.res()
        self.dma("sp", kpb[0:64, :], self.KP[:, :], w=[rkp])
        self.dma("pool", kpb[64:128, :], self.KP[:, :], w=[rkp])
        acc = [[self.alloc(TS, F32) for _ in range(2)] for _ in range(2)]
        racc = [[P.res() for _ in range(2)] for _ in range(2)]
        kng = self.alloc(4 * S, BF16).rearrange("p (h s) -> p h s", h=4); rkn = P.res()
        vg = self.alloc(32 * 512, BF16).rearrange("p (n f) -> p n f", n=32); rvg = P.res()
        qn = [self.alloc(4 * TS, BF16).rearrange("p (h s) -> p h s", h=4) for _ in range(2)]; rqn = [P.res() for _ in range(2)]
        qp = [self.alloc(4 * TS, BF16).rearrange("p (h s) -> p h s", h=4) for _ in range(2)]; rqp = [P.res() for _ in range(2)]
        NPT = 12
        pt = [self.alloc(TS, BF16) for _ in range(NPT)]; rpt = [P.res() for _ in range(NPT)]
        rec = [self.alloc(TS, F32) for _ in range(2)]; rrec = [P.res() for _ in range(2)]
        ot = [self.alloc(TS, BF16) for _ in range(2)]; rot = [P.res() for _ in range(2)]
        qnv = self.QN.rearrange("(h p) s -> p h s", p=128)
        knv = self.KN.rearrange("(h p) s -> p h s", p=128)
        qpv = self.QP.rearrange("h p s -> p h s")
        v2v = self.V2.rearrange("(n p) f -> p n f", p=128)
        ipt = 0
        ih = 0
        sbanks = {}

        def load_q(g, t):
            b = (g * NT + t) % 2
            sl = slice(t * TS, (t + 1) * TS)
            self.dma("sp", qn[b], qnv[:, 4 * g:4 * g + 4, sl], w=[rqn[b]])
            self.dma("pool", qp[b][0:64], qpv[:, 4 * g:4 * g + 4, sl], w=[rqp[b]])
            self.dma("sp", qp[b][64:128], qpv[:, 4 * g:4 * g + 4, sl], w=[rqp[b]])

        def score2(uid, b, hl, m0):
            bs = []
            for m in (m0, m0 + 1):
                sb_, rsb = self.bank()
                self.mm(sb_[:], kng[:, hl, m * 128:(m + 1) * 128], qn[b][:, hl, :], True, False, r=[rkn, rqn[b]], w=[rsb])
                bs.append((sb_, rsb))
            for k_, m in enumerate((m0, m0 + 1)):
                sb_, rsb = bs[k_]
                p0 = 64 * k_
                self.mm(sb_[:], kpb[p0:p0 + 64, m * 128:(m + 1) * 128], qp[b][p0:p0 + 64, hl, :], False, True, r=[rkp, rqp[b]], w=[rsb])
                sbanks[(uid, m)] = (sb_, rsb)

        for g in range(2):
            self.dma("sp", kng[:, 0:2], knv[:, 4 * g:4 * g + 2, :], w=[rkn])
            self.dma("pool", kng[:, 2:4], knv[:, 4 * g + 2:4 * g + 4, :], w=[rkn])
            self.dma("sp", vg[:, 0:16], v2v[:, 0:16, 512 * g:512 * g + 512], w=[rvg])
            self.dma("pool", vg[:, 16:32], v2v[:, 16:32, 512 * g:512 * g + 512], w=[rvg])
            load_q(g, 0)
            for t in range(NT):
                if t + 1 < NT:
                    load_q(g, t + 1)
                b = (g * NT + t) % 2
                sl = slice(t * TS, (t + 1) * TS)
                for hl in range(4):
                    h = 4 * g + hl
                    uid = (g, t, hl)
                    hb = ih % 2
                    ih += 1
                    OT, rOT = self.ps[4 + hb], self.psr[4 + hb]
                    DS, rDS = self.ps[6 + hb], self.psr[6 + hb]
                    if hl < 3:
                        nxt = ((g, t, hl + 1), b, hl + 1)
                    elif t + 1 < NT:
                        nxt = ((g, t + 1, 0), 1 - b, 0)
                    else:
                        nxt = None
                    if (uid, 0) not in sbanks:
                        score2(uid, b, hl, 0)
                    for m in range(32):
                        if m % 2 == 0:
                            if m + 2 < 32:
                                score2(uid, b, hl, m + 2)
                            elif nxt is not None:
                                score2(nxt[0], nxt[1], nxt[2], 0)
                        sb_, rsb = sbanks.pop((uid, m))
                        pi = ipt % NPT
                        ipt += 1
                        self.act(pt[pi], sb_[:], AF.Exp, r=[rsb], w=[rpt[pi]])
                        self.mm(OT[:], vg[:, m, hl * 128:(hl + 1) * 128], pt[pi], m == 0, m == 31, r=[rvg, rpt[pi]], w=[rOT])
                        e_ = m % 2
                        eng = "pool" if e_ == 0 else "dve"
                        if m < 2:
                            self.cp(eng, acc[hb][e_], pt[pi], r=[rpt[pi]], w=[racc[hb][e_]])
                        else:
                            self.tt(eng, acc[hb][e_], acc[hb][e_], pt[pi], ALU.add, r=[rpt[pi], racc[hb][e_]], w=[racc[hb][e_]])
                    self.mm(DS[:], self.cf(C_ONES), acc[hb][0], True, False, r=[rc, racc[hb][0]], w=[rDS])
                    self.mm(DS[:], self.cf(C_ONES), acc[hb][1], False, True, r=[rc, racc[hb][1]], w=[rDS])
                    self.recip(rec[hb], DS[:], r=[rDS], w=[rrec[hb]])
                    self.tt("dve", ot[hb], OT[:], rec[hb], ALU.mult, r=[rOT, rrec[hb]], w=[rot[hb]])
                    self.dma("sp", self.O[h * 128:(h + 1) * 128, sl], ot[hb], r=[rot[hb]])
        self.bank_list = list(range(8))
        P.barrier()

    def pre3(self):
        P = self.P
        rc = self.rc
        self.pre_setup()
        self.bank_list = list(range(8))
        w, rw = self.load_w("hg_w_in", 1024, 5120)
        self.alloc_szt()
        qst = self.alloc(8 * TS, BF16).rearrange("p (c s) -> p c s", c=8); rqst = P.res()
        lf = self.alloc(8 * TS, F32).rearrange("p (c s) -> p c s", c=8); rlf = P.res()
        vt = self.alloc(4 * 1024, BF16).rearrange("p (a f) -> p a f", a=4); rvt = P.res()
        ev = 0
        for i, sq, t, hT, rh in self.pre_iter(3):
            b = i % 2
            sl = slice(t * TS, (t + 1) * TS)
            qsv = self.QS.rearrange("(c p) s -> p c s", p=128)
            viv = self.VI.rearrange("(n p) f -> p n f", p=128)
            for j in range(8):
                pb, rpb = self.proj_fm(w, rw, j * 128, hT, rh)
                self.act(qst[:, j, :], pb[:], AF.Silu, r=[rpb], w=[rqst])
            self.dma("sp", qsv[:, :, sl], qst, r=[rqst])
            for d in range(2):
                for j in range(8):
                    pb, rpb = self.proj_fm(w, rw, 1024 * (1 + d) + j * 128, hT, rh)
                    self.act(lf[:, j, :], pb[:], AF.Sigmoid, r=[rpb], w=[rlf])
                    self.ts("dve", lf[:, j, :], lf[:, j, :], self.oml[:, d, j:j + 1], self.lb[:, d, j:j + 1], ALU.mult, ALU.add,
                            r=[rlf, rc], w=[rlf])
                self.act(lf, lf, AF.Ln, r=[rlf], w=[rlf])
                lfv = self.LF[d].rearrange("(c p) s -> p c s", p=128)
                self.dma("sp", lfv[:, 0:4, sl], lf[:, 0:4, :], r=[rlf])
                self.dma("pool", lfv[:, 4:8, sl], lf[:, 4:8, :], r=[rlf])
            for sub in range(4):
                for ct in range(2):
                    pb, rpb = self.proj_tm(w, rw, 3072 + ct * 512, hT, rh, sub)
                    self.cp("act" if ev % 2 == 0 else "dve", vt[:, sub, ct * 512:(ct + 1) * 512], pb[:], r=[rpb], w=[rvt])
                    ev += 1
            self.dma("pool", viv[:, 4 * t:4 * t + 4, :], vt, r=[rvt])
            self.z_tile(w, rw, 4096, hT, rh, t, b)
        P.barrier()

    def mid3(self, sq):
        P = self.P
        rc = self.rc
        self.areset()
        self.bank_i = 0
        self.bank_list = [0, 1, 2, 3, 4, 5]
        L = 128
        NCH = 32
        rmask = self.alloc(1024, BF16); rrm = P.res()
        self.memset("pool", rmask, 1.0, w=[rrm])
        self.memset("pool", rmask.rearrange("p (n l) -> p n l", l=L)[:, :, 0:1], 0.0, w=[rrm])
        qs = self.alloc(S, BF16); rqs = P.res()
        lf = self.alloc(S, F32); rlf = P.res()
        C = self.alloc(S, F32); rC = P.res()
        E = self.alloc(S, F32); rE = P.res()
        qtb = [[self.alloc(S, BF16) for _ in range(2)] for _ in range(2)]; rqtb = [[P.res() for _ in range(2)] for _ in range(2)]
        ktb_ = [[self.alloc(S, BF16) for _ in range(2)] for _ in range(2)]; rktb = [[P.res() for _ in range(2)] for _ in range(2)]
        eclb = [[self.alloc(NCH, F32) for _ in range(2)] for _ in range(2)]; reclb = [[P.res() for _ in range(2)] for _ in range(2)]
        vhb = [self.alloc(S, BF16).rearrange("p (n d) -> p n d", n=NCH) for _ in range(2)]; rvhb = [P.res() for _ in range(2)]
        ktm = [self.alloc(S, BF16).rearrange("p (n c) -> p n c", n=NCH) for _ in range(2)]; rktm = [P.res() for _ in range(2)]
        Sb = [self.alloc(S, BF16).rearrange("p (n d) -> p n d", n=NCH) for _ in range(2)]; rSb = [P.res() for _ in range(2)]
        Wst = [[self.alloc(128, F32) for _ in range(2)] for _ in range(2)]; rW = [[P.res() for _ in range(2)] for _ in range(2)]
        oT = self.alloc(S, BF16); roT = P.res()
        Am4 = [self.alloc(4 * 256, BF16).rearrange("p (k x) -> p k x", k=4) for _ in range(2)]; rAm4 = [P.res() for _ in range(2)]
        on4 = [self.alloc(512, BF16) for _ in range(2)]; ron4 = [P.res() for _ in range(2)]
        sq4 = [self.alloc(512, F32) for _ in range(2)]; rsq4 = [P.res() for _ in range(2)]
        ss = self.alloc(NCH, F32); rss = P.res()
        rs_ = self.alloc(NCH, F32); rrs = P.res()
        viv = self.VI.rearrange("(n p) f -> p n f", p=128)
        ch = lambda x, n: x[:, n * L:(n + 1) * L]

        def scan_op(dst, src, r, w):
            for q4 in range(4):
                sl_ = slice(q4 * 1024, (q4 + 1) * 1024)
                self.P.op("dve", lambda e, sl_=sl_: e.tensor_tensor_scan(out=dst[:, sl_], data0=rmask, data1=src[:, sl_], initial=0.0,
                                                                          op0=ALU.mult, op1=ALU.add), r=r, w=w)

        def phaseA(h, sl):
            hs = slice(h * 128, (h + 1) * 128)
            qt, kt, ecl = qtb[sl], ktb_[sl], eclb[sl]
            rqt, rkt, recl = rqtb[sl], rktb[sl], reclb[sl]
            T = []
            T.append(lambda: self.dma("sp", qs, self.QS[hs, :], w=[rqs]))
            T.append(lambda: self.dma("pool", vhb[sl], viv[:, :, hs], w=[rvhb[sl]]))
            for d in range(2):
                T.append(lambda d=d: self.dma("sp", lf, self.LF[d, hs, :], w=[rlf]))
                T.append(lambda: scan_op(C, lf, [rrm, rlf], [rC]))
                Cl = C.rearrange("p (n l) -> p n l", l=L)[:, :, L - 1]
                if d == 0:
                    T.append(lambda: self.act(E, C, AF.Exp, r=[rC], w=[rE]))
                    T.append(lambda: self.cp("dve", ecl[0], E.rearrange("p (n l) -> p n l", l=L)[:, :, L - 1], r=[rE], w=[recl[0]]))
                    T.append(lambda: self.act(C, C, AF.Exp, r=[rC], w=[rC], scale=-1.0))
                else:
                    T.append(lambda: self.act(ecl[1], Cl, AF.Exp, r=[rC], w=[recl[1]]))
                    T.append(lambda: self.tt("dve", C, C, lf, ALU.subtract, r=[rC, rlf], w=[rC]))
                    T.append(lambda: self.act(E, C, AF.Exp, r=[rC], w=[rE], scale=-1.0))
                    T.append(lambda: self.act(C, C, AF.Exp, r=[rC], w=[rC]))
                T.append(lambda d=d: self.tt("dve", qt[d], qs, E, ALU.mult, r=[rqs, rE], w=[rqt[d]]))
                T.append(lambda: self.act(E, lf, AF.Exp, r=[rlf], w=[rE]))
                T.append(lambda: self.ts("dve", E, E, -1.0, 1.0, ALU.mult, ALU.add, r=[rE], w=[rE]))
                T.append(lambda d=d: self.tt("dve", kt[d], E, C, ALU.mult, r=[rE, rC], w=[rkt[d]]))
            return T

        pend = phaseA(0, 0)
        for f in pend:
            f()
        for h in range(8):
            sl = h % 2
            hs = slice(h * 128, (h + 1) * 128)
            qt, kt, ecl, vh = qtb[sl], ktb_[sl], eclb[sl], vhb[sl]
            rqt, rkt, recl, rvh = rqtb[sl], rktb[sl], reclb[sl], rvhb[sl]
            pend = phaseA(h + 1, 1 - sl) if h + 1 < 8 else []

            def pop(n=1):
                for _ in range(n):
                    if pend:
                        pend.pop(0)()

            ev = 0
            for d in range(2):
                for n8 in range(4):
                    tb, rtb = self.bank()
                    tbv = tb[:].bitcast(BF16)
                    for k in range(8):
                        n = n8 * 8 + k
                        self.tr(tbv[:, k * 128:(k + 1) * 128], ch(kt[d], n), self.cb(C_ID), r=[rkt[d], rc], w=[rtb])
                    self.cp("act" if ev % 2 == 0 else "dve", ktm[d][:, n8 * 8:(n8 + 1) * 8, :].rearrange("p n c -> p (n c)"), tbv, r=[rtb], w=[rktm[d]])
                    ev += 1
            self.memset("pool", Sb[0][:, 0, :], 0.0, w=[rSb[0]])
            self.memset("pool", Sb[1][:, NCH - 1, :], 0.0, w=[rSb[1]])
            ub = [None, None]
            wi = [0, 0]
            for i in range(NCH):
                for d in range(2):
                    n = i if d == 0 else NCH - 1 - i
                    if i % 4 == 0:
                        ub[d] = self.bank()
                        for k in range(4):
                            nn = i + k if d == 0 else NCH - 1 - i - k
                            self.mm(ub[d][0][:, k * 128:(k + 1) * 128], ktm[d][:, nn, :], vh[:, nn, :], True, True, r=[rktm[d], rvh], w=[ub[d][1]])
                    U = ub[d][0][:, (i % 4) * 128:(i % 4 + 1) * 128]
                    rU = ub[d][1]
                    cur = wi[d]
                    if i == 0:
                        self.cp("dve", Wst[d][cur], U, r=[rU], w=[rW[d][cur]])
                    else:
                        ei = (i - 1) if d == 0 else n
                        esc = ecl[d][:, ei:ei + 1]
                        self.act(Sb[d][:, n, :], Wst[d][cur], AF.Copy, r=[rW[d][cur], recl[d]], w=[rSb[d]], scale=esc)
                        nxt = 1 - cur
                        self.stt(Wst[d][nxt], Wst[d][cur], esc, U, ALU.mult, ALU.add, r=[rW[d][cur], recl[d], rU], w=[rW[d][nxt]])
                        wi[d] = nxt
                if i % 4 == 3:
                    pop(1)
            mask2 = self.cf(C_MU, 256).unsqueeze(1).to_broadcast([128, 2, 256])
            for gq in range(NCH // 4):
                gi = gq % 2
                for half in range(2):
                    ab, rab = self.bank()
                    for k2 in range(2):
                        n = gq * 4 + half * 2 + k2
                        self.mm(ab[:, k2 * 256:k2 * 256 + 128], ch(kt[0], n), ch(qt[0], n), True, True, r=[rkt[0], rqt[0]], w=[rab])
                        self.mm(ab[:, k2 * 256 + 128:k2 * 256 + 256], ch(kt[1], n), ch(qt[1], n), True, True, r=[rkt[1], rqt[1]], w=[rab])
                    self.tt("dve", Am4[gi][:, half * 2:half * 2 + 2, :], ab[:].rearrange("p (k x) -> p k x", k=2), mask2, ALU.mult,
                            r=[rab, rc], w=[rAm4[gi]])
                ob, rob = self.bank()
                ov4 = ob[:].rearrange("p (k d) -> p k d", k=4)
                for k in range(4):
                    n = gq * 4 + k
                    o = ov4[:, k, :]
                    self.mm(o, Am4[gi][:, k, 0:128], vh[:, n, :], True, False, r=[rAm4[gi], rvh], w=[rob])
                    self.mm(o, ch(qt[0], n), Sb[0][:, n, :], False, False, r=[rqt[0], rSb[0]], w=[rob])
                    self.mm(o, Am4[gi][:, k, 128:256], vh[:, n, :], False, False, r=[rAm4[gi], rvh], w=[rob])
                    self.mm(o, ch(qt[1], n), Sb[1][:, n, :], False, True, r=[rqt[1], rSb[1]], w=[rob])
                self.act(sq4[gi], ob[:], AF.Square, r=[rob], w=[rsq4[gi]])
                ssg = ss[:, gq * 4:gq * 4 + 4]
                rsg = rs_[:, gq * 4:gq * 4 + 4]
                self.P.op("dve", lambda e, o_=ssg, i_=sq4[gi].rearrange("p (k d) -> p k d", k=4): e.tensor_reduce(out=o_, in_=i_, axis=AX.X, op=ALU.add),
                          r=[rsq4[gi]], w=[rss])
                self.act(rsg, ssg, AF.Sqrt, r=[rss, rc], w=[rrs], scale=1.0 / 128, bias=self.epsb[:, 0:1])
                self.recip(rsg, rsg, r=[rrs], w=[rrs])
                self.tt("dve", on4[gi].rearrange("p (k d) -> p k d", k=4), ov4, rsg.unsqueeze(2).to_broadcast([128, 4, 128]), ALU.mult,
                        r=[rob, rrs], w=[ron4[gi]])
                if gq % 2 == 0:
                    ti = 6 + (gq // 2) % 2
                    tb, rtb = self.ps[ti], self.psr[ti]
                    tbv = tb[:].bitcast(BF16)
                for k in range(4):
                    col = ((gq % 2) * 4 + k) * 128
                    self.tr(tbv[:, col:col + 128], on4[gi][:, k * 128:(k + 1) * 128], self.cb(C_ID), r=[ron4[gi], rc], w=[rtb])
                if gq % 2 == 1:
                    n0 = (gq - 1) * 4
                    self.act(oT[:, n0 * L:(n0 + 8) * L], tbv, AF.Copy, r=[rtb, rc], w=[roT], scale=self.gout[:, h:h + 1])
                pop(2)
            pop(100)
            self.dma("sp", self.O[hs, :], oT, r=[roT])
        self.bank_list = list(range(8))
        P.barrier()


_CACHE = {}


def _common_inputs(inp):
    cst, m1, rope = make_consts()
    d = {"cst": cst, "m1": m1, "rope": rope}
    for name, k, n in W_SPECS:
        d[name] = np.ascontiguousarray(np.asarray(inp[name], np.float32)[0])
    d["ple_w"] = np.ascontiguousarray(np.asarray(inp["ple_w"], np.float32))
    d["ple_gate_w"] = np.ascontiguousarray(np.asarray(inp["ple_gate_w"], np.float32))
    d["norm_g"] = np.ascontiguousarray(np.asarray(inp["norm_g"], np.float32))
    d["final_g"] = np.ascontiguousarray(np.asarray(inp["final_g"], np.float32))
    d["fn_w_mix"] = np.ascontiguousarray(np.asarray(inp["fn_w_mix"], np.float32)[0])
    d["rpb_flip"] = np.ascontiguousarray(np.asarray(inp["na_rpb"], np.float32)[0][:, :, ::-1])
    d["mla_g_q"] = np.ascontiguousarray(np.asarray(inp["mla_g_q"], np.float32)[0])
    d["mla_g_kv"] = np.ascontiguousarray(np.asarray(inp["mla_g_kv"], np.float32)[0])
    d["hg_lb_raw"] = np.ascontiguousarray(np.asarray(inp["hg_lb_raw"], np.float32))
    d["hg_g_out"] = np.ascontiguousarray(np.asarray(inp["hg_g_out"], np.float32)[0])
    return d


def run_seqs(inp, xs, ps, nseq, layers, final_norm, ncores, trace=False):
    key = (nseq, tuple(layers), final_norm)
    if key not in _CACHE:
        bld = Builder(nseq, layers, final_norm)
        nc = bld.build()
        _CACHE[key] = (nc, bld.stats)
    nc, stats = _CACHE[key]
    com = _common_inputs(inp)
    in_maps = []
    for c in range(ncores):
        m = dict(com)
        m["xT"] = np.ascontiguousarray(np.transpose(xs[c], (0, 2, 1)))
        m["pT"] = np.ascontiguousarray(np.transpose(ps[c], (0, 1, 3, 2)))
        in_maps.append(m)
    if trace:
        res = run_bass_kernel_spmd(nc, in_maps, core_ids=list(range(ncores)), trace=True)
        print('EXEC_TIME_NS', res.exec_time_ns)
    else:
        res = run_bass_kernel_spmd(nc, in_maps, core_ids=list(range(ncores)))
    return [np.transpose(r["yT"], (0, 2, 1)) for r in res.results]


def kernel(**inp):
    xp = np.asarray(inp["x_prompt"], np.float32)
    xsm = np.asarray(inp["x_sample"], np.float32)
    pp = np.asarray(inp["p_prompt"], np.float32)
    psm = np.asarray(inp["p_sample"], np.float32)
    xall = np.concatenate([xp, xsm], 0)
    pall = np.concatenate([pp, psm], 1)
    nseq = 3
    idx = [[c, c + 8, (c + 16) if c + 16 < 20 else c] for c in range(8)]
    xs = [xall[idx[c]] for c in range(8)]
    ps = [pall[:, idx[c]] for c in range(8)]
    outs = run_seqs(inp, xs, ps, nseq, [0, 1, 2, 3], True, 8)
    yall = np.zeros_like(xall)
    for c in range(8):
        for s_, gi in enumerate(idx[c]):
            if s_ < 2 or c + 16 < 20:
                yall[gi] = outs[c][s_]
    return (np.ascontiguousarray(yall[:4]), np.ascontiguousarray(yall[4:]))
```

```python
import math
import numpy as np
from contextlib import ExitStack
import concourse.bass as bass
import concourse.mybir as mybir
from concourse.bass_utils import run_bass_kernel_spmd

F32 = mybir.dt.float32
BF16 = mybir.dt.bfloat16
AF = mybir.ActivationFunctionType
ALU = mybir.AluOpType
AX = mybir.AxisListType

ENGS = ("pe", "act", "dve", "pool", "sp")
S = 4096
D = 1024
NT = 8
TS = 512
EPS = 1e-6
NEG = -30000.0


class Res:
    __slots__ = ("name", "w", "rs")

    def __init__(self, name):
        self.name = name
        self.w = None
        self.rs = []


class Op:
    __slots__ = ("eng", "fn", "deps", "dma", "ticket", "inc")

    def __init__(self, eng, fn, dma):
        self.eng = eng
        self.fn = fn
        self.deps = []
        self.dma = dma
        self.ticket = None
        self.inc = dma


class Prog:
    DMA_K = 8
    DMA_CAP = 1500
    CMP_CAP = 30000

    def __init__(self, nc, stack):
        self.nc = nc
        self.stack = stack
        self.ops = []
        self.eobj = {"pe": nc.tensor, "act": nc.scalar, "dve": nc.vector, "pool": nc.gpsimd, "sp": nc.sync}
        self.dma_hist = {e: [] for e in ENGS}
        self.last = {e: None for e in ENGS}
        self.nres = 0

    def res(self, name=None):
        self.nres += 1
        return Res(name or f"r{self.nres}")

    def op(self, eng, fn, r=(), w=(), dma=False):
        o = Op(eng, fn, dma)
        deps = {}
        for x in r:
            if x.w is not None:
                deps[id(x.w)] = (x.w, "raw")
        for x in w:
            if x.w is not None and id(x.w) not in deps:
                deps[id(x.w)] = (x.w, "waw")
            for rd in x.rs:
                if id(rd) not in deps:
                    deps[id(rd)] = (rd, "war")
        if dma:
            h = self.dma_hist[eng]
            if len(h) >= self.DMA_K:
                p = h[-self.DMA_K]
                deps.setdefault(id(p), (p, "rot"))
            h.append(o)
        for d, kind in deps.values():
            if d is o:
                continue
            if (not d.dma) and (not dma) and d.eng == eng:
                if eng == "pe" or kind != "raw":
                    continue
            o.deps.append(d)
            d.inc = True
        for x in w:
            x.w = o
            x.rs = []
        for x in r:
            x.rs.append(o)
        self.ops.append(o)
        if not dma:
            self.last[eng] = o
        return o

    def barrier(self):
        deps = []
        for e in ENGS:
            if self.last[e] is not None:
                deps.append(self.last[e])
            deps.extend(self.dma_hist[e][-self.DMA_K:])
        for e in ENGS:
            o = Op(e, None, False)
            for d in deps:
                if (not d.dma) and d.eng == e:
                    continue
                o.deps.append(d)
                d.inc = True
            self.ops.append(o)

    def emit(self):
        nc = self.nc
        cmp_sem = {}
        cmp_cnt = {}
        dma_sems = {}
        dma_idx = {e: 0 for e in ENGS}
        known = {e: {} for e in ENGS}
        self.nsem = 0

        def newsem(nm):
            self.nsem += 1
            return self.stack.enter_context(nc.semaphore(f"{nm}_{self.nsem}"))

        nwait = 0
        for o in self.ops:
            E = self.eobj[o.eng]
            kn = known[o.eng]
            for d in o.deps:
                sem, val = d.ticket
                key = id(sem)
                if kn.get(key, 0) >= val:
                    continue
                E.wait_ge(sem, val)
                nwait += 1
                kn[key] = val
            if o.fn is None:
                continue
            ins = o.fn(E)
            if o.dma:
                i = dma_idx[o.eng]
                dma_idx[o.eng] = i + 1
                st = i // (self.DMA_K * self.DMA_CAP)
                j = i % (self.DMA_K * self.DMA_CAP)
                keyq = (o.eng, st)
                if keyq not in dma_sems:
                    dma_sems[keyq] = [newsem(f"d{o.eng}") for _ in range(self.DMA_K)]
                sem = dma_sems[keyq][j % self.DMA_K]
                val = 16 * (j // self.DMA_K + 1)
                ins.then_inc(sem, 16)
                o.ticket = (sem, val)
            elif o.inc:
                c = cmp_cnt.get(o.eng, 0)
                if o.eng not in cmp_sem or c >= self.CMP_CAP:
                    cmp_sem[o.eng] = newsem(f"c{o.eng}")
                    c = 0
                c += 1
                cmp_cnt[o.eng] = c
                ins.then_inc(cmp_sem[o.eng], 1)
                o.ticket = (cmp_sem[o.eng], c)
            o.fn = None
        return dict(nops=len(self.ops), nwait=nwait, nsem=self.nsem)


C_ONES, C_ID, C_CC, C_SC, C_M3, C_MU, C_ML, C_MC = 0, 128, 256, 384, 512, 640, 768, 896
C_TOT = 900


def make_consts():
    c = np.zeros((128, C_TOT), np.float64)
    c[:, C_ONES:C_ONES + 128] = 1.0
    c[:, C_ID:C_ID + 128] = np.eye(128)
    a = np.arange(128)
    ang = 2 * np.pi * np.outer(a, a) / 128.0
    c[:, C_CC:C_CC + 128] = np.cos(ang) / math.sqrt(128.0)
    c[:, C_SC:C_SC + 128] = np.sin(ang) / math.sqrt(128.0)
    s0 = np.arange(64)
    k1 = np.arange(64)
    phi = 2 * np.pi * np.outer(s0, k1) / 64.0
    m3 = np.zeros((128, 128))
    m3[0:64, 0:64] = np.cos(phi)
    m3[64:128, 0:64] = np.sin(phi)
    m3[0:64, 64:128] = -np.sin(phi)
    m3[64:128, 64:128] = np.cos(phi)
    c[:, C_M3:C_M3 + 128] = m3 / 8.0
    c[:, C_MU:C_MU + 128] = (a[:, None] <= a[None, :])
    c[:, C_ML:C_ML + 128] = (a[:, None] >= a[None, :])
    for hm in range(4):
        c[:, C_MC + hm] = np.where(a // 32 == hm, 32.0 ** -0.5, 0.0)
    m1 = np.zeros((128, 32, 128))
    p = np.arange(128)
    s1 = (p % 64)[:, None, None]
    half = (p // 64)[:, None, None]
    j = np.arange(32)[None, :, None]
    k0 = np.arange(64)[None, None, :]
    s0v = 32 * half + j
    th = 2 * np.pi * (s1 * k0 / 64.0 + s0v * k0 / 4096.0)
    m1[:, :, 0:64] = np.cos(th) / 8.0
    m1[:, :, 64:128] = -np.sin(th) / 8.0
    half_d = 32
    inv = 10000.0 ** (-np.arange(half_d, dtype=np.float64) / half_d)
    inv = inv.astype(np.float32).astype(np.float64)
    pos = np.arange(S, dtype=np.float64)
    angf = (pos[None, :].astype(np.float32) * inv[:, None].astype(np.float32)).astype(np.float64)
    cs = np.cos(angf)
    sn = np.sin(angf)
    cosT = np.concatenate([cs, cs], 0)
    sinT = np.concatenate([-sn, sn], 0)
    sc = 192.0 ** -0.5
    rope = np.stack([cosT * sc, sinT * sc, cosT, sinT], 1)
    return c.astype(np.float32), m1.reshape(128, 32 * 128).astype(np.float32), rope.astype(np.float32)


def na_rows():
    kh = 8
    r = np.arange(64)
    rs = np.clip(r - kh // 2, 0, 64 - kh)
    return rs


W_SPECS = [
    ("fn_w_in", 1024, 2048), ("fn_w_out", 1024, 1024),
    ("na_w_in", 1024, 4096), ("na_w_out", 1024, 1024),
    ("mla_w_in", 1024, 1728), ("mla_w_uq", 384, 1536), ("mla_w_ukv", 256, 2048), ("mla_w_out", 1024, 1024),
    ("hg_w_in", 1024, 5120), ("hg_w_out", 1024, 1024),
]


class Builder:
    def __init__(self, nseq, layers, final_norm=True):
        self.nseq = nseq
        self.layers = layers
        self.final_norm = final_norm

    def dma(self, q, out, in_, r=(), w=()):
        if q == "pool" and getattr(self, "pool_busy", False):
            q = "sp"
        return self.P.op(q, lambda e: e.dma_start(out=out, in_=in_), r=r, w=w, dma=True)

    def dma_slow(self, q, out, in_, r=(), w=()):
        return self.P.op(q, lambda e: e.dma_start(out=out, in_=in_, allow_slow_non_contiguous=True), r=r, w=w, dma=True)

    def mm(self, out, lhsT, rhs, start, stop, r=(), w=()):
        return self.P.op("pe", lambda e: e.matmul(out, lhsT=lhsT, rhs=rhs, start=start, stop=stop), r=r, w=w)

    def tr(self, out, in_, ident, r=(), w=()):
        return self.P.op("pe", lambda e: e.transpose(out=out, in_=in_, identity=ident), r=r, w=w)

    def act(self, out, in_, func, r=(), w=(), scale=1.0, bias=None, accum=None):
        def f(e):
            kw = {}
            if bias is not None:
                kw["bias"] = bias
            if accum is not None:
                kw["accum_out"] = accum
            return e.activation(out=out, in_=in_, func=func, scale=scale, **kw)
        return self.P.op("act", f, r=r, w=w)

    def tt(self, eng, out, a, b, op, r=(), w=()):
        return self.P.op(eng, lambda e: e.tensor_tensor(out=out, in0=a, in1=b, op=op), r=r, w=w)

    def ts(self, eng, out, a, s1, s2, op0, op1=None, r=(), w=()):
        def f(e):
            if op1 is None:
                return e.tensor_scalar(out=out, in0=a, scalar1=s1, scalar2=None, op0=op0)
            return e.tensor_scalar(out=out, in0=a, scalar1=s1, scalar2=s2, op0=op0, op1=op1)
        return self.P.op(eng, f, r=r, w=w)

    def stt(self, out, in0, scalar, in1, op0, op1, r=(), w=()):
        return self.P.op("dve", lambda e: e.scalar_tensor_tensor(out=out, in0=in0, scalar=scalar, in1=in1, op0=op0, op1=op1), r=r, w=w)

    def cp(self, eng, out, in_, r=(), w=()):
        if eng == "act":
            return self.act(out, in_, AF.Copy, r=r, w=w)
        return self.P.op(eng, lambda e: e.tensor_copy(out=out, in_=in_), r=r, w=w)

    def memset(self, eng, ap, val, r=(), w=()):
        return self.P.op(eng, lambda e: e.memset(ap, val), r=r, w=w)

    def recip(self, out, in_, r=(), w=()):
        return self.P.op("dve", lambda e: e.reciprocal(out=out, in_=in_), r=r, w=w)

    def areset(self):
        self.aoff = 0

    def alloc(self, cols, dt=BF16):
        nb = cols * (4 if dt == F32 else 2)
        nb = (nb + 63) // 64 * 64
        off = self.aoff
        self.aoff += nb
        assert self.aoff <= self.asize, f"arena overflow {self.aoff} > {self.asize}"
        v = self.arena[:, off // 2:(off + nb) // 2]
        if dt == F32:
            v = v.bitcast(F32)
            return v[:, 0:cols]
        return v[:, 0:cols]

    def bank(self):
        bl = self.bank_list
        i = bl[self.bank_i % len(bl)]
        self.bank_i += 1
        return self.ps[i], self.psr[i]

    def build(self):
        nc = bass.Bass("TRN2", target_bir_lowering=False)
        self.nc = nc
        NS = self.nseq
        dt_in = lambda name, shape: nc.dram_tensor(name, shape, F32, kind="ExternalInput").ap()
        self.xT = dt_in("xT", [NS, D, S])
        self.pT = dt_in("pT", [4, NS, 256, S])
        self.cst_d = dt_in("cst", [128, C_TOT])
        self.m1_d = dt_in("m1", [128, 32 * 128])
        self.rope_d = dt_in("rope", [64, 4, S])
        self.wd = {}
        for name, k, n in W_SPECS:
            self.wd[name] = dt_in(name, [k, n])
        self.wd["ple_w"] = dt_in("ple_w", [4, 256, D])
        self.wd["ple_gate_w"] = dt_in("ple_gate_w", [4, D, D])
        self.norm_g_d = dt_in("norm_g", [4, D])
        self.final_g_d = dt_in("final_g", [D])
        self.fn_w_mix_d = dt_in("fn_w_mix", [8, 128, 128])
        self.rpbf_d = dt_in("rpb_flip", [32, 15, 31])
        self.g_q_d = dt_in("mla_g_q", [384])
        self.g_kv_d = dt_in("mla_g_kv", [256])
        self.lb_d = dt_in("hg_lb_raw", [4, 2, D])
        self.g_out_d = dt_in("hg_g_out", [D])
        self.yT = nc.dram_tensor("yT", [NS, D, S], F32, kind="ExternalOutput").ap()
        scr = lambda name, shape, dt=BF16: nc.dram_tensor(name, shape, dt, kind="Internal").ap()
        self.X = scr("X", [NS, D, S], F32)
        self.SA = {}
        for nm, shp, dt_ in [("SZ", [D, S], BF16), ("O", [D, S], BF16), ("Ud", [S, D], BF16), ("Td", [2, 64, 64, D], BF16),
                             ("QT", [D, S], BF16), ("KT", [D, S], BF16), ("VA", [S, 32 * 33], BF16),
                             ("QN", [D, S], BF16), ("QP", [8, 64, S], BF16), ("KN", [D, S], BF16), ("KP", [64, S], BF16),
                             ("V2", [S, D], BF16), ("QS", [D, S], BF16), ("LF", [2, D, S], F32), ("VI", [S, D], BF16)]:
            self.SA[nm] = scr(nm, [NS] + shp, dt_)
        self.w16 = {}
        for name, k, n in W_SPECS:
            self.w16[name] = scr("b_" + name, [k, n])
        self.w16["ple_w"] = scr("b_ple_w", [4, 256, D])
        self.w16["ple_gate_w"] = scr("b_ple_gate_w", [4, D, D])
        self.BT = scr("BT", [128, 32 * 14 * 64])

        with ExitStack() as st:
            self.P = Prog(nc, st)
            P = self.P
            sbt = lambda name, shape, dt: st.enter_context(nc.sbuf_tensor(name, shape, dt))
            self.cst = sbt("cstf", [128, C_TOT], F32)
            self.cstb = sbt("cstb", [128, C_TOT], BF16)
            self.ng = sbt("ng", [128, 4, 8], F32)
            self.fg = sbt("fg", [128, 8], F32)
            self.gq = sbt("gq", [128, 3], F32)
            self.gkv = sbt("gkv", [128, 2], F32)
            self.gout = sbt("gout", [128, 8], F32)
            self.lbraw = sbt("lbraw", [128, 4, 2, 8], F32)
            self.lb = sbt("lb", [128, 2, 8], F32)
            self.oml = sbt("oml", [128, 2, 8], F32)
            self.epsb = sbt("epsb", [128, 1], F32)
            self.asize = 200 * 1024
            self.arena = sbt("arena", [128, self.asize // 2], BF16)
            self.ps = [st.enter_context(nc.psum_tensor(f"ps{i}", [128, 512], F32)) for i in range(8)]
            self.psr = [P.res(f"ps{i}") for i in range(8)]
            self.bank_i = 0
            self.bank_list = list(range(8))
            self.rc = P.res("consts")

            self.phase_prep()
            for li in self.layers:
                last = (li == self.layers[-1])
                [self.pre0, self.pre1, self.pre2, self.pre3][li]()
                for sq in range(NS):
                    self.bind(sq)
                    [self.mid0, self.mid1, self.mid2, self.mid3][li](sq)
                self.post_layer(li, last, last and self.final_norm)
            stats = P.emit()
            self.stats = stats
        return nc

    def bind(self, sq):
        for nm, ap in self.SA.items():
            setattr(self, nm, ap[sq])

    def xsrc_of(self, li, sq):
        return self.xT[sq] if li == self.layers[0] else self.X[sq]

    def pre_iter(self, li):
        tiles = [(sq, t) for sq in range(self.nseq) for t in range(NT)]
        self._pre_pending = None
        self.pre_load(li, self.xsrc_of(li, tiles[0][0]), tiles[0][1], 0)
        res = [self.pre_norm(li, 0)]
        for i, (sq, t) in enumerate(tiles):
            cur = res[0]
            if i + 1 < len(tiles):
                self.pre_load(li, self.xsrc_of(li, tiles[i + 1][0]), tiles[i + 1][1], i + 1)

                def pend(i=i):
                    res[0] = self.pre_norm(li, i + 1)
                self._pre_pending = pend
            self.bind(sq)
            yield i, sq, t, cur[0], cur[1]
            self.pre_mid()

    def cb(self, off, n=128):
        return self.cstb[:, off:off + n]

    def cf(self, off, n=128):
        return self.cst[:, off:off + n]

    def phase_prep(self):
        P = self.P
        rc = self.rc
        self.areset()
        self.dma("sp", self.cst[:], self.cst_d[:, :], w=[rc])
        self.cp("dve", self.cstb[:], self.cst[:], r=[rc], w=[rc])
        self.memset("pool", self.epsb[:], EPS, w=[rc])
        self.dma_slow("sp", self.ng[:], self.norm_g_d.rearrange("l (c p) -> p l c", p=128), w=[rc])
        self.dma_slow("sp", self.fg[:], self.final_g_d.rearrange("(c p) -> p c", p=128), w=[rc])
        self.dma_slow("sp", self.gq[:], self.g_q_d.rearrange("(c p) -> p c", p=128), w=[rc])
        self.dma_slow("sp", self.gkv[:], self.g_kv_d.rearrange("(c p) -> p c", p=128), w=[rc])
        self.dma_slow("sp", self.gout[:], self.g_out_d.rearrange("(c p) -> p c", p=128), w=[rc])
        for l in range(4):
            for d in range(2):
                self.dma_slow("sp", self.lbraw[:, l, d, :], self.lb_d[l, d].rearrange("(c p) -> p c", p=128), w=[rc])
        ex = self.alloc(64, F32)
        sm = self.alloc(16, F32)
        exv = ex.rearrange("p (l x) -> p l x", l=4)
        self.act(exv, self.lbraw[:].rearrange("p l d c -> p l (d c)"), AF.Exp, r=[rc], w=[rc])
        self.tt("dve", sm, exv[:, 0, :], exv[:, 1, :], ALU.add, r=[rc], w=[rc])
        self.tt("dve", sm, sm, exv[:, 2, :], ALU.add, r=[rc], w=[rc])
        self.tt("dve", sm, sm, exv[:, 3, :], ALU.add, r=[rc], w=[rc])
        self.recip(sm, sm, r=[rc], w=[rc])
        omlv = self.oml[:].rearrange("p d c -> p (d c)")
        lbv = self.lb[:].rearrange("p d c -> p (d c)")
        self.tt("dve", omlv, exv[:, 0, :], sm, ALU.mult, r=[rc], w=[rc])
        self.ts("dve", lbv, omlv, -1.0, 1.0, ALU.mult, ALU.add, r=[rc], w=[rc])
        stg = [self.alloc(5120, F32) for _ in range(2)]
        stb = [self.alloc(5120, BF16) for _ in range(2)]
        rs_ = [P.res() for _ in range(2)]
        rb_ = [P.res() for _ in range(2)]
        cnt = 0
        engs = ["dve", "pool", "act"]
        jobs = []
        for name, k, n in W_SPECS:
            src = self.wd[name]
            dst = self.w16[name]
            for c in range(k // 128):
                jobs.append((src[c * 128:(c + 1) * 128, :], dst[c * 128:(c + 1) * 128, :], n))
        for l in range(4):
            for c in range(2):
                jobs.append((self.wd["ple_w"][l, c * 128:(c + 1) * 128, :], self.w16["ple_w"][l, c * 128:(c + 1) * 128, :], D))
            for c in range(8):
                jobs.append((self.wd["ple_gate_w"][l, c * 128:(c + 1) * 128, :], self.w16["ple_gate_w"][l, c * 128:(c + 1) * 128, :], D))
        for src, dst, n in jobs:
            b = cnt % 2
            self.dma("sp", stg[b][:, 0:n], src, w=[rs_[b]])
            self.cp(engs[cnt % 3], stb[b][:, 0:n], stg[b][:, 0:n], r=[rs_[b]], w=[rb_[b]])
            self.dma("act" if False else "sp", dst, stb[b][:, 0:n], r=[rb_[b]])
            cnt += 1
        P.barrier()
        if 1 in self.layers:
            self.prep_na_bias()

    def prep_na_bias(self):
        P = self.P
        overlap = (self.layers[0] == 0)
        self.aoff = 143 * 1024 if overlap else 0
        bt = self.alloc(32 * 14 * 64, BF16)
        rb = P.res()
        btv = bt.rearrange("p (h s q) -> p h s q", h=32, s=14)
        self.memset("pool", bt[:, 0:14336], NEG, w=[rb])
        self.memset("dve", bt[:, 14336:28672], NEG, w=[rb])
        qs = np.arange(64)
        win = np.clip(qs - 8, 0, 48)
        for a in range(2):
            for kc in range(64):
                valid = np.nonzero((win <= kc) & (kc < win + 16))[0]
                qlo, qhi = int(valid[0]), int(valid[-1])
                assert len(valid) == qhi - qlo + 1
                nq = qhi - qlo + 1
                j0 = 15 - kc + qlo
                assert 0 <= j0 and j0 + nq <= 31
                p = a * 64 + kc
                src = self.rpbf_d[:, a:a + 14, j0:j0 + nq]
                self.dma_slow("pool", btv[p:p + 1, :, :, qlo:qhi + 1], src.unsqueeze(0), w=[rb])
        self.P.op("pool", lambda e: e.dma_start(out=self.BT[:, :], in_=bt), r=[rb], dma=True)
        if overlap:
            self.pool_busy = True
        else:
            P.barrier()

    def load_w(self, name, k, n, sub=None):
        t = self.alloc((k // 128) * n, BF16)
        v = t.rearrange("p (c n) -> p c n", n=n)
        src = self.w16[name] if sub is None else self.w16[name][sub]
        r = self.P.res()
        kc = k // 128
        hc = max(1, kc // 2)
        srcv = src.rearrange("(c p) n -> p c n", p=128)
        self.dma("sp", v[:, 0:hc, :], srcv[:, 0:hc, :], w=[r])
        if kc > hc:
            self.dma("pool", v[:, hc:kc, :], srcv[:, hc:kc, :], w=[r])
        return v, r

    def pre_setup(self):
        P = self.P
        self.areset()
        self.bank_i = 0
        self.xt = [self.alloc(8 * TS, F32).rearrange("p (c s) -> p c s", c=8) for _ in range(2)]
        self.rxt = [P.res() for _ in range(2)]
        self.sq = [self.alloc(8 * TS, BF16).rearrange("p (c s) -> p c s", c=8) for _ in range(2)]
        self.rsq = [P.res() for _ in range(2)]
        self.hT = [self.alloc(8 * TS, BF16).rearrange("p (c s) -> p c s", c=8) for _ in range(2)]
        self.rhT = [P.res() for _ in range(2)]
        self.rstd = [self.alloc(TS, F32) for _ in range(2)]
        self.rrstd = [P.res() for _ in range(2)]

    def rms_to(self, xt, rx, nch, inv_n, gcol, out, rout, b):
        sq, rsq = self.sq[b], self.rsq[b]
        rstd, rr = self.rstd[b], self.rrstd[b]
        self.tt("dve", sq[:, 0:nch, :], xt, xt, ALU.mult, r=[rx], w=[rsq])
        pb, rpb = self.bank()
        for c in range(nch):
            self.mm(pb[:], self.cb(C_ONES), sq[:, c, :], c == 0, c == nch - 1, r=[rsq, self.rc], w=[rpb])
        self.act(rstd, pb[:], AF.Sqrt, r=[rpb, self.rc], w=[rr], scale=inv_n, bias=self.epsb[:, 0:1])
        self.recip(rstd, rstd, r=[rr], w=[rr])
        for c in range(nch):
            self.stt(out[:, c, :], xt[:, c, :], gcol(c), rstd, ALU.mult, ALU.mult, r=[rx, rr, self.rc], w=[rout])

    def pre_load(self, li, xsrc, t, i):
        b = i % 2
        xt, rx = self.xt[b], self.rxt[b]
        xv = xsrc.rearrange("(c p) s -> p c s", p=128)
        self.dma("sp", xt[:, 0:4, :], xv[:, 0:4, t * TS:(t + 1) * TS], w=[rx])
        self.dma("pool", xt[:, 4:8, :], xv[:, 4:8, t * TS:(t + 1) * TS], w=[rx])

    def pre_norm(self, li, i):
        b = i % 2
        self.rms_to(self.xt[b], self.rxt[b], 8, 1.0 / D, lambda c: self.ng[:, li, c:c + 1], self.hT[b], self.rhT[b], b)
        return self.hT[b], self.rhT[b]

    def pre_mid(self):
        f = self._pre_pending
        if f is not None:
            self._pre_pending = None
            f()

    def proj_fm(self, w, rw, col0, hT, rh, m=128):
        pb, rpb = self.bank()
        for k in range(8):
            self.mm(pb[0:m, :], w[:, k, col0:col0 + m], hT[:, k, :], k == 0, k == 7, r=[rw, rh], w=[rpb])
        return pb, rpb

    def proj_tm(self, w, rw, col0, hT, rh, sub):
        pb, rpb = self.bank()
        for k in range(8):
            self.mm(pb[:], hT[:, k, sub * 128:(sub + 1) * 128], w[:, k, col0:col0 + 512], k == 0, k == 7, r=[rw, rh], w=[rpb])
        return pb, rpb

    def z_tile(self, w, rw, zcol0, hT, rh, t, b):
        szt, rsz = self.szt[b], self.rszt[b]
        for j in range(8):
            pb, rpb = self.proj_fm(w, rw, zcol0 + j * 128, hT, rh)
            self.act(szt[:, j, :], pb[:], AF.Silu, r=[rpb], w=[rsz])
        self.dma("sp", self.SZ.rearrange("(c p) s -> p c s", p=128)[:, :, t * TS:(t + 1) * TS], szt, r=[rsz])

    def alloc_szt(self):
        self.szt = [self.alloc(8 * TS, BF16).rearrange("p (c s) -> p c s", c=8) for _ in range(2)]
        self.rszt = [self.P.res() for _ in range(2)]

    def pre0(self):
        P = self.P
        self.pre_setup()
        w, rw = self.load_w("fn_w_in", 1024, 2048)
        self.alloc_szt()
        ut = [self.alloc(4 * 1024, BF16).rearrange("p (a f) -> p a f", a=4) for _ in range(2)]
        rut = [P.res() for _ in range(2)]
        ev = 0
        for i, sq, t, hT, rh in self.pre_iter(0):
            b = i % 2
            udv = self.Ud.rearrange("(n p) f -> p n f", p=128)
            for sub in range(4):
                for ct in range(2):
                    pb, rpb = self.proj_tm(w, rw, ct * 512, hT, rh, sub)
                    self.cp("act" if ev % 2 == 0 else "dve", ut[b][:, sub, ct * 512:(ct + 1) * 512], pb[:], r=[rpb], w=[rut[b]])
                    ev += 1
            self.dma("sp", udv[:, 4 * t:4 * t + 4, :], ut[b], r=[rut[b]])
            self.pre_mid()
            self.z_tile(w, rw, 1024, hT, rh, t, b)
        P.barrier()
        self.pool_busy = False

    def mid0(self, sq):
        P = self.P
        rc = self.rc
        self.areset()
        self.bank_i = 0
        wm = self.alloc(8 * 128, F32).rearrange("p (g d) -> p g d", g=8)
        rwm = P.res()
        self.dma("sp", wm, self.fn_w_mix_d.rearrange("g m d -> m g d"), w=[rwm])
        AB = self.alloc(2 * 8 * 128, BF16).rearrange("p (x g d) -> p x g d", x=2, g=8)
        rAB = P.res()
        for x, off in ((0, C_CC), (1, C_SC)):
            for gh in range(2):
                pb, rpb = self.bank()
                for g4 in range(4):
                    g = gh * 4 + g4
                    self.mm(pb[:, g4 * 128:(g4 + 1) * 128], self.cf(off), wm[:, g, :], True, True, r=[rc, rwm], w=[rpb])
                self.cp("dve", AB[:, x, gh * 4:(gh + 1) * 4, :], pb[:].rearrange("p (g d) -> p g d", g=4), r=[rpb], w=[rAB])
        mark = self.aoff
        m1f = self.alloc(32 * 128, F32)
        m1 = self.alloc(32 * 128, BF16).rearrange("p (j m) -> p j m", j=32)
        rm1 = P.res()
        self.dma("sp", m1f, self.m1_d[:, :], w=[rm1])
        self.cp("pool", m1.rearrange("p j m -> p (j m)"), m1f, r=[rm1], w=[rm1])
        ud1 = self.alloc(32 * 1024, BF16).rearrange("p (j f) -> p j f", j=32)
        rud = P.res()
        udv = self.Ud.rearrange("(a b) f -> a b f", b=64)
        self.dma("sp", ud1[0:64, :, :], udv[:, 0:32, :], w=[rud])
        self.dma("pool", ud1[64:128, :, :], udv[:, 32:64, :], w=[rud])
        t1 = [self.alloc(8 * 1024, BF16).rearrange("p (s f) -> p s f", s=8) for _ in range(2)]
        rt1 = [P.res() for _ in range(2)]
        tdv = self.Td.rearrange("r k s f -> (r k) s f")
        ev = 0
        for sb_ in range(8):
            b = sb_ % 2
            for sl in range(8):
                s0 = sb_ * 8 + sl
                half, j = s0 // 32, s0 % 32
                for ft in range(2):
                    pb, rpb = self.bank()
                    self.mm(pb[:], m1[64 * half:64 * half + 64, j, :], ud1[64 * half:64 * half + 64, j, ft * 512:(ft + 1) * 512],
                            True, True, r=[rm1, rud], w=[rpb])
                    self.cp("act" if ev % 2 == 0 else "dve", t1[b][:, sl, ft * 512:(ft + 1) * 512], pb[:], r=[rpb], w=[rt1[b]])
                    ev += 1
            self.dma("sp", tdv[:, sb_ * 8:(sb_ + 1) * 8, :], t1[b], r=[rt1[b]])
        P.barrier()
        self.aoff = mark
        xT = self.alloc(4 * 2 * S, BF16).rearrange("p (g r k) -> p g r k", g=4, r=2)
        rxT = [P.res() for _ in range(4)]
        t3 = [self.alloc(16 * 512, BF16).rearrange("p (k f) -> p k f", k=16) for _ in range(2)]
        rt3 = [P.res() for _ in range(2)]
        ot = [self.alloc(S, BF16) for _ in range(2)]
        rot = [P.res() for _ in range(2)]
        tdr = self.Td.rearrange("r k s f -> r s k f")
        for hf in range(2):
            for kb in range(4):
                b = kb % 2
                for r_ in range(2):
                    self.dma("sp" if r_ == 0 else "pool", t3[b][64 * r_:64 * r_ + 64, :, :],
                             tdr[r_, :, kb * 16:(kb + 1) * 16, hf * 512:(hf + 1) * 512], w=[rt3[b]])
                for k4 in range(4):
                    for g in range(4):
                        pb, rpb = self.bank()
                        for kk in range(4):
                            kl = k4 * 4 + kk
                            self.mm(pb[:, kk * 128:(kk + 1) * 128], t3[b][:, kl, g * 128:(g + 1) * 128], self.cb(C_M3),
                                    True, True, r=[rt3[b], rc], w=[rpb])
                        k0 = kb * 16 + k4 * 4
                        outv = xT[:, g, :, :].rearrange("p r (k1 k0) -> p r k1 k0", k0=64)[:, :, :, k0:k0 + 4]
                        inv = pb[:].rearrange("p (a r k) -> p r k a", a=4, r=2)
                        self.cp("act" if ev % 2 == 0 else "dve", outv, inv, r=[rpb], w=[rxT[g]])
                        ev += 1
            for g in range(4):
                gg = hf * 4 + g
                b = g % 2
                for t in range(NT):
                    pb, rpb = self.bank()
                    self.mm(pb[:], AB[:, 0, gg, :], xT[:, g, 0, t * TS:(t + 1) * TS], True, False, r=[rAB, rxT[g]], w=[rpb])
                    self.mm(pb[:], AB[:, 1, gg, :], xT[:, g, 1, t * TS:(t + 1) * TS], False, True, r=[rAB, rxT[g]], w=[rpb])
                    self.cp("act" if ev % 2 == 0 else "dve", ot[b][:, t * TS:(t + 1) * TS], pb[:], r=[rpb], w=[rot[b]])
                    ev += 1
                self.dma("sp", self.O[gg * 128:(gg + 1) * 128, :], ot[b], r=[rot[b]])
        P.barrier()

    def post_layer(self, li, last, final):
        P = self.P
        self.areset()
        self.bank_i = 0
        self.bank_list = list(range(8))
        wname = ["fn_w_out", "na_w_out", "mla_w_out", "hg_w_out"][li]
        wo, rwo = self.load_w(wname, 1024, 1024)
        wg, rwg = self.load_w("ple_gate_w", 1024, 1024, sub=li)
        wp, rwp = self.load_w("ple_w", 256, 1024, sub=li)
        A = lambda cols, dt: [self.alloc(cols, dt) for _ in range(2)]
        R = lambda: [P.res() for _ in range(2)]
        v8 = lambda t_: t_.rearrange("p (c s) -> p c s", c=8)
        ot = [v8(x) for x in A(8 * TS, BF16)]; rot = R()
        sz = [v8(x) for x in A(8 * TS, BF16)]; rsz = R()
        xt = [v8(x) for x in A(8 * TS, F32)]; rx = R()
        pt = [x.rearrange("p (c s) -> p c s", c=2) for x in A(2 * TS, F32)]; rpt = R()
        ptb = [x.rearrange("p (c s) -> p c s", c=2) for x in A(2 * TS, BF16)]; rptb = R()
        g = [v8(x) for x in A(8 * TS, BF16)]; rg = R()
        x1b = [v8(x) for x in A(8 * TS, BF16)]; rx1b = R()
        sg = A(TS, F32); rsg = R()
        if final:
            sq1 = v8(self.alloc(8 * TS, BF16)); rsq1 = P.res()
            self.sq = [sq1, sq1]; self.rsq = [rsq1, rsq1]
            self.rstd = A(TS, F32); self.rrstd = R()
        NX = 4 if final else 3
        tiles = [(sq, t) for sq in range(self.nseq) for t in range(NT)]
        fm = lambda ap: ap.rearrange("(c p) s -> p c s", p=128)

        xt3 = xt + [v8(self.alloc(8 * TS, F32)) for _ in range(NX - 2)]
        rx3 = rx + [P.res() for _ in range(NX - 2)]
        ry3 = [P.res() for _ in range(NX)]

        def loads(i):
            sq, t = tiles[i]
            b = i % 2
            b3 = i % NX
            sl = slice(t * TS, (t + 1) * TS)
            xv = fm(self.xsrc_of(li, sq))
            self.dma("sp", ot[b], fm(self.SA["O"][sq])[:, :, sl], w=[rot[b]])
            self.dma("sp", sz[b], fm(self.SA["SZ"][sq])[:, :, sl], w=[rsz[b]])
            self.dma("sp", xt3[b3], xv[:, :, sl], w=[rx3[b3]])
            self.dma("sp", pt[b], fm(self.pT[li, sq])[:, :, sl], w=[rpt[b]])

        def stage_a(i):
            sq, t = tiles[i]
            b = i % 2
            b3 = i % NX
            self.tt("dve", g[b], ot[b], sz[b], ALU.mult, r=[rot[b], rsz[b]], w=[rg[b]])
            self.cp("pool", ptb[b], pt[b], r=[rpt[b]], w=[rptb[b]])
            for j in range(8):
                pb, rpb = self.bank()
                for k in range(8):
                    self.mm(pb[:], wo[:, k, j * 128:(j + 1) * 128], g[b][:, k, :], k == 0, k == 7, r=[rwo, rg[b]], w=[rpb])
                self.tt("dve", xt3[b3][:, j, :], xt3[b3][:, j, :], pb[:], ALU.add, r=[rpb, rx3[b3]], w=[rx3[b3]])
                self.cp("act", x1b[b][:, j, :], xt3[b3][:, j, :], r=[rx3[b3]], w=[rx1b[b]])

        def stage_b(i):
            sq, t = tiles[i]
            b = i % 2
            b3 = i % NX
            sl = slice(t * TS, (t + 1) * TS)
            xdv = fm(self.yT[sq] if last else self.X[sq])
            for j in range(8):
                pg, rpg = self.bank()
                for k in range(8):
                    self.mm(pg[:], wg[:, k, j * 128:(j + 1) * 128], x1b[b][:, k, :], k == 0, k == 7, r=[rwg, rx1b[b]], w=[rpg])
                pp, rpp = self.bank()
                for k in range(2):
                    self.mm(pp[:], wp[:, k, j * 128:(j + 1) * 128], ptb[b][:, k, :], k == 0, k == 1, r=[rwp, rptb[b]], w=[rpp])
                self.act(sg[b], pg[:], AF.Sigmoid, r=[rpg], w=[rsg[b]])
                self.tt("dve", sg[b], sg[b], pp[:], ALU.mult, r=[rsg[b], rpp], w=[rsg[b]])
                self.tt("pool", xt3[b3][:, j, :], xt3[b3][:, j, :], sg[b], ALU.add, r=[rsg[b], rx3[b3]], w=[rx3[b3]])
            if final:
                self.rms_to(xt3[b3], rx3[b3], 8, 1.0 / D, lambda c: self.fg[:, c:c + 1], xt3[b3], ry3[b3], b)
            self.dma("sp", xdv[:, 0:4, sl], xt3[b3][:, 0:4, :], r=[rx3[b3], ry3[b3]])
            self.dma("sp", xdv[:, 4:8, sl], xt3[b3][:, 4:8, :], r=[rx3[b3], ry3[b3]])

        loads(0)
        loads(1)
        stage_a(0)
        for i in range(len(tiles)):
            if i + 1 < len(tiles):
                stage_a(i + 1)
            if i + 2 < len(tiles):
                loads(i + 2)
            stage_b(i)
        P.barrier()

    def pre1(self):
        P = self.P
        self.pre_setup()
        self.bank_list = list(range(8))
        w, rw = self.load_w("na_w_in", 1024, 4096)
        self.alloc_szt()
        qk = [self.alloc(8 * TS, BF16).rearrange("p (c s) -> p c s", c=8) for _ in range(2)]
        rqk = [P.res() for _ in range(2)]
        vt = [self.alloc(4 * 1056, BF16).rearrange("p (a h d) -> p a h d", a=4, h=32) for _ in range(2)]
        rvt = [P.res() for _ in range(2)]
        for b in range(2):
            self.memset("pool", vt[b][:, :, :, 32:33], 1.0, w=[rvt[b]])
        ev = 0
        nq = 0
        for i, sq, t, hT, rh in self.pre_iter(1):
            b = i % 2
            sl = slice(t * TS, (t + 1) * TS)
            vav = self.VA.rearrange("(n p) f -> p n f", p=128)
            for which, dst in ((0, self.QT), (1, self.KT)):
                bq = nq % 2
                nq += 1
                for j in range(8):
                    pb, rpb = self.proj_fm(w, rw, which * 1024 + j * 128, hT, rh)
                    self.cp("act" if ev % 2 == 0 else "dve", qk[bq][:, j, :], pb[:], r=[rpb], w=[rqk[bq]])
                    ev += 1
                self.dma("sp", dst.rearrange("(c p) s -> p c s", p=128)[:, :, sl], qk[bq], r=[rqk[bq]])
            self.pre_mid()
            for sub in range(4):
                for ct in range(2):
                    pb, rpb = self.proj_tm(w, rw, 2048 + ct * 512, hT, rh, sub)
                    self.cp("act" if ev % 2 == 0 else "dve", vt[b][:, sub, 16 * ct:16 * ct + 16, 0:32],
                            pb[:].rearrange("p (h d) -> p h d", d=32), r=[rpb], w=[rvt[b]])
                    ev += 1
            self.dma("pool", vav[:, 4 * t:4 * t + 4, :], vt[b].rearrange("p a h d -> p a (h d)"), r=[rvt[b]])
            self.z_tile(w, rw, 3072, hT, rh, t, b)
        P.barrier()

    def mid1(self, sq):
        P = self.P
        rc = self.rc
        self.areset()
        self.bank_i = 0
        self.bank_list = [0, 1, 2, 3, 4]
        obanks = [5, 6]
        tbank = 7
        bts = self.alloc(32 * 14 * 64, BF16).rearrange("p (h s q) -> p h s q", h=32, s=14)
        rbt = P.res()
        self.dma("sp", bts[:, 0:16], self.BT.rearrange("p (h x) -> p h x", h=32)[:, 0:16, :].rearrange("p h (s q) -> p h s q", s=14), w=[rbt])
        self.dma("pool", bts[:, 16:32], self.BT.rearrange("p (h x) -> p h x", h=32)[:, 16:32, :].rearrange("p h (s q) -> p h s q", s=14), w=[rbt])
        ktb = [self.alloc(8 * 1024, BF16).rearrange("p (c s) -> p c s", c=8) for _ in range(2)]
        rkt = [P.res() for _ in range(2)]
        vbe = self.alloc(8 * 1056, BF16).rearrange("p (m h d) -> p m h d", m=8, h=32)
        vbo = self.alloc(7 * 1056, BF16).rearrange("p (m h d) -> p m h d", m=7, h=32)
        rvb = [P.res(), P.res()]
        qraw = [self.alloc(8 * TS, BF16).rearrange("p (c s) -> p c s", c=8) for _ in range(2)]
        rqr = [P.res() for _ in range(2)]
        qm = self.alloc(8 * 4 * TS, BF16).rearrange("p (c m s) -> p c m s", c=8, m=4)
        rqm = P.res()
        NPT = 4
        ptf = [self.alloc(1152, BF16) for _ in range(NPT)]
        rpt = [P.res() for _ in range(NPT)]
        for i in range(NPT):
            self.memset("pool", ptf[i], 0.0, w=[rpt[i]])
        ptw = [x[:, 0:1152].rearrange("p (r x) -> p r x", x=576)[:, :, 0:512].rearrange("p r (k y) -> p r k y", y=128)[:, :, :, 0:64] for x in ptf]
        ptr_ = [x[:, 0:1024].rearrange("p (r k y) -> p r k y", r=2, k=4) for x in ptf]
        osb = [self.alloc(1024, BF16) for _ in range(2)]
        rosb = [P.res() for _ in range(2)]
        otb = [self.alloc(8 * TS, BF16).rearrange("p (c s) -> p c s", c=8) for _ in range(2)]
        rotb = [P.res() for _ in range(2)]
        rc8 = [self.alloc(8, F32) for _ in range(2)]
        rrc8 = [P.res() for _ in range(2)]
        ktv = self.KT.rearrange("(c p) s -> p c s", p=128)
        qtv = self.QT.rearrange("(c p) s -> p c s", p=128)
        ov = self.O.rearrange("(c p) s -> p c s", p=128)
        ev = [0]
        units = [(t, pr, hg, hl) for t in range(NT) for pr in range(4) for hg in range(4) for hl in range(8)]
        state = {}

        def band_setup(t):
            b = t % 2
            kw0 = int(np.clip(8 * t - 4, 0, 48))
            self.dma("sp", ktb[b], ktv[:, :, 64 * kw0:64 * kw0 + 1024], w=[rkt[b]])
            self.dma("pool", qraw[b], qtv[:, :, t * TS:(t + 1) * TS], w=[rqr[b]])
            for j in range(8):
                for hm in range(4):
                    e3 = ev[0] % 3
                    if e3 == 2:
                        self.act(qm[:, j, hm, :], qraw[b][:, j, :], AF.Copy, r=[rqr[b], rc], w=[rqm], scale=self.cf(C_MC + hm, 1))
                    else:
                        self.ts("dve" if e3 == 0 else "pool", qm[:, j, hm, :], qraw[b][:, j, :], self.cf(C_MC + hm, 1), 0.0,
                                ALU.mult, ALU.add, r=[rqr[b], rc], w=[rqm])
                    ev[0] += 1

        def v_setup(t, hf):
            kw0 = int(np.clip(8 * t - 4, 0, 48))
            hs_ = slice(16 * hf, 16 * hf + 16)
            cs_ = slice(528 * hf, 528 * hf + 528)
            self.dma("pool", vbe[:, :, hs_, :].rearrange("p m h d -> p m (h d)"),
                     self.VA[64 * kw0:64 * kw0 + 1024, cs_].rearrange("(m p) f -> p m f", p=128), w=[rvb[hf]])
            self.dma("sp", vbo[:, :, hs_, :].rearrange("p m h d -> p m (h d)"),
                     self.VA[64 * (kw0 + 1):64 * (kw0 + 1) + 896, cs_].rearrange("(m p) f -> p m f", p=128), w=[rvb[hf]])

        def geom(u):
            t, pr, hg, hl = u
            kw0 = int(np.clip(8 * t - 4, 0, 48))
            rows = (8 * t + 2 * pr, 8 * t + 2 * pr + 1)
            return kw0, rows

        def S_(ui):
            u = units[ui]
            t, pr, hg, hl = u
            b = t % 2
            kw0, rows = geom(u)
            h = hg * 8 + hl
            j, hm = h // 4, h % 4
            sb_, rsb = self.bank()
            sv = sb_[:].rearrange("p (r k q) -> p r k q", r=2, k=4)
            for ri, r_ in enumerate(rows):
                rs = int(np.clip(r_ - 4, 0, 56))
                ro0 = rs - r_ + 7
                self.mm(sv[:, ri, :, :], self.cb(C_ID), bts[:, h, ro0:ro0 + 7:2, :], True, False, r=[rbt, rc], w=[rsb])
                for kc in range(4):
                    koff = 64 * (rs + 2 * kc - kw0)
                    self.mm(sv[:, ri, kc, :], ktb[b][:, j, koff:koff + 128], qm[:, j, hm, (r_ - 8 * t) * 64:(r_ - 8 * t) * 64 + 64],
                            False, True, r=[rkt[b], rqm], w=[rsb])
            state[ui] = (sv, rsb)

        def PV_(ui):
            u = units[ui]
            t, pr, hg, hl = u
            b = t % 2
            kw0, rows = geom(u)
            h = hg * 8 + hl
            sv, rsb = state.pop(ui)
            pi = ui % NPT
            self.act(ptw[pi], sv, AF.Exp, r=[rsb], w=[rpt[pi]])
            gi = ui // 8
            obi = obanks[gi % 2]
            ob, rob = self.ps[obi], self.psr[obi]
            obv = ob[:, 0:264].rearrange("p (h d) -> p h d", h=8)
            for ri, r_ in enumerate(rows):
                rs = int(np.clip(r_ - 4, 0, 56))
                dlt = rs - kw0
                for kc in range(4):
                    if dlt % 2 == 0:
                        vsrc = vbe[:, dlt // 2 + kc, h, :]
                    else:
                        vsrc = vbo[:, (dlt - 1) // 2 + kc, h, :]
                    self.mm(obv[:, hl, :], ptr_[pi][:, ri, kc, :], vsrc, ri == 0 and kc == 0, ri == 1 and kc == 3,
                            r=[rpt[pi], rvb[h // 16]], w=[rob])
            ob_ = (ui // 32) % 2
            if hl == 7:
                self.recip(rc8[hg % 2], obv[:, :, 32], r=[rob], w=[rrc8[hg % 2]])
                self.tt("dve", osb[ob_][:, hg * 256:(hg + 1) * 256].rearrange("p (h d) -> p h d", h=8), obv[:, :, 0:32],
                        rc8[hg % 2].unsqueeze(2).to_broadcast([128, 8, 32]), ALU.mult, r=[rob, rrc8[hg % 2]], w=[rosb[ob_]])
                if hg == 3:
                    tb, rtb = self.ps[tbank], self.psr[tbank]
                    tbv = tb[:].bitcast(BF16)
                    for jj in range(8):
                        self.tr(tbv[:, jj * 128:(jj + 1) * 128], osb[ob_][:, jj * 128:(jj + 1) * 128], self.cb(C_ID), r=[rosb[ob_], rc], w=[rtb])
                    self.cp("dve", otb[b][:, :, pr * 128:(pr + 1) * 128], tbv.rearrange("p (c s) -> p c s", c=8), r=[rtb], w=[rotb[b]])
                    if pr == 3:
                        self.dma("sp", ov[:, :, t * TS:(t + 1) * TS], otb[b], r=[rotb[b]])

        SK = 3
        N_ = len(units)
        band_setup(0)
        v_setup(0, 0)
        v_setup(0, 1)
        for k in range(SK):
            S_(k)
        for ui in range(N_):
            nxt = ui + SK
            if nxt < N_:
                if units[nxt][0] != units[nxt - 1][0]:
                    band_setup(units[nxt][0])
                S_(nxt)
            PV_(ui)
            t_, pr_, hg_, hl_ = units[ui]
            if t_ + 1 < NT and pr_ == 3 and hl_ == 7:
                if hg_ == 1:
                    v_setup(t_ + 1, 0)
                elif hg_ == 3:
                    v_setup(t_ + 1, 1)
        self.bank_list = list(range(8))
        P.barrier()


    def pre2(self):
        P = self.P
        rc = self.rc
        self.pre_setup()
        self.bank_list = list(range(8))
        w, rw = self.load_w("mla_w_in", 1024, 1728)
        wq, rwq = self.load_w("mla_w_uq", 384, 1536)
        wkv, rwkv = self.load_w("mla_w_ukv", 256, 2048)
        self.alloc_szt()
        c3 = lambda x, n: x.rearrange("p (c s) -> p c s", c=n)
        cq = c3(self.alloc(3 * TS, F32), 3); rcq = P.res()
        ckv = c3(self.alloc(2 * TS, F32), 2); rckv = P.res()
        cqn = c3(self.alloc(3 * TS, BF16), 3); rcqn = P.res()
        ckvn = c3(self.alloc(2 * TS, BF16), 2); rckvn = P.res()
        rp = [c3(self.alloc(4 * TS, F32), 4) for _ in range(2)]; rrp = [P.res() for _ in range(2)]
        qnt = c3(self.alloc(8 * TS, BF16), 8); rqnt = P.res()
        qpt = c3(self.alloc(8 * TS, BF16), 8); rqpt = P.res()
        knt = c3(self.alloc(8 * TS, BF16), 8); rknt = P.res()
        kpt = self.alloc(TS, BF16); rkpt = P.res()
        vt = self.alloc(4 * 1024, BF16).rearrange("p (a f) -> p a f", a=4); rvt = P.res()
        t1 = self.alloc(TS, F32); rt1 = P.res()
        t2 = self.alloc(TS, F32); rt2 = P.res()
        wkv4 = wkv.rearrange("p c (h t d) -> p c h t d", h=8, t=2)
        ev = [0]

        def evac(out, in_, r, w_):
            self.cp("act" if ev[0] % 2 == 0 else "dve", out, in_, r=r, w=w_)
            ev[0] += 1

        def swapped(wt, rwt_, nk, c0, rhs, rrhs):
            pb, rpb = self.bank()
            for k in range(nk):
                self.mm(pb[0:32, :], wt[:, k, c0 + 32:c0 + 64], rhs[:, k, :], k == 0, k == nk - 1, r=[rwt_, rrhs], w=[rpb])
            for k in range(nk):
                self.mm(pb[32:64, :], wt[:, k, c0:c0 + 32], rhs[:, k, :], k == 0, k == nk - 1, r=[rwt_, rrhs], w=[rpb])
            return pb, rpb

        def rope(pa, rpa, pbs, rpbs, ct, st_, rtab, out):
            self.tt("dve", t1[0:64, :], pa[0:64, :], ct, ALU.mult, r=[rpa, rtab], w=[rt1])
            self.tt("dve", t2[0:64, :], pbs[0:64, :], st_, ALU.mult, r=[rpbs, rtab], w=[rt2])
            return t1, t2

        for i, sq, t, hT, rh in self.pre_iter(2):
            b = i % 2
            sl = slice(t * TS, (t + 1) * TS)
            qnv = self.QN.rearrange("(h p) s -> p h s", p=128)
            knv = self.KN.rearrange("(h p) s -> p h s", p=128)
            qpv = self.QP.rearrange("h p s -> p h s")
            v2v = self.V2.rearrange("(n p) f -> p n f", p=128)
            self.dma("pool", rp[b][0:64], self.rope_d[:, :, sl], w=[rrp[b]])
            for c in range(3):
                pb, rpb = self.proj_fm(w, rw, c * 128, hT, rh)
                evac(cq[:, c, :], pb[:], [rpb], [rcq])
            for c in range(2):
                pb, rpb = self.proj_fm(w, rw, 384 + c * 128, hT, rh)
                evac(ckv[:, c, :], pb[:], [rpb], [rckv])
            pa, rpa = self.proj_fm(w, rw, 640, hT, rh, m=64)
            pbs, rpbs = swapped(w, rw, 8, 640, hT, rh)
            self.tt("dve", t1[0:64, :], pa[0:64, :], rp[b][0:64, 2, :], ALU.mult, r=[rpa, rrp[b]], w=[rt1])
            self.tt("dve", t2[0:64, :], pbs[0:64, :], rp[b][0:64, 3, :], ALU.mult, r=[rpbs, rrp[b]], w=[rt2])
            self.tt("pool", kpt[0:64, :], t1[0:64, :], t2[0:64, :], ALU.add, r=[rt1, rt2], w=[rkpt])
            self.dma("sp", self.KP[:, sl], kpt[0:64, :], r=[rkpt])
            self.z_tile(w, rw, 704, hT, rh, t, b)
            self.pre_mid()
            self.rms_to(cq, rcq, 3, 1.0 / 384, lambda c: self.gq[:, c:c + 1], cqn, rcqn, b)
            self.rms_to(ckv, rckv, 2, 1.0 / 256, lambda c: self.gkv[:, c:c + 1], ckvn, rckvn, b)
            for h in range(8):
                pb, rpb = self.bank()
                for c in range(3):
                    self.mm(pb[:], wq[:, c, h * 192:h * 192 + 128], cqn[:, c, :], c == 0, c == 2, r=[rwq, rcqn], w=[rpb])
                self.act(qnt[:, h, :], pb[:], AF.Copy, r=[rpb], w=[rqnt], scale=192.0 ** -0.5)
                pa, rpa = self.bank()
                for c in range(3):
                    self.mm(pa[0:64, :], wq[:, c, h * 192 + 128:h * 192 + 192], cqn[:, c, :], c == 0, c == 2, r=[rwq, rcqn], w=[rpa])
                pbs, rpbs = swapped(wq, rwq, 3, h * 192 + 128, cqn, rcqn)
                self.tt("dve", t1[0:64, :], pa[0:64, :], rp[b][0:64, 0, :], ALU.mult, r=[rpa, rrp[b]], w=[rt1])
                self.tt("dve", t2[0:64, :], pbs[0:64, :], rp[b][0:64, 1, :], ALU.mult, r=[rpbs, rrp[b]], w=[rt2])
                self.tt("pool", qpt[0:64, h, :], t1[0:64, :], t2[0:64, :], ALU.add, r=[rt1, rt2], w=[rqpt])
            for h in range(8):
                pb, rpb = self.bank()
                for c in range(2):
                    self.mm(pb[:], wkv[:, c, h * 256:h * 256 + 128], ckvn[:, c, :], c == 0, c == 1, r=[rwkv, rckvn], w=[rpb])
                evac(knt[:, h, :], pb[:], [rpb], [rknt])
            for sub in range(4):
                for hh in range(2):
                    pb, rpb = self.bank()
                    for c in range(2):
                        self.mm(pb[:].rearrange("p (h d) -> p h d", h=4), ckvn[:, c, sub * 128:(sub + 1) * 128],
                                wkv4[:, c, 4 * hh:4 * hh + 4, 1, :], c == 0, c == 1, r=[rwkv, rckvn], w=[rpb])
                    evac(vt[:, sub, hh * 512:(hh + 1) * 512], pb[:], [rpb], [rvt])
            self.dma("sp", qnv[:, :, sl], qnt, r=[rqnt])
            self.dma("pool", qpv[:, :, sl], qpt[0:64], r=[rqpt])
            self.dma("sp", knv[:, :, sl], knt, r=[rknt])
            self.dma("pool", v2v[:, 4 * t:4 * t + 4, :], vt, r=[rvt])
        P.barrier()

    def mid2(self, sq):
        P = self.P
        rc = self.rc
        self.areset()
        self.bank_i = 0
        self.bank_list = [0, 1, 2, 3]
        kpb = self.alloc(S, BF16); rkp = P.res()
        self.dma("sp", kpb[0:64, :], self.KP[:, :], w=[rkp])
        self.dma("pool", kpb[64:128, :], self.KP[:, :], w=[rkp])
        acc = [[self.alloc(TS, F32) for _ in range(2)] for _ in range(2)]
        racc = [[P.res() for _ in range(2)] for _ in range(2)]
        kng = self.alloc(4 * S, BF16).rearrange("p (h s) -> p h s", h=4); rkn = P.res()
        vg = self.alloc(32 * 512, BF16).rearrange("p (n f) -> p n f", n=32); rvg = P.res()
        qn = [self.alloc(4 * TS, BF16).rearrange("p (h s) -> p h s", h=4) for _ in range(2)]; rqn = [P.res() for _ in range(2)]
        qp = [self.alloc(4 * TS, BF16).rearrange("p (h s) -> p h s", h=4) for _ in range(2)]; rqp = [P.res() for _ in range(2)]
        NPT = 12
        pt = [self.alloc(TS, BF16) for _ in range(NPT)]; rpt = [P.res() for _ in range(NPT)]
        rec = [self.alloc(TS, F32) for _ in range(2)]; rrec = [P.res() for _ in range(2)]
        ot = [self.alloc(TS, BF16) for _ in range(2)]; rot = [P.res() for _ in range(2)]
        qnv = self.QN.rearrange("(h p) s -> p h s", p=128)
        knv = self.KN.rearrange("(h p) s -> p h s", p=128)
        qpv = self.QP.rearrange("h p s -> p h s")
        v2v = self.V2.rearrange("(n p) f -> p n f", p=128)
        ipt = 0
        ih = 0
        sbanks = {}

        def load_q(g, t):
            b = (g * NT + t) % 2
            sl = slice(t * TS, (t + 1) * TS)
            self.dma("sp", qn[b], qnv[:, 4 * g:4 * g + 4, sl], w=[rqn[b]])
            self.dma("pool", qp[b][0:64], qpv[:, 4 * g:4 * g + 4, sl], w=[rqp[b]])
            self.dma("sp", qp[b][64:128], qpv[:, 4 * g:4 * g + 4, sl], w=[rqp[b]])

        def score2(uid, b, hl, m0):
            bs = []
            for m in (m0, m0 + 1):
                sb_, rsb = self.bank()
                self.mm(sb_[:], kng[:, hl, m * 128:(m + 1) * 128], qn[b][:, hl, :], True, False, r=[rkn, rqn[b]], w=[rsb])
                bs.append((sb_, rsb))
            for k_, m in enumerate((m0, m0 + 1)):
                sb_, rsb = bs[k_]
                p0 = 64 * k_
                self.mm(sb_[:], kpb[p0:p0 + 64, m * 128:(m + 1) * 128], qp[b][p0:p0 + 64, hl, :], False, True, r=[rkp, rqp[b]], w=[rsb])
                sbanks[(uid, m)] = (sb_, rsb)

        for g in range(2):
            self.dma("sp", kng[:, 0:2], knv[:, 4 * g:4 * g + 2, :], w=[rkn])
            self.dma("pool", kng[:, 2:4], knv[:, 4 * g + 2:4 * g + 4, :], w=[rkn])
            self.dma("sp", vg[:, 0:16], v2v[:, 0:16, 512 * g:512 * g + 512], w=[rvg])
            self.dma("pool", vg[:, 16:32], v2v[:, 16:32, 512 * g:512 * g + 512], w=[rvg])
            load_q(g, 0)
            for t in range(NT):
                if t + 1 < NT:
                    load_q(g, t + 1)
                b = (g * NT + t) % 2
                sl = slice(t * TS, (t + 1) * TS)
                for hl in range(4):
                    h = 4 * g + hl
                    uid = (g, t, hl)
                    hb = ih % 2
                    ih += 1
                    OT, rOT = self.ps[4 + hb], self.psr[4 + hb]
                    DS, rDS = self.ps[6 + hb], self.psr[6 + hb]
                    if hl < 3:
                        nxt = ((g, t, hl + 1), b, hl + 1)
                    elif t + 1 < NT:
                        nxt = ((g, t + 1, 0), 1 - b, 0)
                    else:
                        nxt = None
                    if (uid, 0) not in sbanks:
                        score2(uid, b, hl, 0)
                    for m in range(32):
                        if m % 2 == 0:
                            if m + 2 < 32:
                                score2(uid, b, hl, m + 2)
                            elif nxt is not None:
                                score2(nxt[0], nxt[1], nxt[2], 0)
                        sb_, rsb = sbanks.pop((uid, m))
                        pi = ipt % NPT
                        ipt += 1
                        self.act(pt[pi], sb_[:], AF.Exp, r=[rsb], w=[rpt[pi]])
                        self.mm(OT[:], vg[:, m, hl * 128:(hl + 1) * 128], pt[pi], m == 0, m == 31, r=[rvg, rpt[pi]], w=[rOT])
                        e_ = m % 2
                        eng = "pool" if e_ == 0 else "dve"
                        if m < 2:
                            self.cp(eng, acc[hb][e_], pt[pi], r=[rpt[pi]], w=[racc[hb][e_]])
                        else:
                            self.tt(eng, acc[hb][e_], acc[hb][e_], pt[pi], ALU.add, r=[rpt[pi], racc[hb][e_]], w=[racc[hb][e_]])
                    self.mm(DS[:], self.cf(C_ONES), acc[hb][0], True, False, r=[rc, racc[hb][0]], w=[rDS])
                    self.mm(DS[:], self.cf(C_ONES), acc[hb][1], False, True, r=[rc, racc[hb][1]], w=[rDS])
                    self.recip(rec[hb], DS[:], r=[rDS], w=[rrec[hb]])
                    self.tt("dve", ot[hb], OT[:], rec[hb], ALU.mult, r=[rOT, rrec[hb]], w=[rot[hb]])
                    self.dma("sp", self.O[h * 128:(h + 1) * 128, sl], ot[hb], r=[rot[hb]])
        self.bank_list = list(range(8))
        P.barrier()

    def pre3(self):
        P = self.P
        rc = self.rc
        self.pre_setup()
        self.bank_list = list(range(8))
        w, rw = self.load_w("hg_w_in", 1024, 5120)
        self.alloc_szt()
        qst = self.alloc(8 * TS, BF16).rearrange("p (c s) -> p c s", c=8); rqst = P.res()
        lf = self.alloc(8 * TS, F32).rearrange("p (c s) -> p c s", c=8); rlf = P.res()
        vt = self.alloc(4 * 1024, BF16).rearrange("p (a f) -> p a f", a=4); rvt = P.res()
        ev = 0
        for i, sq, t, hT, rh in self.pre_iter(3):
            b = i % 2
            sl = slice(t * TS, (t + 1) * TS)
            qsv = self.QS.rearrange("(c p) s -> p c s", p=128)
            viv = self.VI.rearrange("(n p) f -> p n f", p=128)
            for j in range(8):
                pb, rpb = self.proj_fm(w, rw, j * 128, hT, rh)
                self.act(qst[:, j, :], pb[:], AF.Silu, r=[rpb], w=[rqst])
            self.dma("sp", qsv[:, :, sl], qst, r=[rqst])
            for d in range(2):
                for j in range(8):
                    pb, rpb = self.proj_fm(w, rw, 1024 * (1 + d) + j * 128, hT, rh)
                    self.act(lf[:, j, :], pb[:], AF.Sigmoid, r=[rpb], w=[rlf])
                    self.ts("dve", lf[:, j, :], lf[:, j, :], self.oml[:, d, j:j + 1], self.lb[:, d, j:j + 1], ALU.mult, ALU.add,
                            r=[rlf, rc], w=[rlf])
                self.act(lf, lf, AF.Ln, r=[rlf], w=[rlf])
                lfv = self.LF[d].rearrange("(c p) s -> p c s", p=128)
                self.dma("sp", lfv[:, 0:4, sl], lf[:, 0:4, :], r=[rlf])
                self.dma("pool", lfv[:, 4:8, sl], lf[:, 4:8, :], r=[rlf])
                if d == 0:
                    self.pre_mid()
            for sub in range(4):
                for ct in range(2):
                    pb, rpb = self.proj_tm(w, rw, 3072 + ct * 512, hT, rh, sub)
                    self.cp("act" if ev % 2 == 0 else "dve", vt[:, sub, ct * 512:(ct + 1) * 512], pb[:], r=[rpb], w=[rvt])
                    ev += 1
            self.dma("pool", viv[:, 4 * t:4 * t + 4, :], vt, r=[rvt])
            self.z_tile(w, rw, 4096, hT, rh, t, b)
        P.barrier()

    def mid3(self, sq):
        P = self.P
        rc = self.rc
        self.areset()
        self.bank_i = 0
        self.bank_list = [0, 1, 2, 3, 4, 5]
        L = 128
        NCH = 32
        rmask = self.alloc(1024, BF16); rrm = P.res()
        self.memset("pool", rmask, 1.0, w=[rrm])
        self.memset("pool", rmask.rearrange("p (n l) -> p n l", l=L)[:, :, 0:1], 0.0, w=[rrm])
        qs = self.alloc(S, BF16); rqs = P.res()
        lf = self.alloc(S, F32); rlf = P.res()
        C = self.alloc(S, F32); rC = P.res()
        E = self.alloc(S, F32); rE = P.res()
        qtb = [[self.alloc(S, BF16) for _ in range(2)] for _ in range(2)]; rqtb = [[P.res() for _ in range(2)] for _ in range(2)]
        ktb_ = [[self.alloc(S, BF16) for _ in range(2)] for _ in range(2)]; rktb = [[P.res() for _ in range(2)] for _ in range(2)]
        eclb = [[self.alloc(NCH, F32) for _ in range(2)] for _ in range(2)]; reclb = [[P.res() for _ in range(2)] for _ in range(2)]
        vhb = [self.alloc(S, BF16).rearrange("p (n d) -> p n d", n=NCH) for _ in range(2)]; rvhb = [P.res() for _ in range(2)]
        ktm = [self.alloc(S, BF16).rearrange("p (n c) -> p n c", n=NCH) for _ in range(2)]; rktm = [P.res() for _ in range(2)]
        Sb = [self.alloc(S, BF16).rearrange("p (n d) -> p n d", n=NCH) for _ in range(2)]; rSb = [P.res() for _ in range(2)]
        Wst = [[self.alloc(128, F32) for _ in range(2)] for _ in range(2)]; rW = [[P.res() for _ in range(2)] for _ in range(2)]
        oT = self.alloc(S, BF16); roT = P.res()
        Am4 = [self.alloc(4 * 256, BF16).rearrange("p (k x) -> p k x", k=4) for _ in range(2)]; rAm4 = [P.res() for _ in range(2)]
        on4 = [self.alloc(512, BF16) for _ in range(2)]; ron4 = [P.res() for _ in range(2)]
        sq4 = [self.alloc(512, F32) for _ in range(2)]; rsq4 = [P.res() for _ in range(2)]
        ss = self.alloc(NCH, F32); rss = P.res()
        rs_ = self.alloc(NCH, F32); rrs = P.res()
        viv = self.VI.rearrange("(n p) f -> p n f", p=128)
        ch = lambda x, n: x[:, n * L:(n + 1) * L]

        def scan_op(dst, src, r, w):
            for q4 in range(4):
                sl_ = slice(q4 * 1024, (q4 + 1) * 1024)
                self.P.op("dve", lambda e, sl_=sl_: e.tensor_tensor_scan(out=dst[:, sl_], data0=rmask, data1=src[:, sl_], initial=0.0,
                                                                          op0=ALU.mult, op1=ALU.add), r=r, w=w)

        def phaseA(h, sl):
            hs = slice(h * 128, (h + 1) * 128)
            qt, kt, ecl = qtb[sl], ktb_[sl], eclb[sl]
            rqt, rkt, recl = rqtb[sl], rktb[sl], reclb[sl]
            T = []
            T.append(lambda: self.dma("sp", qs, self.QS[hs, :], w=[rqs]))
            T.append(lambda: self.dma("pool", vhb[sl], viv[:, :, hs], w=[rvhb[sl]]))
            for d in range(2):
                T.append(lambda d=d: self.dma("sp", lf, self.LF[d, hs, :], w=[rlf]))
                T.append(lambda: scan_op(C, lf, [rrm, rlf], [rC]))
                Cl = C.rearrange("p (n l) -> p n l", l=L)[:, :, L - 1]
                if d == 0:
                    T.append(lambda: self.act(E, C, AF.Exp, r=[rC], w=[rE]))
                    T.append(lambda: self.cp("dve", ecl[0], E.rearrange("p (n l) -> p n l", l=L)[:, :, L - 1], r=[rE], w=[recl[0]]))
                    T.append(lambda: self.act(C, C, AF.Exp, r=[rC], w=[rC], scale=-1.0))
                else:
                    T.append(lambda: self.act(ecl[1], Cl, AF.Exp, r=[rC], w=[recl[1]]))
                    T.append(lambda: self.tt("dve", C, C, lf, ALU.subtract, r=[rC, rlf], w=[rC]))
                    T.append(lambda: self.act(E, C, AF.Exp, r=[rC], w=[rE], scale=-1.0))
                    T.append(lambda: self.act(C, C, AF.Exp, r=[rC], w=[rC]))
                T.append(lambda d=d: self.tt("dve", qt[d], qs, E, ALU.mult, r=[rqs, rE], w=[rqt[d]]))
                T.append(lambda: self.act(E, lf, AF.Exp, r=[rlf], w=[rE]))
                T.append(lambda: self.ts("dve", E, E, -1.0, 1.0, ALU.mult, ALU.add, r=[rE], w=[rE]))
                T.append(lambda d=d: self.tt("dve", kt[d], E, C, ALU.mult, r=[rE, rC], w=[rkt[d]]))
            return T

        pend = phaseA(0, 0)
        for f in pend:
            f()
        for h in range(8):
            sl = h % 2
            hs = slice(h * 128, (h + 1) * 128)
            qt, kt, ecl, vh = qtb[sl], ktb_[sl], eclb[sl], vhb[sl]
            rqt, rkt, recl, rvh = rqtb[sl], rktb[sl], reclb[sl], rvhb[sl]
            pend = phaseA(h + 1, 1 - sl) if h + 1 < 8 else []

            def pop(n=1):
                for _ in range(n):
                    if pend:
                        pend.pop(0)()

            ev = 0
            for d in range(2):
                for n8 in range(4):
                    tb, rtb = self.bank()
                    tbv = tb[:].bitcast(BF16)
                    for k in range(8):
                        n = n8 * 8 + k
                        self.tr(tbv[:, k * 128:(k + 1) * 128], ch(kt[d], n), self.cb(C_ID), r=[rkt[d], rc], w=[rtb])
                    self.cp("act" if ev % 2 == 0 else "dve", ktm[d][:, n8 * 8:(n8 + 1) * 8, :].rearrange("p n c -> p (n c)"), tbv, r=[rtb], w=[rktm[d]])
                    ev += 1
            self.memset("pool", Sb[0][:, 0, :], 0.0, w=[rSb[0]])
            self.memset("pool", Sb[1][:, NCH - 1, :], 0.0, w=[rSb[1]])
            ub = [None, None]
            wi = [0, 0]
            for i in range(NCH):
                for d in range(2):
                    n = i if d == 0 else NCH - 1 - i
                    if i % 4 == 0:
                        ub[d] = self.bank()
                        for k in range(4):
                            nn = i + k if d == 0 else NCH - 1 - i - k
                            self.mm(ub[d][0][:, k * 128:(k + 1) * 128], ktm[d][:, nn, :], vh[:, nn, :], True, True, r=[rktm[d], rvh], w=[ub[d][1]])
                    U = ub[d][0][:, (i % 4) * 128:(i % 4 + 1) * 128]
                    rU = ub[d][1]
                    cur = wi[d]
                    if i == 0:
                        self.cp("dve", Wst[d][cur], U, r=[rU], w=[rW[d][cur]])
                    else:
                        ei = (i - 1) if d == 0 else n
                        esc = ecl[d][:, ei:ei + 1]
                        self.act(Sb[d][:, n, :], Wst[d][cur], AF.Copy, r=[rW[d][cur], recl[d]], w=[rSb[d]], scale=esc)
                        nxt = 1 - cur
                        self.stt(Wst[d][nxt], Wst[d][cur], esc, U, ALU.mult, ALU.add, r=[rW[d][cur], recl[d], rU], w=[rW[d][nxt]])
                        wi[d] = nxt
                if i % 4 == 3:
                    pop(1)
            mask2 = self.cf(C_MU, 256).unsqueeze(1).to_broadcast([128, 2, 256])
            for gq in range(NCH // 4):
                gi = gq % 2
                for half in range(2):
                    ab, rab = self.bank()
                    for k2 in range(2):
                        n = gq * 4 + half * 2 + k2
                        self.mm(ab[:, k2 * 256:k2 * 256 + 128], ch(kt[0], n), ch(qt[0], n), True, True, r=[rkt[0], rqt[0]], w=[rab])
                        self.mm(ab[:, k2 * 256 + 128:k2 * 256 + 256], ch(kt[1], n), ch(qt[1], n), True, True, r=[rkt[1], rqt[1]], w=[rab])
                    self.tt("dve", Am4[gi][:, half * 2:half * 2 + 2, :], ab[:].rearrange("p (k x) -> p k x", k=2), mask2, ALU.mult,
                            r=[rab, rc], w=[rAm4[gi]])
                ob, rob = self.bank()
                ov4 = ob[:].rearrange("p (k d) -> p k d", k=4)
                for k in range(4):
                    n = gq * 4 + k
                    o = ov4[:, k, :]
                    self.mm(o, Am4[gi][:, k, 0:128], vh[:, n, :], True, False, r=[rAm4[gi], rvh], w=[rob])
                    self.mm(o, ch(qt[0], n), Sb[0][:, n, :], False, False, r=[rqt[0], rSb[0]], w=[rob])
                    self.mm(o, Am4[gi][:, k, 128:256], vh[:, n, :], False, False, r=[rAm4[gi], rvh], w=[rob])
                    self.mm(o, ch(qt[1], n), Sb[1][:, n, :], False, True, r=[rqt[1], rSb[1]], w=[rob])
                self.act(sq4[gi], ob[:], AF.Square, r=[rob], w=[rsq4[gi]])
                ssg = ss[:, gq * 4:gq * 4 + 4]
                rsg = rs_[:, gq * 4:gq * 4 + 4]
                self.P.op("dve", lambda e, o_=ssg, i_=sq4[gi].rearrange("p (k d) -> p k d", k=4): e.tensor_reduce(out=o_, in_=i_, axis=AX.X, op=ALU.add),
                          r=[rsq4[gi]], w=[rss])
                self.act(rsg, ssg, AF.Sqrt, r=[rss, rc], w=[rrs], scale=1.0 / 128, bias=self.epsb[:, 0:1])
                self.recip(rsg, rsg, r=[rrs], w=[rrs])
                self.tt("dve", on4[gi].rearrange("p (k d) -> p k d", k=4), ov4, rsg.unsqueeze(2).to_broadcast([128, 4, 128]), ALU.mult,
                        r=[rob, rrs], w=[ron4[gi]])
                if gq % 2 == 0:
                    ti = 6 + (gq // 2) % 2
                    tb, rtb = self.ps[ti], self.psr[ti]
                    tbv = tb[:].bitcast(BF16)
                for k in range(4):
                    col = ((gq % 2) * 4 + k) * 128
                    self.tr(tbv[:, col:col + 128], on4[gi][:, k * 128:(k + 1) * 128], self.cb(C_ID), r=[ron4[gi], rc], w=[rtb])
                if gq % 2 == 1:
                    n0 = (gq - 1) * 4
                    self.act(oT[:, n0 * L:(n0 + 8) * L], tbv, AF.Copy, r=[rtb, rc], w=[roT], scale=self.gout[:, h:h + 1])
                pop(2)
            pop(100)
            self.dma("sp", self.O[hs, :], oT, r=[roT])
        self.bank_list = list(range(8))
        P.barrier()


_CACHE = {}


def _common_inputs(inp):
    cst, m1, rope = make_consts()
    d = {"cst": cst, "m1": m1, "rope": rope}
    for name, k, n in W_SPECS:
        d[name] = np.ascontiguousarray(np.asarray(inp[name], np.float32)[0])
    d["ple_w"] = np.ascontiguousarray(np.asarray(inp["ple_w"], np.float32))
    d["ple_gate_w"] = np.ascontiguousarray(np.asarray(inp["ple_gate_w"], np.float32))
    d["norm_g"] = np.ascontiguousarray(np.asarray(inp["norm_g"], np.float32))
    d["final_g"] = np.ascontiguousarray(np.asarray(inp["final_g"], np.float32))
    d["fn_w_mix"] = np.ascontiguousarray(np.asarray(inp["fn_w_mix"], np.float32)[0])
    d["rpb_flip"] = np.ascontiguousarray(np.asarray(inp["na_rpb"], np.float32)[0][:, :, ::-1])
    d["mla_g_q"] = np.ascontiguousarray(np.asarray(inp["mla_g_q"], np.float32)[0])
    d["mla_g_kv"] = np.ascontiguousarray(np.asarray(inp["mla_g_kv"], np.float32)[0])
    d["hg_lb_raw"] = np.ascontiguousarray(np.asarray(inp["hg_lb_raw"], np.float32))
    d["hg_g_out"] = np.ascontiguousarray(np.asarray(inp["hg_g_out"], np.float32)[0])
    return d


def run_seqs(inp, xs, ps, nseq, layers, final_norm, ncores, trace=False):
    key = (nseq, tuple(layers), final_norm)
    if key not in _CACHE:
        bld = Builder(nseq, layers, final_norm)
        nc = bld.build()
        _CACHE[key] = (nc, bld.stats)
    nc, stats = _CACHE[key]
    com = _common_inputs(inp)
    in_maps = []
    for c in range(ncores):
        m = dict(com)
        m["xT"] = np.ascontiguousarray(np.transpose(xs[c], (0, 2, 1)))
        m["pT"] = np.ascontiguousarray(np.transpose(ps[c], (0, 1, 3, 2)))
        in_maps.append(m)
    if trace:
        res = run_bass_kernel_spmd(nc, in_maps, core_ids=list(range(ncores)), trace=True)
        print('EXEC_TIME_NS', res.exec_time_ns)
    else:
        res = run_bass_kernel_spmd(nc, in_maps, core_ids=list(range(ncores)))
    return [np.transpose(r["yT"], (0, 2, 1)) for r in res.results]


def kernel(**inp):
    xp = np.asarray(inp["x_prompt"], np.float32)
    xsm = np.asarray(inp["x_sample"], np.float32)
    pp = np.asarray(inp["p_prompt"], np.float32)
    psm = np.asarray(inp["p_sample"], np.float32)
    xall = np.concatenate([xp, xsm], 0)
    pall = np.concatenate([pp, psm], 1)
    nseq = 3
    idx = [[c, c + 8, (c + 16) if c + 16 < 20 else c] for c in range(8)]
    xs = [xall[idx[c]] for c in range(8)]
    ps = [pall[:, idx[c]] for c in range(8)]
    outs = run_seqs(inp, xs, ps, nseq, [0, 1, 2, 3], True, 8)
    yall = np.zeros_like(xall)
    for c in range(8):
        for s_, gi in enumerate(idx[c]):
            if s_ < 2 or c + 16 < 20:
                yall[gi] = outs[c][s_]
    return (np.ascontiguousarray(yall[:4]), np.ascontiguousarray(yall[4:]))
```

```python
import math
import numpy as np
from contextlib import ExitStack
import concourse.bass as bass
import concourse.mybir as mybir
from concourse.bass_utils import run_bass_kernel_spmd

F32 = mybir.dt.float32
BF16 = mybir.dt.bfloat16
AF = mybir.ActivationFunctionType
ALU = mybir.AluOpType
AX = mybir.AxisListType

ENGS = ("pe", "act", "dve", "pool", "sp")
S = 4096
D = 1024
NT = 8
TS = 512
EPS = 1e-6
NEG = -30000.0


class Res:
    __slots__ = ("name", "w", "rs")

    def __init__(self, name):
        self.name = name
        self.w = None
        self.rs = []


class Op:
    __slots__ = ("eng", "fn", "deps", "dma", "ticket", "inc")

    def __init__(self, eng, fn, dma):
        self.eng = eng
        self.fn = fn
        self.deps = []
        self.dma = dma
        self.ticket = None
        self.inc = dma


class Prog:
    DMA_K = 8
    DMA_CAP = 1500
    CMP_CAP = 30000

    def __init__(self, nc, stack):
        self.nc = nc
        self.stack = stack
        self.ops = []
        self.eobj = {"pe": nc.tensor, "act": nc.scalar, "dve": nc.vector, "pool": nc.gpsimd, "sp": nc.sync}
        self.dma_hist = {e: [] for e in ENGS}
        self.last = {e: None for e in ENGS}
        self.nres = 0

    def res(self, name=None):
        self.nres += 1
        return Res(name or f"r{self.nres}")

    def op(self, eng, fn, r=(), w=(), dma=False):
        o = Op(eng, fn, dma)
        deps = {}
        for x in r:
            if x.w is not None:
                deps[id(x.w)] = (x.w, "raw")
        for x in w:
            if x.w is not None and id(x.w) not in deps:
                deps[id(x.w)] = (x.w, "waw")
            for rd in x.rs:
                if id(rd) not in deps:
                    deps[id(rd)] = (rd, "war")
        if dma:
            h = self.dma_hist[eng]
            if len(h) >= self.DMA_K:
                p = h[-self.DMA_K]
                deps.setdefault(id(p), (p, "rot"))
            h.append(o)
        for d, kind in deps.values():
            if d is o:
                continue
            if (not d.dma) and (not dma) and d.eng == eng:
                if eng == "pe" or kind != "raw":
                    continue
            o.deps.append(d)
            d.inc = True
        for x in w:
            x.w = o
            x.rs = []
        for x in r:
            x.rs.append(o)
        self.ops.append(o)
        if not dma:
            self.last[eng] = o
        return o

    def barrier(self):
        deps = []
        for e in ENGS:
            if self.last[e] is not None:
                deps.append(self.last[e])
            deps.extend(self.dma_hist[e][-self.DMA_K:])
        for e in ENGS:
            o = Op(e, None, False)
            for d in deps:
                if (not d.dma) and d.eng == e:
                    continue
                o.deps.append(d)
                d.inc = True
            self.ops.append(o)

    def emit(self):
        nc = self.nc
        cmp_sem = {}
        cmp_cnt = {}
        dma_sems = {}
        dma_idx = {e: 0 for e in ENGS}
        known = {e: {} for e in ENGS}
        self.nsem = 0

        def newsem(nm):
            self.nsem += 1
            return self.stack.enter_context(nc.semaphore(f"{nm}_{self.nsem}"))

        nwait = 0
        for o in self.ops:
            E = self.eobj[o.eng]
            kn = known[o.eng]
            for d in o.deps:
                sem, val = d.ticket
                key = id(sem)
                if kn.get(key, 0) >= val:
                    continue
                E.wait_ge(sem, val)
                nwait += 1
                kn[key] = val
            if o.fn is None:
                continue
            ins = o.fn(E)
            if o.dma:
                i = dma_idx[o.eng]
                dma_idx[o.eng] = i + 1
                st = i // (self.DMA_K * self.DMA_CAP)
                j = i % (self.DMA_K * self.DMA_CAP)
                keyq = (o.eng, st)
                if keyq not in dma_sems:
                    dma_sems[keyq] = [newsem(f"d{o.eng}") for _ in range(self.DMA_K)]
                sem = dma_sems[keyq][j % self.DMA_K]
                val = 16 * (j // self.DMA_K + 1)
                ins.then_inc(sem, 16)
                o.ticket = (sem, val)
            elif o.inc:
                c = cmp_cnt.get(o.eng, 0)
                if o.eng not in cmp_sem or c >= self.CMP_CAP:
                    cmp_sem[o.eng] = newsem(f"c{o.eng}")
                    c = 0
                c += 1
                cmp_cnt[o.eng] = c
                ins.then_inc(cmp_sem[o.eng], 1)
                o.ticket = (cmp_sem[o.eng], c)
            o.fn = None
        return dict(nops=len(self.ops), nwait=nwait, nsem=self.nsem)


C_ONES, C_ID, C_CC, C_SC, C_M3, C_MU, C_ML, C_MC = 0, 128, 256, 384, 512, 640, 768, 896
C_TOT = 900


def make_consts():
    c = np.zeros((128, C_TOT), np.float64)
    c[:, C_ONES:C_ONES + 128] = 1.0
    c[:, C_ID:C_ID + 128] = np.eye(128)
    a = np.arange(128)
    ang = 2 * np.pi * np.outer(a, a) / 128.0
    c[:, C_CC:C_CC + 128] = np.cos(ang) / math.sqrt(128.0)
    c[:, C_SC:C_SC + 128] = np.sin(ang) / math.sqrt(128.0)
    s0 = np.arange(64)
    k1 = np.arange(64)
    phi = 2 * np.pi * np.outer(s0, k1) / 64.0
    m3 = np.zeros((128, 128))
    m3[0:64, 0:64] = np.cos(phi)
    m3[64:128, 0:64] = np.sin(phi)
    m3[0:64, 64:128] = -np.sin(phi)
    m3[64:128, 64:128] = np.cos(phi)
    c[:, C_M3:C_M3 + 128] = m3 / 8.0
    c[:, C_MU:C_MU + 128] = (a[:, None] <= a[None, :])
    c[:, C_ML:C_ML + 128] = (a[:, None] >= a[None, :])
    for hm in range(4):
        c[:, C_MC + hm] = np.where(a // 32 == hm, 32.0 ** -0.5, 0.0)
    m1 = np.zeros((128, 32, 128))
    p = np.arange(128)
    s1 = (p % 64)[:, None, None]
    half = (p // 64)[:, None, None]
    j = np.arange(32)[None, :, None]
    k0 = np.arange(64)[None, None, :]
    s0v = 32 * half + j
    th = 2 * np.pi * (s1 * k0 / 64.0 + s0v * k0 / 4096.0)
    m1[:, :, 0:64] = np.cos(th) / 8.0
    m1[:, :, 64:128] = -np.sin(th) / 8.0
    half_d = 32
    inv = 10000.0 ** (-np.arange(half_d, dtype=np.float64) / half_d)
    inv = inv.astype(np.float32).astype(np.float64)
    pos = np.arange(S, dtype=np.float64)
    angf = (pos[None, :].astype(np.float32) * inv[:, None].astype(np.float32)).astype(np.float64)
    cs = np.cos(angf)
    sn = np.sin(angf)
    cosT = np.concatenate([cs, cs], 0)
    sinT = np.concatenate([-sn, sn], 0)
    sc = 192.0 ** -0.5
    rope = np.stack([cosT * sc, sinT * sc, cosT, sinT], 1)
    return c.astype(np.float32), m1.reshape(128, 32 * 128).astype(np.float32), rope.astype(np.float32)


def na_rows():
    kh = 8
    r = np.arange(64)
    rs = np.clip(r - kh // 2, 0, 64 - kh)
    return rs


W_SPECS = [
    ("fn_w_in", 1024, 2048), ("fn_w_out", 1024, 1024),
    ("na_w_in", 1024, 4096), ("na_w_out", 1024, 1024),
    ("mla_w_in", 1024, 1728), ("mla_w_uq", 384, 1536), ("mla_w_ukv", 256, 2048), ("mla_w_out", 1024, 1024),
    ("hg_w_in", 1024, 5120), ("hg_w_out", 1024, 1024),
]


class Builder:
    def __init__(self, nseq, layers, final_norm=True):
        self.nseq = nseq
        self.layers = layers
        self.final_norm = final_norm

    def dma(self, q, out, in_, r=(), w=()):
        if q == "pool" and getattr(self, "pool_busy", False):
            q = "sp"
        return self.P.op(q, lambda e: e.dma_start(out=out, in_=in_), r=r, w=w, dma=True)

    def dma_slow(self, q, out, in_, r=(), w=()):
        return self.P.op(q, lambda e: e.dma_start(out=out, in_=in_, allow_slow_non_contiguous=True), r=r, w=w, dma=True)

    def mm(self, out, lhsT, rhs, start, stop, r=(), w=()):
        return self.P.op("pe", lambda e: e.matmul(out, lhsT=lhsT, rhs=rhs, start=start, stop=stop), r=r, w=w)

    def tr(self, out, in_, ident, r=(), w=()):
        return self.P.op("pe", lambda e: e.transpose(out=out, in_=in_, identity=ident), r=r, w=w)

    def act(self, out, in_, func, r=(), w=(), scale=1.0, bias=None, accum=None):
        def f(e):
            kw = {}
            if bias is not None:
                kw["bias"] = bias
            if accum is not None:
                kw["accum_out"] = accum
            return e.activation(out=out, in_=in_, func=func, scale=scale, **kw)
        return self.P.op("act", f, r=r, w=w)

    def tt(self, eng, out, a, b, op, r=(), w=()):
        return self.P.op(eng, lambda e: e.tensor_tensor(out=out, in0=a, in1=b, op=op), r=r, w=w)

    def ts(self, eng, out, a, s1, s2, op0, op1=None, r=(), w=()):
        def f(e):
            if op1 is None:
                return e.tensor_scalar(out=out, in0=a, scalar1=s1, scalar2=None, op0=op0)
            return e.tensor_scalar(out=out, in0=a, scalar1=s1, scalar2=s2, op0=op0, op1=op1)
        return self.P.op(eng, f, r=r, w=w)

    def stt(self, out, in0, scalar, in1, op0, op1, r=(), w=()):
        return self.P.op("dve", lambda e: e.scalar_tensor_tensor(out=out, in0=in0, scalar=scalar, in1=in1, op0=op0, op1=op1), r=r, w=w)

    def cp(self, eng, out, in_, r=(), w=()):
        if eng == "act":
            return self.act(out, in_, AF.Copy, r=r, w=w)
        return self.P.op(eng, lambda e: e.tensor_copy(out=out, in_=in_), r=r, w=w)

    def memset(self, eng, ap, val, r=(), w=()):
        return self.P.op(eng, lambda e: e.memset(ap, val), r=r, w=w)

    def recip(self, out, in_, r=(), w=()):
        return self.P.op("dve", lambda e: e.reciprocal(out=out, in_=in_), r=r, w=w)

    def areset(self):
        self.aoff = 0

    def alloc(self, cols, dt=BF16):
        nb = cols * (4 if dt == F32 else 2)
        nb = (nb + 63) // 64 * 64
        off = self.aoff
        self.aoff += nb
        assert self.aoff <= self.asize, f"arena overflow {self.aoff} > {self.asize}"
        v = self.arena[:, off // 2:(off + nb) // 2]
        if dt == F32:
            v = v.bitcast(F32)
            return v[:, 0:cols]
        return v[:, 0:cols]

    def bank(self):
        bl = self.bank_list
        i = bl[self.bank_i % len(bl)]
        self.bank_i += 1
        return self.ps[i], self.psr[i]

    def build(self):
        nc = bass.Bass("TRN2", target_bir_lowering=False)
        self.nc = nc
        NS = self.nseq
        dt_in = lambda name, shape: nc.dram_tensor(name, shape, F32, kind="ExternalInput").ap()
        self.xT = dt_in("xT", [NS, D, S])
        self.pT = dt_in("pT", [4, NS, 256, S])
        self.cst_d = dt_in("cst", [128, C_TOT])
        self.m1_d = dt_in("m1", [128, 32 * 128])
        self.rope_d = dt_in("rope", [64, 4, S])
        self.wd = {}
        for name, k, n in W_SPECS:
            self.wd[name] = dt_in(name, [k, n])
        self.wd["ple_w"] = dt_in("ple_w", [4, 256, D])
        self.wd["ple_gate_w"] = dt_in("ple_gate_w", [4, D, D])
        self.norm_g_d = dt_in("norm_g", [4, D])
        self.final_g_d = dt_in("final_g", [D])
        self.fn_w_mix_d = dt_in("fn_w_mix", [8, 128, 128])
        self.rpbf_d = dt_in("rpb_flip", [32, 15, 31])
        self.g_q_d = dt_in("mla_g_q", [384])
        self.g_kv_d = dt_in("mla_g_kv", [256])
        self.lb_d = dt_in("hg_lb_raw", [4, 2, D])
        self.g_out_d = dt_in("hg_g_out", [D])
        self.yT = nc.dram_tensor("yT", [NS, D, S], F32, kind="ExternalOutput").ap()
        scr = lambda name, shape, dt=BF16: nc.dram_tensor(name, shape, dt, kind="Internal").ap()
        self.X = scr("X", [NS, D, S], F32)
        self.SA = {}
        for nm, shp, dt_ in [("SZ", [D, S], BF16), ("O", [D, S], BF16), ("Ud", [S, D], BF16), ("Td", [2, 64, 64, D], BF16),
                             ("QT", [D, S], BF16), ("KT", [D, S], BF16), ("VA", [S, 32 * 33], BF16),
                             ("QN", [D, S], BF16), ("QP", [8, 64, S], BF16), ("KN", [D, S], BF16), ("KP", [64, S], BF16),
                             ("V2", [S, D], BF16), ("QS", [D, S], BF16), ("LF", [2, D, S], F32), ("VI", [S, D], BF16)]:
            self.SA[nm] = scr(nm, [NS] + shp, dt_)
        self.w16 = {}
        for name, k, n in W_SPECS:
            self.w16[name] = scr("b_" + name, [k, n])
        self.w16["ple_w"] = scr("b_ple_w", [4, 256, D])
        self.w16["ple_gate_w"] = scr("b_ple_gate_w", [4, D, D])
        self.BT = scr("BT", [128, 32 * 14 * 64])

        with ExitStack() as st:
            self.P = Prog(nc, st)
            P = self.P
            sbt = lambda name, shape, dt: st.enter_context(nc.sbuf_tensor(name, shape, dt))
            self.cst = sbt("cstf", [128, C_TOT], F32)
            self.cstb = sbt("cstb", [128, C_TOT], BF16)
            self.ng = sbt("ng", [128, 4, 8], F32)
            self.fg = sbt("fg", [128, 8], F32)
            self.gq = sbt("gq", [128, 3], F32)
            self.gkv = sbt("gkv", [128, 2], F32)
            self.gout = sbt("gout", [128, 8], F32)
            self.lbraw = sbt("lbraw", [128, 4, 2, 8], F32)
            self.lb = sbt("lb", [128, 2, 8], F32)
            self.oml = sbt("oml", [128, 2, 8], F32)
            self.epsb = sbt("epsb", [128, 1], F32)
            self.asize = 200 * 1024
            self.arena = sbt("arena", [128, self.asize // 2], BF16)
            self.ps = [st.enter_context(nc.psum_tensor(f"ps{i}", [128, 512], F32)) for i in range(8)]
            self.psr = [P.res(f"ps{i}") for i in range(8)]
            self.bank_i = 0
            self.bank_list = list(range(8))
            self.rc = P.res("consts")

            self.phase_prep()
            for li in self.layers:
                last = (li == self.layers[-1])
                [self.pre0, self.pre1, self.pre2, self.pre3][li]()
                for sq in range(NS):
                    self.bind(sq)
                    [self.mid0, self.mid1, self.mid2, self.mid3][li](sq)
                self.post_layer(li, last, last and self.final_norm)
            stats = P.emit()
            self.stats = stats
        return nc

    def bind(self, sq):
        for nm, ap in self.SA.items():
            setattr(self, nm, ap[sq])

    def xsrc_of(self, li, sq):
        return self.xT[sq] if li == self.layers[0] else self.X[sq]

    def pre_iter(self, li):
        tiles = [(sq, t) for sq in range(self.nseq) for t in range(NT)]
        self._pre_pending = None
        self.pre_load(li, self.xsrc_of(li, tiles[0][0]), tiles[0][1], 0)
        res = [self.pre_norm(li, 0)]
        for i, (sq, t) in enumerate(tiles):
            cur = res[0]
            if i + 1 < len(tiles):
                self.pre_load(li, self.xsrc_of(li, tiles[i + 1][0]), tiles[i + 1][1], i + 1)

                def pend(i=i):
                    res[0] = self.pre_norm(li, i + 1)
                self._pre_pending = pend
            self.bind(sq)
            yield i, sq, t, cur[0], cur[1]
            self.pre_mid()

    def cb(self, off, n=128):
        return self.cstb[:, off:off + n]

    def cf(self, off, n=128):
        return self.cst[:, off:off + n]

    def phase_prep(self):
        P = self.P
        rc = self.rc
        self.areset()
        self.dma("sp", self.cst[:], self.cst_d[:, :], w=[rc])
        self.cp("dve", self.cstb[:], self.cst[:], r=[rc], w=[rc])
        self.memset("pool", self.epsb[:], EPS, w=[rc])
        self.dma_slow("sp", self.ng[:], self.norm_g_d.rearrange("l (c p) -> p l c", p=128), w=[rc])
        self.dma_slow("sp", self.fg[:], self.final_g_d.rearrange("(c p) -> p c", p=128), w=[rc])
        self.dma_slow("sp", self.gq[:], self.g_q_d.rearrange("(c p) -> p c", p=128), w=[rc])
        self.dma_slow("sp", self.gkv[:], self.g_kv_d.rearrange("(c p) -> p c", p=128), w=[rc])
        self.dma_slow("sp", self.gout[:], self.g_out_d.rearrange("(c p) -> p c", p=128), w=[rc])
        for l in range(4):
            for d in range(2):
                self.dma_slow("sp", self.lbraw[:, l, d, :], self.lb_d[l, d].rearrange("(c p) -> p c", p=128), w=[rc])
        ex = self.alloc(64, F32)
        sm = self.alloc(16, F32)
        exv = ex.rearrange("p (l x) -> p l x", l=4)
        self.act(exv, self.lbraw[:].rearrange("p l d c -> p l (d c)"), AF.Exp, r=[rc], w=[rc])
        self.tt("dve", sm, exv[:, 0, :], exv[:, 1, :], ALU.add, r=[rc], w=[rc])
        self.tt("dve", sm, sm, exv[:, 2, :], ALU.add, r=[rc], w=[rc])
        self.tt("dve", sm, sm, exv[:, 3, :], ALU.add, r=[rc], w=[rc])
        self.recip(sm, sm, r=[rc], w=[rc])
        omlv = self.oml[:].rearrange("p d c -> p (d c)")
        lbv = self.lb[:].rearrange("p d c -> p (d c)")
        self.tt("dve", omlv, exv[:, 0, :], sm, ALU.mult, r=[rc], w=[rc])
        self.ts("dve", lbv, omlv, -1.0, 1.0, ALU.mult, ALU.add, r=[rc], w=[rc])
        stg = [self.alloc(5120, F32) for _ in range(2)]
        stb = [self.alloc(5120, BF16) for _ in range(2)]
        rs_ = [P.res() for _ in range(2)]
        rb_ = [P.res() for _ in range(2)]
        cnt = 0
        engs = ["dve", "pool", "act"]
        jobs = []
        for name, k, n in W_SPECS:
            src = self.wd[name]
            dst = self.w16[name]
            for c in range(k // 128):
                jobs.append((src[c * 128:(c + 1) * 128, :], dst[c * 128:(c + 1) * 128, :], n))
        for l in range(4):
            for c in range(2):
                jobs.append((self.wd["ple_w"][l, c * 128:(c + 1) * 128, :], self.w16["ple_w"][l, c * 128:(c + 1) * 128, :], D))
            for c in range(8):
                jobs.append((self.wd["ple_gate_w"][l, c * 128:(c + 1) * 128, :], self.w16["ple_gate_w"][l, c * 128:(c + 1) * 128, :], D))
        for src, dst, n in jobs:
            b = cnt % 2
            self.dma("sp", stg[b][:, 0:n], src, w=[rs_[b]])
            self.cp(engs[cnt % 3], stb[b][:, 0:n], stg[b][:, 0:n], r=[rs_[b]], w=[rb_[b]])
            self.dma("act" if False else "sp", dst, stb[b][:, 0:n], r=[rb_[b]])
            cnt += 1
        P.barrier()
        if 1 in self.layers:
            self.prep_na_bias()

    def prep_na_bias(self):
        P = self.P
        overlap = (self.layers[0] == 0)
        self.aoff = 143 * 1024 if overlap else 0
        bt = self.alloc(32 * 14 * 64, BF16)
        rb = P.res()
        btv = bt.rearrange("p (h s q) -> p h s q", h=32, s=14)
        self.memset("pool", bt[:, 0:14336], NEG, w=[rb])
        self.memset("dve", bt[:, 14336:28672], NEG, w=[rb])
        qs = np.arange(64)
        win = np.clip(qs - 8, 0, 48)
        for a in range(2):
            for kc in range(64):
                valid = np.nonzero((win <= kc) & (kc < win + 16))[0]
                qlo, qhi = int(valid[0]), int(valid[-1])
                assert len(valid) == qhi - qlo + 1
                nq = qhi - qlo + 1
                j0 = 15 - kc + qlo
                assert 0 <= j0 and j0 + nq <= 31
                p = a * 64 + kc
                src = self.rpbf_d[:, a:a + 14, j0:j0 + nq]
                self.dma_slow("pool", btv[p:p + 1, :, :, qlo:qhi + 1], src.unsqueeze(0), w=[rb])
        self.P.op("pool", lambda e: e.dma_start(out=self.BT[:, :], in_=bt), r=[rb], dma=True)
        if overlap:
            self.pool_busy = True
        else:
            P.barrier()

    def load_w(self, name, k, n, sub=None):
        t = self.alloc((k // 128) * n, BF16)
        v = t.rearrange("p (c n) -> p c n", n=n)
        src = self.w16[name] if sub is None else self.w16[name][sub]
        r = self.P.res()
        kc = k // 128
        hc = max(1, kc // 2)
        srcv = src.rearrange("(c p) n -> p c n", p=128)
        self.dma("sp", v[:, 0:hc, :], srcv[:, 0:hc, :], w=[r])
        if kc > hc:
            self.dma("pool", v[:, hc:kc, :], srcv[:, hc:kc, :], w=[r])
        return v, r

    def pre_setup(self):
        P = self.P
        self.areset()
        self.bank_i = 0
        self.xt = [self.alloc(8 * TS, F32).rearrange("p (c s) -> p c s", c=8) for _ in range(2)]
        self.rxt = [P.res() for _ in range(2)]
        self.sq = [self.alloc(8 * TS, BF16).rearrange("p (c s) -> p c s", c=8) for _ in range(2)]
        self.rsq = [P.res() for _ in range(2)]
        self.hT = [self.alloc(8 * TS, BF16).rearrange("p (c s) -> p c s", c=8) for _ in range(2)]
        self.rhT = [P.res() for _ in range(2)]
        self.rstd = [self.alloc(TS, F32) for _ in range(2)]
        self.rrstd = [P.res() for _ in range(2)]

    def rms_to(self, xt, rx, nch, inv_n, gcol, out, rout, b):
        sq, rsq = self.sq[b], self.rsq[b]
        rstd, rr = self.rstd[b], self.rrstd[b]
        self.tt("dve", sq[:, 0:nch, :], xt, xt, ALU.mult, r=[rx], w=[rsq])
        pb, rpb = self.bank()
        for c in range(nch):
            self.mm(pb[:], self.cb(C_ONES), sq[:, c, :], c == 0, c == nch - 1, r=[rsq, self.rc], w=[rpb])
        self.act(rstd, pb[:], AF.Sqrt, r=[rpb, self.rc], w=[rr], scale=inv_n, bias=self.epsb[:, 0:1])
        self.recip(rstd, rstd, r=[rr], w=[rr])
        for c in range(nch):
            self.stt(out[:, c, :], xt[:, c, :], gcol(c), rstd, ALU.mult, ALU.mult, r=[rx, rr, self.rc], w=[rout])

    def pre_load(self, li, xsrc, t, i):
        b = i % 2
        xt, rx = self.xt[b], self.rxt[b]
        xv = xsrc.rearrange("(c p) s -> p c s", p=128)
        self.dma("sp", xt[:, 0:4, :], xv[:, 0:4, t * TS:(t + 1) * TS], w=[rx])
        self.dma("pool", xt[:, 4:8, :], xv[:, 4:8, t * TS:(t + 1) * TS], w=[rx])

    def pre_norm(self, li, i):
        b = i % 2
        self.rms_to(self.xt[b], self.rxt[b], 8, 1.0 / D, lambda c: self.ng[:, li, c:c + 1], self.hT[b], self.rhT[b], b)
        return self.hT[b], self.rhT[b]

    def pre_mid(self):
        f = self._pre_pending
        if f is not None:
            self._pre_pending = None
            f()

    def proj_fm(self, w, rw, col0, hT, rh, m=128):
        pb, rpb = self.bank()
        for k in range(8):
            self.mm(pb[0:m, :], w[:, k, col0:col0 + m], hT[:, k, :], k == 0, k == 7, r=[rw, rh], w=[rpb])
        return pb, rpb

    def proj_tm(self, w, rw, col0, hT, rh, sub):
        pb, rpb = self.bank()
        for k in range(8):
            self.mm(pb[:], hT[:, k, sub * 128:(sub + 1) * 128], w[:, k, col0:col0 + 512], k == 0, k == 7, r=[rw, rh], w=[rpb])
        return pb, rpb

    def z_tile(self, w, rw, zcol0, hT, rh, t, b):
        szt, rsz = self.szt[b], self.rszt[b]
        for j in range(8):
            pb, rpb = self.proj_fm(w, rw, zcol0 + j * 128, hT, rh)
            self.act(szt[:, j, :], pb[:], AF.Silu, r=[rpb], w=[rsz])
        self.dma("sp", self.SZ.rearrange("(c p) s -> p c s", p=128)[:, :, t * TS:(t + 1) * TS], szt, r=[rsz])

    def alloc_szt(self):
        self.szt = [self.alloc(8 * TS, BF16).rearrange("p (c s) -> p c s", c=8) for _ in range(2)]
        self.rszt = [self.P.res() for _ in range(2)]

    def pre0(self):
        P = self.P
        self.pre_setup()
        w, rw = self.load_w("fn_w_in", 1024, 2048)
        self.alloc_szt()
        ut = [self.alloc(4 * 1024, BF16).rearrange("p (a f) -> p a f", a=4) for _ in range(2)]
        rut = [P.res() for _ in range(2)]
        ev = 0
        for i, sq, t, hT, rh in self.pre_iter(0):
            b = i % 2
            udv = self.Ud.rearrange("(n p) f -> p n f", p=128)
            for sub in range(4):
                for ct in range(2):
                    pb, rpb = self.proj_tm(w, rw, ct * 512, hT, rh, sub)
                    self.cp("act" if ev % 2 == 0 else "dve", ut[b][:, sub, ct * 512:(ct + 1) * 512], pb[:], r=[rpb], w=[rut[b]])
                    ev += 1
            self.dma("sp", udv[:, 4 * t:4 * t + 4, :], ut[b], r=[rut[b]])
            self.pre_mid()
            self.z_tile(w, rw, 1024, hT, rh, t, b)
        P.barrier()
        self.pool_busy = False

    def mid0(self, sq):
        P = self.P
        rc = self.rc
        self.areset()
        self.bank_i = 0
        wm = self.alloc(8 * 128, F32).rearrange("p (g d) -> p g d", g=8)
        rwm = P.res()
        self.dma("sp", wm, self.fn_w_mix_d.rearrange("g m d -> m g d"), w=[rwm])
        AB = self.alloc(2 * 8 * 128, BF16).rearrange("p (x g d) -> p x g d", x=2, g=8)
        rAB = P.res()
        for x, off in ((0, C_CC), (1, C_SC)):
            for gh in range(2):
                pb, rpb = self.bank()
                for g4 in range(4):
                    g = gh * 4 + g4
                    self.mm(pb[:, g4 * 128:(g4 + 1) * 128], self.cf(off), wm[:, g, :], True, True, r=[rc, rwm], w=[rpb])
                self.cp("dve", AB[:, x, gh * 4:(gh + 1) * 4, :], pb[:].rearrange("p (g d) -> p g d", g=4), r=[rpb], w=[rAB])
        mark = self.aoff
        m1f = self.alloc(32 * 128, F32)
        m1 = self.alloc(32 * 128, BF16).rearrange("p (j m) -> p j m", j=32)
        rm1 = P.res()
        self.dma("sp", m1f, self.m1_d[:, :], w=[rm1])
        self.cp("pool", m1.rearrange("p j m -> p (j m)"), m1f, r=[rm1], w=[rm1])
        ud1 = self.alloc(32 * 1024, BF16).rearrange("p (j f) -> p j f", j=32)
        rud = P.res()
        udv = self.Ud.rearrange("(a b) f -> a b f", b=64)
        self.dma("sp", ud1[0:64, :, :], udv[:, 0:32, :], w=[rud])
        self.dma("pool", ud1[64:128, :, :], udv[:, 32:64, :], w=[rud])
        t1 = [self.alloc(8 * 1024, BF16).rearrange("p (s f) -> p s f", s=8) for _ in range(2)]
        rt1 = [P.res() for _ in range(2)]
        tdv = self.Td.rearrange("r k s f -> (r k) s f")
        ev = 0
        for sb_ in range(8):
            b = sb_ % 2
            for sl in range(8):
                s0 = sb_ * 8 + sl
                half, j = s0 // 32, s0 % 32
                for ft in range(2):
                    pb, rpb = self.bank()
                    self.mm(pb[:], m1[64 * half:64 * half + 64, j, :], ud1[64 * half:64 * half + 64, j, ft * 512:(ft + 1) * 512],
                            True, True, r=[rm1, rud], w=[rpb])
                    self.cp("act" if ev % 2 == 0 else "dve", t1[b][:, sl, ft * 512:(ft + 1) * 512], pb[:], r=[rpb], w=[rt1[b]])
                    ev += 1
            self.dma("sp", tdv[:, sb_ * 8:(sb_ + 1) * 8, :], t1[b], r=[rt1[b]])
        P.barrier()
        self.aoff = mark
        xT = self.alloc(4 * 2 * S, BF16).rearrange("p (g r k) -> p g r k", g=4, r=2)
        rxT = [P.res() for _ in range(4)]
        t3 = [self.alloc(16 * 512, BF16).rearrange("p (k f) -> p k f", k=16) for _ in range(2)]
        rt3 = [P.res() for _ in range(2)]
        ot = [self.alloc(S, BF16) for _ in range(2)]
        rot = [P.res() for _ in range(2)]
        tdr = self.Td.rearrange("r k s f -> r s k f")
        for hf in range(2):
            for kb in range(4):
                b = kb % 2
                for r_ in range(2):
                    self.dma("sp" if r_ == 0 else "pool", t3[b][64 * r_:64 * r_ + 64, :, :],
                             tdr[r_, :, kb * 16:(kb + 1) * 16, hf * 512:(hf + 1) * 512], w=[rt3[b]])
                for k4 in range(4):
                    for g in range(4):
                        pb, rpb = self.bank()
                        for kk in range(4):
                            kl = k4 * 4 + kk
                            self.mm(pb[:, kk * 128:(kk + 1) * 128], t3[b][:, kl, g * 128:(g + 1) * 128], self.cb(C_M3),
                                    True, True, r=[rt3[b], rc], w=[rpb])
                        k0 = kb * 16 + k4 * 4
                        outv = xT[:, g, :, :].rearrange("p r (k1 k0) -> p r k1 k0", k0=64)[:, :, :, k0:k0 + 4]
                        inv = pb[:].rearrange("p (a r k) -> p r k a", a=4, r=2)
                        self.cp("act" if ev % 2 == 0 else "dve", outv, inv, r=[rpb], w=[rxT[g]])
                        ev += 1
            for g in range(4):
                gg = hf * 4 + g
                b = g % 2
                for t in range(NT):
                    pb, rpb = self.bank()
                    self.mm(pb[:], AB[:, 0, gg, :], xT[:, g, 0, t * TS:(t + 1) * TS], True, False, r=[rAB, rxT[g]], w=[rpb])
                    self.mm(pb[:], AB[:, 1, gg, :], xT[:, g, 1, t * TS:(t + 1) * TS], False, True, r=[rAB, rxT[g]], w=[rpb])
                    self.cp("act" if ev % 2 == 0 else "dve", ot[b][:, t * TS:(t + 1) * TS], pb[:], r=[rpb], w=[rot[b]])
                    ev += 1
                self.dma("sp", self.O[gg * 128:(gg + 1) * 128, :], ot[b], r=[rot[b]])
        P.barrier()

    def post_layer(self, li, last, final):
        P = self.P
        self.areset()
        self.bank_i = 0
        self.bank_list = list(range(8))
        wname = ["fn_w_out", "na_w_out", "mla_w_out", "hg_w_out"][li]
        wo, rwo = self.load_w(wname, 1024, 1024)
        wg, rwg = self.load_w("ple_gate_w", 1024, 1024, sub=li)
        wp, rwp = self.load_w("ple_w", 256, 1024, sub=li)
        A = lambda cols, dt: [self.alloc(cols, dt) for _ in range(2)]
        R = lambda: [P.res() for _ in range(2)]
        v8 = lambda t_: t_.rearrange("p (c s) -> p c s", c=8)
        ot = [v8(x) for x in A(8 * TS, BF16)]; rot = R()
        sz = [v8(x) for x in A(8 * TS, BF16)]; rsz = R()
        xt = [v8(x) for x in A(8 * TS, F32)]; rx = R()
        pt = [x.rearrange("p (c s) -> p c s", c=2) for x in A(2 * TS, F32)]; rpt = R()
        ptb = [x.rearrange("p (c s) -> p c s", c=2) for x in A(2 * TS, BF16)]; rptb = R()
        g = [v8(x) for x in A(8 * TS, BF16)]; rg = R()
        x1b = [v8(x) for x in A(8 * TS, BF16)]; rx1b = R()
        sg = A(TS, F32); rsg = R()
        if final:
            sq1 = v8(self.alloc(8 * TS, BF16)); rsq1 = P.res()
            self.sq = [sq1, sq1]; self.rsq = [rsq1, rsq1]
            self.rstd = A(TS, F32); self.rrstd = R()
        NX = 4 if final else 3
        tiles = [(sq, t) for sq in range(self.nseq) for t in range(NT)]
        fm = lambda ap: ap.rearrange("(c p) s -> p c s", p=128)

        xt3 = xt + [v8(self.alloc(8 * TS, F32)) for _ in range(NX - 2)]
        rx3 = rx + [P.res() for _ in range(NX - 2)]
        ry3 = [P.res() for _ in range(NX)]

        def loads(i):
            sq, t = tiles[i]
            b = i % 2
            b3 = i % NX
            sl = slice(t * TS, (t + 1) * TS)
            xv = fm(self.xsrc_of(li, sq))
            self.dma("sp", ot[b], fm(self.SA["O"][sq])[:, :, sl], w=[rot[b]])
            self.dma("sp", sz[b], fm(self.SA["SZ"][sq])[:, :, sl], w=[rsz[b]])
            self.dma("sp", xt3[b3], xv[:, :, sl], w=[rx3[b3]])
            self.dma("sp", pt[b], fm(self.pT[li, sq])[:, :, sl], w=[rpt[b]])

        def stage_a(i):
            sq, t = tiles[i]
            b = i % 2
            b3 = i % NX
            self.tt("dve", g[b], ot[b], sz[b], ALU.mult, r=[rot[b], rsz[b]], w=[rg[b]])
            self.cp("pool", ptb[b], pt[b], r=[rpt[b]], w=[rptb[b]])
            for j in range(8):
                pb, rpb = self.bank()
                for k in range(8):
                    self.mm(pb[:], wo[:, k, j * 128:(j + 1) * 128], g[b][:, k, :], k == 0, k == 7, r=[rwo, rg[b]], w=[rpb])
                self.tt("dve", xt3[b3][:, j, :], xt3[b3][:, j, :], pb[:], ALU.add, r=[rpb, rx3[b3]], w=[rx3[b3]])
                self.cp("act", x1b[b][:, j, :], xt3[b3][:, j, :], r=[rx3[b3]], w=[rx1b[b]])

        def stage_b(i):
            sq, t = tiles[i]
            b = i % 2
            b3 = i % NX
            sl = slice(t * TS, (t + 1) * TS)
            xdv = fm(self.yT[sq] if last else self.X[sq])
            for j in range(8):
                pg, rpg = self.bank()
                for k in range(8):
                    self.mm(pg[:], wg[:, k, j * 128:(j + 1) * 128], x1b[b][:, k, :], k == 0, k == 7, r=[rwg, rx1b[b]], w=[rpg])
                pp, rpp = self.bank()
                for k in range(2):
                    self.mm(pp[:], wp[:, k, j * 128:(j + 1) * 128], ptb[b][:, k, :], k == 0, k == 1, r=[rwp, rptb[b]], w=[rpp])
                self.act(sg[b], pg[:], AF.Sigmoid, r=[rpg], w=[rsg[b]])
                self.tt("dve", sg[b], sg[b], pp[:], ALU.mult, r=[rsg[b], rpp], w=[rsg[b]])
                self.tt("pool", xt3[b3][:, j, :], xt3[b3][:, j, :], sg[b], ALU.add, r=[rsg[b], rx3[b3]], w=[rx3[b3]])
            if not final:
                stage_c(i)

        def stage_c(i):
            sq, t = tiles[i]
            b = i % 2
            b3 = i % NX
            sl = slice(t * TS, (t + 1) * TS)
            xdv = fm(self.yT[sq] if last else self.X[sq])
            if final:
                self.rms_to(xt3[b3], rx3[b3], 8, 1.0 / D, lambda c: self.fg[:, c:c + 1], xt3[b3], ry3[b3], b)
            self.dma("sp", xdv[:, 0:4, sl], xt3[b3][:, 0:4, :], r=[rx3[b3], ry3[b3]])
            self.dma("sp", xdv[:, 4:8, sl], xt3[b3][:, 4:8, :], r=[rx3[b3], ry3[b3]])

        loads(0)
        loads(1)
        stage_a(0)
        for i in range(len(tiles)):
            if i + 1 < len(tiles):
                stage_a(i + 1)
            if i + 2 < len(tiles):
                loads(i + 2)
            stage_b(i)
            if final and i >= 1:
                stage_c(i - 1)
        if final:
            stage_c(len(tiles) - 1)
        P.barrier()

    def pre1(self):
        P = self.P
        self.pre_setup()
        self.bank_list = list(range(8))
        w, rw = self.load_w("na_w_in", 1024, 4096)
        self.alloc_szt()
        qk = [self.alloc(8 * TS, BF16).rearrange("p (c s) -> p c s", c=8) for _ in range(2)]
        rqk = [P.res() for _ in range(2)]
        vt = [self.alloc(4 * 1056, BF16).rearrange("p (a h d) -> p a h d", a=4, h=32) for _ in range(2)]
        rvt = [P.res() for _ in range(2)]
        for b in range(2):
            self.memset("pool", vt[b][:, :, :, 32:33], 1.0, w=[rvt[b]])
        ev = 0
        nq = 0
        for i, sq, t, hT, rh in self.pre_iter(1):
            b = i % 2
            sl = slice(t * TS, (t + 1) * TS)
            vav = self.VA.rearrange("(n p) f -> p n f", p=128)
            for which, dst in ((0, self.QT), (1, self.KT)):
                bq = nq % 2
                nq += 1
                for j in range(8):
                    pb, rpb = self.proj_fm(w, rw, which * 1024 + j * 128, hT, rh)
                    self.cp("act" if ev % 2 == 0 else "dve", qk[bq][:, j, :], pb[:], r=[rpb], w=[rqk[bq]])
                    ev += 1
                self.dma("sp", dst.rearrange("(c p) s -> p c s", p=128)[:, :, sl], qk[bq], r=[rqk[bq]])
            self.pre_mid()
            for sub in range(4):
                for ct in range(2):
                    pb, rpb = self.proj_tm(w, rw, 2048 + ct * 512, hT, rh, sub)
                    self.cp("act" if ev % 2 == 0 else "dve", vt[b][:, sub, 16 * ct:16 * ct + 16, 0:32],
                            pb[:].rearrange("p (h d) -> p h d", d=32), r=[rpb], w=[rvt[b]])
                    ev += 1
            self.dma("pool", vav[:, 4 * t:4 * t + 4, :], vt[b].rearrange("p a h d -> p a (h d)"), r=[rvt[b]])
            self.z_tile(w, rw, 3072, hT, rh, t, b)
        P.barrier()

    def mid1(self, sq):
        P = self.P
        rc = self.rc
        self.areset()
        self.bank_i = 0
        self.bank_list = [0, 1, 2, 3, 4]
        obanks = [5, 6]
        tbank = 7
        bts = self.alloc(32 * 14 * 64, BF16).rearrange("p (h s q) -> p h s q", h=32, s=14)
        rbt = P.res()
        self.dma("sp", bts[:, 0:16], self.BT.rearrange("p (h x) -> p h x", h=32)[:, 0:16, :].rearrange("p h (s q) -> p h s q", s=14), w=[rbt])
        self.dma("pool", bts[:, 16:32], self.BT.rearrange("p (h x) -> p h x", h=32)[:, 16:32, :].rearrange("p h (s q) -> p h s q", s=14), w=[rbt])
        ktb = [self.alloc(8 * 1024, BF16).rearrange("p (c s) -> p c s", c=8) for _ in range(2)]
        rkt = [P.res() for _ in range(2)]
        vbe = self.alloc(8 * 1056, BF16).rearrange("p (m h d) -> p m h d", m=8, h=32)
        vbo = self.alloc(7 * 1056, BF16).rearrange("p (m h d) -> p m h d", m=7, h=32)
        rvb = [P.res(), P.res()]
        qraw = [self.alloc(8 * TS, BF16).rearrange("p (c s) -> p c s", c=8) for _ in range(2)]
        rqr = [P.res() for _ in range(2)]
        qm = self.alloc(8 * 4 * TS, BF16).rearrange("p (c m s) -> p c m s", c=8, m=4)
        rqm = P.res()
        NPT = 4
        ptf = [self.alloc(1152, BF16) for _ in range(NPT)]
        rpt = [P.res() for _ in range(NPT)]
        for i in range(NPT):
            self.memset("pool", ptf[i], 0.0, w=[rpt[i]])
        ptw = [x[:, 0:1152].rearrange("p (r x) -> p r x", x=576)[:, :, 0:512].rearrange("p r (k y) -> p r k y", y=128)[:, :, :, 0:64] for x in ptf]
        ptr_ = [x[:, 0:1024].rearrange("p (r k y) -> p r k y", r=2, k=4) for x in ptf]
        osb = [self.alloc(1024, BF16) for _ in range(2)]
        rosb = [P.res() for _ in range(2)]
        otb = [self.alloc(8 * TS, BF16).rearrange("p (c s) -> p c s", c=8) for _ in range(2)]
        rotb = [P.res() for _ in range(2)]
        rc8 = [self.alloc(8, F32) for _ in range(2)]
        rrc8 = [P.res() for _ in range(2)]
        ktv = self.KT.rearrange("(c p) s -> p c s", p=128)
        qtv = self.QT.rearrange("(c p) s -> p c s", p=128)
        ov = self.O.rearrange("(c p) s -> p c s", p=128)
        ev = [0]
        units = [(t, pr, hg, hl) for t in range(NT) for pr in range(4) for hg in range(4) for hl in range(8)]
        state = {}

        def band_setup(t):
            b = t % 2
            kw0 = int(np.clip(8 * t - 4, 0, 48))
            self.dma("sp", ktb[b], ktv[:, :, 64 * kw0:64 * kw0 + 1024], w=[rkt[b]])
            self.dma("pool", qraw[b], qtv[:, :, t * TS:(t + 1) * TS], w=[rqr[b]])
            for j in range(8):
                for hm in range(4):
                    e3 = ev[0] % 3
                    if e3 == 2:
                        self.act(qm[:, j, hm, :], qraw[b][:, j, :], AF.Copy, r=[rqr[b], rc], w=[rqm], scale=self.cf(C_MC + hm, 1))
                    else:
                        self.ts("dve" if e3 == 0 else "pool", qm[:, j, hm, :], qraw[b][:, j, :], self.cf(C_MC + hm, 1), 0.0,
                                ALU.mult, ALU.add, r=[rqr[b], rc], w=[rqm])
                    ev[0] += 1

        def v_setup(t, hf):
            kw0 = int(np.clip(8 * t - 4, 0, 48))
            hs_ = slice(16 * hf, 16 * hf + 16)
            cs_ = slice(528 * hf, 528 * hf + 528)
            self.dma("pool", vbe[:, :, hs_, :].rearrange("p m h d -> p m (h d)"),
                     self.VA[64 * kw0:64 * kw0 + 1024, cs_].rearrange("(m p) f -> p m f", p=128), w=[rvb[hf]])
            self.dma("sp", vbo[:, :, hs_, :].rearrange("p m h d -> p m (h d)"),
                     self.VA[64 * (kw0 + 1):64 * (kw0 + 1) + 896, cs_].rearrange("(m p) f -> p m f", p=128), w=[rvb[hf]])

        def geom(u):
            t, pr, hg, hl = u
            kw0 = int(np.clip(8 * t - 4, 0, 48))
            rows = (8 * t + 2 * pr, 8 * t + 2 * pr + 1)
            return kw0, rows

        def S_(ui):
            u = units[ui]
            t, pr, hg, hl = u
            b = t % 2
            kw0, rows = geom(u)
            h = hg * 8 + hl
            j, hm = h // 4, h % 4
            sb_, rsb = self.bank()
            sv = sb_[:].rearrange("p (r k q) -> p r k q", r=2, k=4)
            for ri, r_ in enumerate(rows):
                rs = int(np.clip(r_ - 4, 0, 56))
                ro0 = rs - r_ + 7
                self.mm(sv[:, ri, :, :], self.cb(C_ID), bts[:, h, ro0:ro0 + 7:2, :], True, False, r=[rbt, rc], w=[rsb])
                for kc in range(4):
                    koff = 64 * (rs + 2 * kc - kw0)
                    self.mm(sv[:, ri, kc, :], ktb[b][:, j, koff:koff + 128], qm[:, j, hm, (r_ - 8 * t) * 64:(r_ - 8 * t) * 64 + 64],
                            False, True, r=[rkt[b], rqm], w=[rsb])
            state[ui] = (sv, rsb)

        def PV_(ui):
            u = units[ui]
            t, pr, hg, hl = u
            b = t % 2
            kw0, rows = geom(u)
            h = hg * 8 + hl
            sv, rsb = state.pop(ui)
            pi = ui % NPT
            self.act(ptw[pi], sv, AF.Exp, r=[rsb], w=[rpt[pi]])
            gi = ui // 8
            obi = obanks[gi % 2]
            ob, rob = self.ps[obi], self.psr[obi]
            obv = ob[:, 0:264].rearrange("p (h d) -> p h d", h=8)
            for ri, r_ in enumerate(rows):
                rs = int(np.clip(r_ - 4, 0, 56))
                dlt = rs - kw0
                for kc in range(4):
                    if dlt % 2 == 0:
                        vsrc = vbe[:, dlt // 2 + kc, h, :]
                    else:
                        vsrc = vbo[:, (dlt - 1) // 2 + kc, h, :]
                    self.mm(obv[:, hl, :], ptr_[pi][:, ri, kc, :], vsrc, ri == 0 and kc == 0, ri == 1 and kc == 3,
                            r=[rpt[pi], rvb[h // 16]], w=[rob])
            ob_ = (ui // 32) % 2
            if hl == 7:
                self.recip(rc8[hg % 2], obv[:, :, 32], r=[rob], w=[rrc8[hg % 2]])
                self.tt("dve", osb[ob_][:, hg * 256:(hg + 1) * 256].rearrange("p (h d) -> p h d", h=8), obv[:, :, 0:32],
                        rc8[hg % 2].unsqueeze(2).to_broadcast([128, 8, 32]), ALU.mult, r=[rob, rrc8[hg % 2]], w=[rosb[ob_]])
                if hg == 3:
                    tb, rtb = self.ps[tbank], self.psr[tbank]
                    tbv = tb[:].bitcast(BF16)
                    for jj in range(8):
                        self.tr(tbv[:, jj * 128:(jj + 1) * 128], osb[ob_][:, jj * 128:(jj + 1) * 128], self.cb(C_ID), r=[rosb[ob_], rc], w=[rtb])
                    self.cp("dve", otb[b][:, :, pr * 128:(pr + 1) * 128], tbv.rearrange("p (c s) -> p c s", c=8), r=[rtb], w=[rotb[b]])
                    if pr == 3:
                        self.dma("sp", ov[:, :, t * TS:(t + 1) * TS], otb[b], r=[rotb[b]])

        SK = 3
        N_ = len(units)
        band_setup(0)
        v_setup(0, 0)
        v_setup(0, 1)
        for k in range(SK):
            S_(k)
        for ui in range(N_):
            nxt = ui + SK
            if nxt < N_:
                if units[nxt][0] != units[nxt - 1][0]:
                    band_setup(units[nxt][0])
                S_(nxt)
            PV_(ui)
            t_, pr_, hg_, hl_ = units[ui]
            if t_ + 1 < NT and pr_ == 3 and hl_ == 7:
                if hg_ == 1:
                    v_setup(t_ + 1, 0)
                elif hg_ == 3:
                    v_setup(t_ + 1, 1)
        self.bank_list = list(range(8))
        P.barrier()


    def pre2(self):
        P = self.P
        rc = self.rc
        self.pre_setup()
        self.bank_list = list(range(8))
        w, rw = self.load_w("mla_w_in", 1024, 1728)
        wq, rwq = self.load_w("mla_w_uq", 384, 1536)
        wkv, rwkv = self.load_w("mla_w_ukv", 256, 2048)
        self.alloc_szt()
        c3 = lambda x, n: x.rearrange("p (c s) -> p c s", c=n)
        cq = c3(self.alloc(3 * TS, F32), 3); rcq = P.res()
        ckv = c3(self.alloc(2 * TS, F32), 2); rckv = P.res()
        cqn = c3(self.alloc(3 * TS, BF16), 3); rcqn = P.res()
        ckvn = c3(self.alloc(2 * TS, BF16), 2); rckvn = P.res()
        rp = [c3(self.alloc(4 * TS, F32), 4) for _ in range(2)]; rrp = [P.res() for _ in range(2)]
        qnt = c3(self.alloc(8 * TS, BF16), 8); rqnt = P.res()
        qpt = c3(self.alloc(8 * TS, BF16), 8); rqpt = P.res()
        knt = c3(self.alloc(8 * TS, BF16), 8); rknt = P.res()
        kpt = self.alloc(TS, BF16); rkpt = P.res()
        vt = self.alloc(4 * 1024, BF16).rearrange("p (a f) -> p a f", a=4); rvt = P.res()
        t1 = self.alloc(TS, F32); rt1 = P.res()
        t2 = self.alloc(TS, F32); rt2 = P.res()
        wkv4 = wkv.rearrange("p c (h t d) -> p c h t d", h=8, t=2)
        ev = [0]

        def evac(out, in_, r, w_):
            self.cp("act" if ev[0] % 4 != 3 else "dve", out, in_, r=r, w=w_)
            ev[0] += 1

        def swapped(wt, rwt_, nk, c0, rhs, rrhs):
            pb, rpb = self.bank()
            for k in range(nk):
                self.mm(pb[0:32, :], wt[:, k, c0 + 32:c0 + 64], rhs[:, k, :], k == 0, k == nk - 1, r=[rwt_, rrhs], w=[rpb])
            for k in range(nk):
                self.mm(pb[32:64, :], wt[:, k, c0:c0 + 32], rhs[:, k, :], k == 0, k == nk - 1, r=[rwt_, rrhs], w=[rpb])
            return pb, rpb

        def rope(pa, rpa, pbs, rpbs, ct, st_, rtab, out):
            self.tt("dve", t1[0:64, :], pa[0:64, :], ct, ALU.mult, r=[rpa, rtab], w=[rt1])
            self.tt("dve", t2[0:64, :], pbs[0:64, :], st_, ALU.mult, r=[rpbs, rtab], w=[rt2])
            return t1, t2

        for i, sq, t, hT, rh in self.pre_iter(2):
            b = i % 2
            sl = slice(t * TS, (t + 1) * TS)
            qnv = self.QN.rearrange("(h p) s -> p h s", p=128)
            knv = self.KN.rearrange("(h p) s -> p h s", p=128)
            qpv = self.QP.rearrange("h p s -> p h s")
            v2v = self.V2.rearrange("(n p) f -> p n f", p=128)
            self.dma("pool", rp[b][0:64], self.rope_d[:, :, sl], w=[rrp[b]])
            for c in range(3):
                pb, rpb = self.proj_fm(w, rw, c * 128, hT, rh)
                evac(cq[:, c, :], pb[:], [rpb], [rcq])
            for c in range(2):
                pb, rpb = self.proj_fm(w, rw, 384 + c * 128, hT, rh)
                evac(ckv[:, c, :], pb[:], [rpb], [rckv])
            pa, rpa = self.proj_fm(w, rw, 640, hT, rh, m=64)
            pbs, rpbs = swapped(w, rw, 8, 640, hT, rh)
            self.tt("dve", t1[0:64, :], pa[0:64, :], rp[b][0:64, 2, :], ALU.mult, r=[rpa, rrp[b]], w=[rt1])
            self.tt("dve", t2[0:64, :], pbs[0:64, :], rp[b][0:64, 3, :], ALU.mult, r=[rpbs, rrp[b]], w=[rt2])
            self.tt("pool", kpt[0:64, :], t1[0:64, :], t2[0:64, :], ALU.add, r=[rt1, rt2], w=[rkpt])
            self.dma("sp", self.KP[:, sl], kpt[0:64, :], r=[rkpt])
            self.z_tile(w, rw, 704, hT, rh, t, b)
            self.pre_mid()
            self.rms_to(cq, rcq, 3, 1.0 / 384, lambda c: self.gq[:, c:c + 1], cqn, rcqn, b)
            self.rms_to(ckv, rckv, 2, 1.0 / 256, lambda c: self.gkv[:, c:c + 1], ckvn, rckvn, b)
            for h in range(8):
                pb, rpb = self.bank()
                for c in range(3):
                    self.mm(pb[:], wq[:, c, h * 192:h * 192 + 128], cqn[:, c, :], c == 0, c == 2, r=[rwq, rcqn], w=[rpb])
                self.act(qnt[:, h, :], pb[:], AF.Copy, r=[rpb], w=[rqnt], scale=192.0 ** -0.5)
                pa, rpa = self.bank()
                for c in range(3):
                    self.mm(pa[0:64, :], wq[:, c, h * 192 + 128:h * 192 + 192], cqn[:, c, :], c == 0, c == 2, r=[rwq, rcqn], w=[rpa])
                pbs, rpbs = swapped(wq, rwq, 3, h * 192 + 128, cqn, rcqn)
                self.tt("dve", t1[0:64, :], pa[0:64, :], rp[b][0:64, 0, :], ALU.mult, r=[rpa, rrp[b]], w=[rt1])
                self.tt("dve", t2[0:64, :], pbs[0:64, :], rp[b][0:64, 1, :], ALU.mult, r=[rpbs, rrp[b]], w=[rt2])
                self.tt("pool", qpt[0:64, h, :], t1[0:64, :], t2[0:64, :], ALU.add, r=[rt1, rt2], w=[rqpt])
            for h in range(8):
                pb, rpb = self.bank()
                for c in range(2):
                    self.mm(pb[:], wkv[:, c, h * 256:h * 256 + 128], ckvn[:, c, :], c == 0, c == 1, r=[rwkv, rckvn], w=[rpb])
                evac(knt[:, h, :], pb[:], [rpb], [rknt])
            for sub in range(4):
                for hh in range(2):
                    pb, rpb = self.bank()
                    for c in range(2):
                        self.mm(pb[:].rearrange("p (h d) -> p h d", h=4), ckvn[:, c, sub * 128:(sub + 1) * 128],
                                wkv4[:, c, 4 * hh:4 * hh + 4, 1, :], c == 0, c == 1, r=[rwkv, rckvn], w=[rpb])
                    evac(vt[:, sub, hh * 512:(hh + 1) * 512], pb[:], [rpb], [rvt])
            self.dma("sp", qnv[:, :, sl], qnt, r=[rqnt])
            self.dma("pool", qpv[:, :, sl], qpt[0:64], r=[rqpt])
            self.dma("sp", knv[:, :, sl], knt, r=[rknt])
            self.dma("pool", v2v[:, 4 * t:4 * t + 4, :], vt, r=[rvt])
        P.barrier()

    def mid2(self, sq):
        P = self.P
        rc = self.rc
        self.areset()
        self.bank_i = 0
        self.bank_list = [0, 1, 2, 3]
        kpb = self.alloc(S, BF16); rkp = P.res()
        self.dma("sp", kpb[0:64, :], self.KP[:, :], w=[rkp])
        self.dma("pool", kpb[64:128, :], self.KP[:, :], w=[rkp])
        acc = [[self.alloc(TS, F32) for _ in range(2)] for _ in range(2)]
        racc = [[P.res() for _ in range(2)] for _ in range(2)]
        kng = self.alloc(4 * S, BF16).rearrange("p (h s) -> p h s", h=4); rkn = P.res()
        vg = self.alloc(32 * 512, BF16).rearrange("p (n f) -> p n f", n=32); rvg = P.res()
        qn = [self.alloc(4 * TS, BF16).rearrange("p (h s) -> p h s", h=4) for _ in range(2)]; rqn = [P.res() for _ in range(2)]
        qp = [self.alloc(4 * TS, BF16).rearrange("p (h s) -> p h s", h=4) for _ in range(2)]; rqp = [P.res() for _ in range(2)]
        NPT = 12
        pt = [self.alloc(TS, BF16) for _ in range(NPT)]; rpt = [P.res() for _ in range(NPT)]
        rec = [self.alloc(TS, F32) for _ in range(2)]; rrec = [P.res() for _ in range(2)]
        ot = [self.alloc(TS, BF16) for _ in range(2)]; rot = [P.res() for _ in range(2)]
        qnv = self.QN.rearrange("(h p) s -> p h s", p=128)
        knv = self.KN.rearrange("(h p) s -> p h s", p=128)
        qpv = self.QP.rearrange("h p s -> p h s")
        v2v = self.V2.rearrange("(n p) f -> p n f", p=128)
        ipt = 0
        ih = 0
        sbanks = {}

        def load_q(g, t):
            b = (g * NT + t) % 2
            sl = slice(t * TS, (t + 1) * TS)
            self.dma("sp", qn[b], qnv[:, 4 * g:4 * g + 4, sl], w=[rqn[b]])
            self.dma("pool", qp[b][0:64], qpv[:, 4 * g:4 * g + 4, sl], w=[rqp[b]])
            self.dma("sp", qp[b][64:128], qpv[:, 4 * g:4 * g + 4, sl], w=[rqp[b]])

        def score2(uid, b, hl, m0):
            bs = []
            for m in (m0, m0 + 1):
                sb_, rsb = self.bank()
                self.mm(sb_[:], kng[:, hl, m * 128:(m + 1) * 128], qn[b][:, hl, :], True, False, r=[rkn, rqn[b]], w=[rsb])
                bs.append((sb_, rsb))
            for k_, m in enumerate((m0, m0 + 1)):
                sb_, rsb = bs[k_]
                p0 = 64 * k_
                self.mm(sb_[:], kpb[p0:p0 + 64, m * 128:(m + 1) * 128], qp[b][p0:p0 + 64, hl, :], False, True, r=[rkp, rqp[b]], w=[rsb])
                sbanks[(uid, m)] = (sb_, rsb)

        for g in range(2):
            self.dma("sp", kng[:, 0:2], knv[:, 4 * g:4 * g + 2, :], w=[rkn])
            self.dma("pool", kng[:, 2:4], knv[:, 4 * g + 2:4 * g + 4, :], w=[rkn])
            self.dma("sp", vg[:, 0:16], v2v[:, 0:16, 512 * g:512 * g + 512], w=[rvg])
            self.dma("pool", vg[:, 16:32], v2v[:, 16:32, 512 * g:512 * g + 512], w=[rvg])
            load_q(g, 0)
            for t in range(NT):
                if t + 1 < NT:
                    load_q(g, t + 1)
                b = (g * NT + t) % 2
                sl = slice(t * TS, (t + 1) * TS)
                for hl in range(4):
                    h = 4 * g + hl
                    uid = (g, t, hl)
                    hb = ih % 2
                    ih += 1
                    OT, rOT = self.ps[4 + hb], self.psr[4 + hb]
                    DS, rDS = self.ps[6 + hb], self.psr[6 + hb]
                    if hl < 3:
                        nxt = ((g, t, hl + 1), b, hl + 1)
                    elif t + 1 < NT:
                        nxt = ((g, t + 1, 0), 1 - b, 0)
                    else:
                        nxt = None
                    if (uid, 0) not in sbanks:
                        score2(uid, b, hl, 0)
                    for m in range(32):
                        if m % 2 == 0:
                            if m + 2 < 32:
                                score2(uid, b, hl, m + 2)
                            elif nxt is not None:
                                score2(nxt[0], nxt[1], nxt[2], 0)
                        sb_, rsb = sbanks.pop((uid, m))
                        pi = ipt % NPT
                        ipt += 1
                        self.act(pt[pi], sb_[:], AF.Exp, r=[rsb], w=[rpt[pi]])
                        self.mm(OT[:], vg[:, m, hl * 128:(hl + 1) * 128], pt[pi], m == 0, m == 31, r=[rvg, rpt[pi]], w=[rOT])
                        e_ = m % 2
                        eng = "pool" if e_ == 0 else "dve"
                        if m < 2:
                            self.cp(eng, acc[hb][e_], pt[pi], r=[rpt[pi]], w=[racc[hb][e_]])
                        else:
                            self.tt(eng, acc[hb][e_], acc[hb][e_], pt[pi], ALU.add, r=[rpt[pi], racc[hb][e_]], w=[racc[hb][e_]])
                    self.mm(DS[:], self.cf(C_ONES), acc[hb][0], True, False, r=[rc, racc[hb][0]], w=[rDS])
                    self.mm(DS[:], self.cf(C_ONES), acc[hb][1], False, True, r=[rc, racc[hb][1]], w=[rDS])
                    self.recip(rec[hb], DS[:], r=[rDS], w=[rrec[hb]])
                    self.tt("dve", ot[hb], OT[:], rec[hb], ALU.mult, r=[rOT, rrec[hb]], w=[rot[hb]])
                    self.dma("sp", self.O[h * 128:(h + 1) * 128, sl], ot[hb], r=[rot[hb]])
        self.bank_list = list(range(8))
        P.barrier()

    def pre3(self):
        P = self.P
        rc = self.rc
        self.pre_setup()
        self.bank_list = list(range(8))
        w, rw = self.load_w("hg_w_in", 1024, 5120)
        self.alloc_szt()
        qst = self.alloc(8 * TS, BF16).rearrange("p (c s) -> p c s", c=8); rqst = P.res()
        lf = self.alloc(8 * TS, F32).rearrange("p (c s) -> p c s", c=8); rlf = P.res()
        vt = self.alloc(4 * 1024, BF16).rearrange("p (a f) -> p a f", a=4); rvt = P.res()
        ev = 0
        for i, sq, t, hT, rh in self.pre_iter(3):
            b = i % 2
            sl = slice(t * TS, (t + 1) * TS)
            qsv = self.QS.rearrange("(c p) s -> p c s", p=128)
            viv = self.VI.rearrange("(n p) f -> p n f", p=128)
            for j in range(8):
                pb, rpb = self.proj_fm(w, rw, j * 128, hT, rh)
                self.act(qst[:, j, :], pb[:], AF.Silu, r=[rpb], w=[rqst])
            self.dma("sp", qsv[:, :, sl], qst, r=[rqst])
            for d in range(2):
                for j in range(8):
                    pb, rpb = self.proj_fm(w, rw, 1024 * (1 + d) + j * 128, hT, rh)
                    self.act(lf[:, j, :], pb[:], AF.Sigmoid, r=[rpb], w=[rlf])
                    self.ts("dve", lf[:, j, :], lf[:, j, :], self.oml[:, d, j:j + 1], self.lb[:, d, j:j + 1], ALU.mult, ALU.add,
                            r=[rlf, rc], w=[rlf])
                self.act(lf, lf, AF.Ln, r=[rlf], w=[rlf])
                lfv = self.LF[d].rearrange("(c p) s -> p c s", p=128)
                self.dma("sp", lfv[:, 0:4, sl], lf[:, 0:4, :], r=[rlf])
                self.dma("pool", lfv[:, 4:8, sl], lf[:, 4:8, :], r=[rlf])
                if d == 0:
                    self.pre_mid()
            for sub in range(4):
                for ct in range(2):
                    pb, rpb = self.proj_tm(w, rw, 3072 + ct * 512, hT, rh, sub)
                    self.cp("act" if ev % 2 == 0 else "dve", vt[:, sub, ct * 512:(ct + 1) * 512], pb[:], r=[rpb], w=[rvt])
                    ev += 1
            self.dma("pool", viv[:, 4 * t:4 * t + 4, :], vt, r=[rvt])
            self.z_tile(w, rw, 4096, hT, rh, t, b)
        P.barrier()

    def mid3(self, sq):
        P = self.P
        rc = self.rc
        self.areset()
        self.bank_i = 0
        self.bank_list = [0, 1, 2, 3, 4, 5]
        L = 128
        NCH = 32
        rmask = self.alloc(1024, BF16); rrm = P.res()
        self.memset("pool", rmask, 1.0, w=[rrm])
        self.memset("pool", rmask.rearrange("p (n l) -> p n l", l=L)[:, :, 0:1], 0.0, w=[rrm])
        qs = self.alloc(S, BF16); rqs = P.res()
        lf = self.alloc(S, F32); rlf = P.res()
        C = self.alloc(S, F32); rC = P.res()
        E = self.alloc(S, F32); rE = P.res()
        qtb = [[self.alloc(S, BF16) for _ in range(2)] for _ in range(2)]; rqtb = [[P.res() for _ in range(2)] for _ in range(2)]
        ktb_ = [[self.alloc(S, BF16) for _ in range(2)] for _ in range(2)]; rktb = [[P.res() for _ in range(2)] for _ in range(2)]
        eclb = [[self.alloc(NCH, F32) for _ in range(2)] for _ in range(2)]; reclb = [[P.res() for _ in range(2)] for _ in range(2)]
        vhb = [self.alloc(S, BF16).rearrange("p (n d) -> p n d", n=NCH) for _ in range(2)]; rvhb = [P.res() for _ in range(2)]
        ktm = [self.alloc(S, BF16).rearrange("p (n c) -> p n c", n=NCH) for _ in range(2)]; rktm = [P.res() for _ in range(2)]
        Sb = [self.alloc(S, BF16).rearrange("p (n d) -> p n d", n=NCH) for _ in range(2)]; rSb = [P.res() for _ in range(2)]
        Wst = [[self.alloc(128, F32) for _ in range(2)] for _ in range(2)]; rW = [[P.res() for _ in range(2)] for _ in range(2)]
        oT = self.alloc(S, BF16); roT = P.res()
        Am4 = [self.alloc(4 * 256, BF16).rearrange("p (k x) -> p k x", k=4) for _ in range(2)]; rAm4 = [P.res() for _ in range(2)]
        on4 = [self.alloc(512, BF16) for _ in range(2)]; ron4 = [P.res() for _ in range(2)]
        sq4 = [self.alloc(512, F32) for _ in range(2)]; rsq4 = [P.res() for _ in range(2)]
        ss = self.alloc(NCH, F32); rss = P.res()
        rs_ = self.alloc(NCH, F32); rrs = P.res()
        viv = self.VI.rearrange("(n p) f -> p n f", p=128)
        ch = lambda x, n: x[:, n * L:(n + 1) * L]

        def scan_op(dst, src, r, w):
            for q4 in range(4):
                sl_ = slice(q4 * 1024, (q4 + 1) * 1024)
                self.P.op("dve", lambda e, sl_=sl_: e.tensor_tensor_scan(out=dst[:, sl_], data0=rmask, data1=src[:, sl_], initial=0.0,
                                                                          op0=ALU.mult, op1=ALU.add), r=r, w=w)

        def phaseA(h, sl):
            hs = slice(h * 128, (h + 1) * 128)
            qt, kt, ecl = qtb[sl], ktb_[sl], eclb[sl]
            rqt, rkt, recl = rqtb[sl], rktb[sl], reclb[sl]
            T = []
            T.append(lambda: self.dma("sp", qs, self.QS[hs, :], w=[rqs]))
            T.append(lambda: self.dma("pool", vhb[sl], viv[:, :, hs], w=[rvhb[sl]]))
            for d in range(2):
                T.append(lambda d=d: self.dma("sp", lf, self.LF[d, hs, :], w=[rlf]))
                T.append(lambda: scan_op(C, lf, [rrm, rlf], [rC]))
                Cl = C.rearrange("p (n l) -> p n l", l=L)[:, :, L - 1]
                if d == 0:
                    T.append(lambda: self.act(E, C, AF.Exp, r=[rC], w=[rE]))
                    T.append(lambda: self.cp("dve", ecl[0], E.rearrange("p (n l) -> p n l", l=L)[:, :, L - 1], r=[rE], w=[recl[0]]))
                    T.append(lambda: self.act(C, C, AF.Exp, r=[rC], w=[rC], scale=-1.0))
                else:
                    T.append(lambda: self.act(ecl[1], Cl, AF.Exp, r=[rC], w=[recl[1]]))
                    T.append(lambda: self.tt("dve", C, C, lf, ALU.subtract, r=[rC, rlf], w=[rC]))
                    T.append(lambda: self.act(E, C, AF.Exp, r=[rC], w=[rE], scale=-1.0))
                    T.append(lambda: self.act(C, C, AF.Exp, r=[rC], w=[rC]))
                T.append(lambda d=d: self.tt("dve", qt[d], qs, E, ALU.mult, r=[rqs, rE], w=[rqt[d]]))
                T.append(lambda: self.act(E, lf, AF.Exp, r=[rlf], w=[rE]))
                T.append(lambda: self.ts("pool", E, E, -1.0, 1.0, ALU.mult, ALU.add, r=[rE], w=[rE]))
                T.append(lambda d=d: self.tt("pool", kt[d], E, C, ALU.mult, r=[rE, rC], w=[rkt[d]]))
            return T

        pend = phaseA(0, 0)
        for f in pend:
            f()
        for h in range(8):
            sl = h % 2
            hs = slice(h * 128, (h + 1) * 128)
            qt, kt, ecl, vh = qtb[sl], ktb_[sl], eclb[sl], vhb[sl]
            rqt, rkt, recl, rvh = rqtb[sl], rktb[sl], reclb[sl], rvhb[sl]
            pend = phaseA(h + 1, 1 - sl) if h + 1 < 8 else []

            def pop(n=1):
                for _ in range(n):
                    if pend:
                        pend.pop(0)()

            ev = 0
            for d in range(2):
                for n8 in range(4):
                    tb, rtb = self.bank()
                    tbv = tb[:].bitcast(BF16)
                    for k in range(8):
                        n = n8 * 8 + k
                        self.tr(tbv[:, k * 128:(k + 1) * 128], ch(kt[d], n), self.cb(C_ID), r=[rkt[d], rc], w=[rtb])
                    self.cp("act" if ev % 2 == 0 else "dve", ktm[d][:, n8 * 8:(n8 + 1) * 8, :].rearrange("p n c -> p (n c)"), tbv, r=[rtb], w=[rktm[d]])
                    ev += 1
            self.memset("pool", Sb[0][:, 0, :], 0.0, w=[rSb[0]])
            self.memset("pool", Sb[1][:, NCH - 1, :], 0.0, w=[rSb[1]])
            ub = [None, None]
            wi = [0, 0]
            for i in range(NCH):
                for d in range(2):
                    n = i if d == 0 else NCH - 1 - i
                    if i % 4 == 0:
                        ub[d] = self.bank()
                        for k in range(4):
                            nn = i + k if d == 0 else NCH - 1 - i - k
                            self.mm(ub[d][0][:, k * 128:(k + 1) * 128], ktm[d][:, nn, :], vh[:, nn, :], True, True, r=[rktm[d], rvh], w=[ub[d][1]])
                    U = ub[d][0][:, (i % 4) * 128:(i % 4 + 1) * 128]
                    rU = ub[d][1]
                    cur = wi[d]
                    if i == 0:
                        self.cp("dve", Wst[d][cur], U, r=[rU], w=[rW[d][cur]])
                    else:
                        ei = (i - 1) if d == 0 else n
                        esc = ecl[d][:, ei:ei + 1]
                        self.act(Sb[d][:, n, :], Wst[d][cur], AF.Copy, r=[rW[d][cur], recl[d]], w=[rSb[d]], scale=esc)
                        nxt = 1 - cur
                        self.stt(Wst[d][nxt], Wst[d][cur], esc, U, ALU.mult, ALU.add, r=[rW[d][cur], recl[d], rU], w=[rW[d][nxt]])
                        wi[d] = nxt
                if i % 4 == 3:
                    pop(1)
            mask2 = self.cf(C_MU, 256).unsqueeze(1).to_broadcast([128, 2, 256])
            for gq in range(NCH // 4):
                gi = gq % 2
                for half in range(2):
                    ab, rab = self.bank()
                    for k2 in range(2):
                        n = gq * 4 + half * 2 + k2
                        self.mm(ab[:, k2 * 256:k2 * 256 + 128], ch(kt[0], n), ch(qt[0], n), True, True, r=[rkt[0], rqt[0]], w=[rab])
                        self.mm(ab[:, k2 * 256 + 128:k2 * 256 + 256], ch(kt[1], n), ch(qt[1], n), True, True, r=[rkt[1], rqt[1]], w=[rab])
                    self.tt("dve", Am4[gi][:, half * 2:half * 2 + 2, :], ab[:].rearrange("p (k x) -> p k x", k=2), mask2, ALU.mult,
                            r=[rab, rc], w=[rAm4[gi]])
                ob, rob = self.bank()
                ov4 = ob[:].rearrange("p (k d) -> p k d", k=4)
                for k in range(4):
                    n = gq * 4 + k
                    o = ov4[:, k, :]
                    self.mm(o, Am4[gi][:, k, 0:128], vh[:, n, :], True, False, r=[rAm4[gi], rvh], w=[rob])
                    self.mm(o, ch(qt[0], n), Sb[0][:, n, :], False, False, r=[rqt[0], rSb[0]], w=[rob])
                    self.mm(o, Am4[gi][:, k, 128:256], vh[:, n, :], False, False, r=[rAm4[gi], rvh], w=[rob])
                    self.mm(o, ch(qt[1], n), Sb[1][:, n, :], False, True, r=[rqt[1], rSb[1]], w=[rob])
                self.act(sq4[gi], ob[:], AF.Square, r=[rob], w=[rsq4[gi]])
                ssg = ss[:, gq * 4:gq * 4 + 4]
                rsg = rs_[:, gq * 4:gq * 4 + 4]
                self.P.op("dve", lambda e, o_=ssg, i_=sq4[gi].rearrange("p (k d) -> p k d", k=4): e.tensor_reduce(out=o_, in_=i_, axis=AX.X, op=ALU.add),
                          r=[rsq4[gi]], w=[rss])
                self.act(rsg, ssg, AF.Sqrt, r=[rss, rc], w=[rrs], scale=1.0 / 128, bias=self.epsb[:, 0:1])
                self.recip(rsg, rsg, r=[rrs], w=[rrs])
                self.tt("dve", on4[gi].rearrange("p (k d) -> p k d", k=4), ov4, rsg.unsqueeze(2).to_broadcast([128, 4, 128]), ALU.mult,
                        r=[rob, rrs], w=[ron4[gi]])
                if gq % 2 == 0:
                    ti = 6 + (gq // 2) % 2
                    tb, rtb = self.ps[ti], self.psr[ti]
                    tbv = tb[:].bitcast(BF16)
                for k in range(4):
                    col = ((gq % 2) * 4 + k) * 128
                    self.tr(tbv[:, col:col + 128], on4[gi][:, k * 128:(k + 1) * 128], self.cb(C_ID), r=[ron4[gi], rc], w=[rtb])
                if gq % 2 == 1:
                    n0 = (gq - 1) * 4
                    self.act(oT[:, n0 * L:(n0 + 8) * L], tbv, AF.Copy, r=[rtb, rc], w=[roT], scale=self.gout[:, h:h + 1])
                pop(2)
            pop(100)
            self.dma("sp", self.O[hs, :], oT, r=[roT])
        self.bank_list = list(range(8))
        P.barrier()


_CACHE = {}


def _common_inputs(inp):
    cst, m1, rope = make_consts()
    d = {"cst": cst, "m1": m1, "rope": rope}
    for name, k, n in W_SPECS:
        d[name] = np.ascontiguousarray(np.asarray(inp[name], np.float32)[0])
    d["ple_w"] = np.ascontiguousarray(np.asarray(inp["ple_w"], np.float32))
    d["ple_gate_w"] = np.ascontiguousarray(np.asarray(inp["ple_gate_w"], np.float32))
    d["norm_g"] = np.ascontiguousarray(np.asarray(inp["norm_g"], np.float32))
    d["final_g"] = np.ascontiguousarray(np.asarray(inp["final_g"], np.float32))
    d["fn_w_mix"] = np.ascontiguousarray(np.asarray(inp["fn_w_mix"], np.float32)[0])
    d["rpb_flip"] = np.ascontiguousarray(np.asarray(inp["na_rpb"], np.float32)[0][:, :, ::-1])
    d["mla_g_q"] = np.ascontiguousarray(np.asarray(inp["mla_g_q"], np.float32)[0])
    d["mla_g_kv"] = np.ascontiguousarray(np.asarray(inp["mla_g_kv"], np.float32)[0])
    d["hg_lb_raw"] = np.ascontiguousarray(np.asarray(inp["hg_lb_raw"], np.float32))
    d["hg_g_out"] = np.ascontiguousarray(np.asarray(inp["hg_g_out"], np.float32)[0])
    return d


def run_seqs(inp, xs, ps, nseq, layers, final_norm, ncores, trace=False):
    key = (nseq, tuple(layers), final_norm)
    if key not in _CACHE:
        bld = Builder(nseq, layers, final_norm)
        nc = bld.build()
        _CACHE[key] = (nc, bld.stats)
    nc, stats = _CACHE[key]
    com = _common_inputs(inp)
    in_maps = []
    for c in range(ncores):
        m = dict(com)
        m["xT"] = np.ascontiguousarray(np.transpose(xs[c], (0, 2, 1)))
        m["pT"] = np.ascontiguousarray(np.transpose(ps[c], (0, 1, 3, 2)))
        in_maps.append(m)
    if trace:
        res = run_bass_kernel_spmd(nc, in_maps, core_ids=list(range(ncores)), trace=True)
        print('EXEC_TIME_NS', res.exec_time_ns)
    else:
        res = run_bass_kernel_spmd(nc, in_maps, core_ids=list(range(ncores)))
    return [np.transpose(r["yT"], (0, 2, 1)) for r in res.results]


def kernel(**inp):
    xp = np.asarray(inp["x_prompt"], np.float32)
    xsm = np.asarray(inp["x_sample"], np.float32)
    pp = np.asarray(inp["p_prompt"], np.float32)
    psm = np.asarray(inp["p_sample"], np.float32)
    xall = np.concatenate([xp, xsm], 0)
    pall = np.concatenate([pp, psm], 1)
    nseq = 3
    idx = [[c, c + 8, (c + 16) if c + 16 < 20 else c] for c in range(8)]
    xs = [xall[idx[c]] for c in range(8)]
    ps = [pall[:, idx[c]] for c in range(8)]
    outs = run_seqs(inp, xs, ps, nseq, [0, 1, 2, 3], True, 8)
    yall = np.zeros_like(xall)
    for c in range(8):
        for s_, gi in enumerate(idx[c]):
            if s_ < 2 or c + 16 < 20:
                yall[gi] = outs[c][s_]
    return (np.ascontiguousarray(yall[:4]), np.ascontiguousarray(yall[4:]))
```
